# Optimizing a Trainium2 kernel written in Bass

```python
import math
import jax, jax.numpy as jnp
from jax import lax
import numpy as np


D_MODEL = 1024
BATCH = 4
SEQ = 4096
DEPTH = 1

GRID_W = 64
NA_HEADS = 8
NA_DH = 64
NA_ROWS = 8
NA_COLS = 16
DF_HEADS = 4
DF_DH = 64
Q_BLOCK = 128
NA_WIDTH = NA_HEADS * NA_DH
DF_WIDTH = DF_HEADS * 2 * DF_DH
MIX_WIDTH = NA_WIDTH + DF_WIDTH
IN_COLS = 3 * NA_WIDTH + 3 * DF_WIDTH
D_FF = 2816
DEEPNORM_ALPHA = (2.0 * DEPTH) ** 0.25
DEEPNORM_BETA = (8.0 * DEPTH) ** -0.25
LN_EPS = 1e-5

kernel_name = "hybrid_na_diffattn_macaron_deepnorm"


def layer_norm(x, g, b):
    xf = x.astype(jnp.float32)
    mu = jnp.mean(xf, axis=-1, keepdims=True)
    xc = xf - mu
    var = jnp.mean(xc * xc, axis=-1, keepdims=True)
    return (xc * lax.rsqrt(var + LN_EPS) * g + b).astype(x.dtype)


def rms_norm(x, g):
    xf = x.astype(jnp.float32)
    y = xf * lax.rsqrt(jnp.mean(xf * xf, axis=-1, keepdims=True) + LN_EPS)
    return (y * g).astype(x.dtype)


def swiglu(x, w_gate, w_up, w_down):
    return (jax.nn.silu(x @ w_gate) * (x @ w_up)) @ w_down


def neighbourhood_attention(q, k, v, rpb):
    B, T, H, dh = q.shape
    rows = T // GRID_W
    wr = min(NA_ROWS, rows)
    scale = dh ** -0.5
    qg = q.reshape(B, rows, GRID_W, H, dh).transpose(1, 0, 3, 2, 4)
    kg = k.reshape(B, rows, GRID_W, H, dh).transpose(0, 3, 1, 2, 4)
    vg = v.reshape(B, rows, GRID_W, H, dh).transpose(0, 3, 1, 2, 4)
    cols = jnp.arange(GRID_W)
    c0 = jnp.clip(cols - NA_COLS // 2, 0, GRID_W - NA_COLS)
    col_idx = c0[:, None] + jnp.arange(NA_COLS)
    dc = col_idx - cols[:, None] + (NA_COLS - 1)

    def row_block(args):
        r, qr = args
        r0 = jnp.clip(r - wr // 2, 0, rows - wr)
        kb = lax.dynamic_slice_in_dim(kg, r0, wr, axis=2)[:, :, :, col_idx]
        vb = lax.dynamic_slice_in_dim(vg, r0, wr, axis=2)[:, :, :, col_idx]
        dr = r0 + jnp.arange(wr) - r + (NA_ROWS - 1)
        bias = rpb[:, dr[None, :, None], dc[:, None, :]]
        s = jnp.einsum('bhcd,bhwcjd->bhcwj', qr, kb).astype(jnp.float32) * scale + bias
        p = jax.nn.softmax(s.reshape(B, H, GRID_W, wr * NA_COLS), axis=-1)
        p = p.reshape(B, H, GRID_W, wr, NA_COLS).astype(v.dtype)
        return jnp.einsum('bhcwj,bhwcjd->bhcd', p, vb)

    out = lax.map(row_block, (jnp.arange(rows), qg))
    return out.transpose(1, 0, 3, 2, 4).reshape(B, T, H * dh)


def differential_attention(q, k, v, lam, subln_g, lam_init):
    B, T, H, _, dh = q.shape
    nb = T // Q_BLOCK
    scale = dh ** -0.5
    slopes = jnp.exp2(-8.0 * jnp.arange(1, H + 1, dtype=jnp.float32) / H)
    qb = q.reshape(B, nb, Q_BLOCK, H, 2, dh).transpose(1, 0, 2, 3, 4, 5)
    kpos = jnp.arange(T)

    def block(args):
        i, qi = args
        qpos = i * Q_BLOCK + jnp.arange(Q_BLOCK)
        dist = jnp.abs(qpos[:, None] - kpos[None, :]).astype(jnp.float32)
        alibi = -slopes[:, None, None] * dist
        s = jnp.einsum('bqhmd,bkhmd->bhmqk', qi, k).astype(jnp.float32) * scale
        p = jax.nn.softmax(s + alibi[None, :, None], axis=-1)
        a = (p[:, :, 0] - lam * p[:, :, 1]).astype(v.dtype)
        return jnp.einsum('bhqk,bkhe->bqhe', a, v)

    o = lax.map(block, (jnp.arange(nb), qb))
    o = o.transpose(1, 0, 2, 3, 4).reshape(B, T, H, 2 * dh)
    o = rms_norm(o, subln_g) * (1.0 - lam_init)
    return o.reshape(B, T, H * 2 * dh)


def hybrid_mixer(h, w_in, w_out, na_rpb, lq1, lk1, lq2, lk2, subln_g, lam_init):
    B, T, _ = h.shape
    p = h @ w_in
    splits = np.cumsum([NA_WIDTH, NA_WIDTH, NA_WIDTH, DF_WIDTH, DF_WIDTH])
    na_q, na_k, na_v, df_q, df_k, df_v = jnp.split(p, splits, axis=-1)
    na_out = neighbourhood_attention(
        na_q.reshape(B, T, NA_HEADS, NA_DH),
        na_k.reshape(B, T, NA_HEADS, NA_DH),
        na_v.reshape(B, T, NA_HEADS, NA_DH), na_rpb)
    lam = (jnp.exp(jnp.sum(lq1.astype(jnp.float32) * lk1.astype(jnp.float32)))
           - jnp.exp(jnp.sum(lq2.astype(jnp.float32) * lk2.astype(jnp.float32)))
           + lam_init)
    df_out = differential_attention(
        df_q.reshape(B, T, DF_HEADS, 2, DF_DH),
        df_k.reshape(B, T, DF_HEADS, 2, DF_DH),
        df_v.reshape(B, T, DF_HEADS, 2 * DF_DH), lam, subln_g, lam_init)
    return jnp.concatenate([na_out, df_out], axis=-1) @ w_out


def setup_inputs(seed: int = 0) -> dict:
    key = jax.random.key(seed)
    ks = jax.random.split(key, 24)
    D, L = D_MODEL, DEPTH

    def nrm(k, shape, scale):
        return jax.random.normal(k, shape, jnp.float32) * scale

    beta = DEEPNORM_BETA
    col_scale = jnp.concatenate([
        jnp.ones((2 * NA_WIDTH,), jnp.float32), jnp.full((NA_WIDTH,), beta, jnp.float32),
        jnp.ones((2 * DF_WIDTH,), jnp.float32), jnp.full((DF_WIDTH,), beta, jnp.float32)])
    return {
        "x": nrm(ks[0], (BATCH, SEQ, D), 1.0),
        "ln1_g": 1.0 + nrm(ks[1], (L, D), 0.02),
        "ln1_b": nrm(ks[2], (L, D), 0.02),
        "ffn1_w_gate": nrm(ks[3], (L, D, D_FF), beta * D ** -0.5),
        "ffn1_w_up": nrm(ks[4], (L, D, D_FF), beta * D ** -0.5),
        "ffn1_w_down": nrm(ks[5], (L, D_FF, D), beta * D_FF ** -0.5),
        "w_in": nrm(ks[6], (L, D, IN_COLS), D ** -0.5) * col_scale,
        "na_rpb": nrm(ks[7], (L, NA_HEADS, 2 * NA_ROWS - 1, 2 * NA_COLS - 1), 0.1),
        "diff_lambda_q1": nrm(ks[8], (L, DF_DH), 0.1),
        "diff_lambda_k1": nrm(ks[9], (L, DF_DH), 0.1),
        "diff_lambda_q2": nrm(ks[10], (L, DF_DH), 0.1),
        "diff_lambda_k2": nrm(ks[11], (L, DF_DH), 0.1),
        "diff_subln_g": 1.0 + nrm(ks[12], (L, 2 * DF_DH), 0.02),
        "w_out": nrm(ks[13], (L, MIX_WIDTH, D), beta * MIX_WIDTH ** -0.5),
        "ln2_g": 1.0 + nrm(ks[14], (L, D), 0.02),
        "ln2_b": nrm(ks[15], (L, D), 0.02),
        "ffn2_w_gate": nrm(ks[16], (L, D, D_FF), beta * D ** -0.5),
        "ffn2_w_up": nrm(ks[17], (L, D, D_FF), beta * D ** -0.5),
        "ffn2_w_down": nrm(ks[18], (L, D_FF, D), beta * D_FF ** -0.5),
        "ln3_g": 1.0 + nrm(ks[19], (L, D), 0.02),
        "ln3_b": nrm(ks[20], (L, D), 0.02),
    }


def reference(x, ln1_g, ln1_b, ffn1_w_gate, ffn1_w_up, ffn1_w_down, w_in, na_rpb,
              diff_lambda_q1, diff_lambda_k1, diff_lambda_q2, diff_lambda_k2, diff_subln_g,
              w_out, ln2_g, ln2_b, ffn2_w_gate, ffn2_w_up, ffn2_w_down, ln3_g, ln3_b):
    a = DEEPNORM_ALPHA
    for l in range(DEPTH):
        lam_init = 0.8 - 0.6 * math.exp(-0.3 * l)
        x = layer_norm(a * x + 0.5 * swiglu(x, ffn1_w_gate[l], ffn1_w_up[l], ffn1_w_down[l]),
                       ln1_g[l], ln1_b[l])
        mix = hybrid_mixer(x, w_in[l], w_out[l], na_rpb[l],
                           diff_lambda_q1[l], diff_lambda_k1[l],
                           diff_lambda_q2[l], diff_lambda_k2[l],
                           diff_subln_g[l], lam_init)
        x = layer_norm(a * x + mix, ln2_g[l], ln2_b[l])
        x = layer_norm(a * x + 0.5 * swiglu(x, ffn2_w_gate[l], ffn2_w_up[l], ffn2_w_down[l]),
                       ln3_g[l], ln3_b[l])
    return x
```

```python
import numpy as np
from contextlib import ExitStack
import concourse.bass as bass
import concourse.mybir as mybir
from concourse.bass_utils import run_bass_kernel_spmd
import ml_dtypes

F32, BF16 = mybir.dt.float32, mybir.dt.bfloat16
AF = mybir.ActivationFunctionType
ALU = mybir.AluOpType

D = 1024
DFF = 2816
NFC = DFF // 128
SEQ = 4096
OWN = 2048
ALPHA = 2.0 ** 0.25
EPS = 1e-5
LAM_INIT = 0.2
NKT_NA = 18


def _flat(toks):
    out = []
    for t in toks:
        if t is None:
            continue
        if isinstance(t, list):
            out.extend(_flat(t))
        else:
            out.append(t)
    return out


class Eng:
    def __init__(self, nc, e, name, es):
        self.e = e
        self.name = name
        self.sem = es.enter_context(nc.semaphore("sem_" + name))
        self.n = 0
        self.seen = {}

    def wait(self, *toks):
        best = {}
        for src, v in _flat(list(toks)):
            if best.get(id(src), (None, 0))[1] < v:
                best[id(src)] = (src, v)
        for src, v in best.values():
            if self.seen.get(id(src), 0) >= v:
                continue
            self.e.wait_ge(src.sem, v)
            self.seen[id(src)] = v

    def mark(self, ins):
        ins.then_inc(self.sem, 1)
        self.n += 1
        return (self, self.n)


class Slot:
    def __init__(self, nc, name, es):
        self.sem = es.enter_context(nc.semaphore("dsem_" + name))
        self.n = 0
        self.busy = False

    def dma(self, q, out, in_):
        q.e.dma_start(out=out, in_=in_).then_inc(self.sem, 16)
        self.n += 16
        return (self, self.n)


class K:
    def slots(self, es, n):
        got = []
        for sl in self.slot_pool:
            if not sl.busy and len(got) < n:
                sl.busy = True
                got.append(sl)
        while len(got) < n:
            sl = Slot(self.nc, f"p{len(self.slot_pool)}", self.es_global)
            sl.busy = True
            self.slot_pool.append(sl)
            got.append(sl)

        def release():
            for sl in got:
                sl.busy = False
        es.callback(release)
        return got


def barrier(k, toks):
    toks = _flat(toks) + [(e, e.n) for e in k.engs if e.n > 0]
    for e in k.engs:
        e.wait(toks)


def copy_cast(eng, k, out, in_):
    if eng is k.act:
        return eng.e.activation(out=out, in_=in_, func=AF.Copy)
    return eng.e.tensor_copy(out=out, in_=in_)


class WeightLoader:
    def __init__(self, k, es, name, jobs, nslots=3, width=1408):
        nc = k.nc
        self.k = k
        self.jobs = jobs
        self.stg = [es.enter_context(nc.sbuf_tensor(f"{name}_stg{i}", [128, width], F32)) for i in range(nslots)]
        self.slots = k.slots(es, nslots)
        self.cast_tok = [None] * nslots
        self.engs = [k.dve, k.pool, k.act]
        self.toks = []
        self.i = 0
        k.last_stg = self.stg

    def emit(self, n):
        k = self.k
        nslots = len(self.stg)
        for _ in range(n):
            if self.i >= len(self.jobs):
                return
            i = self.i
            self.i += 1
            dst, src = self.jobs[i]
            s = i % nslots
            if len(src.shape) == 3:
                nel = src.shape[1] * src.shape[2]
                sv = self.stg[s][:, :nel].rearrange("p (a b) -> p a b", b=src.shape[2])
            else:
                nel = src.shape[-1]
                sv = self.stg[s][:, :nel]
            k.sp.wait(self.cast_tok[s])
            lt = self.slots[s].dma(k.sp, sv, src)
            e = self.engs[i % 3]
            e.wait(lt)
            self.cast_tok[s] = e.mark(copy_cast(e, k, dst, sv))
            self.toks.append(self.cast_tok[s])

    def done(self):
        return self.i >= len(self.jobs)


def load_cast_weights(k, es, name, jobs, nslots=3, width=1408):
    wl = WeightLoader(k, es, name, jobs, nslots, width)
    wl.emit(len(jobs))
    return wl.toks


def load_bf16_weights(k, q, slot, jobs):
    tok = None
    for dst, src in jobs:
        tok = slot.dma(q, dst, src)
    return tok


class BgCast:
    def __init__(self, k, es, name, jobs, in_bufs, first_tok):
        nc = k.nc
        self.k = k
        self.jobs = jobs
        self.inb = in_bufs
        self.outb = [es.enter_context(nc.sbuf_tensor(f"{name}_bgo{i}", [128, 1408], BF16)) for i in range(2)]
        self.in_slot = k.slots(es, 2)
        self.out_slot = k.slots(es, 2)
        self.load_tok = [first_tok, first_tok]
        self.cast_tok = [None, None]
        self.store_tok = [None, None]
        self.i = 0

    def step(self):
        k = self.k
        i = self.i
        n_jobs = len(self.jobs)
        if i > n_jobs + 1:
            return
        self.i += 1
        if 0 <= i - 2 < n_jobs:
            j = i - 2
            n = self.jobs[j][0].shape[-1]
            k.act.wait(self.cast_tok[j % 2])
            self.store_tok[j % 2] = self.out_slot[j % 2].dma(k.act, self.jobs[j][1], self.outb[j % 2][:, :n])
        if i < n_jobs:
            n = self.jobs[i][0].shape[-1]
            k.act.wait(self.cast_tok[i % 2], self.load_tok[i % 2] if i < 2 else None)
            self.load_tok[i % 2] = self.in_slot[i % 2].dma(k.act, self.inb[i % 2][:, :n], self.jobs[i][0])
        if 0 <= i - 1 < n_jobs:
            j = i - 1
            n = self.jobs[j][0].shape[-1]
            k.act.wait(self.load_tok[j % 2], self.store_tok[j % 2])
            self.cast_tok[j % 2] = k.act.mark(k.act.e.activation(out=self.outb[j % 2][:, :n], in_=self.inb[j % 2][:, :n], func=AF.Copy))

    def finish(self):
        while self.i <= len(self.jobs) + 1:
            self.step()
        return [t for t in self.store_tok if t is not None]


def ln_part_a(k, ctx, t, psum_halves, xres, xscale, eps, dst_ap, pre_toks):
    nsl = ctx["n"]
    dve, pool, sp = k.dve, k.pool, k.sp
    s = t % nsl
    r = ctx["r"][s]
    stats, mv, ve, rstd = ctx["stats"][s], ctx["mv"][s], ctx["ve"][s], ctx["rstd"][s]
    stt_toks = []
    for hh in range(2):
        dve.wait(pre_toks, ctx["store_tok"][s])
        ins = dve.e.scalar_tensor_tensor(out=r[:, hh * 512:(hh + 1) * 512], in0=xres[:, hh * 512:(hh + 1) * 512],
                                         scalar=float(xscale), op0=ALU.mult, in1=psum_halves[hh], op1=ALU.add)
        stt_toks.append(dve.mark(ins))
    st_toks = []
    for hh in range(2):
        dve.wait(stt_toks[hh])
        st_toks.append(dve.mark(dve.e.bn_stats(out=stats[:, hh * 6:(hh + 1) * 6], in_=r[:, hh * 512:(hh + 1) * 512])))
    dve.wait(st_toks)
    t1 = dve.mark(dve.e.bn_aggr(out=mv[:], in_=stats[:]))
    dve.wait(t1)
    t2 = dve.mark(dve.e.tensor_scalar(out=ve[:], in0=mv[:, 1:2], scalar1=float(eps), scalar2=None, op0=ALU.add))
    pool.wait(t2, ctx["gb_tok"])
    t3 = pool.mark(pool.e.tensor_tensor(out=rstd[:], in0=ve[:], in1=ctx["mhalf"][:], op=ALU.pow))
    dve.wait(t1, ctx["gb_tok"])
    t4 = dve.mark(dve.e.scalar_tensor_tensor(out=r[:], in0=r[:], scalar=mv[:, 0:1], op0=ALU.subtract, in1=ctx["g"][:], op1=ALU.mult))
    ctx["pend"][t] = (t3, t4, dst_ap)
    return stt_toks


def ln_part_b(k, ctx, t):
    dve, sp = k.dve, k.sp
    s = t % ctx["n"]
    r = ctx["r"][s]
    t3, t4, dst_ap = ctx["pend"].pop(t)
    dve.wait(t3, t4)
    t6 = dve.mark(dve.e.scalar_tensor_tensor(out=r[:], in0=r[:], scalar=ctx["rstd"][s][:, 0:1], op0=ALU.mult, in1=ctx["b"][:], op1=ALU.add))
    sp.wait(t6)
    ctx["store_tok"][s] = ctx["store_slot"][s].dma(sp, dst_ap, r[:])


def ln_ctx(k, es, name, g_d, b_d, nsl=2):
    nc = k.nc
    sb = lambda n, shape, dt: es.enter_context(nc.sbuf_tensor(f"{name}_{n}", shape, dt))
    ctx = {
        "n": nsl,
        "pend": {},
        "r": [sb(f"r{i}", [128, D], F32) for i in range(nsl)],
        "stats": [sb(f"stats{i}", [128, 12], F32) for i in range(nsl)],
        "mv": [sb(f"mv{i}", [128, 2], F32) for i in range(nsl)],
        "ve": [sb(f"ve{i}", [128, 1], F32) for i in range(nsl)],
        "rstd": [sb(f"rstd{i}", [128, 1], F32) for i in range(nsl)],
        "mhalf": sb("mhalf", [128, 1], F32),
        "g": sb("g", [128, D], F32),
        "b": sb("b", [128, D], F32),
        "store_slot": k.slots(es, nsl),
        "store_tok": [None] * nsl,
    }
    gs = k.slots(es, 1)[0]
    gs.dma(k.sp, ctx["g"][:], g_d)
    tg = gs.dma(k.sp, ctx["b"][:], b_d)
    tm = k.pool.mark(k.pool.e.memset(ctx["mhalf"][:], -0.5))
    ctx["gb_tok"] = [tg, tm]
    return ctx


def alloc_ffn_weights(k, es, name, with_wd=True):
    nc = k.nc
    wg = es.enter_context(nc.sbuf_tensor(f"{name}_wg", [128, 8, DFF], BF16))
    wu = es.enter_context(nc.sbuf_tensor(f"{name}_wu", [128, 8, DFF], BF16))
    wd = es.enter_context(nc.sbuf_tensor(f"{name}_wd", [128, NFC, D], BF16)) if with_wd else None
    return wg, wu, wd


def ffn_phase(k, name, x_src, T, wg_d, wu_d, wd_d, g_d, b_d, dst, pre=None, bg_jobs=None):
    nc = k.nc
    pe, act, dve, pool, sp = k.pe, k.act, k.dve, k.pool, k.sp
    NB = T // 256
    NH = 4
    with ExitStack() as es:
        sb = lambda n, shape, dt: es.enter_context(nc.sbuf_tensor(f"{name}_{n}", shape, dt))
        ps = lambda n, shape, dt: es.enter_context(nc.psum_tensor(f"{name}_{n}", shape, dt))
        bg = None
        emit_weights = None
        wl = None
        if pre is not None:
            wg, wu, wdA, wd_bf_d, wtoks = pre
            wdB = es.enter_context(nc.sbuf_tensor(f"{name}_wdB", [128, NFC // 2, D], BF16))
            wdv = wd_bf_d.rearrange("(c p) f -> p c f", p=128)
            wdB_tok = load_bf16_weights(k, k.act, k.slots(es, 1)[0],
                                        [(wdB[:, c:c + 1, :], wdv[:, NFC // 2 + c:NFC // 2 + c + 1, :]) for c in range(NFC // 2)])
            wd_ap = lambda fd, lo, hi: (wdA if fd < NFC // 2 else wdB)[:, fd % (NFC // 2), lo:hi]
            gu_wtok = {f: wtoks for f in range(NFC)}
            d_wtok = {f: None for f in range(NFC)}
        else:
            wg, wu, wd = alloc_ffn_weights(k, es, name)
            wgv = wg_d.rearrange("(c p) f -> p c f", p=128)
            wuv = wu_d.rearrange("(c p) f -> p c f", p=128)
            wdv = wd_d.rearrange("(c p) d -> p c d", p=128)
            jobs = []
            for fg in range(NFC // 2):
                cs = slice(fg * 256, (fg + 1) * 256)
                for c0 in (0, 4):
                    jobs.append((wg[:, c0:c0 + 4, cs], wgv[:, c0:c0 + 4, cs]))
                    jobs.append((wu[:, c0:c0 + 4, cs], wuv[:, c0:c0 + 4, cs]))
                for c in (2 * fg, 2 * fg + 1):
                    jobs.append((wd[:, c, :], wdv[:, c, :]))
            wdB_tok = None
            wd_ap = lambda fd, lo, hi: wd[:, fd, lo:hi]
            gu_wtok, d_wtok = {}, {}

            wl = WeightLoader(k, es, name, jobs)
            for f in range(NFC):
                gu_wtok[f] = ("wl", (f // 2) * 6, (f // 2) * 6 + 4)
                d_wtok[f] = ("wl", (f // 2) * 6 + 4 + (f % 2), (f // 2) * 6 + 5 + (f % 2))

            def emit_weights():
                wl.emit(12)
        ctx = ln_ctx(k, es, name, g_d, b_d)

        xA = [sb(f"xA{i}", [128, D], F32) for i in range(2)]
        xA_slot = k.slots(es, 2)
        xR = [sb(f"xR{i}", [128, D], F32) for i in range(2)]
        xR_slot = k.slots(es, 2)
        xbf = [sb(f"xbf{i}", [128, D], BF16) for i in range(2)]
        xT = [sb(f"xT{i}", [128, 8, 256], BF16) for i in range(2)]
        hT = [sb(f"hT{i}", [128, 256], BF16) for i in range(NH)]
        sg = [sb(f"sg{i}", [128, 256], F32) for i in range(2)]
        ident = k.ident
        Tps = [ps(f"T{i}", [128, 8, 128], BF16) for i in range(2)]
        gu = [ps(f"gu{i}", [128, 2, 256], F32) for i in range(2)]
        yp = [[ps(f"y{t}{h}", [128, 512], F32) for h in range(2)] for t in range(2)]

        cast_tok = [None, None]
        T_tok = [None, None]
        XE_tok = {}
        xR_free = [None, None]
        xR_tok = [None, None]
        gu_last = {}
        mult_tok = {}
        D_tok = {}
        ep_tok = {}

        def stage_load_cast(b):
            for t in range(2):
                sp.wait(cast_tok[t])
                lt = xA_slot[t].dma(sp, xA[t][:], x_src[(b * 2 + t) * 128:(b * 2 + t + 1) * 128, :])
                pool.wait(lt, T_tok[t])
                cast_tok[t] = pool.mark(pool.e.tensor_copy(out=xbf[t][:], in_=xA[t][:]))

        def stage_T(b):
            for t in range(2):
                prev = XE_tok.get((b - 1, t))
                pe.wait(cast_tok[t], prev, k.ident_tok)
                for c in range(8):
                    ins = pe.e.transpose(out=Tps[t][:, c, :], in_=xbf[t][:, c * 128:(c + 1) * 128], identity=ident[:])
                T_tok[t] = pe.mark(ins)

        def stage_XE(b):
            for t in range(2):
                act.wait(T_tok[t], gu_last.get(b - 2))
                XE_tok[(b, t)] = act.mark(act.e.activation(out=xT[b % 2][:, :, t * 128:(t + 1) * 128], in_=Tps[t][:], func=AF.Copy))

        def stage_xR(b):
            for t in range(2):
                sp.wait(xR_free[t])
                xR_tok[t] = xR_slot[t].dma(sp, xR[t][:], x_src[(b * 2 + t) * 128:(b * 2 + t + 1) * 128, :])

        stage_load_cast(0)
        if emit_weights is not None:
            emit_weights()
        stage_T(0)
        stage_XE(0)
        def wres(t):
            if isinstance(t, tuple) and len(t) == 3 and t[0] == "wl":
                return list(wl.toks[t[1]:t[2]])
            return t

        def stage_down(b, fd):
            gd = b * NFC + fd
            pe.wait(mult_tok[gd], ep_tok.get(b - 1) if fd == 0 else None, wdB_tok if (b == 0 and fd == NFC // 2) else None,
                    wres(d_wtok[fd]) if b == 0 else None)
            for t in range(2):
                for hh in range(2):
                    ins = pe.e.matmul(yp[t][hh][:], lhsT=hT[gd % NH][:, t * 128:(t + 1) * 128],
                                      rhs=wd_ap(fd, hh * 512, (hh + 1) * 512), start=(fd == 0), stop=(fd == NFC - 1))
            D_tok[gd] = pe.mark(ins)

        for b in range(NB):
            if b + 1 < NB:
                stage_load_cast(b + 1)
            stage_xR(b)
            xt = xT[b % 2]
            for f in range(NFC):
                gi = b * NFC + f
                if wl is not None and b == 0 and f % 2 == 0:
                    wl.emit(6 * (f // 2 + 3) - wl.i)
                    if wl.done() and bg is None and bg_jobs:
                        bg = BgCast(k, es, name, bg_jobs, k.last_stg[:2], list(wl.toks))
                pe.wait(XE_tok[(b, 0)], XE_tok[(b, 1)], mult_tok.get(gi - 2), wres(gu_wtok[f]) if b == 0 else None)
                for c in range(8):
                    pe.e.matmul(gu[gi % 2][:, 0, :], lhsT=wg[:, c, f * 128:(f + 1) * 128], rhs=xt[:, c, :],
                                start=(c == 0), stop=(c == 7))
                for c in range(8):
                    ins = pe.e.matmul(gu[gi % 2][:, 1, :], lhsT=wu[:, c, f * 128:(f + 1) * 128], rhs=xt[:, c, :],
                                      start=(c == 0), stop=(c == 7))
                gtok = pe.mark(ins)
                if f == NFC - 1:
                    gu_last[b] = gtok
                act.wait(gtok, mult_tok.get(gi - 2))
                stok = act.mark(act.e.activation(out=sg[gi % 2][:], in_=gu[gi % 2][:, 0, :], func=AF.Silu))
                dve.wait(stok, D_tok.get(gi - NH))
                mult_tok[gi] = dve.mark(dve.e.tensor_tensor(out=hT[gi % NH][:], in0=sg[gi % 2][:], in1=gu[gi % 2][:, 1, :], op=ALU.mult))
                if f == 10 and b + 1 < NB:
                    stage_T(b + 1)
                    stage_XE(b + 1)
                if bg is not None and f in (1, 5, 9, 13, 17, 20):
                    bg.step()
                if f >= 1:
                    stage_down(b, f - 1)
            stage_down(b, NFC - 1)
            etoks = []
            for t in range(2):
                gt = b * 2 + t
                stt = ln_part_a(k, ctx, gt, [yp[t][0][:], yp[t][1][:]], xR[t], 2.0 * ALPHA, 4.0 * EPS,
                                dst[gt * 128:(gt + 1) * 128, :], [D_tok[b * NFC + NFC - 1], xR_tok[t]])
                xR_free[t] = stt
                etoks.extend(stt)
            for t in range(2):
                ln_part_b(k, ctx, b * 2 + t)
            ep_tok[b] = etoks
        barrier(k, [ctx["store_tok"], bg.finish() if bg is not None else None])


def xT_block_loader(k, es, name, src, ntile_list):
    nc = k.nc
    sb = lambda n, shape, dt: es.enter_context(nc.sbuf_tensor(f"{name}_{n}", shape, dt))
    st = {
        "xA": [sb(f"lxA{i}", [128, D], F32) for i in range(2)],
        "xbf": [sb(f"lxbf{i}", [128, D], BF16) for i in range(2)],
        "slot": k.slots(es, 2),
        "Tps": [es.enter_context(nc.psum_tensor(f"{name}_lT{i}", [128, 8, 128], BF16)) for i in range(2)],
        "cast_tok": [None, None], "T_tok": [None, None], "XE_tok": [None, None], "i": 0,
    }

    def emit(tile, dstT, dst_free_tok=None):
        i = st["i"]; st["i"] += 1
        s_ = i % 2
        k.sp.wait(st["cast_tok"][s_])
        lt = st["slot"][s_].dma(k.sp, st["xA"][s_][:], src[tile * 128:(tile + 1) * 128, :])
        k.pool.wait(lt, st["T_tok"][s_])
        st["cast_tok"][s_] = k.pool.mark(k.pool.e.tensor_copy(out=st["xbf"][s_][:], in_=st["xA"][s_][:]))
        k.pe.wait(st["cast_tok"][s_], st["XE_tok"][s_], k.ident_tok)
        for c in range(8):
            ins = k.pe.e.transpose(out=st["Tps"][s_][:, c, :], in_=st["xbf"][s_][:, c * 128:(c + 1) * 128], identity=k.ident[:])
        st["T_tok"][s_] = k.pe.mark(ins)
        k.act.wait(st["T_tok"][s_], dst_free_tok)
        st["XE_tok"][s_] = k.act.mark(k.act.e.activation(out=dstT, in_=st["Tps"][s_][:], func=AF.Copy))
        return st["XE_tok"][s_]
    return emit


def win_jobs(win_sb, win_d, col0, ncols):
    v = win_d.rearrange("(c p) f -> p c f", p=128)
    return [(win_sb[:, c, :], v[:, c, col0:col0 + ncols]) for c in range(8)]


def na_kt_set(il):
    return [0, 1, 2, 3] if il < 2 else list(range(il - 2, il + 3))


def na_variant(il, kt):
    return il * 4 + kt if il < 2 else 8 + (kt - il + 2)


def na_phase(k, x1_d, win_d, nab_d, mixT):
    nc = k.nc
    pe, act, dve, pool, sp = k.pe, k.act, k.dve, k.pool, k.sp
    with ExitStack() as es:
        sb = lambda n, shape, dt: es.enter_context(nc.sbuf_tensor(f"na_{n}", shape, dt))
        ps = lambda n, shape, dt: es.enter_context(nc.psum_tensor(f"na_{n}", shape, dt))
        KT = sb("KT", [128, 4, NKT_NA * 128], BF16)
        QT = [sb(f"QT{i}", [128, 4, OWN], BF16) for i in range(2)]
        VA = sb("VA", [128, NKT_NA, 8, 65], BF16)
        zer = sb("zer", [128, 512], BF16)
        nab = sb("nab", [128, 13, 1024], F32)
        nsl = k.slots(es, 1)[0]
        for v in range(13):
            nab_tok = nsl.dma(act, nab[:, v, :], nab_d[v])
        tz = pool.mark(pool.e.memset(zer[:], 0.0))
        dve.e.memset(QT[0][64:128, :, :], 0.0)
        dve.e.memset(QT[1][0:64, :, :], 0.0)
        tv1 = dve.mark(dve.e.memset(VA[:, :, :, 64:65], 1.0))
        pad_tok = tv1
        with ExitStack() as es2:
            sb2 = lambda n, shape, dt: es2.enter_context(nc.sbuf_tensor(f"nap_{n}", shape, dt))
            win = sb2("win", [128, 8, 1536], BF16)
            wtoks = load_bf16_weights(k, sp, k.slots(es2, 1)[0], win_jobs(win, win_d, 0, 1536))
            x1T = [sb2(f"x1T{i}", [128, 8, 512], BF16) for i in range(2)]
            pp = [es2.enter_context(nc.psum_tensor(f"nap_pp{i}", [128, 512], F32)) for i in range(3)]
            emit = xT_block_loader(k, es2, "nap", x1_d, None)
            pp_free = [None] * 3
            blk_last_pe = [None, None]
            npp = 0
            for blk in range(5):
                ntile = 4 if blk < 4 else 2
                ntok = ntile * 128
                xt = x1T[blk % 2]
                xe = [emit(blk * 4 + t, xt[:, :, t * 128:(t + 1) * 128], blk_last_pe[blk % 2]) for t in range(ntile)]
                pe.wait(xe, wtoks)
                for kind in range(2):
                    if kind == 1 and blk >= 4:
                        continue
                    for hp in range(4):
                        col = (512 if kind == 0 else 0) + hp * 128
                        b_ = npp % 3; npp += 1
                        pe.wait(pp_free[b_])
                        for c in range(8):
                            ins = pe.e.matmul(pp[b_][:, :ntok], lhsT=win[:, c, col:col + 128], rhs=xt[:, c, :ntok], start=(c == 0), stop=(c == 7))
                        tk = pe.mark(ins)
                        act.wait(tk)
                        if kind == 0:
                            pp_free[b_] = act.mark(act.e.activation(out=KT[:, hp, blk * 512:blk * 512 + ntok], in_=pp[b_][:, :ntok], func=AF.Copy))
                        else:
                            act.wait(tv1)
                            act.e.activation(out=QT[0][0:64, hp, blk * 512:blk * 512 + ntok], in_=pp[b_][0:64, :ntok], func=AF.Copy, scale=0.125)
                            pp_free[b_] = act.mark(act.e.activation(out=QT[1][64:128, hp, blk * 512:blk * 512 + ntok], in_=pp[b_][64:128, :ntok], func=AF.Copy, scale=0.125))
                for t in range(ntile):
                    b_ = npp % 3; npp += 1
                    pe.wait(pp_free[b_])
                    for c in range(8):
                        ins = pe.e.matmul(pp[b_][:], lhsT=xt[:, c, t * 128:(t + 1) * 128], rhs=win[:, c, 1024:1536], start=(c == 0), stop=(c == 7))
                    tk = pe.mark(ins)
                    dve.wait(tk, tv1)
                    pp_free[b_] = dve.mark(dve.e.tensor_copy(out=VA[:, blk * 4 + t, :, 0:64], in_=pp[b_][:].rearrange("p (h e) -> p h e", e=64)))
                blk_last_pe[blk % 2] = tk
            barrier(k, [])
        NSP, NTM, NE = 2, 2, 3
        sps = [ps(f"s{i}", [128, 2, 512], F32) for i in range(NSP)]
        acc = [ps(f"acc{i}", [128, 512], F32) for i in range(2)]
        tpo = ps("tpo", [128, 4, 128], BF16)
        tmp = [sb(f"tmp{i}", [128, 2, 512], F32) for i in range(NTM)]
        E = [sb(f"E{i}", [128, 2, 512], BF16) for i in range(NE)]
        rr = sb("rr", [128, 8], F32)
        nao = [sb(f"nao{i}", [128, 512], BF16) for i in range(2)]
        sps_free = [None] * NSP
        tmp_free = [None] * NTM
        E_free = [None] * NE
        acc_free = [None, None]
        nao_free = [None, None]
        nao_tok = {}
        tpo_free = None
        ns = 0

        def finish_il(il_):
            nonlocal tpo_free
            s2 = il_ % 2
            pe.wait(nao_tok[il_], tpo_free)
            for hp in range(4):
                ins = pe.e.transpose(out=tpo[:, hp, :], in_=nao[s2][:, hp * 128:(hp + 1) * 128], identity=k.ident[:])
            tt = pe.mark(ins)
            nao_free[s2] = tt
            act.wait(tt)
            tpo_free = act.mark(act.e.activation(out=mixT[:, 0:4, il_ * 128:(il_ + 1) * 128], in_=tpo[:], func=AF.Copy))

        pend_il = None
        for il in range(16):
            kts = na_kt_set(il)
            for hb in range(2):
                pe.wait(acc_free[hb], tz)
                pe.e.matmul(acc[hb][:], lhsT=zer[:, 0:128], rhs=zer[:], start=True, stop=False)
            E_tok = {}

            def emit_S(si):
                nonlocal ns
                kt = kts[si]
                g = ns; ns += 1
                pe.wait(sps_free[g % NSP], pad_tok)
                for hb in range(2):
                    for hl in range(4):
                        ins = pe.e.matmul(sps[g % NSP][:, hb, hl * 128:(hl + 1) * 128], lhsT=KT[:, hl, kt * 128:(kt + 1) * 128],
                                          rhs=QT[hb][:, hl, il * 128:(il + 1) * 128], start=True, stop=True)
                tk = pe.mark(ins)
                v = na_variant(il, kt)
                dve.wait(tk, tmp_free[g % NTM], nab_tok)
                t1 = dve.mark(dve.e.tensor_tensor(out=tmp[g % NTM][:], in0=nab[:, v, :].rearrange("p (a b) -> p a b", a=2), in1=sps[g % NSP][:], op=ALU.add))
                sps_free[g % NSP] = t1
                act.wait(t1, E_free[g % NE])
                t2 = act.mark(act.e.activation(out=E[g % NE][:], in_=tmp[g % NTM][:], func=AF.Exp))
                tmp_free[g % NTM] = t2
                E_tok[si] = (t2, g)

            def emit_AV(si, last):
                kt = kts[si]
                t2, g = E_tok[si]
                pe.wait(t2)
                for hb in range(2):
                    for hl in range(4):
                        h = 2 * hl + hb
                        ins = pe.e.matmul(acc[hb][:, hl * 65:(hl + 1) * 65], lhsT=E[g % NE][:, hb, hl * 128:(hl + 1) * 128],
                                          rhs=VA[:, kt, h, :], start=False, stop=(last and hl == 3))
                E_free[g % NE] = pe.mark(ins)
                return E_free[g % NE]

            nst = len(kts)
            emit_S(0)
            emit_S(1)
            for si in range(nst):
                if si + 2 < nst:
                    emit_S(si + 2)
                last_av = emit_AV(si, si == nst - 1)
                if si == 1 and pend_il is not None:
                    finish_il(pend_il)
                    pend_il = None
            s_ = il % 2
            for hb in range(2):
                accv = acc[hb][:, 0:260].rearrange("p (h e) -> p h e", e=65)
                dve.wait(last_av, nao_free[s_])
                tr = dve.mark(dve.e.reciprocal(out=rr[:, hb * 4:(hb + 1) * 4], in_=accv[:, :, 64]))
                dve.wait(tr)
                for hl in range(4):
                    h = 2 * hl + hb
                    ins = dve.e.tensor_scalar(out=nao[s_][:, h * 64:(h + 1) * 64], in0=accv[:, hl, 0:64], scalar1=rr[:, hb * 4 + hl:hb * 4 + hl + 1], scalar2=None, op0=ALU.mult)
                acc_free[hb] = dve.mark(ins)
            nao_tok[il] = acc_free[1]
            pend_il = il
        finish_il(pend_il)
        barrier(k, [])


def diff_phase(k, x1_d, win_d, aug_d, atab_d, cst_d, lamv_d, subg_d, mixT):
    nc = k.nc
    pe, act, dve, pool, sp = k.pe, k.act, k.dve, k.pool, k.sp
    SL = [2.0 ** (-8.0 * (h + 1) / 4) for h in range(4)]
    with ExitStack() as es:
        sb = lambda n, shape, dt: es.enter_context(nc.sbuf_tensor(f"df_{n}", shape, dt))
        ps = lambda n, shape, dt: es.enter_context(nc.psum_tensor(f"df_{n}", shape, dt))
        KT = [sb(f"KT{i}", [128, 4, SEQ], BF16) for i in range(2)]
        QT = sb("QT", [128, 4, OWN], BF16)
        VA = sb("VA", [128, 32, 4, 129], BF16)
        cst = sb("cst", [128, 256], F32)
        lamv = sb("lamv", [128, 4, 64], F32)
        g8 = sb("g8", [128, 128], F32)
        zer = sb("zer", [128, 512], BF16)
        sm = sb("sm", [128, 8], F32)
        junk = sb("junk", [128, 64], F32)
        mhalf = sb("mhalf", [128, 1], F32)
        tz = pool.mark(pool.e.memset(zer[:], 0.0))
        pool.e.memset(mhalf[:], -0.5)
        ones_t = sb("ones_t", [128, 512], BF16)
        pad_tok = pool.mark(pool.e.memset(ones_t[:], 1.0))
        tv1 = dve.mark(dve.e.memset(VA[:, :, :, 128:129], 1.0))
        csl = k.slots(es, 1)[0]
        csl.dma(sp, cst[:], cst_d)
        csl.dma(sp, lamv[:], lamv_d)
        ctok = csl.dma(sp, g8[:], subg_d)
        dve.wait(ctok)
        a0 = dve.mark(dve.e.tensor_scalar(out=g8[:], in0=g8[:], scalar1=1.0 - LAM_INIT, scalar2=None, op0=ALU.mult))
        dve.wait(a0)
        a1 = dve.mark(dve.e.scalar_tensor_tensor(out=junk[:], in0=lamv[:, 0, :], scalar=1.0, op0=ALU.mult, in1=lamv[:, 1, :], op1=ALU.mult, accum_out=sm[:, 0:1]))
        dve.wait(a1)
        a2 = dve.mark(dve.e.scalar_tensor_tensor(out=junk[:], in0=lamv[:, 2, :], scalar=1.0, op0=ALU.mult, in1=lamv[:, 3, :], op1=ALU.mult, accum_out=sm[:, 1:2]))
        act.wait(a2)
        a3 = act.mark(act.e.activation(out=sm[:, 2:4], in_=sm[:, 0:2], func=AF.Exp))
        dve.wait(a3)
        a4 = dve.mark(dve.e.tensor_tensor(out=sm[:, 5:6], in0=sm[:, 3:4], in1=sm[:, 2:3], op=ALU.subtract))
        dve.wait(a4)
        lam_tok = dve.mark(dve.e.tensor_scalar(out=sm[:, 4:5], in0=sm[:, 5:6], scalar1=-LAM_INIT, scalar2=None, op0=ALU.add))
        neglam = sm[:, 4:5]
        with ExitStack() as es2:
            sb2 = lambda n, shape, dt: es2.enter_context(nc.sbuf_tensor(f"dfp_{n}", shape, dt))
            win = sb2("win", [128, 8, 1536], BF16)
            wtoks = load_bf16_weights(k, sp, k.slots(es2, 1)[0], win_jobs(win, win_d, 1536, 1536))
            x1T = [sb2(f"x1T{i}", [128, 8, 512], BF16) for i in range(2)]
            pp = [es2.enter_context(nc.psum_tensor(f"dfp_pp{i}", [128, 512], F32)) for i in range(3)]
            emit = xT_block_loader(k, es2, "dfp", x1_d, None)
            pp_free = [None] * 3
            blk_last_pe = [None, None]
            npp = 0
            for blk in range(8):
                xt = x1T[blk % 2]
                xe = [emit(blk * 4 + t, xt[:, :, t * 128:(t + 1) * 128], blk_last_pe[blk % 2]) for t in range(4)]
                pe.wait(xe, wtoks)
                for kind in range(2):
                    if kind == 1 and blk >= 4:
                        continue
                    for h in range(4):
                        col = (512 if kind == 0 else 0) + h * 128
                        b_ = npp % 3; npp += 1
                        pe.wait(pp_free[b_])
                        for c in range(8):
                            ins = pe.e.matmul(pp[b_][:], lhsT=win[:, c, col:col + 128], rhs=xt[:, c, :], start=(c == 0), stop=(c == 7))
                        tk = pe.mark(ins)
                        act.wait(tk)
                        if kind == 0:
                            act.wait(pad_tok)
                            k0 = act.mark(act.e.activation(out=KT[0][:, h, blk * 512:(blk + 1) * 512], in_=pp[b_][:], func=AF.Copy))
                            k1 = act.mark(act.e.activation(out=KT[1][:, h, blk * 512:(blk + 1) * 512], in_=pp[b_][:], func=AF.Copy))
                            pp_free[b_] = k1
                            act.wait(k0, k1)
                            act.e.activation(out=KT[0][64:66, h, blk * 512:(blk + 1) * 512], in_=ones_t[64:66, :], func=AF.Copy)
                            kfix_tok = act.mark(act.e.activation(out=KT[1][0:2, h, blk * 512:(blk + 1) * 512], in_=ones_t[0:2, :], func=AF.Copy))
                        else:
                            pp_free[b_] = act.mark(act.e.activation(out=QT[:, h, blk * 512:(blk + 1) * 512], in_=pp[b_][:], func=AF.Copy, scale=0.125))
                for t in range(4):
                    b_ = npp % 3; npp += 1
                    pe.wait(pp_free[b_])
                    for c in range(8):
                        ins = pe.e.matmul(pp[b_][:], lhsT=xt[:, c, t * 128:(t + 1) * 128], rhs=win[:, c, 1024:1536], start=(c == 0), stop=(c == 7))
                    tk = pe.mark(ins)
                    dve.wait(tk, tv1)
                    pp_free[b_] = dve.mark(dve.e.tensor_copy(out=VA[:, blk * 4 + t, :, 0:128], in_=pp[b_][:].rearrange("p (h e) -> p h e", e=128)))
                blk_last_pe[blk % 2] = tk
            barrier(k, [])
        NSP, NTM, NE = 2, 2, 4
        atab = sb("atab", [128, 2, 896], F32)
        augtab = sb("augtab", [128, 4, 2, 512], BF16)
        Qs = [[[sb(f"Qs{u}{v}{m}", [128, 512], BF16) for m in range(2)] for v in range(3)] for u in range(2)]
        asl = k.slots(es, 1)[0]
        asl.dma(sp, atab[:], atab_d)
        atok = asl.dma(sp, augtab[:], aug_d)
        qz = None
        for u in range(2):
            for v in range(3):
                for m in range(2):
                    qz = pool.mark(pool.e.memset(Qs[u][v][m][:], 0.0))
        sps = [ps(f"s{i}", [128, 2, 512], F32) for i in range(NSP)]
        acc = [ps(f"acc{i}", [128, 512], F32) for i in range(3)]
        tpo = ps("tpo", [128, 128], BF16)
        tmp = [sb(f"tmp{i}", [128, 2, 512], F32) for i in range(NTM)]
        E = [sb(f"E{i}", [128, 2, 512], BF16) for i in range(NE)]
        accs = sb("accs", [128, 3, 387], F32)
        rr = sb("rr", [128, 4], F32)
        tq = sb("tq", [128, 128], F32)
        oq = sb("oq", [128, 128], F32)
        sps_free = [None] * NSP
        tmp_free = [None] * NTM
        E_free = [None] * NE
        acc_free = [None] * 3
        accs_free = None
        tpo_free = None
        ns = 0
        unit_last_S = {}
        units = [(h_, qb_) for h_ in range(4) for qb_ in range(4)]
        yq = [[sb(f"yq{u_}{q_}", [128, 128], BF16) for q_ in range(4)] for u_ in range(2)]
        yq_free = {}
        yq_tok = {}
        qtoks = {}

        def build_Qs(ui):
            h_, qb_ = units[ui]
            u_ = ui % 2
            pool.wait(qz, atok, unit_last_S.get(ui - 2), ctok)
            for m in range(2):
                r0 = 64 * m
                a0 = 64 - 64 * m
                for v in range(3):
                    qtok = pool.mark(pool.e.tensor_copy(out=Qs[u_][v][m][r0:r0 + 64, :], in_=QT[r0:r0 + 64, h_, qb_ * 512:(qb_ + 1) * 512]))
                for v in range(2):
                    qtok = pool.mark(pool.e.tensor_copy(out=Qs[u_][v][m][a0:a0 + 2, :], in_=augtab[a0:a0 + 2, h_, v, :]))
            qtoks[ui] = qtok

        def finish_transposes(ui):
            nonlocal tpo_free
            h_, qb_ = units[ui]
            for qt in range(4):
                pe.wait(yq_tok[(ui, qt)], tpo_free)
                tt = pe.mark(pe.e.transpose(out=tpo[:], in_=yq[ui % 2][qt][:], identity=k.ident[:]))
                yq_free[(ui % 2, qt)] = tt
                act.wait(tt)
                tok0 = (qb_ * 4 + qt) * 128
                tpo_free = act.mark(act.e.activation(out=mixT[:, 4 + h_, tok0:tok0 + 128], in_=tpo[:], func=AF.Copy))

        evts = {}

        def epilogue(ui):
            nonlocal accs_free
            u = ui % 2
            evt = evts[ui]
            for qt in range(4):
                g0, g1 = qt, 4 + qt
                O0 = accs[:, g0 // 3, (g0 % 3) * 129:(g0 % 3) * 129 + 129]
                O1 = accs[:, g1 // 3, (g1 % 3) * 129:(g1 % 3) * 129 + 129]
                dve.wait(evt, lam_tok)
                e1 = dve.mark(dve.e.reciprocal(out=rr[:, 0:1], in_=O0[:, 128:129]))
                e2 = dve.mark(dve.e.reciprocal(out=rr[:, 1:2], in_=O1[:, 128:129]))
                dve.wait(e1, e2)
                e3 = dve.mark(dve.e.tensor_tensor(out=rr[:, 2:3], in0=rr[:, 1:2], in1=neglam, op=ALU.mult))
                dve.wait(e3)
                e4 = dve.mark(dve.e.tensor_scalar(out=tq[:], in0=O1[:, 0:128], scalar1=rr[:, 2:3], scalar2=None, op0=ALU.mult))
                dve.wait(e4)
                e5 = dve.mark(dve.e.scalar_tensor_tensor(out=oq[:], in0=O0[:, 0:128], scalar=rr[:, 0:1], op0=ALU.mult, in1=tq[:], op1=ALU.add))
                dve.wait(e5)
                e6 = dve.mark(dve.e.scalar_tensor_tensor(out=tq[:], in0=oq[:], scalar=1.0 / 128.0, op0=ALU.mult, in1=oq[:], op1=ALU.mult, accum_out=rr[:, 3:4]))
                dve.wait(e6)
                e7 = dve.mark(dve.e.tensor_scalar(out=rr[:, 3:4], in0=rr[:, 3:4], scalar1=EPS, scalar2=None, op0=ALU.add))
                pool.wait(e7)
                e8 = pool.mark(pool.e.tensor_tensor(out=rr[:, 3:4], in0=rr[:, 3:4], in1=mhalf[:], op=ALU.pow))
                dve.wait(e8, yq_free.get((u, qt)))
                e9 = dve.mark(dve.e.scalar_tensor_tensor(out=yq[u][qt][:], in0=oq[:], scalar=rr[:, 3:4], op0=ALU.mult, in1=g8[:], op1=ALU.mult))
                yq_tok[(ui, qt)] = e9
                if qt == 3:
                    accs_free = e9

        build_Qs(0)
        pending = None
        pend_epi = None
        tr_at = -1
        for ui, (h, qb) in enumerate(units):
            if True:
                u = ui % 2
                for j in range(3):
                    pe.wait(acc_free[j], tz)
                    pe.e.matmul(acc[j][:], lhsT=zer[:, 0:128], rhs=zer[:], start=True, stop=False)
                E_tok = {}

                def emit_S(kt):
                    nonlocal ns
                    g = ns; ns += 1
                    delta = qb * 512 - kt * 128
                    v = 0 if delta >= 128 else (1 if delta <= -512 else 2)
                    pe.wait(sps_free[g % NSP], qtoks[ui], pad_tok)
                    for m in range(2):
                        ins = pe.e.matmul(sps[g % NSP][:, m, :], lhsT=KT[m][:, h, kt * 128:(kt + 1) * 128],
                                          rhs=Qs[u][v][m][:], start=True, stop=True)
                    tk = pe.mark(ins)
                    unit_last_S[ui] = tk
                    if v == 2:
                        dve.wait(tk, tmp_free[g % NTM], atok)
                        t1 = dve.mark(dve.e.scalar_tensor_tensor(out=tmp[g % NTM][:], in0=atab[:, :, delta + 384:delta + 384 + 512], scalar=float(-SL[h]),
                                                                 op0=ALU.mult, in1=sps[g % NSP][:], op1=ALU.add))
                        sps_free[g % NSP] = t1
                        act.wait(t1, E_free[g % NE])
                        t2 = act.mark(act.e.activation(out=E[g % NE][:], in_=tmp[g % NTM][:], func=AF.Exp))
                        tmp_free[g % NTM] = t2
                    else:
                        n = abs(delta) // 128
                        col = (h * 32 + n) * 2 + v
                        act.wait(tk, E_free[g % NE], ctok)
                        t2 = act.mark(act.e.activation(out=E[g % NE][:], in_=sps[g % NSP][:], func=AF.Exp, bias=cst[:, col:col + 1], scale=1.0))
                        sps_free[g % NSP] = t2
                    E_tok[kt] = (t2, g)

                def emit_AV(kt):
                    t2, g = E_tok[kt]
                    pe.wait(t2, tv1)
                    for m in range(2):
                        for qt in range(4):
                            gi = m * 4 + qt
                            ins = pe.e.matmul(acc[gi // 3][:, (gi % 3) * 129:(gi % 3) * 129 + 129], lhsT=E[g % NE][:, m, qt * 128:(qt + 1) * 128],
                                              rhs=VA[:, kt, h, :], start=False, stop=(kt == 31 and gi in (2, 5, 7)))
                    E_free[g % NE] = pe.mark(ins)
                    return E_free[g % NE]

                emit_S(0)
                emit_S(1)
                for kt in range(32):
                    if kt + 2 < 32:
                        emit_S(kt + 2)
                    last = emit_AV(kt)
                    if kt == 6 and ui + 1 < len(units):
                        build_Qs(ui + 1)
                    if pend_epi is not None and kt == min(4 * qb + 2, 14):
                        epilogue(pend_epi)
                        pending = pend_epi
                        pend_epi = None
                        tr_at = kt + 12
                    if pending is not None and pend_epi is None and kt == tr_at:
                        finish_transposes(pending)
                        pending = None
                evt = []
                for j in range(3):
                    act.wait(last, accs_free)
                    acc_free[j] = act.mark(act.e.activation(out=accs[:, j, :], in_=acc[j][:, 0:387], func=AF.Copy))
                    evt.append(acc_free[j])
                evts[ui] = evt
                pend_epi = ui
        epilogue(pend_epi)
        finish_transposes(pend_epi)
        barrier(k, [])


def wout_phase(k, x1_d, wout_d, g_d, b_d, mixT, x2_d):
    nc = k.nc
    pe, act, dve, pool, sp = k.pe, k.act, k.dve, k.pool, k.sp
    with ExitStack() as es:
        sb = lambda n, shape, dt: es.enter_context(nc.sbuf_tensor(f"wo_{n}", shape, dt))
        wo = sb("wo", [128, 8, D], BF16)
        v = wout_d.rearrange("(c p) f -> p c f", p=128)
        wtoks = load_bf16_weights(k, sp, k.slots(es, 1)[0], [(wo[:, c, :], v[:, c, :]) for c in range(8)])
        NW = 4
        ctx = ln_ctx(k, es, "wo", g_d, b_d, nsl=NW)
        xR = [sb(f"xR{i}", [128, D], F32) for i in range(NW)]
        xsl = k.slots(es, NW)
        yp = [[es.enter_context(nc.psum_tensor(f"wo_y{t}{h}", [128, 512], F32)) for h in range(2)] for t in range(NW)]
        xR_free = [None] * NW
        yp_free = [None] * NW
        lts = {}

        def issue_load(t_):
            sl_ = t_ % NW
            sp.wait(xR_free[sl_])
            lts[t_] = xsl[sl_].dma(sp, xR[sl_][:], x1_d[t_ * 128:(t_ + 1) * 128, :])

        for t_ in range(NW - 1):
            issue_load(t_)
        for t in range(16):
            s_ = t % NW
            if t + NW - 1 < 16:
                issue_load(t + NW - 1)
            lt = lts[t]
            pe.wait(wtoks, yp_free[s_])
            for hh in range(2):
                for c in range(8):
                    ins = pe.e.matmul(yp[s_][hh][:], lhsT=mixT[:, c, t * 128:(t + 1) * 128], rhs=wo[:, c, hh * 512:(hh + 1) * 512], start=(c == 0), stop=(c == 7))
            tk = pe.mark(ins)
            stt = ln_part_a(k, ctx, t, [yp[s_][0][:], yp[s_][1][:]], xR[s_], ALPHA, EPS, x2_d[t * 128:(t + 1) * 128, :], [tk, lt])
            xR_free[s_] = stt
            yp_free[s_] = stt
            if t >= 1:
                ln_part_b(k, ctx, t - 1)
        ln_part_b(k, ctx, 15)
        barrier(k, [ctx["store_tok"]])


def build_program(stop=None):
    nc = bass.Bass("TRN2", target_bir_lowering=False)
    dram_in = lambda n, shape, dt=F32: nc.dram_tensor(n, shape, dt, kind="ExternalInput").ap()
    x = dram_in("x", [SEQ, D])
    wg1 = dram_in("wg1", [D, DFF]); wu1 = dram_in("wu1", [D, DFF]); wd1 = dram_in("wd1", [DFF, D])
    wg2 = dram_in("wg2", [D, DFF]); wu2 = dram_in("wu2", [D, DFF]); wd2 = dram_in("wd2", [DFF, D])
    win = dram_in("win", [D, 3072]); wout = dram_in("wout", [D, D])
    ln1g = dram_in("ln1g", [128, D]); ln1b = dram_in("ln1b", [128, D])
    ln2g = dram_in("ln2g", [128, D]); ln2b = dram_in("ln2b", [128, D])
    ln3g = dram_in("ln3g", [128, D]); ln3b = dram_in("ln3b", [128, D])
    ident_d = dram_in("ident", [128, 128], BF16)
    nab = dram_in("nab", [13, 128, 1024])
    augt = dram_in("augt", [128, 4, 2, 512], BF16); atab = dram_in("atab", [128, 2, 896]); cst = dram_in("cst", [128, 256])
    lamv = dram_in("lamv", [128, 4, 64]); subg = dram_in("subg", [128, 128])
    zeros_ones = dram_in("zeros_ones", [2, 64, 4 * SEQ], BF16)
    out = nc.dram_tensor("out", [OWN, D], F32, kind="ExternalOutput").ap()
    x1_d = nc.dram_tensor("x1_scratch", [SEQ, D], F32, kind="Internal").ap()
    x2_d = nc.dram_tensor("x2_scratch", [OWN, D], F32, kind="Internal").ap()
    win_bf = nc.dram_tensor("win_bf", [D, 3072], BF16, kind="Internal").ap()
    wout_bf = nc.dram_tensor("wout_bf", [D, D], BF16, kind="Internal").ap()
    wg2_bf = nc.dram_tensor("wg2_bf", [D, DFF], BF16, kind="Internal").ap()
    wu2_bf = nc.dram_tensor("wu2_bf", [D, DFF], BF16, kind="Internal").ap()
    wd2_bf = nc.dram_tensor("wd2_bf", [DFF, D], BF16, kind="Internal").ap()

    def pieces(src, dst, width):
        sv = src.rearrange("(c p) f -> p c f", p=128)
        dv = dst.rearrange("(c p) f -> p c f", p=128)
        out_ = []
        for c in range(sv.shape[1]):
            for o in range(0, sv.shape[2], width):
                out_.append((sv[:, c, o:o + width], dv[:, c, o:o + width]))
        return out_
    bg_jobs = (pieces(win, win_bf, 1024) + pieces(wout, wout_bf, 1024) + pieces(wg2, wg2_bf, 1408)
               + pieces(wu2, wu2_bf, 1408) + pieces(wd2, wd2_bf, 1024))
    dbg = None
    if stop is not None:
        dbg = nc.dram_tensor("dbg", [SEQ, D], F32, kind="ExternalOutput").ap()
    with ExitStack() as es:
        k = K()
        k.nc = nc
        k.pe = Eng(nc, nc.tensor, "pe", es)
        k.act = Eng(nc, nc.scalar, "act", es)
        k.dve = Eng(nc, nc.vector, "dve", es)
        k.pool = Eng(nc, nc.gpsimd, "pool", es)
        k.sp = Eng(nc, nc.sync, "sp", es)
        k.engs = [k.pe, k.act, k.dve, k.pool, k.sp]
        k.es_global = es
        k.slot_pool = []
        k.zeros_ones = zeros_ones
        k.ident = es.enter_context(nc.sbuf_tensor("ident_sb", [128, 128], BF16))
        isl = k.slots(es, 1)[0]
        k.ident_tok = isl.dma(k.sp, k.ident[:], ident_d)
        if stop == "A":
            ffn_phase(k, "f1", x, SEQ, wg1, wu1, wd1, ln1g, ln1b, dbg, bg_jobs=bg_jobs)
            return nc
        if stop in ("NA", "DF", "W"):
            x1_src = x
            with ExitStack() as esb:
                inb = [esb.enter_context(nc.sbuf_tensor(f"dbg_in{i}", [128, 1408], F32)) for i in range(2)]
                bgc = BgCast(k, esb, "dbgc", bg_jobs[:32], inb, None)
                barrier(k, [bgc.finish()])
        else:
            ffn_phase(k, "f1", x, SEQ, wg1, wu1, wd1, ln1g, ln1b, x1_d, bg_jobs=bg_jobs)
            x1_src = x1_d
        mix_cm = nc.sbuf_tensor("mixT", [128, 8, OWN], BF16, side="right")
        mixT = mix_cm.__enter__()
        if stop != "DF":
            na_phase(k, x1_src, win_bf, nab, mixT)
        if stop != "NA":
            diff_phase(k, x1_src, win_bf, augt, atab, cst, lamv, subg, mixT)
        if stop in ("NA", "DF"):
            with ExitStack() as es3:
                tmpf = es3.enter_context(nc.sbuf_tensor("dbg_tmp", [128, 8, OWN], F32))
                k.dve.wait((k.act, k.act.n))
                c0_ = 0 if stop == "NA" else 4
                k.pool.wait((k.act, k.act.n))
                tk0 = k.pool.mark(k.pool.e.memset(tmpf[:], 0.0))
                k.dve.wait(tk0)
                tk = k.dve.mark(k.dve.e.tensor_copy(out=tmpf[:, c0_:c0_ + 4, :], in_=mixT[:, c0_:c0_ + 4, :]))
                k.sp.wait(tk)
                sl = k.slots(es3, 1)[0]
                for c in range(8):
                    for hf in range(2):
                        t_ = sl.dma(k.sp, dbg[(c * 2 + hf) * 128:(c * 2 + hf + 1) * 128, :], tmpf[:, c, hf * 1024:(hf + 1) * 1024])
                barrier(k, [t_])
            mix_cm.__exit__(None, None, None)
            return nc
        with ExitStack() as esf2:
            pre = None
            if stop is None:
                wg2s, wu2s, _ = alloc_ffn_weights(k, esf2, "f2", with_wd=False)
                wdA2 = esf2.enter_context(nc.sbuf_tensor("f2_wdA", [128, NFC // 2, D], BF16))
                wsl = k.slots(esf2, 1)[0]
                jobs2 = ([(wg2s[:, c, :], wg2_bf.rearrange("(c p) f -> p c f", p=128)[:, c, :]) for c in range(8)]
                         + [(wu2s[:, c, :], wu2_bf.rearrange("(c p) f -> p c f", p=128)[:, c, :]) for c in range(8)]
                         + [(wdA2[:, c:c + 1, :], wd2_bf.rearrange("(c p) f -> p c f", p=128)[:, c:c + 1, :]) for c in range(NFC // 2)])
                pre = (wg2s, wu2s, wdA2, wd2_bf, load_bf16_weights(k, k.act, wsl, jobs2))
            wout_phase(k, x1_src, wout_bf, ln2g, ln2b, mixT, x2_d if stop is None else dbg)
            mix_cm.__exit__(None, None, None)
            if stop == "W":
                return nc
            ffn_phase(k, "f2", x2_d, OWN, wg2, wu2, wd2, ln3g, ln3b, out, pre=pre)
    return nc


def _na_tables(rpb, rev):
    out = np.full((13, 128, 8, 128), -30000.0, np.float32)
    p = np.arange(128)

    def coords(tile):
        t = tile * 128 + p
        r, c = t // 64, t % 64
        if rev:
            r, c = 63 - r, 63 - c
        return r, c

    def fill(v, il, kt):
        rk, ck = coords(kt)
        rq, cq = coords(il)
        r0 = np.clip(rq - 4, 0, 56)
        c0 = np.clip(cq - 8, 0, 48)
        RK, RQ = rk[:, None], rq[None, :]
        CK, CQ = ck[:, None], cq[None, :]
        ok = (RK >= r0[None, :]) & (RK <= r0[None, :] + 7) & (CK >= c0[None, :]) & (CK <= c0[None, :] + 15)
        dr = np.clip(RK - RQ + 7, 0, 14)
        dc = np.clip(CK - CQ + 15, 0, 30)
        vals = rpb[:, dr, dc]
        tile = np.where(ok[None], vals, np.float32(-30000.0)).astype(np.float32)
        out[v] = tile.transpose(1, 0, 2)[:, [0, 2, 4, 6, 1, 3, 5, 7], :]
        return ok

    for il in range(2):
        for kt in range(4):
            fill(na_variant(il, kt), il, kt)
    for dj in range(-2, 3):
        fill(na_variant(8, 8 + dj), 8, 8 + dj)
    return np.ascontiguousarray(out.reshape(13, 128, 1024))


def prep_inputs(inputs, c):
    b, h = c // 2, c % 2
    xb = np.ascontiguousarray(inputs["x"][b])
    if h == 1:
        xb = np.ascontiguousarray(xb[::-1])
    f32 = lambda v: np.ascontiguousarray(np.asarray(v, np.float32))
    rep = lambda v: np.ascontiguousarray(np.broadcast_to(np.asarray(v, np.float32).reshape(1, -1), (128, np.asarray(v).size)))
    p = np.arange(128, dtype=np.float32)[:, None]
    jtab = (np.arange(512, dtype=np.float32)[None, :] - p).astype(np.float32)
    atab = np.abs(np.arange(896, dtype=np.float32)[None, :] - p - 384.0).astype(np.float32)
    atab = np.ascontiguousarray(np.stack([atab, atab], axis=1))
    cst = np.zeros((128, 4, 32, 2), np.float32)
    augt = np.zeros((128, 4, 2, 512), np.float32)
    jj = np.arange(512, dtype=np.float32)
    pp_ = np.arange(128, dtype=np.float32)
    for hh in range(4):
        sl = 2.0 ** (-8.0 * (hh + 1) / 4)
        for v in range(2):
            sgn = 1.0 if v == 0 else -1.0
            cst[:, hh, :, v] = sgn * sl * pp_[:, None] - sl * 128.0 * np.arange(32, dtype=np.float32)[None, :]
            hi = -sgn * sl * 256.0 * np.floor(jj / 256.0)
            lo = -sgn * sl * np.mod(jj, 256.0)
            for base in (0, 64):
                augt[base, hh, v] = hi
                augt[base + 1, hh, v] = lo
    cst = np.ascontiguousarray(cst.reshape(128, 256))
    augt = augt.astype(ml_dtypes.bfloat16)
    lamv = np.stack([rep(inputs["diff_lambda_q1"][0]), rep(inputs["diff_lambda_k1"][0]),
                     rep(inputs["diff_lambda_q2"][0]), rep(inputs["diff_lambda_k2"][0])], axis=1)
    m = {
        "x": xb,
        "wg1": f32(inputs["ffn1_w_gate"][0]), "wu1": f32(inputs["ffn1_w_up"][0]), "wd1": f32(inputs["ffn1_w_down"][0]),
        "wg2": f32(inputs["ffn2_w_gate"][0]), "wu2": f32(inputs["ffn2_w_up"][0]), "wd2": f32(inputs["ffn2_w_down"][0]),
        "win": f32(inputs["w_in"][0]), "wout": f32(inputs["w_out"][0]),
        "ln1g": rep(inputs["ln1_g"][0]), "ln1b": rep(inputs["ln1_b"][0]),
        "ln2g": rep(inputs["ln2_g"][0]), "ln2b": rep(inputs["ln2_b"][0]),
        "ln3g": rep(inputs["ln3_g"][0]), "ln3b": rep(inputs["ln3_b"][0]),
        "ident": np.eye(128, dtype=np.float32).astype(ml_dtypes.bfloat16),
        "nab": _na_tables(f32(inputs["na_rpb"][0]), h == 1),
        "augt": augt, "atab": atab, "cst": cst,
        "lamv": np.ascontiguousarray(lamv.astype(np.float32)), "subg": rep(inputs["diff_subln_g"][0]),
        "zeros_ones": np.stack([np.zeros((64, 4 * SEQ), np.float32), np.ones((64, 4 * SEQ), np.float32)]).astype(ml_dtypes.bfloat16),
    }
    return m


def kernel(**inputs):
    inputs = {k_: np.asarray(v) for k_, v in inputs.items()}
    nc = build_program()
    in_maps = [prep_inputs(inputs, c) for c in range(8)]
    res = run_bass_kernel_spmd(nc, in_maps, core_ids=list(range(8)))
    outp = np.empty((4, SEQ, D), np.float32)
    for c in range(8):
        b, h = c // 2, c % 2
        o = np.asarray(res.results[c]["out"])
        if h == 0:
            outp[b, :OWN] = o
        else:
            outp[b, OWN:] = o[::-1]
    return outp
```

```python
import numpy as np
from contextlib import ExitStack
import concourse.bass as bass
import concourse.mybir as mybir
from concourse.bass_utils import run_bass_kernel_spmd
import ml_dtypes

F32, BF16 = mybir.dt.float32, mybir.dt.bfloat16
AF = mybir.ActivationFunctionType
ALU = mybir.AluOpType

D = 1024
DFF = 2816
NFC = DFF // 128
SEQ = 4096
OWN = 2048
ALPHA = 2.0 ** 0.25
EPS = 1e-5
LAM_INIT = 0.2
NKT_NA = 18


def _flat(toks):
    out = []
    for t in toks:
        if t is None:
            continue
        if isinstance(t, list):
            out.extend(_flat(t))
        else:
            out.append(t)
    return out


class Eng:
    def __init__(self, nc, e, name, es):
        self.e = e
        self.name = name
        self.sem = es.enter_context(nc.semaphore("sem_" + name))
        self.n = 0
        self.seen = {}

    def wait(self, *toks):
        best = {}
        for src, v in _flat(list(toks)):
            if best.get(id(src), (None, 0))[1] < v:
                best[id(src)] = (src, v)
        for src, v in best.values():
            if self.seen.get(id(src), 0) >= v:
                continue
            self.e.wait_ge(src.sem, v)
            self.seen[id(src)] = v

    def mark(self, ins):
        ins.then_inc(self.sem, 1)
        self.n += 1
        return (self, self.n)


class Slot:
    def __init__(self, nc, name, es):
        self.sem = es.enter_context(nc.semaphore("dsem_" + name))
        self.n = 0
        self.busy = False

    def dma(self, q, out, in_):
        q.e.dma_start(out=out, in_=in_).then_inc(self.sem, 16)
        self.n += 16
        return (self, self.n)


class K:
    def slots(self, es, n):
        got = []
        for sl in self.slot_pool:
            if not sl.busy and len(got) < n:
                sl.busy = True
                got.append(sl)
        while len(got) < n:
            sl = Slot(self.nc, f"p{len(self.slot_pool)}", self.es_global)
            sl.busy = True
            self.slot_pool.append(sl)
            got.append(sl)

        def release():
            for sl in got:
                sl.busy = False
        es.callback(release)
        return got


def barrier(k, toks):
    toks = _flat(toks) + [(e, e.n) for e in k.engs if e.n > 0]
    for e in k.engs:
        e.wait(toks)


def copy_cast(eng, k, out, in_):
    if eng is k.act:
        return eng.e.activation(out=out, in_=in_, func=AF.Copy)
    return eng.e.tensor_copy(out=out, in_=in_)


class WeightLoader:
    def __init__(self, k, es, name, jobs, nslots=3, width=1408):
        nc = k.nc
        self.k = k
        self.jobs = jobs
        self.stg = [es.enter_context(nc.sbuf_tensor(f"{name}_stg{i}", [128, width], F32)) for i in range(nslots)]
        self.slots = k.slots(es, nslots)
        self.cast_tok = [None] * nslots
        self.engs = [k.dve, k.pool, k.act]
        self.toks = []
        self.i = 0
        k.last_stg = self.stg

    def emit(self, n):
        k = self.k
        nslots = len(self.stg)
        for _ in range(n):
            if self.i >= len(self.jobs):
                return
            i = self.i
            self.i += 1
            dst, src = self.jobs[i]
            s = i % nslots
            if len(src.shape) == 3:
                nel = src.shape[1] * src.shape[2]
                sv = self.stg[s][:, :nel].rearrange("p (a b) -> p a b", b=src.shape[2])
            else:
                nel = src.shape[-1]
                sv = self.stg[s][:, :nel]
            k.sp.wait(self.cast_tok[s])
            lt = self.slots[s].dma(k.sp, sv, src)
            e = self.engs[i % 3]
            e.wait(lt)
            self.cast_tok[s] = e.mark(copy_cast(e, k, dst, sv))
            self.toks.append(self.cast_tok[s])

    def done(self):
        return self.i >= len(self.jobs)


def load_cast_weights(k, es, name, jobs, nslots=3, width=1408):
    wl = WeightLoader(k, es, name, jobs, nslots, width)
    wl.emit(len(jobs))
    return wl.toks


def load_bf16_weights(k, q, slot, jobs):
    tok = None
    for dst, src in jobs:
        tok = slot.dma(q, dst, src)
    return tok


class BgCast:
    def __init__(self, k, es, name, jobs, in_bufs, first_tok):
        nc = k.nc
        self.k = k
        self.jobs = jobs
        self.inb = in_bufs
        self.outb = [es.enter_context(nc.sbuf_tensor(f"{name}_bgo{i}", [128, 1408], BF16)) for i in range(2)]
        self.in_slot = k.slots(es, 2)
        self.out_slot = k.slots(es, 2)
        self.load_tok = [first_tok, first_tok]
        self.cast_tok = [None, None]
        self.store_tok = [None, None]
        self.i = 0

    def step(self):
        k = self.k
        i = self.i
        n_jobs = len(self.jobs)
        if i > n_jobs + 1:
            return
        self.i += 1
        if 0 <= i - 2 < n_jobs:
            j = i - 2
            n = self.jobs[j][0].shape[-1]
            k.act.wait(self.cast_tok[j % 2])
            self.store_tok[j % 2] = self.out_slot[j % 2].dma(k.act, self.jobs[j][1], self.outb[j % 2][:, :n])
        if i < n_jobs:
            n = self.jobs[i][0].shape[-1]
            k.act.wait(self.cast_tok[i % 2], self.load_tok[i % 2] if i < 2 else None)
            self.load_tok[i % 2] = self.in_slot[i % 2].dma(k.act, self.inb[i % 2][:, :n], self.jobs[i][0])
        if 0 <= i - 1 < n_jobs:
            j = i - 1
            n = self.jobs[j][0].shape[-1]
            k.act.wait(self.load_tok[j % 2], self.store_tok[j % 2])
            self.cast_tok[j % 2] = k.act.mark(k.act.e.activation(out=self.outb[j % 2][:, :n], in_=self.inb[j % 2][:, :n], func=AF.Copy))

    def finish(self):
        while self.i <= len(self.jobs) + 1:
            self.step()
        return [t for t in self.store_tok if t is not None]


def ln_part_a(k, ctx, t, psum_halves, xres, xscale, eps, dst_ap, pre_toks):
    nsl = ctx["n"]
    dve, pool, sp = k.dve, k.pool, k.sp
    s = t % nsl
    r = ctx["r"][s]
    stats, mv, ve, rstd = ctx["stats"][s], ctx["mv"][s], ctx["ve"][s], ctx["rstd"][s]
    stt_toks = []
    for hh in range(2):
        dve.wait(pre_toks, ctx["store_tok"][s])
        ins = dve.e.scalar_tensor_tensor(out=r[:, hh * 512:(hh + 1) * 512], in0=xres[:, hh * 512:(hh + 1) * 512],
                                         scalar=float(xscale), op0=ALU.mult, in1=psum_halves[hh], op1=ALU.add)
        stt_toks.append(dve.mark(ins))
    st_toks = []
    for hh in range(2):
        dve.wait(stt_toks[hh])
        st_toks.append(dve.mark(dve.e.bn_stats(out=stats[:, hh * 6:(hh + 1) * 6], in_=r[:, hh * 512:(hh + 1) * 512])))
    dve.wait(st_toks)
    t1 = dve.mark(dve.e.bn_aggr(out=mv[:], in_=stats[:]))
    dve.wait(t1)
    t2 = dve.mark(dve.e.tensor_scalar(out=ve[:], in0=mv[:, 1:2], scalar1=float(eps), scalar2=None, op0=ALU.add))
    pool.wait(t2, ctx["gb_tok"])
    t3 = pool.mark(pool.e.tensor_tensor(out=rstd[:], in0=ve[:], in1=ctx["mhalf"][:], op=ALU.pow))
    dve.wait(t1, ctx["gb_tok"])
    t4 = dve.mark(dve.e.scalar_tensor_tensor(out=r[:], in0=r[:], scalar=mv[:, 0:1], op0=ALU.subtract, in1=ctx["g"][:], op1=ALU.mult))
    ctx["pend"][t] = (t3, t4, dst_ap)
    return stt_toks


def ln_part_b(k, ctx, t):
    dve, sp = k.dve, k.sp
    s = t % ctx["n"]
    r = ctx["r"][s]
    t3, t4, dst_ap = ctx["pend"].pop(t)
    dve.wait(t3, t4)
    t6 = dve.mark(dve.e.scalar_tensor_tensor(out=r[:], in0=r[:], scalar=ctx["rstd"][s][:, 0:1], op0=ALU.mult, in1=ctx["b"][:], op1=ALU.add))
    sp.wait(t6)
    ctx["store_tok"][s] = ctx["store_slot"][s].dma(sp, dst_ap, r[:])


def ln_ctx(k, es, name, g_d, b_d, nsl=2):
    nc = k.nc
    sb = lambda n, shape, dt: es.enter_context(nc.sbuf_tensor(f"{name}_{n}", shape, dt))
    ctx = {
        "n": nsl,
        "pend": {},
        "r": [sb(f"r{i}", [128, D], F32) for i in range(nsl)],
        "stats": [sb(f"stats{i}", [128, 12], F32) for i in range(nsl)],
        "mv": [sb(f"mv{i}", [128, 2], F32) for i in range(nsl)],
        "ve": [sb(f"ve{i}", [128, 1], F32) for i in range(nsl)],
        "rstd": [sb(f"rstd{i}", [128, 1], F32) for i in range(nsl)],
        "mhalf": sb("mhalf", [128, 1], F32),
        "g": sb("g", [128, D], F32),
        "b": sb("b", [128, D], F32),
        "store_slot": k.slots(es, nsl),
        "store_tok": [None] * nsl,
    }
    gs = k.slots(es, 1)[0]
    gs.dma(k.sp, ctx["g"][:], g_d)
    tg = gs.dma(k.sp, ctx["b"][:], b_d)
    tm = k.pool.mark(k.pool.e.memset(ctx["mhalf"][:], -0.5))
    ctx["gb_tok"] = [tg, tm]
    return ctx


def alloc_ffn_weights(k, es, name, with_wd=True):
    nc = k.nc
    wg = es.enter_context(nc.sbuf_tensor(f"{name}_wg", [128, 8, DFF], BF16))
    wu = es.enter_context(nc.sbuf_tensor(f"{name}_wu", [128, 8, DFF], BF16))
    wd = es.enter_context(nc.sbuf_tensor(f"{name}_wd", [128, NFC, D], BF16)) if with_wd else None
    return wg, wu, wd


def ffn_phase(k, name, x_src, T, wg_d, wu_d, wd_d, g_d, b_d, dst, pre=None, bg_jobs=None):
    nc = k.nc
    pe, act, dve, pool, sp = k.pe, k.act, k.dve, k.pool, k.sp
    NB = T // 256
    NH = 4
    with ExitStack() as es:
        sb = lambda n, shape, dt: es.enter_context(nc.sbuf_tensor(f"{name}_{n}", shape, dt))
        ps = lambda n, shape, dt: es.enter_context(nc.psum_tensor(f"{name}_{n}", shape, dt))
        bg = None
        emit_weights = None
        wl = None
        if pre is not None:
            wg, wu, wdA, wd_bf_d, wtoks = pre
            wdB = es.enter_context(nc.sbuf_tensor(f"{name}_wdB", [128, NFC // 2, D], BF16))
            wdv = wd_bf_d.rearrange("(c p) f -> p c f", p=128)
            wdB_tok = load_bf16_weights(k, k.act, k.slots(es, 1)[0],
                                        [(wdB[:, c:c + 1, :], wdv[:, NFC // 2 + c:NFC // 2 + c + 1, :]) for c in range(NFC // 2)])
            wd_ap = lambda fd, lo, hi: (wdA if fd < NFC // 2 else wdB)[:, fd % (NFC // 2), lo:hi]
            gu_wtok = {f: wtoks for f in range(NFC)}
            d_wtok = {f: None for f in range(NFC)}
        else:
            wg, wu, wd = alloc_ffn_weights(k, es, name)
            wgv = wg_d.rearrange("(c p) f -> p c f", p=128)
            wuv = wu_d.rearrange("(c p) f -> p c f", p=128)
            wdv = wd_d.rearrange("(c p) d -> p c d", p=128)
            jobs = []
            for fg in range(NFC // 2):
                cs = slice(fg * 256, (fg + 1) * 256)
                for c0 in (0, 4):
                    jobs.append((wg[:, c0:c0 + 4, cs], wgv[:, c0:c0 + 4, cs]))
                    jobs.append((wu[:, c0:c0 + 4, cs], wuv[:, c0:c0 + 4, cs]))
                for c in (2 * fg, 2 * fg + 1):
                    jobs.append((wd[:, c, :], wdv[:, c, :]))
            wdB_tok = None
            wd_ap = lambda fd, lo, hi: wd[:, fd, lo:hi]
            gu_wtok, d_wtok = {}, {}

            wl = WeightLoader(k, es, name, jobs)
            for f in range(NFC):
                gu_wtok[f] = ("wl", (f // 2) * 6, (f // 2) * 6 + 4)
                d_wtok[f] = ("wl", (f // 2) * 6 + 4 + (f % 2), (f // 2) * 6 + 5 + (f % 2))

            def emit_weights():
                wl.emit(12)
        ctx = ln_ctx(k, es, name, g_d, b_d)

        xA = [sb(f"xA{i}", [128, D], F32) for i in range(2)]
        xA_slot = k.slots(es, 2)
        xR = [sb(f"xR{i}", [128, D], F32) for i in range(2)]
        xR_slot = k.slots(es, 2)
        xbf = [sb(f"xbf{i}", [128, D], BF16) for i in range(2)]
        xT = [sb(f"xT{i}", [128, 8, 256], BF16) for i in range(2)]
        hT = [sb(f"hT{i}", [128, 256], BF16) for i in range(NH)]
        sg = [sb(f"sg{i}", [128, 256], F32) for i in range(2)]
        ident = k.ident
        Tps = [ps(f"T{i}", [128, 8, 128], BF16) for i in range(2)]
        gu = [ps(f"gu{i}", [128, 2, 256], F32) for i in range(2)]
        yp = [[ps(f"y{t}{h}", [128, 512], F32) for h in range(2)] for t in range(2)]

        cast_tok = [None, None]
        T_tok = [None, None]
        XE_tok = {}
        xR_free = [None, None]
        xR_tok = [None, None]
        gu_last = {}
        mult_tok = {}
        D_tok = {}
        ep_tok = {}

        def stage_load_cast(b):
            for t in range(2):
                sp.wait(cast_tok[t])
                lt = xA_slot[t].dma(sp, xA[t][:], x_src[(b * 2 + t) * 128:(b * 2 + t + 1) * 128, :])
                pool.wait(lt, T_tok[t])
                cast_tok[t] = pool.mark(pool.e.tensor_copy(out=xbf[t][:], in_=xA[t][:]))

        def stage_T(b):
            for t in range(2):
                prev = XE_tok.get((b - 1, t))
                pe.wait(cast_tok[t], prev, k.ident_tok)
                for c in range(8):
                    ins = pe.e.transpose(out=Tps[t][:, c, :], in_=xbf[t][:, c * 128:(c + 1) * 128], identity=ident[:])
                T_tok[t] = pe.mark(ins)

        def stage_XE(b):
            for t in range(2):
                act.wait(T_tok[t], gu_last.get(b - 2))
                XE_tok[(b, t)] = act.mark(act.e.activation(out=xT[b % 2][:, :, t * 128:(t + 1) * 128], in_=Tps[t][:], func=AF.Copy))

        def stage_xR(b):
            for t in range(2):
                sp.wait(xR_free[t])
                xR_tok[t] = xR_slot[t].dma(sp, xR[t][:], x_src[(b * 2 + t) * 128:(b * 2 + t + 1) * 128, :])

        stage_load_cast(0)
        if emit_weights is not None:
            emit_weights()
        stage_T(0)
        stage_XE(0)
        def wres(t):
            if isinstance(t, tuple) and len(t) == 3 and t[0] == "wl":
                return list(wl.toks[t[1]:t[2]])
            return t

        def stage_down(b, fd):
            gd = b * NFC + fd
            pe.wait(mult_tok[gd], ep_tok.get(b - 1) if fd == 0 else None, wdB_tok if (b == 0 and fd == NFC // 2) else None,
                    wres(d_wtok[fd]) if b == 0 else None)
            for t in range(2):
                for hh in range(2):
                    ins = pe.e.matmul(yp[t][hh][:], lhsT=hT[gd % NH][:, t * 128:(t + 1) * 128],
                                      rhs=wd_ap(fd, hh * 512, (hh + 1) * 512), start=(fd == 0), stop=(fd == NFC - 1))
            D_tok[gd] = pe.mark(ins)

        for b in range(NB):
            if b + 1 < NB:
                stage_load_cast(b + 1)
            stage_xR(b)
            xt = xT[b % 2]
            for f in range(NFC):
                gi = b * NFC + f
                if wl is not None and b == 0 and f % 2 == 0:
                    wl.emit(6 * (f // 2 + 3) - wl.i)
                    if wl.done() and bg is None and bg_jobs:
                        bg = BgCast(k, es, name, bg_jobs, k.last_stg[:2], list(wl.toks))
                pe.wait(XE_tok[(b, 0)], XE_tok[(b, 1)], mult_tok.get(gi - 2), wres(gu_wtok[f]) if b == 0 else None)
                for c in range(8):
                    pe.e.matmul(gu[gi % 2][:, 0, :], lhsT=wg[:, c, f * 128:(f + 1) * 128], rhs=xt[:, c, :],
                                start=(c == 0), stop=(c == 7))
                for c in range(8):
                    ins = pe.e.matmul(gu[gi % 2][:, 1, :], lhsT=wu[:, c, f * 128:(f + 1) * 128], rhs=xt[:, c, :],
                                      start=(c == 0), stop=(c == 7))
                gtok = pe.mark(ins)
                if f == NFC - 1:
                    gu_last[b] = gtok
                act.wait(gtok, mult_tok.get(gi - 2))
                stok = act.mark(act.e.activation(out=sg[gi % 2][:], in_=gu[gi % 2][:, 0, :], func=AF.Silu))
                dve.wait(stok, D_tok.get(gi - NH))
                mult_tok[gi] = dve.mark(dve.e.tensor_tensor(out=hT[gi % NH][:], in0=sg[gi % 2][:], in1=gu[gi % 2][:, 1, :], op=ALU.mult))
                if f == 10 and b + 1 < NB:
                    stage_T(b + 1)
                    stage_XE(b + 1)
                if bg is not None and f in (1, 5, 9, 13, 17, 20):
                    bg.step()
                if f >= 1:
                    stage_down(b, f - 1)
            stage_down(b, NFC - 1)
            etoks = []
            for t in range(2):
                gt = b * 2 + t
                stt = ln_part_a(k, ctx, gt, [yp[t][0][:], yp[t][1][:]], xR[t], 2.0 * ALPHA, 4.0 * EPS,
                                dst[gt * 128:(gt + 1) * 128, :], [D_tok[b * NFC + NFC - 1], xR_tok[t]])
                xR_free[t] = stt
                etoks.extend(stt)
            for t in range(2):
                ln_part_b(k, ctx, b * 2 + t)
            ep_tok[b] = etoks
        barrier(k, [ctx["store_tok"], bg.finish() if bg is not None else None])


def xT_block_loader(k, es, name, src, ntile_list):
    nc = k.nc
    sb = lambda n, shape, dt: es.enter_context(nc.sbuf_tensor(f"{name}_{n}", shape, dt))
    st = {
        "xA": [sb(f"lxA{i}", [128, D], F32) for i in range(2)],
        "xbf": [sb(f"lxbf{i}", [128, D], BF16) for i in range(2)],
        "slot": k.slots(es, 2),
        "Tps": [es.enter_context(nc.psum_tensor(f"{name}_lT{i}", [128, 8, 128], BF16)) for i in range(2)],
        "cast_tok": [None, None], "T_tok": [None, None], "XE_tok": [None, None], "i": 0,
    }

    def emit(tile, dstT, dst_free_tok=None):
        i = st["i"]; st["i"] += 1
        s_ = i % 2
        k.sp.wait(st["cast_tok"][s_])
        lt = st["slot"][s_].dma(k.sp, st["xA"][s_][:], src[tile * 128:(tile + 1) * 128, :])
        k.pool.wait(lt, st["T_tok"][s_])
        st["cast_tok"][s_] = k.pool.mark(k.pool.e.tensor_copy(out=st["xbf"][s_][:], in_=st["xA"][s_][:]))
        k.pe.wait(st["cast_tok"][s_], st["XE_tok"][s_], k.ident_tok)
        for c in range(8):
            ins = k.pe.e.transpose(out=st["Tps"][s_][:, c, :], in_=st["xbf"][s_][:, c * 128:(c + 1) * 128], identity=k.ident[:])
        st["T_tok"][s_] = k.pe.mark(ins)
        k.act.wait(st["T_tok"][s_], dst_free_tok)
        st["XE_tok"][s_] = k.act.mark(k.act.e.activation(out=dstT, in_=st["Tps"][s_][:], func=AF.Copy))
        return st["XE_tok"][s_]
    return emit


def win_jobs(win_sb, win_d, col0, ncols):
    v = win_d.rearrange("(c p) f -> p c f", p=128)
    return [(win_sb[:, c, :], v[:, c, col0:col0 + ncols]) for c in range(8)]


def na_kt_set(il):
    return [0, 1, 2, 3] if il < 2 else list(range(il - 2, il + 3))


def na_variant(il, kt):
    return il * 4 + kt if il < 2 else 8 + (kt - il + 2)


def na_phase(k, x1_d, win_d, nab_d, mixT):
    nc = k.nc
    pe, act, dve, pool, sp = k.pe, k.act, k.dve, k.pool, k.sp
    with ExitStack() as es:
        sb = lambda n, shape, dt: es.enter_context(nc.sbuf_tensor(f"na_{n}", shape, dt))
        ps = lambda n, shape, dt: es.enter_context(nc.psum_tensor(f"na_{n}", shape, dt))
        KT = sb("KT", [128, 4, NKT_NA * 128], BF16)
        QT = [sb(f"QT{i}", [128, 4, OWN], BF16) for i in range(2)]
        VA = sb("VA", [128, NKT_NA, 8, 65], BF16)
        zer = sb("zer", [128, 512], BF16)
        nab = sb("nab", [128, 13, 1024], F32)
        nsl = k.slots(es, 1)[0]
        for v in range(13):
            nab_tok = nsl.dma(act, nab[:, v, :], nab_d[v])
        tz = pool.mark(pool.e.memset(zer[:], 0.0))
        dve.e.memset(QT[0][64:128, :, :], 0.0)
        dve.e.memset(QT[1][0:64, :, :], 0.0)
        tv1 = dve.mark(dve.e.memset(VA[:, :, :, 64:65], 1.0))
        pad_tok = tv1
        with ExitStack() as es2:
            sb2 = lambda n, shape, dt: es2.enter_context(nc.sbuf_tensor(f"nap_{n}", shape, dt))
            win = sb2("win", [128, 8, 1536], BF16)
            wtoks = load_bf16_weights(k, sp, k.slots(es2, 1)[0], win_jobs(win, win_d, 0, 1536))
            x1T = [sb2(f"x1T{i}", [128, 8, 512], BF16) for i in range(2)]
            pp = [es2.enter_context(nc.psum_tensor(f"nap_pp{i}", [128, 512], F32)) for i in range(3)]
            emit = xT_block_loader(k, es2, "nap", x1_d, None)
            pp_free = [None] * 3
            blk_last_pe = [None, None]
            npp = 0
            for blk in range(5):
                ntile = 4 if blk < 4 else 2
                ntok = ntile * 128
                xt = x1T[blk % 2]
                xe = [emit(blk * 4 + t, xt[:, :, t * 128:(t + 1) * 128], blk_last_pe[blk % 2]) for t in range(ntile)]
                pe.wait(xe, wtoks)
                for kind in range(2):
                    if kind == 1 and blk >= 4:
                        continue
                    for hp in range(4):
                        col = (512 if kind == 0 else 0) + hp * 128
                        b_ = npp % 3; npp += 1
                        pe.wait(pp_free[b_])
                        for c in range(8):
                            ins = pe.e.matmul(pp[b_][:, :ntok], lhsT=win[:, c, col:col + 128], rhs=xt[:, c, :ntok], start=(c == 0), stop=(c == 7))
                        tk = pe.mark(ins)
                        act.wait(tk)
                        if kind == 0:
                            pp_free[b_] = act.mark(act.e.activation(out=KT[:, hp, blk * 512:blk * 512 + ntok], in_=pp[b_][:, :ntok], func=AF.Copy))
                        else:
                            act.wait(tv1)
                            act.e.activation(out=QT[0][0:64, hp, blk * 512:blk * 512 + ntok], in_=pp[b_][0:64, :ntok], func=AF.Copy, scale=0.125)
                            pp_free[b_] = act.mark(act.e.activation(out=QT[1][64:128, hp, blk * 512:blk * 512 + ntok], in_=pp[b_][64:128, :ntok], func=AF.Copy, scale=0.125))
                for t in range(ntile):
                    b_ = npp % 3; npp += 1
                    pe.wait(pp_free[b_])
                    for c in range(8):
                        ins = pe.e.matmul(pp[b_][:], lhsT=xt[:, c, t * 128:(t + 1) * 128], rhs=win[:, c, 1024:1536], start=(c == 0), stop=(c == 7))
                    tk = pe.mark(ins)
                    dve.wait(tk, tv1)
                    pp_free[b_] = dve.mark(dve.e.tensor_copy(out=VA[:, blk * 4 + t, :, 0:64], in_=pp[b_][:].rearrange("p (h e) -> p h e", e=64)))
                blk_last_pe[blk % 2] = tk
            barrier(k, [])
        NSP, NTM, NE = 2, 2, 3
        sps = [ps(f"s{i}", [128, 2, 512], F32) for i in range(NSP)]
        acc = [ps(f"acc{i}", [128, 512], F32) for i in range(2)]
        tpo = ps("tpo", [128, 4, 128], BF16)
        tmp = [sb(f"tmp{i}", [128, 2, 512], F32) for i in range(NTM)]
        E = [sb(f"E{i}", [128, 2, 512], BF16) for i in range(NE)]
        rr = sb("rr", [128, 8], F32)
        nao = [sb(f"nao{i}", [128, 512], BF16) for i in range(2)]
        sps_free = [None] * NSP
        tmp_free = [None] * NTM
        E_free = [None] * NE
        acc_free = [None, None]
        nao_free = [None, None]
        nao_tok = {}
        tpo_free = None
        ns = 0

        def finish_il(il_):
            nonlocal tpo_free
            s2 = il_ % 2
            pe.wait(nao_tok[il_], tpo_free)
            for hp in range(4):
                ins = pe.e.transpose(out=tpo[:, hp, :], in_=nao[s2][:, hp * 128:(hp + 1) * 128], identity=k.ident[:])
            tt = pe.mark(ins)
            nao_free[s2] = tt
            act.wait(tt)
            tpo_free = act.mark(act.e.activation(out=mixT[:, 0:4, il_ * 128:(il_ + 1) * 128], in_=tpo[:], func=AF.Copy))

        pend_il = None
        accs = sb("accs", [128, 2, 260], F32)
        accs_free = None
        E_tok = {}

        def emit_S(il_, si):
            nonlocal ns
            kt = na_kt_set(il_)[si]
            g = ns; ns += 1
            pe.wait(sps_free[g % NSP], pad_tok)
            for hb in range(2):
                for hl in range(4):
                    ins = pe.e.matmul(sps[g % NSP][:, hb, hl * 128:(hl + 1) * 128], lhsT=KT[:, hl, kt * 128:(kt + 1) * 128],
                                      rhs=QT[hb][:, hl, il_ * 128:(il_ + 1) * 128], start=True, stop=True)
            tk = pe.mark(ins)
            v = na_variant(il_, kt)
            dve.wait(tk, tmp_free[g % NTM], nab_tok)
            t1 = dve.mark(dve.e.tensor_tensor(out=tmp[g % NTM][:], in0=nab[:, v, :].rearrange("p (a b) -> p a b", a=2), in1=sps[g % NSP][:], op=ALU.add))
            sps_free[g % NSP] = t1
            act.wait(t1, E_free[g % NE])
            t2 = act.mark(act.e.activation(out=E[g % NE][:], in_=tmp[g % NTM][:], func=AF.Exp))
            tmp_free[g % NTM] = t2
            E_tok[(il_, si)] = (t2, g)

        def emit_AV(il_, si, last):
            kt = na_kt_set(il_)[si]
            t2, g = E_tok.pop((il_, si))
            pe.wait(t2)
            for hb in range(2):
                for hl in range(4):
                    h = 2 * hl + hb
                    ins = pe.e.matmul(acc[hb][:, hl * 65:(hl + 1) * 65], lhsT=E[g % NE][:, hb, hl * 128:(hl + 1) * 128],
                                      rhs=VA[:, kt, h, :], start=False, stop=(last and hl == 3))
            E_free[g % NE] = pe.mark(ins)
            return E_free[g % NE]

        emit_S(0, 0)
        emit_S(0, 1)
        for il in range(16):
            nst = len(na_kt_set(il))
            for hb in range(2):
                pe.wait(acc_free[hb], tz)
                pe.e.matmul(acc[hb][:], lhsT=zer[:, 0:128], rhs=zer[:], start=True, stop=False)
            for si in range(nst):
                if si + 2 < nst:
                    emit_S(il, si + 2)
                last_av = emit_AV(il, si, si == nst - 1)
                if si == 1 and pend_il is not None:
                    finish_il(pend_il)
                    pend_il = None
            if il + 1 < 16:
                emit_S(il + 1, 0)
                emit_S(il + 1, 1)
            s_ = il % 2
            evt = []
            for hb in range(2):
                act.wait(last_av, accs_free)
                acc_free[hb] = act.mark(act.e.activation(out=accs[:, hb, :], in_=acc[hb][:, 0:260], func=AF.Copy))
                evt.append(acc_free[hb])
            for hb in range(2):
                accv = accs[:, hb, :].rearrange("p (h e) -> p h e", e=65)
                dve.wait(evt, nao_free[s_])
                tr = dve.mark(dve.e.reciprocal(out=rr[:, hb * 4:(hb + 1) * 4], in_=accv[:, :, 64]))
                dve.wait(tr)
                for hl in range(4):
                    h = 2 * hl + hb
                    ins = dve.e.tensor_scalar(out=nao[s_][:, h * 64:(h + 1) * 64], in0=accv[:, hl, 0:64], scalar1=rr[:, hb * 4 + hl:hb * 4 + hl + 1], scalar2=None, op0=ALU.mult)
                accs_free = dve.mark(ins)
            nao_tok[il] = accs_free
            pend_il = il
        finish_il(pend_il)
        barrier(k, [])


def diff_phase(k, x1_d, win_d, aug_d, atab_d, cst_d, lamv_d, subg_d, mixT):
    nc = k.nc
    pe, act, dve, pool, sp = k.pe, k.act, k.dve, k.pool, k.sp
    SL = [2.0 ** (-8.0 * (h + 1) / 4) for h in range(4)]
    with ExitStack() as es:
        sb = lambda n, shape, dt: es.enter_context(nc.sbuf_tensor(f"df_{n}", shape, dt))
        ps = lambda n, shape, dt: es.enter_context(nc.psum_tensor(f"df_{n}", shape, dt))
        KT = [sb(f"KT{i}", [128, 4, SEQ], BF16) for i in range(2)]
        QT = sb("QT", [128, 4, OWN], BF16)
        VA = sb("VA", [128, 32, 4, 129], BF16)
        cst = sb("cst", [128, 256], F32)
        lamv = sb("lamv", [128, 4, 64], F32)
        g8 = sb("g8", [128, 128], F32)
        zer = sb("zer", [128, 512], BF16)
        sm = sb("sm", [128, 8], F32)
        junk = sb("junk", [128, 64], F32)
        mhalf = sb("mhalf", [128, 1], F32)
        tz = pool.mark(pool.e.memset(zer[:], 0.0))
        pool.e.memset(mhalf[:], -0.5)
        ones_t = sb("ones_t", [128, 512], BF16)
        pad_tok = pool.mark(pool.e.memset(ones_t[:], 1.0))
        tv1 = dve.mark(dve.e.memset(VA[:, :, :, 128:129], 1.0))
        csl = k.slots(es, 1)[0]
        csl.dma(sp, cst[:], cst_d)
        csl.dma(sp, lamv[:], lamv_d)
        ctok = csl.dma(sp, g8[:], subg_d)
        dve.wait(ctok)
        a0 = dve.mark(dve.e.tensor_scalar(out=g8[:], in0=g8[:], scalar1=1.0 - LAM_INIT, scalar2=None, op0=ALU.mult))
        dve.wait(a0)
        a1 = dve.mark(dve.e.scalar_tensor_tensor(out=junk[:], in0=lamv[:, 0, :], scalar=1.0, op0=ALU.mult, in1=lamv[:, 1, :], op1=ALU.mult, accum_out=sm[:, 0:1]))
        dve.wait(a1)
        a2 = dve.mark(dve.e.scalar_tensor_tensor(out=junk[:], in0=lamv[:, 2, :], scalar=1.0, op0=ALU.mult, in1=lamv[:, 3, :], op1=ALU.mult, accum_out=sm[:, 1:2]))
        act.wait(a2)
        a3 = act.mark(act.e.activation(out=sm[:, 2:4], in_=sm[:, 0:2], func=AF.Exp))
        dve.wait(a3)
        a4 = dve.mark(dve.e.tensor_tensor(out=sm[:, 5:6], in0=sm[:, 3:4], in1=sm[:, 2:3], op=ALU.subtract))
        dve.wait(a4)
        lam_tok = dve.mark(dve.e.tensor_scalar(out=sm[:, 4:5], in0=sm[:, 5:6], scalar1=-LAM_INIT, scalar2=None, op0=ALU.add))
        neglam = sm[:, 4:5]
        with ExitStack() as es2:
            sb2 = lambda n, shape, dt: es2.enter_context(nc.sbuf_tensor(f"dfp_{n}", shape, dt))
            win = sb2("win", [128, 8, 1536], BF16)
            wtoks = load_bf16_weights(k, sp, k.slots(es2, 1)[0], win_jobs(win, win_d, 1536, 1536))
            x1T = [sb2(f"x1T{i}", [128, 8, 512], BF16) for i in range(2)]
            pp = [es2.enter_context(nc.psum_tensor(f"dfp_pp{i}", [128, 512], F32)) for i in range(3)]
            emit = xT_block_loader(k, es2, "dfp", x1_d, None)
            pp_free = [None] * 3
            blk_last_pe = [None, None]
            npp = 0
            for blk in range(8):
                xt = x1T[blk % 2]
                xe = [emit(blk * 4 + t, xt[:, :, t * 128:(t + 1) * 128], blk_last_pe[blk % 2]) for t in range(4)]
                pe.wait(xe, wtoks)
                for kind in range(2):
                    if kind == 1 and blk >= 4:
                        continue
                    for h in range(4):
                        col = (512 if kind == 0 else 0) + h * 128
                        b_ = npp % 3; npp += 1
                        pe.wait(pp_free[b_])
                        for c in range(8):
                            ins = pe.e.matmul(pp[b_][:], lhsT=win[:, c, col:col + 128], rhs=xt[:, c, :], start=(c == 0), stop=(c == 7))
                        tk = pe.mark(ins)
                        act.wait(tk)
                        if kind == 0:
                            act.wait(pad_tok)
                            k0 = act.mark(act.e.activation(out=KT[0][:, h, blk * 512:(blk + 1) * 512], in_=pp[b_][:], func=AF.Copy))
                            k1 = act.mark(act.e.activation(out=KT[1][:, h, blk * 512:(blk + 1) * 512], in_=pp[b_][:], func=AF.Copy))
                            pp_free[b_] = k1
                            act.wait(k0, k1)
                            act.e.activation(out=KT[0][64:66, h, blk * 512:(blk + 1) * 512], in_=ones_t[64:66, :], func=AF.Copy)
                            kfix_tok = act.mark(act.e.activation(out=KT[1][0:2, h, blk * 512:(blk + 1) * 512], in_=ones_t[0:2, :], func=AF.Copy))
                        else:
                            pp_free[b_] = act.mark(act.e.activation(out=QT[:, h, blk * 512:(blk + 1) * 512], in_=pp[b_][:], func=AF.Copy, scale=0.125))
                for t in range(4):
                    b_ = npp % 3; npp += 1
                    pe.wait(pp_free[b_])
                    for c in range(8):
                        ins = pe.e.matmul(pp[b_][:], lhsT=xt[:, c, t * 128:(t + 1) * 128], rhs=win[:, c, 1024:1536], start=(c == 0), stop=(c == 7))
                    tk = pe.mark(ins)
                    dve.wait(tk, tv1)
                    pp_free[b_] = dve.mark(dve.e.tensor_copy(out=VA[:, blk * 4 + t, :, 0:128], in_=pp[b_][:].rearrange("p (h e) -> p h e", e=128)))
                blk_last_pe[blk % 2] = tk
            barrier(k, [])
        NSP, NTM, NE = 2, 2, 4
        atab = sb("atab", [128, 2, 896], F32)
        augtab = sb("augtab", [128, 4, 2, 512], BF16)
        Qs = [[[sb(f"Qs{u}{v}{m}", [128, 512], BF16) for m in range(2)] for v in range(3)] for u in range(2)]
        asl = k.slots(es, 1)[0]
        asl.dma(sp, atab[:], atab_d)
        atok = asl.dma(sp, augtab[:], aug_d)
        qz = None
        for u in range(2):
            for v in range(3):
                for m in range(2):
                    qz = pool.mark(pool.e.memset(Qs[u][v][m][:], 0.0))
        sps = [ps(f"s{i}", [128, 2, 512], F32) for i in range(NSP)]
        acc = [ps(f"acc{i}", [128, 512], F32) for i in range(3)]
        tpo = ps("tpo", [128, 128], BF16)
        tmp = [sb(f"tmp{i}", [128, 2, 512], F32) for i in range(NTM)]
        E = [sb(f"E{i}", [128, 2, 512], BF16) for i in range(NE)]
        accs = sb("accs", [128, 3, 387], F32)
        rr = sb("rr", [128, 4], F32)
        tq = sb("tq", [128, 128], F32)
        oq = sb("oq", [128, 128], F32)
        sps_free = [None] * NSP
        tmp_free = [None] * NTM
        E_free = [None] * NE
        acc_free = [None] * 3
        accs_free = None
        tpo_free = None
        ns = 0
        unit_last_S = {}
        units = [(h_, qb_) for h_ in range(4) for qb_ in range(4)]
        yq = [[sb(f"yq{u_}{q_}", [128, 128], BF16) for q_ in range(4)] for u_ in range(2)]
        yq_free = {}
        yq_tok = {}
        qtoks = {}

        def build_Qs(ui):
            h_, qb_ = units[ui]
            u_ = ui % 2
            pool.wait(qz, atok, unit_last_S.get(ui - 2), ctok)
            for m in range(2):
                r0 = 64 * m
                a0 = 64 - 64 * m
                for v in range(3):
                    qtok = pool.mark(pool.e.tensor_copy(out=Qs[u_][v][m][r0:r0 + 64, :], in_=QT[r0:r0 + 64, h_, qb_ * 512:(qb_ + 1) * 512]))
                for v in range(2):
                    qtok = pool.mark(pool.e.tensor_copy(out=Qs[u_][v][m][a0:a0 + 2, :], in_=augtab[a0:a0 + 2, h_, v, :]))
            qtoks[ui] = qtok

        def finish_transposes(ui):
            nonlocal tpo_free
            h_, qb_ = units[ui]
            for qt in range(4):
                pe.wait(yq_tok[(ui, qt)], tpo_free)
                tt = pe.mark(pe.e.transpose(out=tpo[:], in_=yq[ui % 2][qt][:], identity=k.ident[:]))
                yq_free[(ui % 2, qt)] = tt
                act.wait(tt)
                tok0 = (qb_ * 4 + qt) * 128
                tpo_free = act.mark(act.e.activation(out=mixT[:, 4 + h_, tok0:tok0 + 128], in_=tpo[:], func=AF.Copy))

        evts = {}

        def epilogue(ui):
            nonlocal accs_free
            u = ui % 2
            evt = evts[ui]
            for qt in range(4):
                g0, g1 = qt, 4 + qt
                O0 = accs[:, g0 // 3, (g0 % 3) * 129:(g0 % 3) * 129 + 129]
                O1 = accs[:, g1 // 3, (g1 % 3) * 129:(g1 % 3) * 129 + 129]
                dve.wait(evt, lam_tok)
                e1 = dve.mark(dve.e.reciprocal(out=rr[:, 0:1], in_=O0[:, 128:129]))
                e2 = dve.mark(dve.e.reciprocal(out=rr[:, 1:2], in_=O1[:, 128:129]))
                dve.wait(e1, e2)
                e3 = dve.mark(dve.e.tensor_tensor(out=rr[:, 2:3], in0=rr[:, 1:2], in1=neglam, op=ALU.mult))
                dve.wait(e3)
                e4 = dve.mark(dve.e.tensor_scalar(out=tq[:], in0=O1[:, 0:128], scalar1=rr[:, 2:3], scalar2=None, op0=ALU.mult))
                dve.wait(e4)
                e5 = dve.mark(dve.e.scalar_tensor_tensor(out=oq[:], in0=O0[:, 0:128], scalar=rr[:, 0:1], op0=ALU.mult, in1=tq[:], op1=ALU.add))
                dve.wait(e5)
                e6 = dve.mark(dve.e.scalar_tensor_tensor(out=tq[:], in0=oq[:], scalar=1.0 / 128.0, op0=ALU.mult, in1=oq[:], op1=ALU.mult, accum_out=rr[:, 3:4]))
                dve.wait(e6)
                e7 = dve.mark(dve.e.tensor_scalar(out=rr[:, 3:4], in0=rr[:, 3:4], scalar1=EPS, scalar2=None, op0=ALU.add))
                pool.wait(e7)
                e8 = pool.mark(pool.e.tensor_tensor(out=rr[:, 3:4], in0=rr[:, 3:4], in1=mhalf[:], op=ALU.pow))
                dve.wait(e8, yq_free.get((u, qt)))
                e9 = dve.mark(dve.e.scalar_tensor_tensor(out=yq[u][qt][:], in0=oq[:], scalar=rr[:, 3:4], op0=ALU.mult, in1=g8[:], op1=ALU.mult))
                yq_tok[(ui, qt)] = e9
                if qt == 3:
                    accs_free = e9

        build_Qs(0)
        pending = None
        pend_epi = None
        tr_at = -1
        for ui, (h, qb) in enumerate(units):
            if True:
                u = ui % 2
                for j in range(3):
                    pe.wait(acc_free[j], tz)
                    pe.e.matmul(acc[j][:], lhsT=zer[:, 0:128], rhs=zer[:], start=True, stop=False)
                E_tok = {}

                def emit_S(kt):
                    nonlocal ns
                    g = ns; ns += 1
                    delta = qb * 512 - kt * 128
                    v = 0 if delta >= 128 else (1 if delta <= -512 else 2)
                    pe.wait(sps_free[g % NSP], qtoks[ui], pad_tok)
                    for m in range(2):
                        ins = pe.e.matmul(sps[g % NSP][:, m, :], lhsT=KT[m][:, h, kt * 128:(kt + 1) * 128],
                                          rhs=Qs[u][v][m][:], start=True, stop=True)
                    tk = pe.mark(ins)
                    unit_last_S[ui] = tk
                    if v == 2:
                        dve.wait(tk, tmp_free[g % NTM], atok)
                        t1 = dve.mark(dve.e.scalar_tensor_tensor(out=tmp[g % NTM][:], in0=atab[:, :, delta + 384:delta + 384 + 512], scalar=float(-SL[h]),
                                                                 op0=ALU.mult, in1=sps[g % NSP][:], op1=ALU.add))
                        sps_free[g % NSP] = t1
                        act.wait(t1, E_free[g % NE])
                        t2 = act.mark(act.e.activation(out=E[g % NE][:], in_=tmp[g % NTM][:], func=AF.Exp))
                        tmp_free[g % NTM] = t2
                    else:
                        n = abs(delta) // 128
                        col = (h * 32 + n) * 2 + v
                        act.wait(tk, E_free[g % NE], ctok)
                        t2 = act.mark(act.e.activation(out=E[g % NE][:], in_=sps[g % NSP][:], func=AF.Exp, bias=cst[:, col:col + 1], scale=1.0))
                        sps_free[g % NSP] = t2
                    E_tok[kt] = (t2, g)

                def emit_AV(kt):
                    t2, g = E_tok[kt]
                    pe.wait(t2, tv1)
                    for m in range(2):
                        for qt in range(4):
                            gi = m * 4 + qt
                            ins = pe.e.matmul(acc[gi // 3][:, (gi % 3) * 129:(gi % 3) * 129 + 129], lhsT=E[g % NE][:, m, qt * 128:(qt + 1) * 128],
                                              rhs=VA[:, kt, h, :], start=False, stop=(kt == 31 and gi in (2, 5, 7)))
                    E_free[g % NE] = pe.mark(ins)
                    return E_free[g % NE]

                emit_S(0)
                emit_S(1)
                for kt in range(32):
                    if kt + 2 < 32:
                        emit_S(kt + 2)
                    last = emit_AV(kt)
                    if kt == 6 and ui + 1 < len(units):
                        build_Qs(ui + 1)
                    if pend_epi is not None and kt == min(4 * qb + 2, 14):
                        epilogue(pend_epi)
                        pending = pend_epi
                        pend_epi = None
                        tr_at = kt + 12
                    if pending is not None and pend_epi is None and kt == tr_at:
                        finish_transposes(pending)
                        pending = None
                evt = []
                for j in range(3):
                    act.wait(last, accs_free)
                    acc_free[j] = act.mark(act.e.activation(out=accs[:, j, :], in_=acc[j][:, 0:387], func=AF.Copy))
                    evt.append(acc_free[j])
                evts[ui] = evt
                pend_epi = ui
        epilogue(pend_epi)
        finish_transposes(pend_epi)
        barrier(k, [])


def wout_phase(k, x1_d, wout_d, g_d, b_d, mixT, x2_d):
    nc = k.nc
    pe, act, dve, pool, sp = k.pe, k.act, k.dve, k.pool, k.sp
    with ExitStack() as es:
        sb = lambda n, shape, dt: es.enter_context(nc.sbuf_tensor(f"wo_{n}", shape, dt))
        wo = sb("wo", [128, 8, D], BF16)
        v = wout_d.rearrange("(c p) f -> p c f", p=128)
        wtoks = load_bf16_weights(k, sp, k.slots(es, 1)[0], [(wo[:, c, :], v[:, c, :]) for c in range(8)])
        NW = 4
        ctx = ln_ctx(k, es, "wo", g_d, b_d, nsl=NW)
        xR = [sb(f"xR{i}", [128, D], F32) for i in range(NW)]
        xsl = k.slots(es, NW)
        yp = [[es.enter_context(nc.psum_tensor(f"wo_y{t}{h}", [128, 512], F32)) for h in range(2)] for t in range(NW)]
        xR_free = [None] * NW
        yp_free = [None] * NW
        lts = {}

        def issue_load(t_):
            sl_ = t_ % NW
            sp.wait(xR_free[sl_])
            lts[t_] = xsl[sl_].dma(sp, xR[sl_][:], x1_d[t_ * 128:(t_ + 1) * 128, :])

        for t_ in range(NW - 1):
            issue_load(t_)
        for t in range(16):
            s_ = t % NW
            if t + NW - 1 < 16:
                issue_load(t + NW - 1)
            lt = lts[t]
            pe.wait(wtoks, yp_free[s_])
            for hh in range(2):
                for c in range(8):
                    ins = pe.e.matmul(yp[s_][hh][:], lhsT=mixT[:, c, t * 128:(t + 1) * 128], rhs=wo[:, c, hh * 512:(hh + 1) * 512], start=(c == 0), stop=(c == 7))
            tk = pe.mark(ins)
            stt = ln_part_a(k, ctx, t, [yp[s_][0][:], yp[s_][1][:]], xR[s_], ALPHA, EPS, x2_d[t * 128:(t + 1) * 128, :], [tk, lt])
            xR_free[s_] = stt
            yp_free[s_] = stt
            if t >= 1:
                ln_part_b(k, ctx, t - 1)
        ln_part_b(k, ctx, 15)
        barrier(k, [ctx["store_tok"]])


def build_program(stop=None):
    nc = bass.Bass("TRN2", target_bir_lowering=False)
    dram_in = lambda n, shape, dt=F32: nc.dram_tensor(n, shape, dt, kind="ExternalInput").ap()
    x = dram_in("x", [SEQ, D])
    wg1 = dram_in("wg1", [D, DFF]); wu1 = dram_in("wu1", [D, DFF]); wd1 = dram_in("wd1", [DFF, D])
    wg2 = dram_in("wg2", [D, DFF]); wu2 = dram_in("wu2", [D, DFF]); wd2 = dram_in("wd2", [DFF, D])
    win = dram_in("win", [D, 3072]); wout = dram_in("wout", [D, D])
    ln1g = dram_in("ln1g", [128, D]); ln1b = dram_in("ln1b", [128, D])
    ln2g = dram_in("ln2g", [128, D]); ln2b = dram_in("ln2b", [128, D])
    ln3g = dram_in("ln3g", [128, D]); ln3b = dram_in("ln3b", [128, D])
    ident_d = dram_in("ident", [128, 128], BF16)
    nab = dram_in("nab", [13, 128, 1024])
    augt = dram_in("augt", [128, 4, 2, 512], BF16); atab = dram_in("atab", [128, 2, 896]); cst = dram_in("cst", [128, 256])
    lamv = dram_in("lamv", [128, 4, 64]); subg = dram_in("subg", [128, 128])
    zeros_ones = dram_in("zeros_ones", [2, 64, 4 * SEQ], BF16)
    out = nc.dram_tensor("out", [OWN, D], F32, kind="ExternalOutput").ap()
    x1_d = nc.dram_tensor("x1_scratch", [SEQ, D], F32, kind="Internal").ap()
    x2_d = nc.dram_tensor("x2_scratch", [OWN, D], F32, kind="Internal").ap()
    win_bf = nc.dram_tensor("win_bf", [D, 3072], BF16, kind="Internal").ap()
    wout_bf = nc.dram_tensor("wout_bf", [D, D], BF16, kind="Internal").ap()
    wg2_bf = nc.dram_tensor("wg2_bf", [D, DFF], BF16, kind="Internal").ap()
    wu2_bf = nc.dram_tensor("wu2_bf", [D, DFF], BF16, kind="Internal").ap()
    wd2_bf = nc.dram_tensor("wd2_bf", [DFF, D], BF16, kind="Internal").ap()

    def pieces(src, dst, width):
        sv = src.rearrange("(c p) f -> p c f", p=128)
        dv = dst.rearrange("(c p) f -> p c f", p=128)
        out_ = []
        for c in range(sv.shape[1]):
            for o in range(0, sv.shape[2], width):
                out_.append((sv[:, c, o:o + width], dv[:, c, o:o + width]))
        return out_
    bg_jobs = (pieces(win, win_bf, 1024) + pieces(wout, wout_bf, 1024) + pieces(wg2, wg2_bf, 1408)
               + pieces(wu2, wu2_bf, 1408) + pieces(wd2, wd2_bf, 1024))
    dbg = None
    if stop is not None:
        dbg = nc.dram_tensor("dbg", [SEQ, D], F32, kind="ExternalOutput").ap()
    with ExitStack() as es:
        k = K()
        k.nc = nc
        k.pe = Eng(nc, nc.tensor, "pe", es)
        k.act = Eng(nc, nc.scalar, "act", es)
        k.dve = Eng(nc, nc.vector, "dve", es)
        k.pool = Eng(nc, nc.gpsimd, "pool", es)
        k.sp = Eng(nc, nc.sync, "sp", es)
        k.engs = [k.pe, k.act, k.dve, k.pool, k.sp]
        k.es_global = es
        k.slot_pool = []
        k.zeros_ones = zeros_ones
        k.ident = es.enter_context(nc.sbuf_tensor("ident_sb", [128, 128], BF16))
        isl = k.slots(es, 1)[0]
        k.ident_tok = isl.dma(k.sp, k.ident[:], ident_d)
        if stop == "A":
            ffn_phase(k, "f1", x, SEQ, wg1, wu1, wd1, ln1g, ln1b, dbg, bg_jobs=bg_jobs)
            return nc
        if stop in ("NA", "DF", "W"):
            x1_src = x
            with ExitStack() as esb:
                inb = [esb.enter_context(nc.sbuf_tensor(f"dbg_in{i}", [128, 1408], F32)) for i in range(2)]
                bgc = BgCast(k, esb, "dbgc", bg_jobs[:32], inb, None)
                barrier(k, [bgc.finish()])
        else:
            ffn_phase(k, "f1", x, SEQ, wg1, wu1, wd1, ln1g, ln1b, x1_d, bg_jobs=bg_jobs)
            x1_src = x1_d
        mix_cm = nc.sbuf_tensor("mixT", [128, 8, OWN], BF16, side="right")
        mixT = mix_cm.__enter__()
        if stop != "DF":
            na_phase(k, x1_src, win_bf, nab, mixT)
        if stop != "NA":
            diff_phase(k, x1_src, win_bf, augt, atab, cst, lamv, subg, mixT)
        if stop in ("NA", "DF"):
            with ExitStack() as es3:
                tmpf = es3.enter_context(nc.sbuf_tensor("dbg_tmp", [128, 8, OWN], F32))
                k.dve.wait((k.act, k.act.n))
                c0_ = 0 if stop == "NA" else 4
                k.pool.wait((k.act, k.act.n))
                tk0 = k.pool.mark(k.pool.e.memset(tmpf[:], 0.0))
                k.dve.wait(tk0)
                tk = k.dve.mark(k.dve.e.tensor_copy(out=tmpf[:, c0_:c0_ + 4, :], in_=mixT[:, c0_:c0_ + 4, :]))
                k.sp.wait(tk)
                sl = k.slots(es3, 1)[0]
                for c in range(8):
                    for hf in range(2):
                        t_ = sl.dma(k.sp, dbg[(c * 2 + hf) * 128:(c * 2 + hf + 1) * 128, :], tmpf[:, c, hf * 1024:(hf + 1) * 1024])
                barrier(k, [t_])
            mix_cm.__exit__(None, None, None)
            return nc
        with ExitStack() as esf2:
            pre = None
            if stop is None:
                wg2s, wu2s, _ = alloc_ffn_weights(k, esf2, "f2", with_wd=False)
                wdA2 = esf2.enter_context(nc.sbuf_tensor("f2_wdA", [128, NFC // 2, D], BF16))
                wsl = k.slots(esf2, 1)[0]
                jobs2 = ([(wg2s[:, c, :], wg2_bf.rearrange("(c p) f -> p c f", p=128)[:, c, :]) for c in range(8)]
                         + [(wu2s[:, c, :], wu2_bf.rearrange("(c p) f -> p c f", p=128)[:, c, :]) for c in range(8)]
                         + [(wdA2[:, c:c + 1, :], wd2_bf.rearrange("(c p) f -> p c f", p=128)[:, c:c + 1, :]) for c in range(NFC // 2)])
                pre = (wg2s, wu2s, wdA2, wd2_bf, load_bf16_weights(k, k.act, wsl, jobs2))
            wout_phase(k, x1_src, wout_bf, ln2g, ln2b, mixT, x2_d if stop is None else dbg)
            mix_cm.__exit__(None, None, None)
            if stop == "W":
                return nc
            ffn_phase(k, "f2", x2_d, OWN, wg2, wu2, wd2, ln3g, ln3b, out, pre=pre)
    return nc


def _na_tables(rpb, rev):
    out = np.full((13, 128, 8, 128), -30000.0, np.float32)
    p = np.arange(128)

    def coords(tile):
        t = tile * 128 + p
        r, c = t // 64, t % 64
        if rev:
            r, c = 63 - r, 63 - c
        return r, c

    def fill(v, il, kt):
        rk, ck = coords(kt)
        rq, cq = coords(il)
        r0 = np.clip(rq - 4, 0, 56)
        c0 = np.clip(cq - 8, 0, 48)
        RK, RQ = rk[:, None], rq[None, :]
        CK, CQ = ck[:, None], cq[None, :]
        ok = (RK >= r0[None, :]) & (RK <= r0[None, :] + 7) & (CK >= c0[None, :]) & (CK <= c0[None, :] + 15)
        dr = np.clip(RK - RQ + 7, 0, 14)
        dc = np.clip(CK - CQ + 15, 0, 30)
        vals = rpb[:, dr, dc]
        tile = np.where(ok[None], vals, np.float32(-30000.0)).astype(np.float32)
        out[v] = tile.transpose(1, 0, 2)[:, [0, 2, 4, 6, 1, 3, 5, 7], :]
        return ok

    for il in range(2):
        for kt in range(4):
            fill(na_variant(il, kt), il, kt)
    for dj in range(-2, 3):
        fill(na_variant(8, 8 + dj), 8, 8 + dj)
    return np.ascontiguousarray(out.reshape(13, 128, 1024))


def prep_inputs(inputs, c):
    b, h = c // 2, c % 2
    xb = np.ascontiguousarray(inputs["x"][b])
    if h == 1:
        xb = np.ascontiguousarray(xb[::-1])
    f32 = lambda v: np.ascontiguousarray(np.asarray(v, np.float32))
    rep = lambda v: np.ascontiguousarray(np.broadcast_to(np.asarray(v, np.float32).reshape(1, -1), (128, np.asarray(v).size)))
    p = np.arange(128, dtype=np.float32)[:, None]
    jtab = (np.arange(512, dtype=np.float32)[None, :] - p).astype(np.float32)
    atab = np.abs(np.arange(896, dtype=np.float32)[None, :] - p - 384.0).astype(np.float32)
    atab = np.ascontiguousarray(np.stack([atab, atab], axis=1))
    cst = np.zeros((128, 4, 32, 2), np.float32)
    augt = np.zeros((128, 4, 2, 512), np.float32)
    jj = np.arange(512, dtype=np.float32)
    pp_ = np.arange(128, dtype=np.float32)
    for hh in range(4):
        sl = 2.0 ** (-8.0 * (hh + 1) / 4)
        for v in range(2):
            sgn = 1.0 if v == 0 else -1.0
            cst[:, hh, :, v] = sgn * sl * pp_[:, None] - sl * 128.0 * np.arange(32, dtype=np.float32)[None, :]
            hi = -sgn * sl * 256.0 * np.floor(jj / 256.0)
            lo = -sgn * sl * np.mod(jj, 256.0)
            for base in (0, 64):
                augt[base, hh, v] = hi
                augt[base + 1, hh, v] = lo
    cst = np.ascontiguousarray(cst.reshape(128, 256))
    augt = augt.astype(ml_dtypes.bfloat16)
    lamv = np.stack([rep(inputs["diff_lambda_q1"][0]), rep(inputs["diff_lambda_k1"][0]),
                     rep(inputs["diff_lambda_q2"][0]), rep(inputs["diff_lambda_k2"][0])], axis=1)
    m = {
        "x": xb,
        "wg1": f32(inputs["ffn1_w_gate"][0]), "wu1": f32(inputs["ffn1_w_up"][0]), "wd1": f32(inputs["ffn1_w_down"][0]),
        "wg2": f32(inputs["ffn2_w_gate"][0]), "wu2": f32(inputs["ffn2_w_up"][0]), "wd2": f32(inputs["ffn2_w_down"][0]),
        "win": f32(inputs["w_in"][0]), "wout": f32(inputs["w_out"][0]),
        "ln1g": rep(inputs["ln1_g"][0]), "ln1b": rep(inputs["ln1_b"][0]),
        "ln2g": rep(inputs["ln2_g"][0]), "ln2b": rep(inputs["ln2_b"][0]),
        "ln3g": rep(inputs["ln3_g"][0]), "ln3b": rep(inputs["ln3_b"][0]),
        "ident": np.eye(128, dtype=np.float32).astype(ml_dtypes.bfloat16),
        "nab": _na_tables(f32(inputs["na_rpb"][0]), h == 1),
        "augt": augt, "atab": atab, "cst": cst,
        "lamv": np.ascontiguousarray(lamv.astype(np.float32)), "subg": rep(inputs["diff_subln_g"][0]),
        "zeros_ones": np.stack([np.zeros((64, 4 * SEQ), np.float32), np.ones((64, 4 * SEQ), np.float32)]).astype(ml_dtypes.bfloat16),
    }
    return m


def kernel(**inputs):
    inputs = {k_: np.asarray(v) for k_, v in inputs.items()}
    nc = build_program()
    in_maps = [prep_inputs(inputs, c) for c in range(8)]
    res = run_bass_kernel_spmd(nc, in_maps, core_ids=list(range(8)))
    outp = np.empty((4, SEQ, D), np.float32)
    for c in range(8):
        b, h = c // 2, c % 2
        o = np.asarray(res.results[c]["out"])
        if h == 0:
            outp[b, :OWN] = o
        else:
            outp[b, OWN:] = o[::-1]
    return outp
```

```python
import numpy as np
from contextlib import ExitStack
import concourse.bass as bass
import concourse.mybir as mybir
from concourse.bass_utils import run_bass_kernel_spmd
import ml_dtypes

F32, BF16 = mybir.dt.float32, mybir.dt.bfloat16
AF = mybir.ActivationFunctionType
ALU = mybir.AluOpType

D = 1024
DFF = 2816
NFC = DFF // 128
SEQ = 4096
OWN = 2048
ALPHA = 2.0 ** 0.25
EPS = 1e-5
LAM_INIT = 0.2
NKT_NA = 18


def _flat(toks):
    out = []
    for t in toks:
        if t is None:
            continue
        if isinstance(t, list):
            out.extend(_flat(t))
        else:
            out.append(t)
    return out


class Eng:
    def __init__(self, nc, e, name, es):
        self.e = e
        self.name = name
        self.sem = es.enter_context(nc.semaphore("sem_" + name))
        self.n = 0
        self.seen = {}

    def wait(self, *toks):
        best = {}
        for src, v in _flat(list(toks)):
            if best.get(id(src), (None, 0))[1] < v:
                best[id(src)] = (src, v)
        for src, v in best.values():
            if self.seen.get(id(src), 0) >= v:
                continue
            self.e.wait_ge(src.sem, v)
            self.seen[id(src)] = v

    def mark(self, ins):
        ins.then_inc(self.sem, 1)
        self.n += 1
        return (self, self.n)


class Slot:
    def __init__(self, nc, name, es):
        self.sem = es.enter_context(nc.semaphore("dsem_" + name))
        self.n = 0
        self.busy = False

    def dma(self, q, out, in_):
        q.e.dma_start(out=out, in_=in_).then_inc(self.sem, 16)
        self.n += 16
        return (self, self.n)


class K:
    def slots(self, es, n):
        got = []
        for sl in self.slot_pool:
            if not sl.busy and len(got) < n:
                sl.busy = True
                got.append(sl)
        while len(got) < n:
            sl = Slot(self.nc, f"p{len(self.slot_pool)}", self.es_global)
            sl.busy = True
            self.slot_pool.append(sl)
            got.append(sl)

        def release():
            for sl in got:
                sl.busy = False
        es.callback(release)
        return got


def barrier(k, toks):
    toks = _flat(toks) + [(e, e.n) for e in k.engs if e.n > 0]
    for e in k.engs:
        e.wait(toks)


def copy_cast(eng, k, out, in_):
    if eng is k.act:
        return eng.e.activation(out=out, in_=in_, func=AF.Copy)
    return eng.e.tensor_copy(out=out, in_=in_)


class WeightLoader:
    def __init__(self, k, es, name, jobs, nslots=3, width=1408):
        nc = k.nc
        self.k = k
        self.jobs = jobs
        self.stg = [es.enter_context(nc.sbuf_tensor(f"{name}_stg{i}", [128, width], F32)) for i in range(nslots)]
        self.slots = k.slots(es, nslots)
        self.cast_tok = [None] * nslots
        self.engs = [k.dve, k.pool, k.act]
        self.toks = []
        self.i = 0
        k.last_stg = self.stg

    def emit(self, n):
        k = self.k
        nslots = len(self.stg)
        for _ in range(n):
            if self.i >= len(self.jobs):
                return
            i = self.i
            self.i += 1
            dst, src = self.jobs[i]
            s = i % nslots
            if len(src.shape) == 3:
                nel = src.shape[1] * src.shape[2]
                sv = self.stg[s][:, :nel].rearrange("p (a b) -> p a b", b=src.shape[2])
            else:
                nel = src.shape[-1]
                sv = self.stg[s][:, :nel]
            k.sp.wait(self.cast_tok[s])
            lt = self.slots[s].dma(k.sp, sv, src)
            e = self.engs[i % 3]
            e.wait(lt)
            self.cast_tok[s] = e.mark(copy_cast(e, k, dst, sv))
            self.toks.append(self.cast_tok[s])

    def done(self):
        return self.i >= len(self.jobs)


def load_cast_weights(k, es, name, jobs, nslots=3, width=1408):
    wl = WeightLoader(k, es, name, jobs, nslots, width)
    wl.emit(len(jobs))
    return wl.toks


def load_bf16_weights(k, q, slot, jobs):
    tok = None
    for dst, src in jobs:
        tok = slot.dma(q, dst, src)
    return tok


class BgCast:
    def __init__(self, k, es, name, jobs, in_bufs, first_tok):
        nc = k.nc
        self.k = k
        self.jobs = jobs
        self.inb = in_bufs
        self.outb = [es.enter_context(nc.sbuf_tensor(f"{name}_bgo{i}", [128, 1408], BF16)) for i in range(2)]
        self.in_slot = k.slots(es, 2)
        self.out_slot = k.slots(es, 2)
        self.load_tok = [first_tok, first_tok]
        self.cast_tok = [None, None]
        self.store_tok = [None, None]
        self.i = 0

    def step(self):
        k = self.k
        i = self.i
        n_jobs = len(self.jobs)
        if i > n_jobs + 1:
            return
        self.i += 1
        if 0 <= i - 2 < n_jobs:
            j = i - 2
            n = self.jobs[j][0].shape[-1]
            k.act.wait(self.cast_tok[j % 2])
            self.store_tok[j % 2] = self.out_slot[j % 2].dma(k.act, self.jobs[j][1], self.outb[j % 2][:, :n])
        if i < n_jobs:
            n = self.jobs[i][0].shape[-1]
            k.act.wait(self.cast_tok[i % 2], self.load_tok[i % 2] if i < 2 else None)
            self.load_tok[i % 2] = self.in_slot[i % 2].dma(k.act, self.inb[i % 2][:, :n], self.jobs[i][0])
        if 0 <= i - 1 < n_jobs:
            j = i - 1
            n = self.jobs[j][0].shape[-1]
            k.act.wait(self.load_tok[j % 2], self.store_tok[j % 2])
            self.cast_tok[j % 2] = k.act.mark(k.act.e.activation(out=self.outb[j % 2][:, :n], in_=self.inb[j % 2][:, :n], func=AF.Copy))

    def finish(self):
        while self.i <= len(self.jobs) + 1:
            self.step()
        return [t for t in self.store_tok if t is not None]


def ln_part_a(k, ctx, t, psum_halves, xres, xscale, eps, dst_ap, pre_toks):
    nsl = ctx["n"]
    dve, pool, sp = k.dve, k.pool, k.sp
    s = t % nsl
    r = ctx["r"][s]
    stats, mv, ve, rstd = ctx["stats"][s], ctx["mv"][s], ctx["ve"][s], ctx["rstd"][s]
    stt_toks = []
    for hh in range(2):
        dve.wait(pre_toks, ctx["store_tok"][s])
        ins = dve.e.scalar_tensor_tensor(out=r[:, hh * 512:(hh + 1) * 512], in0=xres[:, hh * 512:(hh + 1) * 512],
                                         scalar=float(xscale), op0=ALU.mult, in1=psum_halves[hh], op1=ALU.add)
        stt_toks.append(dve.mark(ins))
    st_toks = []
    for hh in range(2):
        dve.wait(stt_toks[hh])
        st_toks.append(dve.mark(dve.e.bn_stats(out=stats[:, hh * 6:(hh + 1) * 6], in_=r[:, hh * 512:(hh + 1) * 512])))
    dve.wait(st_toks)
    t1 = dve.mark(dve.e.bn_aggr(out=mv[:], in_=stats[:]))
    dve.wait(t1)
    t2 = dve.mark(dve.e.tensor_scalar(out=ve[:], in0=mv[:, 1:2], scalar1=float(eps), scalar2=None, op0=ALU.add))
    pool.wait(t2, ctx["gb_tok"])
    t3 = pool.mark(pool.e.tensor_tensor(out=rstd[:], in0=ve[:], in1=ctx["mhalf"][:], op=ALU.pow))
    dve.wait(t1, ctx["gb_tok"])
    t4 = dve.mark(dve.e.scalar_tensor_tensor(out=r[:], in0=r[:], scalar=mv[:, 0:1], op0=ALU.subtract, in1=ctx["g"][:], op1=ALU.mult))
    ctx["pend"][t] = (t3, t4, dst_ap)
    return stt_toks


def ln_part_b(k, ctx, t):
    dve, sp = k.dve, k.sp
    s = t % ctx["n"]
    r = ctx["r"][s]
    t3, t4, dst_ap = ctx["pend"].pop(t)
    dve.wait(t3, t4)
    t6 = dve.mark(dve.e.scalar_tensor_tensor(out=r[:], in0=r[:], scalar=ctx["rstd"][s][:, 0:1], op0=ALU.mult, in1=ctx["b"][:], op1=ALU.add))
    sp.wait(t6)
    ctx["store_tok"][s] = ctx["store_slot"][s].dma(sp, dst_ap, r[:])


def ln_ctx(k, es, name, g_d, b_d, nsl=2):
    nc = k.nc
    sb = lambda n, shape, dt: es.enter_context(nc.sbuf_tensor(f"{name}_{n}", shape, dt))
    ctx = {
        "n": nsl,
        "pend": {},
        "r": [sb(f"r{i}", [128, D], F32) for i in range(nsl)],
        "stats": [sb(f"stats{i}", [128, 12], F32) for i in range(nsl)],
        "mv": [sb(f"mv{i}", [128, 2], F32) for i in range(nsl)],
        "ve": [sb(f"ve{i}", [128, 1], F32) for i in range(nsl)],
        "rstd": [sb(f"rstd{i}", [128, 1], F32) for i in range(nsl)],
        "mhalf": sb("mhalf", [128, 1], F32),
        "g": sb("g", [128, D], F32),
        "b": sb("b", [128, D], F32),
        "store_slot": k.slots(es, nsl),
        "store_tok": [None] * nsl,
    }
    gs = k.slots(es, 1)[0]
    gs.dma(k.sp, ctx["g"][:], g_d)
    tg = gs.dma(k.sp, ctx["b"][:], b_d)
    tm = k.pool.mark(k.pool.e.memset(ctx["mhalf"][:], -0.5))
    ctx["gb_tok"] = [tg, tm]
    return ctx


def alloc_ffn_weights(k, es, name, with_wd=True):
    nc = k.nc
    wg = es.enter_context(nc.sbuf_tensor(f"{name}_wg", [128, 8, DFF], BF16))
    wu = es.enter_context(nc.sbuf_tensor(f"{name}_wu", [128, 8, DFF], BF16))
    wd = es.enter_context(nc.sbuf_tensor(f"{name}_wd", [128, NFC, D], BF16)) if with_wd else None
    return wg, wu, wd


def ffn_phase(k, name, x_src, T, wg_d, wu_d, wd_d, g_d, b_d, dst, pre=None, bg_jobs=None):
    nc = k.nc
    pe, act, dve, pool, sp = k.pe, k.act, k.dve, k.pool, k.sp
    NB = T // 256
    NH = 4
    with ExitStack() as es:
        sb = lambda n, shape, dt: es.enter_context(nc.sbuf_tensor(f"{name}_{n}", shape, dt))
        ps = lambda n, shape, dt: es.enter_context(nc.psum_tensor(f"{name}_{n}", shape, dt))
        bg = None
        emit_weights = None
        wl = None
        if pre is not None:
            wg, wu, wdA, wd_bf_d, wtoks = pre
            wdB = es.enter_context(nc.sbuf_tensor(f"{name}_wdB", [128, NFC // 2, D], BF16))
            wdv = wd_bf_d.rearrange("(c p) f -> p c f", p=128)
            wdB_tok = load_bf16_weights(k, k.act, k.slots(es, 1)[0],
                                        [(wdB[:, c:c + 1, :], wdv[:, NFC // 2 + c:NFC // 2 + c + 1, :]) for c in range(NFC // 2)])
            wd_ap = lambda fd, lo, hi: (wdA if fd < NFC // 2 else wdB)[:, fd % (NFC // 2), lo:hi]
            gu_wtok = {f: wtoks for f in range(NFC)}
            d_wtok = {f: None for f in range(NFC)}
        else:
            wg, wu, wd = alloc_ffn_weights(k, es, name)
            wgv = wg_d.rearrange("(c p) f -> p c f", p=128)
            wuv = wu_d.rearrange("(c p) f -> p c f", p=128)
            wdv = wd_d.rearrange("(c p) d -> p c d", p=128)
            jobs = []
            for fg in range(NFC // 2):
                cs = slice(fg * 256, (fg + 1) * 256)
                for c0 in (0, 4):
                    jobs.append((wg[:, c0:c0 + 4, cs], wgv[:, c0:c0 + 4, cs]))
                    jobs.append((wu[:, c0:c0 + 4, cs], wuv[:, c0:c0 + 4, cs]))
                for c in (2 * fg, 2 * fg + 1):
                    jobs.append((wd[:, c, :], wdv[:, c, :]))
            wdB_tok = None
            wd_ap = lambda fd, lo, hi: wd[:, fd, lo:hi]
            gu_wtok, d_wtok = {}, {}

            wl = WeightLoader(k, es, name, jobs)
            for f in range(NFC):
                gu_wtok[f] = ("wl", (f // 2) * 6, (f // 2) * 6 + 4)
                d_wtok[f] = ("wl", (f // 2) * 6 + 4 + (f % 2), (f // 2) * 6 + 5 + (f % 2))

            def emit_weights():
                wl.emit(12)
        ctx = ln_ctx(k, es, name, g_d, b_d)

        xA = [sb(f"xA{i}", [128, D], F32) for i in range(2)]
        xA_slot = k.slots(es, 2)
        xR = [sb(f"xR{i}", [128, D], F32) for i in range(2)]
        xR_slot = k.slots(es, 2)
        xbf = [sb(f"xbf{i}", [128, D], BF16) for i in range(2)]
        xT = [sb(f"xT{i}", [128, 8, 256], BF16) for i in range(2)]
        hT = [sb(f"hT{i}", [128, 256], BF16) for i in range(NH)]
        sg = [sb(f"sg{i}", [128, 256], F32) for i in range(2)]
        ident = k.ident
        Tps = [ps(f"T{i}", [128, 8, 128], BF16) for i in range(2)]
        gu = [ps(f"gu{i}", [128, 2, 256], F32) for i in range(2)]
        yp = [[ps(f"y{t}{h}", [128, 512], F32) for h in range(2)] for t in range(2)]

        cast_tok = [None, None]
        T_tok = [None, None]
        XE_tok = {}
        xR_free = [None, None]
        xR_tok = [None, None]
        gu_last = {}
        mult_tok = {}
        D_tok = {}
        ep_tok = {}

        def stage_load_cast(b):
            for t in range(2):
                sp.wait(cast_tok[t])
                lt = xA_slot[t].dma(sp, xA[t][:], x_src[(b * 2 + t) * 128:(b * 2 + t + 1) * 128, :])
                pool.wait(lt, T_tok[t])
                cast_tok[t] = pool.mark(pool.e.tensor_copy(out=xbf[t][:], in_=xA[t][:]))

        def stage_T(b):
            for t in range(2):
                prev = XE_tok.get((b - 1, t))
                pe.wait(cast_tok[t], prev, k.ident_tok)
                for c in range(8):
                    ins = pe.e.transpose(out=Tps[t][:, c, :], in_=xbf[t][:, c * 128:(c + 1) * 128], identity=ident[:])
                T_tok[t] = pe.mark(ins)

        def stage_XE(b):
            for t in range(2):
                act.wait(T_tok[t], gu_last.get(b - 2))
                XE_tok[(b, t)] = act.mark(act.e.activation(out=xT[b % 2][:, :, t * 128:(t + 1) * 128], in_=Tps[t][:], func=AF.Copy))

        def stage_xR(b):
            for t in range(2):
                sp.wait(xR_free[t])
                xR_tok[t] = xR_slot[t].dma(sp, xR[t][:], x_src[(b * 2 + t) * 128:(b * 2 + t + 1) * 128, :])

        stage_load_cast(0)
        if emit_weights is not None:
            emit_weights()
        stage_T(0)
        stage_XE(0)
        def wres(t):
            if isinstance(t, tuple) and len(t) == 3 and t[0] == "wl":
                return list(wl.toks[t[1]:t[2]])
            return t

        def stage_down(b, fd):
            gd = b * NFC + fd
            pe.wait(mult_tok[gd], ep_tok.get(b - 1) if fd == 0 else None, wdB_tok if (b == 0 and fd == NFC // 2) else None,
                    wres(d_wtok[fd]) if b == 0 else None)
            for t in range(2):
                for hh in range(2):
                    ins = pe.e.matmul(yp[t][hh][:], lhsT=hT[gd % NH][:, t * 128:(t + 1) * 128],
                                      rhs=wd_ap(fd, hh * 512, (hh + 1) * 512), start=(fd == 0), stop=(fd == NFC - 1))
            D_tok[gd] = pe.mark(ins)

        for b in range(NB):
            if b + 1 < NB:
                stage_load_cast(b + 1)
            stage_xR(b)
            xt = xT[b % 2]
            for f in range(NFC):
                gi = b * NFC + f
                if wl is not None and b == 0 and f % 2 == 0:
                    wl.emit(6 * (f // 2 + 3) - wl.i)
                    if wl.done() and bg is None and bg_jobs:
                        bg = BgCast(k, es, name, bg_jobs, k.last_stg[:2], list(wl.toks))
                pe.wait(XE_tok[(b, 0)], XE_tok[(b, 1)], mult_tok.get(gi - 2), wres(gu_wtok[f]) if b == 0 else None)
                for c in range(8):
                    pe.e.matmul(gu[gi % 2][:, 0, :], lhsT=wg[:, c, f * 128:(f + 1) * 128], rhs=xt[:, c, :],
                                start=(c == 0), stop=(c == 7))
                for c in range(8):
                    ins = pe.e.matmul(gu[gi % 2][:, 1, :], lhsT=wu[:, c, f * 128:(f + 1) * 128], rhs=xt[:, c, :],
                                      start=(c == 0), stop=(c == 7))
                gtok = pe.mark(ins)
                if f == NFC - 1:
                    gu_last[b] = gtok
                act.wait(gtok, mult_tok.get(gi - 2))
                stok = act.mark(act.e.activation(out=sg[gi % 2][:], in_=gu[gi % 2][:, 0, :], func=AF.Silu))
                dve.wait(stok, D_tok.get(gi - NH))
                mult_tok[gi] = dve.mark(dve.e.tensor_tensor(out=hT[gi % NH][:], in0=sg[gi % 2][:], in1=gu[gi % 2][:, 1, :], op=ALU.mult))
                if f == 10 and b + 1 < NB:
                    stage_T(b + 1)
                    stage_XE(b + 1)
                if bg is not None and f in (1, 5, 9, 13, 17, 20):
                    bg.step()
                if f >= 1:
                    stage_down(b, f - 1)
            stage_down(b, NFC - 1)
            etoks = []
            for t in range(2):
                gt = b * 2 + t
                stt = ln_part_a(k, ctx, gt, [yp[t][0][:], yp[t][1][:]], xR[t], 2.0 * ALPHA, 4.0 * EPS,
                                dst[gt * 128:(gt + 1) * 128, :], [D_tok[b * NFC + NFC - 1], xR_tok[t]])
                xR_free[t] = stt
                etoks.extend(stt)
            for t in range(2):
                ln_part_b(k, ctx, b * 2 + t)
            ep_tok[b] = etoks
        barrier(k, [ctx["store_tok"], bg.finish() if bg is not None else None])


def xT_block_loader(k, es, name, src, ntile_list):
    nc = k.nc
    sb = lambda n, shape, dt: es.enter_context(nc.sbuf_tensor(f"{name}_{n}", shape, dt))
    st = {
        "xA": [sb(f"lxA{i}", [128, D], F32) for i in range(2)],
        "xbf": [sb(f"lxbf{i}", [128, D], BF16) for i in range(2)],
        "slot": k.slots(es, 2),
        "Tps": [es.enter_context(nc.psum_tensor(f"{name}_lT{i}", [128, 8, 128], BF16)) for i in range(2)],
        "cast_tok": [None, None], "T_tok": [None, None], "XE_tok": [None, None], "i": 0,
    }

    def emit(tile, dstT, dst_free_tok=None):
        i = st["i"]; st["i"] += 1
        s_ = i % 2
        k.sp.wait(st["cast_tok"][s_])
        lt = st["slot"][s_].dma(k.sp, st["xA"][s_][:], src[tile * 128:(tile + 1) * 128, :])
        k.pool.wait(lt, st["T_tok"][s_])
        st["cast_tok"][s_] = k.pool.mark(k.pool.e.tensor_copy(out=st["xbf"][s_][:], in_=st["xA"][s_][:]))
        k.pe.wait(st["cast_tok"][s_], st["XE_tok"][s_], k.ident_tok)
        for c in range(8):
            ins = k.pe.e.transpose(out=st["Tps"][s_][:, c, :], in_=st["xbf"][s_][:, c * 128:(c + 1) * 128], identity=k.ident[:])
        st["T_tok"][s_] = k.pe.mark(ins)
        k.act.wait(st["T_tok"][s_], dst_free_tok)
        st["XE_tok"][s_] = k.act.mark(k.act.e.activation(out=dstT, in_=st["Tps"][s_][:], func=AF.Copy))
        return st["XE_tok"][s_]
    return emit


def win_jobs(win_sb, win_d, col0, ncols):
    v = win_d.rearrange("(c p) f -> p c f", p=128)
    return [(win_sb[:, c, :], v[:, c, col0:col0 + ncols]) for c in range(8)]


def na_kt_set(il):
    return [0, 1, 2, 3] if il < 2 else list(range(il - 2, il + 3))


def na_variant(il, kt):
    return il * 4 + kt if il < 2 else 8 + (kt - il + 2)


def na_phase(k, x1_d, win_d, nab_d, mixT):
    nc = k.nc
    pe, act, dve, pool, sp = k.pe, k.act, k.dve, k.pool, k.sp
    with ExitStack() as es:
        sb = lambda n, shape, dt: es.enter_context(nc.sbuf_tensor(f"na_{n}", shape, dt))
        ps = lambda n, shape, dt: es.enter_context(nc.psum_tensor(f"na_{n}", shape, dt))
        KT = sb("KT", [128, 4, NKT_NA * 128], BF16)
        QT = [sb(f"QT{i}", [128, 4, OWN], BF16) for i in range(2)]
        VA = sb("VA", [128, NKT_NA, 8, 65], BF16)
        zer = sb("zer", [128, 512], BF16)
        nab = sb("nab", [128, 13, 1024], F32)
        nsl = k.slots(es, 1)[0]
        for v in range(13):
            nab_tok = nsl.dma(act, nab[:, v, :], nab_d[v])
        tz = pool.mark(pool.e.memset(zer[:], 0.0))
        dve.e.memset(QT[0][64:128, :, :], 0.0)
        dve.e.memset(QT[1][0:64, :, :], 0.0)
        tv1 = dve.mark(dve.e.memset(VA[:, :, :, 64:65], 1.0))
        pad_tok = tv1
        with ExitStack() as es2:
            sb2 = lambda n, shape, dt: es2.enter_context(nc.sbuf_tensor(f"nap_{n}", shape, dt))
            win = sb2("win", [128, 8, 1536], BF16)
            wtoks = load_bf16_weights(k, sp, k.slots(es2, 1)[0], win_jobs(win, win_d, 0, 1536))
            x1T = [sb2(f"x1T{i}", [128, 8, 512], BF16) for i in range(2)]
            pp = [es2.enter_context(nc.psum_tensor(f"nap_pp{i}", [128, 512], F32)) for i in range(3)]
            emit = xT_block_loader(k, es2, "nap", x1_d, None)
            pp_free = [None] * 3
            blk_last_pe = [None, None]
            npp = 0
            for blk in range(5):
                ntile = 4 if blk < 4 else 2
                ntok = ntile * 128
                xt = x1T[blk % 2]
                xe = [emit(blk * 4 + t, xt[:, :, t * 128:(t + 1) * 128], blk_last_pe[blk % 2]) for t in range(ntile)]
                pe.wait(xe, wtoks)
                for kind in range(2):
                    if kind == 1 and blk >= 4:
                        continue
                    for hp in range(4):
                        col = (512 if kind == 0 else 0) + hp * 128
                        b_ = npp % 3; npp += 1
                        pe.wait(pp_free[b_])
                        for c in range(8):
                            ins = pe.e.matmul(pp[b_][:, :ntok], lhsT=win[:, c, col:col + 128], rhs=xt[:, c, :ntok], start=(c == 0), stop=(c == 7))
                        tk = pe.mark(ins)
                        act.wait(tk)
                        if kind == 0:
                            pp_free[b_] = act.mark(act.e.activation(out=KT[:, hp, blk * 512:blk * 512 + ntok], in_=pp[b_][:, :ntok], func=AF.Copy))
                        else:
                            act.wait(tv1)
                            act.e.activation(out=QT[0][0:64, hp, blk * 512:blk * 512 + ntok], in_=pp[b_][0:64, :ntok], func=AF.Copy, scale=0.125)
                            pp_free[b_] = act.mark(act.e.activation(out=QT[1][64:128, hp, blk * 512:blk * 512 + ntok], in_=pp[b_][64:128, :ntok], func=AF.Copy, scale=0.125))
                for t in range(ntile):
                    b_ = npp % 3; npp += 1
                    pe.wait(pp_free[b_])
                    for c in range(8):
                        ins = pe.e.matmul(pp[b_][:], lhsT=xt[:, c, t * 128:(t + 1) * 128], rhs=win[:, c, 1024:1536], start=(c == 0), stop=(c == 7))
                    tk = pe.mark(ins)
                    dve.wait(tk, tv1)
                    pp_free[b_] = dve.mark(dve.e.tensor_copy(out=VA[:, blk * 4 + t, :, 0:64], in_=pp[b_][:].rearrange("p (h e) -> p h e", e=64)))
                blk_last_pe[blk % 2] = tk
            barrier(k, [])
        NSP, NTM, NE, LA = 4, 3, 5, 3
        sps = [ps(f"s{i}", [128, 512], F32) for i in range(NSP)]
        acc = [ps(f"acc{i}", [128, 512], F32) for i in range(2)]
        tpo = ps("tpo", [128, 4, 128], BF16)
        tmp = [sb(f"tmp{i}", [128, 512], F32) for i in range(NTM)]
        E = [sb(f"E{i}", [128, 512], BF16) for i in range(NE)]
        rr = sb("rr", [128, 8], F32)
        nao = [sb(f"nao{i}", [128, 512], BF16) for i in range(2)]
        accs = sb("accs", [128, 2, 260], F32)
        sps_free = [None] * NSP
        tmp_free = [None] * NTM
        E_free = [None] * NE
        acc_free = [None, None]
        nao_free = [None, None]
        nao_tok = {}
        tpo_free = None
        accs_free = None
        ns = 0
        E_tok = {}

        def finish_il(il_):
            nonlocal tpo_free
            s2 = il_ % 2
            pe.wait(nao_tok[il_], tpo_free)
            for hp in range(4):
                ins = pe.e.transpose(out=tpo[:, hp, :], in_=nao[s2][:, hp * 128:(hp + 1) * 128], identity=k.ident[:])
            tt = pe.mark(ins)
            nao_free[s2] = tt
            act.wait(tt)
            tpo_free = act.mark(act.e.activation(out=mixT[:, 0:4, il_ * 128:(il_ + 1) * 128], in_=tpo[:], func=AF.Copy))

        def steps_of(il_):
            return [(kt, hb) for kt in na_kt_set(il_) for hb in range(2)]

        def emit_S(il_, si):
            nonlocal ns
            kt, hb = steps_of(il_)[si]
            g = ns; ns += 1
            pe.wait(sps_free[g % NSP], pad_tok)
            for hl in range(4):
                ins = pe.e.matmul(sps[g % NSP][:, hl * 128:(hl + 1) * 128], lhsT=KT[:, hl, kt * 128:(kt + 1) * 128],
                                  rhs=QT[hb][:, hl, il_ * 128:(il_ + 1) * 128], start=True, stop=True)
            tk = pe.mark(ins)
            v = na_variant(il_, kt)
            dve.wait(tk, tmp_free[g % NTM], nab_tok)
            t1 = dve.mark(dve.e.tensor_tensor(out=tmp[g % NTM][:], in0=nab[:, v, hb * 512:(hb + 1) * 512], in1=sps[g % NSP][:], op=ALU.add))
            sps_free[g % NSP] = t1
            act.wait(t1, E_free[g % NE])
            t2 = act.mark(act.e.activation(out=E[g % NE][:], in_=tmp[g % NTM][:], func=AF.Exp))
            tmp_free[g % NTM] = t2
            E_tok[(il_, si)] = (t2, g)

        def emit_AV(il_, si, last):
            kt, hb = steps_of(il_)[si]
            t2, g = E_tok.pop((il_, si))
            pe.wait(t2)
            for hl in range(4):
                h = 2 * hl + hb
                ins = pe.e.matmul(acc[hb][:, hl * 65:(hl + 1) * 65], lhsT=E[g % NE][:, hl * 128:(hl + 1) * 128],
                                  rhs=VA[:, kt, h, :], start=False, stop=(last and hl == 3))
            E_free[g % NE] = pe.mark(ins)
            return E_free[g % NE]

        pend_il = None
        for si in range(LA):
            emit_S(0, si)
        for il in range(16):
            nst = len(steps_of(il))
            for hb in range(2):
                pe.wait(acc_free[hb], tz)
                pe.e.matmul(acc[hb][:], lhsT=zer[:, 0:128], rhs=zer[:], start=True, stop=False)
            last_av = [None, None]
            for si in range(nst):
                if si + LA < nst:
                    emit_S(il, si + LA)
                last_av[steps_of(il)[si][1]] = emit_AV(il, si, si >= nst - 2)
                if si == 3 and pend_il is not None:
                    finish_il(pend_il)
                    pend_il = None
            if il + 1 < 16:
                for si in range(LA):
                    emit_S(il + 1, si)
            s_ = il % 2
            evt = []
            for hb in range(2):
                act.wait(last_av[hb], accs_free)
                acc_free[hb] = act.mark(act.e.activation(out=accs[:, hb, :], in_=acc[hb][:, 0:260], func=AF.Copy))
                evt.append(acc_free[hb])
            for hb in range(2):
                accv = accs[:, hb, :].rearrange("p (h e) -> p h e", e=65)
                dve.wait(evt, nao_free[s_])
                tr = dve.mark(dve.e.reciprocal(out=rr[:, hb * 4:(hb + 1) * 4], in_=accv[:, :, 64]))
                dve.wait(tr)
                for hl in range(4):
                    h = 2 * hl + hb
                    ins = dve.e.tensor_scalar(out=nao[s_][:, h * 64:(h + 1) * 64], in0=accv[:, hl, 0:64], scalar1=rr[:, hb * 4 + hl:hb * 4 + hl + 1], scalar2=None, op0=ALU.mult)
                accs_free = dve.mark(ins)
            nao_tok[il] = accs_free
            pend_il = il
        finish_il(pend_il)
        barrier(k, [])


def diff_phase(k, x1_d, win_d, aug_d, atab_d, cst_d, lamv_d, subg_d, mixT):
    nc = k.nc
    pe, act, dve, pool, sp = k.pe, k.act, k.dve, k.pool, k.sp
    SL = [2.0 ** (-8.0 * (h + 1) / 4) for h in range(4)]
    with ExitStack() as es:
        sb = lambda n, shape, dt: es.enter_context(nc.sbuf_tensor(f"df_{n}", shape, dt))
        ps = lambda n, shape, dt: es.enter_context(nc.psum_tensor(f"df_{n}", shape, dt))
        KT = [sb(f"KT{i}", [128, 4, SEQ], BF16) for i in range(2)]
        QT = sb("QT", [128, 4, OWN], BF16)
        VA = sb("VA", [128, 32, 4, 129], BF16)
        cst = sb("cst", [128, 256], F32)
        lamv = sb("lamv", [128, 4, 64], F32)
        g8 = sb("g8", [128, 128], F32)
        zer = sb("zer", [128, 512], BF16)
        sm = sb("sm", [128, 8], F32)
        junk = sb("junk", [128, 64], F32)
        mhalf = sb("mhalf", [128, 1], F32)
        tz = pool.mark(pool.e.memset(zer[:], 0.0))
        pool.e.memset(mhalf[:], -0.5)
        ones_t = sb("ones_t", [128, 512], BF16)
        pad_tok = pool.mark(pool.e.memset(ones_t[:], 1.0))
        tv1 = dve.mark(dve.e.memset(VA[:, :, :, 128:129], 1.0))
        csl = k.slots(es, 1)[0]
        csl.dma(sp, cst[:], cst_d)
        csl.dma(sp, lamv[:], lamv_d)
        ctok = csl.dma(sp, g8[:], subg_d)
        dve.wait(ctok)
        a0 = dve.mark(dve.e.tensor_scalar(out=g8[:], in0=g8[:], scalar1=1.0 - LAM_INIT, scalar2=None, op0=ALU.mult))
        dve.wait(a0)
        a1 = dve.mark(dve.e.scalar_tensor_tensor(out=junk[:], in0=lamv[:, 0, :], scalar=1.0, op0=ALU.mult, in1=lamv[:, 1, :], op1=ALU.mult, accum_out=sm[:, 0:1]))
        dve.wait(a1)
        a2 = dve.mark(dve.e.scalar_tensor_tensor(out=junk[:], in0=lamv[:, 2, :], scalar=1.0, op0=ALU.mult, in1=lamv[:, 3, :], op1=ALU.mult, accum_out=sm[:, 1:2]))
        act.wait(a2)
        a3 = act.mark(act.e.activation(out=sm[:, 2:4], in_=sm[:, 0:2], func=AF.Exp))
        dve.wait(a3)
        a4 = dve.mark(dve.e.tensor_tensor(out=sm[:, 5:6], in0=sm[:, 3:4], in1=sm[:, 2:3], op=ALU.subtract))
        dve.wait(a4)
        lam_tok = dve.mark(dve.e.tensor_scalar(out=sm[:, 4:5], in0=sm[:, 5:6], scalar1=-LAM_INIT, scalar2=None, op0=ALU.add))
        neglam = sm[:, 4:5]
        with ExitStack() as es2:
            sb2 = lambda n, shape, dt: es2.enter_context(nc.sbuf_tensor(f"dfp_{n}", shape, dt))
            win = sb2("win", [128, 8, 1536], BF16)
            wtoks = load_bf16_weights(k, sp, k.slots(es2, 1)[0], win_jobs(win, win_d, 1536, 1536))
            x1T = [sb2(f"x1T{i}", [128, 8, 512], BF16) for i in range(2)]
            pp = [es2.enter_context(nc.psum_tensor(f"dfp_pp{i}", [128, 512], F32)) for i in range(3)]
            emit = xT_block_loader(k, es2, "dfp", x1_d, None)
            pp_free = [None] * 3
            blk_last_pe = [None, None]
            npp = 0
            for blk in range(8):
                xt = x1T[blk % 2]
                xe = [emit(blk * 4 + t, xt[:, :, t * 128:(t + 1) * 128], blk_last_pe[blk % 2]) for t in range(4)]
                pe.wait(xe, wtoks)
                for kind in range(2):
                    if kind == 1 and blk >= 4:
                        continue
                    for h in range(4):
                        col = (512 if kind == 0 else 0) + h * 128
                        b_ = npp % 3; npp += 1
                        pe.wait(pp_free[b_])
                        for c in range(8):
                            ins = pe.e.matmul(pp[b_][:], lhsT=win[:, c, col:col + 128], rhs=xt[:, c, :], start=(c == 0), stop=(c == 7))
                        tk = pe.mark(ins)
                        act.wait(tk)
                        if kind == 0:
                            act.wait(pad_tok)
                            k0 = act.mark(act.e.activation(out=KT[0][:, h, blk * 512:(blk + 1) * 512], in_=pp[b_][:], func=AF.Copy))
                            k1 = act.mark(act.e.activation(out=KT[1][:, h, blk * 512:(blk + 1) * 512], in_=pp[b_][:], func=AF.Copy))
                            pp_free[b_] = k1
                            act.wait(k0, k1)
                            act.e.activation(out=KT[0][64:66, h, blk * 512:(blk + 1) * 512], in_=ones_t[64:66, :], func=AF.Copy)
                            kfix_tok = act.mark(act.e.activation(out=KT[1][0:2, h, blk * 512:(blk + 1) * 512], in_=ones_t[0:2, :], func=AF.Copy))
                        else:
                            pp_free[b_] = act.mark(act.e.activation(out=QT[:, h, blk * 512:(blk + 1) * 512], in_=pp[b_][:], func=AF.Copy, scale=0.125))
                for t in range(4):
                    b_ = npp % 3; npp += 1
                    pe.wait(pp_free[b_])
                    for c in range(8):
                        ins = pe.e.matmul(pp[b_][:], lhsT=xt[:, c, t * 128:(t + 1) * 128], rhs=win[:, c, 1024:1536], start=(c == 0), stop=(c == 7))
                    tk = pe.mark(ins)
                    dve.wait(tk, tv1)
                    pp_free[b_] = dve.mark(dve.e.tensor_copy(out=VA[:, blk * 4 + t, :, 0:128], in_=pp[b_][:].rearrange("p (h e) -> p h e", e=128)))
                blk_last_pe[blk % 2] = tk
            barrier(k, [])
        NSP, NTM, NE = 2, 2, 4
        atab = sb("atab", [128, 2, 896], F32)
        augtab = sb("augtab", [128, 4, 2, 512], BF16)
        Qs = [[[sb(f"Qs{u}{v}{m}", [128, 512], BF16) for m in range(2)] for v in range(3)] for u in range(2)]
        asl = k.slots(es, 1)[0]
        asl.dma(sp, atab[:], atab_d)
        atok = asl.dma(sp, augtab[:], aug_d)
        qz = None
        for u in range(2):
            for v in range(3):
                for m in range(2):
                    qz = pool.mark(pool.e.memset(Qs[u][v][m][:], 0.0))
        sps = [ps(f"s{i}", [128, 2, 512], F32) for i in range(NSP)]
        acc = [ps(f"acc{i}", [128, 512], F32) for i in range(3)]
        tpo = ps("tpo", [128, 128], BF16)
        tmp = [sb(f"tmp{i}", [128, 2, 512], F32) for i in range(NTM)]
        E = [sb(f"E{i}", [128, 2, 512], BF16) for i in range(NE)]
        accs = sb("accs", [128, 3, 387], F32)
        rr = sb("rr", [128, 4], F32)
        tq = sb("tq", [128, 128], F32)
        oq = sb("oq", [128, 128], F32)
        sps_free = [None] * NSP
        tmp_free = [None] * NTM
        E_free = [None] * NE
        acc_free = [None] * 3
        accs_free = None
        tpo_free = None
        ns = 0
        unit_last_S = {}
        units = [(h_, qb_) for h_ in range(4) for qb_ in range(4)]
        yq = [[sb(f"yq{u_}{q_}", [128, 128], BF16) for q_ in range(4)] for u_ in range(2)]
        yq_free = {}
        yq_tok = {}
        qtoks = {}

        def build_Qs(ui):
            h_, qb_ = units[ui]
            u_ = ui % 2
            pool.wait(qz, atok, unit_last_S.get(ui - 2), ctok)
            for m in range(2):
                r0 = 64 * m
                a0 = 64 - 64 * m
                for v in range(3):
                    qtok = pool.mark(pool.e.tensor_copy(out=Qs[u_][v][m][r0:r0 + 64, :], in_=QT[r0:r0 + 64, h_, qb_ * 512:(qb_ + 1) * 512]))
                for v in range(2):
                    qtok = pool.mark(pool.e.tensor_copy(out=Qs[u_][v][m][a0:a0 + 2, :], in_=augtab[a0:a0 + 2, h_, v, :]))
            qtoks[ui] = qtok

        def finish_transposes(ui):
            nonlocal tpo_free
            h_, qb_ = units[ui]
            for qt in range(4):
                pe.wait(yq_tok[(ui, qt)], tpo_free)
                tt = pe.mark(pe.e.transpose(out=tpo[:], in_=yq[ui % 2][qt][:], identity=k.ident[:]))
                yq_free[(ui % 2, qt)] = tt
                act.wait(tt)
                tok0 = (qb_ * 4 + qt) * 128
                tpo_free = act.mark(act.e.activation(out=mixT[:, 4 + h_, tok0:tok0 + 128], in_=tpo[:], func=AF.Copy))

        evts = {}

        def epilogue(ui):
            nonlocal accs_free
            u = ui % 2
            evt = evts[ui]
            for qt in range(4):
                g0, g1 = qt, 4 + qt
                O0 = accs[:, g0 // 3, (g0 % 3) * 129:(g0 % 3) * 129 + 129]
                O1 = accs[:, g1 // 3, (g1 % 3) * 129:(g1 % 3) * 129 + 129]
                dve.wait(evt, lam_tok)
                e1 = dve.mark(dve.e.reciprocal(out=rr[:, 0:1], in_=O0[:, 128:129]))
                e2 = dve.mark(dve.e.reciprocal(out=rr[:, 1:2], in_=O1[:, 128:129]))
                dve.wait(e1, e2)
                e3 = dve.mark(dve.e.tensor_tensor(out=rr[:, 2:3], in0=rr[:, 1:2], in1=neglam, op=ALU.mult))
                dve.wait(e3)
                e4 = dve.mark(dve.e.tensor_scalar(out=tq[:], in0=O1[:, 0:128], scalar1=rr[:, 2:3], scalar2=None, op0=ALU.mult))
                dve.wait(e4)
                e5 = dve.mark(dve.e.scalar_tensor_tensor(out=oq[:], in0=O0[:, 0:128], scalar=rr[:, 0:1], op0=ALU.mult, in1=tq[:], op1=ALU.add))
                dve.wait(e5)
                e6 = dve.mark(dve.e.scalar_tensor_tensor(out=tq[:], in0=oq[:], scalar=1.0 / 128.0, op0=ALU.mult, in1=oq[:], op1=ALU.mult, accum_out=rr[:, 3:4]))
                dve.wait(e6)
                e7 = dve.mark(dve.e.tensor_scalar(out=rr[:, 3:4], in0=rr[:, 3:4], scalar1=EPS, scalar2=None, op0=ALU.add))
                pool.wait(e7)
                e8 = pool.mark(pool.e.tensor_tensor(out=rr[:, 3:4], in0=rr[:, 3:4], in1=mhalf[:], op=ALU.pow))
                dve.wait(e8, yq_free.get((u, qt)))
                e9 = dve.mark(dve.e.scalar_tensor_tensor(out=yq[u][qt][:], in0=oq[:], scalar=rr[:, 3:4], op0=ALU.mult, in1=g8[:], op1=ALU.mult))
                yq_tok[(ui, qt)] = e9
                if qt == 3:
                    accs_free = e9

        build_Qs(0)
        pending = None
        pend_epi = None
        tr_at = -1
        for ui, (h, qb) in enumerate(units):
            if True:
                u = ui % 2
                for j in range(3):
                    pe.wait(acc_free[j], tz)
                    pe.e.matmul(acc[j][:], lhsT=zer[:, 0:128], rhs=zer[:], start=True, stop=False)
                E_tok = {}

                def emit_S(kt):
                    nonlocal ns
                    g = ns; ns += 1
                    delta = qb * 512 - kt * 128
                    v = 0 if delta >= 128 else (1 if delta <= -512 else 2)
                    pe.wait(sps_free[g % NSP], qtoks[ui], pad_tok)
                    for m in range(2):
                        ins = pe.e.matmul(sps[g % NSP][:, m, :], lhsT=KT[m][:, h, kt * 128:(kt + 1) * 128],
                                          rhs=Qs[u][v][m][:], start=True, stop=True)
                    tk = pe.mark(ins)
                    unit_last_S[ui] = tk
                    if v == 2:
                        dve.wait(tk, tmp_free[g % NTM], atok)
                        t1 = dve.mark(dve.e.scalar_tensor_tensor(out=tmp[g % NTM][:], in0=atab[:, :, delta + 384:delta + 384 + 512], scalar=float(-SL[h]),
                                                                 op0=ALU.mult, in1=sps[g % NSP][:], op1=ALU.add))
                        sps_free[g % NSP] = t1
                        act.wait(t1, E_free[g % NE])
                        t2 = act.mark(act.e.activation(out=E[g % NE][:], in_=tmp[g % NTM][:], func=AF.Exp))
                        tmp_free[g % NTM] = t2
                    else:
                        n = abs(delta) // 128
                        col = (h * 32 + n) * 2 + v
                        act.wait(tk, E_free[g % NE], ctok)
                        t2 = act.mark(act.e.activation(out=E[g % NE][:], in_=sps[g % NSP][:], func=AF.Exp, bias=cst[:, col:col + 1], scale=1.0))
                        sps_free[g % NSP] = t2
                    E_tok[kt] = (t2, g)

                def emit_AV(kt):
                    t2, g = E_tok[kt]
                    pe.wait(t2, tv1)
                    for m in range(2):
                        for qt in range(4):
                            gi = m * 4 + qt
                            ins = pe.e.matmul(acc[gi // 3][:, (gi % 3) * 129:(gi % 3) * 129 + 129], lhsT=E[g % NE][:, m, qt * 128:(qt + 1) * 128],
                                              rhs=VA[:, kt, h, :], start=False, stop=(kt == 31 and gi in (2, 5, 7)))
                    E_free[g % NE] = pe.mark(ins)
                    return E_free[g % NE]

                emit_S(0)
                emit_S(1)
                for kt in range(32):
                    if kt + 2 < 32:
                        emit_S(kt + 2)
                    last = emit_AV(kt)
                    if kt == 6 and ui + 1 < len(units):
                        build_Qs(ui + 1)
                    if pend_epi is not None and kt == min(4 * qb + 2, 14):
                        epilogue(pend_epi)
                        pending = pend_epi
                        pend_epi = None
                        tr_at = kt + 12
                    if pending is not None and pend_epi is None and kt == tr_at:
                        finish_transposes(pending)
                        pending = None
                evt = []
                for j in range(3):
                    act.wait(last, accs_free)
                    acc_free[j] = act.mark(act.e.activation(out=accs[:, j, :], in_=acc[j][:, 0:387], func=AF.Copy))
                    evt.append(acc_free[j])
                evts[ui] = evt
                pend_epi = ui
        epilogue(pend_epi)
        finish_transposes(pend_epi)
        barrier(k, [])


def wout_phase(k, x1_d, wout_d, g_d, b_d, mixT, x2_d):
    nc = k.nc
    pe, act, dve, pool, sp = k.pe, k.act, k.dve, k.pool, k.sp
    with ExitStack() as es:
        sb = lambda n, shape, dt: es.enter_context(nc.sbuf_tensor(f"wo_{n}", shape, dt))
        wo = sb("wo", [128, 8, D], BF16)
        v = wout_d.rearrange("(c p) f -> p c f", p=128)
        wtoks = load_bf16_weights(k, sp, k.slots(es, 1)[0], [(wo[:, c, :], v[:, c, :]) for c in range(8)])
        NW = 4
        ctx = ln_ctx(k, es, "wo", g_d, b_d, nsl=NW)
        xR = [sb(f"xR{i}", [128, D], F32) for i in range(NW)]
        xsl = k.slots(es, NW)
        yp = [[es.enter_context(nc.psum_tensor(f"wo_y{t}{h}", [128, 512], F32)) for h in range(2)] for t in range(NW)]
        xR_free = [None] * NW
        yp_free = [None] * NW
        lts = {}

        def issue_load(t_):
            sl_ = t_ % NW
            sp.wait(xR_free[sl_])
            lts[t_] = xsl[sl_].dma(sp, xR[sl_][:], x1_d[t_ * 128:(t_ + 1) * 128, :])

        for t_ in range(NW - 1):
            issue_load(t_)
        for t in range(16):
            s_ = t % NW
            if t + NW - 1 < 16:
                issue_load(t + NW - 1)
            lt = lts[t]
            pe.wait(wtoks, yp_free[s_])
            for hh in range(2):
                for c in range(8):
                    ins = pe.e.matmul(yp[s_][hh][:], lhsT=mixT[:, c, t * 128:(t + 1) * 128], rhs=wo[:, c, hh * 512:(hh + 1) * 512], start=(c == 0), stop=(c == 7))
            tk = pe.mark(ins)
            stt = ln_part_a(k, ctx, t, [yp[s_][0][:], yp[s_][1][:]], xR[s_], ALPHA, EPS, x2_d[t * 128:(t + 1) * 128, :], [tk, lt])
            xR_free[s_] = stt
            yp_free[s_] = stt
            if t >= 1:
                ln_part_b(k, ctx, t - 1)
        ln_part_b(k, ctx, 15)
        barrier(k, [ctx["store_tok"]])


def build_program(stop=None):
    nc = bass.Bass("TRN2", target_bir_lowering=False)
    dram_in = lambda n, shape, dt=F32: nc.dram_tensor(n, shape, dt, kind="ExternalInput").ap()
    x = dram_in("x", [SEQ, D])
    wg1 = dram_in("wg1", [D, DFF]); wu1 = dram_in("wu1", [D, DFF]); wd1 = dram_in("wd1", [DFF, D])
    wg2 = dram_in("wg2", [D, DFF]); wu2 = dram_in("wu2", [D, DFF]); wd2 = dram_in("wd2", [DFF, D])
    win = dram_in("win", [D, 3072]); wout = dram_in("wout", [D, D])
    ln1g = dram_in("ln1g", [128, D]); ln1b = dram_in("ln1b", [128, D])
    ln2g = dram_in("ln2g", [128, D]); ln2b = dram_in("ln2b", [128, D])
    ln3g = dram_in("ln3g", [128, D]); ln3b = dram_in("ln3b", [128, D])
    ident_d = dram_in("ident", [128, 128], BF16)
    nab = dram_in("nab", [13, 128, 1024])
    augt = dram_in("augt", [128, 4, 2, 512], BF16); atab = dram_in("atab", [128, 2, 896]); cst = dram_in("cst", [128, 256])
    lamv = dram_in("lamv", [128, 4, 64]); subg = dram_in("subg", [128, 128])
    zeros_ones = dram_in("zeros_ones", [2, 64, 4 * SEQ], BF16)
    out = nc.dram_tensor("out", [OWN, D], F32, kind="ExternalOutput").ap()
    x1_d = nc.dram_tensor("x1_scratch", [SEQ, D], F32, kind="Internal").ap()
    x2_d = nc.dram_tensor("x2_scratch", [OWN, D], F32, kind="Internal").ap()
    win_bf = nc.dram_tensor("win_bf", [D, 3072], BF16, kind="Internal").ap()
    wout_bf = nc.dram_tensor("wout_bf", [D, D], BF16, kind="Internal").ap()
    wg2_bf = nc.dram_tensor("wg2_bf", [D, DFF], BF16, kind="Internal").ap()
    wu2_bf = nc.dram_tensor("wu2_bf", [D, DFF], BF16, kind="Internal").ap()
    wd2_bf = nc.dram_tensor("wd2_bf", [DFF, D], BF16, kind="Internal").ap()

    def pieces(src, dst, width):
        sv = src.rearrange("(c p) f -> p c f", p=128)
        dv = dst.rearrange("(c p) f -> p c f", p=128)
        out_ = []
        for c in range(sv.shape[1]):
            for o in range(0, sv.shape[2], width):
                out_.append((sv[:, c, o:o + width], dv[:, c, o:o + width]))
        return out_
    bg_jobs = (pieces(win, win_bf, 1024) + pieces(wout, wout_bf, 1024) + pieces(wg2, wg2_bf, 1408)
               + pieces(wu2, wu2_bf, 1408) + pieces(wd2, wd2_bf, 1024))
    dbg = None
    if stop is not None:
        dbg = nc.dram_tensor("dbg", [SEQ, D], F32, kind="ExternalOutput").ap()
    with ExitStack() as es:
        k = K()
        k.nc = nc
        k.pe = Eng(nc, nc.tensor, "pe", es)
        k.act = Eng(nc, nc.scalar, "act", es)
        k.dve = Eng(nc, nc.vector, "dve", es)
        k.pool = Eng(nc, nc.gpsimd, "pool", es)
        k.sp = Eng(nc, nc.sync, "sp", es)
        k.engs = [k.pe, k.act, k.dve, k.pool, k.sp]
        k.es_global = es
        k.slot_pool = []
        k.zeros_ones = zeros_ones
        k.ident = es.enter_context(nc.sbuf_tensor("ident_sb", [128, 128], BF16))
        isl = k.slots(es, 1)[0]
        k.ident_tok = isl.dma(k.sp, k.ident[:], ident_d)
        if stop == "A":
            ffn_phase(k, "f1", x, SEQ, wg1, wu1, wd1, ln1g, ln1b, dbg, bg_jobs=bg_jobs)
            return nc
        if stop in ("NA", "DF", "W"):
            x1_src = x
            with ExitStack() as esb:
                inb = [esb.enter_context(nc.sbuf_tensor(f"dbg_in{i}", [128, 1408], F32)) for i in range(2)]
                bgc = BgCast(k, esb, "dbgc", bg_jobs[:32], inb, None)
                barrier(k, [bgc.finish()])
        else:
            ffn_phase(k, "f1", x, SEQ, wg1, wu1, wd1, ln1g, ln1b, x1_d, bg_jobs=bg_jobs)
            x1_src = x1_d
        mix_cm = nc.sbuf_tensor("mixT", [128, 8, OWN], BF16, side="right")
        mixT = mix_cm.__enter__()
        if stop != "DF":
            na_phase(k, x1_src, win_bf, nab, mixT)
        if stop != "NA":
            diff_phase(k, x1_src, win_bf, augt, atab, cst, lamv, subg, mixT)
        if stop in ("NA", "DF"):
            with ExitStack() as es3:
                tmpf = es3.enter_context(nc.sbuf_tensor("dbg_tmp", [128, 8, OWN], F32))
                k.dve.wait((k.act, k.act.n))
                c0_ = 0 if stop == "NA" else 4
                k.pool.wait((k.act, k.act.n))
                tk0 = k.pool.mark(k.pool.e.memset(tmpf[:], 0.0))
                k.dve.wait(tk0)
                tk = k.dve.mark(k.dve.e.tensor_copy(out=tmpf[:, c0_:c0_ + 4, :], in_=mixT[:, c0_:c0_ + 4, :]))
                k.sp.wait(tk)
                sl = k.slots(es3, 1)[0]
                for c in range(8):
                    for hf in range(2):
                        t_ = sl.dma(k.sp, dbg[(c * 2 + hf) * 128:(c * 2 + hf + 1) * 128, :], tmpf[:, c, hf * 1024:(hf + 1) * 1024])
                barrier(k, [t_])
            mix_cm.__exit__(None, None, None)
            return nc
        with ExitStack() as esf2:
            pre = None
            if stop is None:
                wg2s, wu2s, _ = alloc_ffn_weights(k, esf2, "f2", with_wd=False)
                wdA2 = esf2.enter_context(nc.sbuf_tensor("f2_wdA", [128, NFC // 2, D], BF16))
                wsl = k.slots(esf2, 1)[0]
                jobs2 = ([(wg2s[:, c, :], wg2_bf.rearrange("(c p) f -> p c f", p=128)[:, c, :]) for c in range(8)]
                         + [(wu2s[:, c, :], wu2_bf.rearrange("(c p) f -> p c f", p=128)[:, c, :]) for c in range(8)]
                         + [(wdA2[:, c:c + 1, :], wd2_bf.rearrange("(c p) f -> p c f", p=128)[:, c:c + 1, :]) for c in range(NFC // 2)])
                pre = (wg2s, wu2s, wdA2, wd2_bf, load_bf16_weights(k, k.act, wsl, jobs2))
            wout_phase(k, x1_src, wout_bf, ln2g, ln2b, mixT, x2_d if stop is None else dbg)
            mix_cm.__exit__(None, None, None)
            if stop == "W":
                return nc
            ffn_phase(k, "f2", x2_d, OWN, wg2, wu2, wd2, ln3g, ln3b, out, pre=pre)
    return nc


def _na_tables(rpb, rev):
    out = np.full((13, 128, 8, 128), -30000.0, np.float32)
    p = np.arange(128)

    def coords(tile):
        t = tile * 128 + p
        r, c = t // 64, t % 64
        if rev:
            r, c = 63 - r, 63 - c
        return r, c

    def fill(v, il, kt):
        rk, ck = coords(kt)
        rq, cq = coords(il)
        r0 = np.clip(rq - 4, 0, 56)
        c0 = np.clip(cq - 8, 0, 48)
        RK, RQ = rk[:, None], rq[None, :]
        CK, CQ = ck[:, None], cq[None, :]
        ok = (RK >= r0[None, :]) & (RK <= r0[None, :] + 7) & (CK >= c0[None, :]) & (CK <= c0[None, :] + 15)
        dr = np.clip(RK - RQ + 7, 0, 14)
        dc = np.clip(CK - CQ + 15, 0, 30)
        vals = rpb[:, dr, dc]
        tile = np.where(ok[None], vals, np.float32(-30000.0)).astype(np.float32)
        out[v] = tile.transpose(1, 0, 2)[:, [0, 2, 4, 6, 1, 3, 5, 7], :]
        return ok

    for il in range(2):
        for kt in range(4):
            fill(na_variant(il, kt), il, kt)
    for dj in range(-2, 3):
        fill(na_variant(8, 8 + dj), 8, 8 + dj)
    return np.ascontiguousarray(out.reshape(13, 128, 1024))


def prep_inputs(inputs, c):
    b, h = c // 2, c % 2
    xb = np.ascontiguousarray(inputs["x"][b])
    if h == 1:
        xb = np.ascontiguousarray(xb[::-1])
    f32 = lambda v: np.ascontiguousarray(np.asarray(v, np.float32))
    rep = lambda v: np.ascontiguousarray(np.broadcast_to(np.asarray(v, np.float32).reshape(1, -1), (128, np.asarray(v).size)))
    p = np.arange(128, dtype=np.float32)[:, None]
    jtab = (np.arange(512, dtype=np.float32)[None, :] - p).astype(np.float32)
    atab = np.abs(np.arange(896, dtype=np.float32)[None, :] - p - 384.0).astype(np.float32)
    atab = np.ascontiguousarray(np.stack([atab, atab], axis=1))
    cst = np.zeros((128, 4, 32, 2), np.float32)
    augt = np.zeros((128, 4, 2, 512), np.float32)
    jj = np.arange(512, dtype=np.float32)
    pp_ = np.arange(128, dtype=np.float32)
    for hh in range(4):
        sl = 2.0 ** (-8.0 * (hh + 1) / 4)
        for v in range(2):
            sgn = 1.0 if v == 0 else -1.0
            cst[:, hh, :, v] = sgn * sl * pp_[:, None] - sl * 128.0 * np.arange(32, dtype=np.float32)[None, :]
            hi = -sgn * sl * 256.0 * np.floor(jj / 256.0)
            lo = -sgn * sl * np.mod(jj, 256.0)
            for base in (0, 64):
                augt[base, hh, v] = hi
                augt[base + 1, hh, v] = lo
    cst = np.ascontiguousarray(cst.reshape(128, 256))
    augt = augt.astype(ml_dtypes.bfloat16)
    lamv = np.stack([rep(inputs["diff_lambda_q1"][0]), rep(inputs["diff_lambda_k1"][0]),
                     rep(inputs["diff_lambda_q2"][0]), rep(inputs["diff_lambda_k2"][0])], axis=1)
    m = {
        "x": xb,
        "wg1": f32(inputs["ffn1_w_gate"][0]), "wu1": f32(inputs["ffn1_w_up"][0]), "wd1": f32(inputs["ffn1_w_down"][0]),
        "wg2": f32(inputs["ffn2_w_gate"][0]), "wu2": f32(inputs["ffn2_w_up"][0]), "wd2": f32(inputs["ffn2_w_down"][0]),
        "win": f32(inputs["w_in"][0]), "wout": f32(inputs["w_out"][0]),
        "ln1g": rep(inputs["ln1_g"][0]), "ln1b": rep(inputs["ln1_b"][0]),
        "ln2g": rep(inputs["ln2_g"][0]), "ln2b": rep(inputs["ln2_b"][0]),
        "ln3g": rep(inputs["ln3_g"][0]), "ln3b": rep(inputs["ln3_b"][0]),
        "ident": np.eye(128, dtype=np.float32).astype(ml_dtypes.bfloat16),
        "nab": _na_tables(f32(inputs["na_rpb"][0]), h == 1),
        "augt": augt, "atab": atab, "cst": cst,
        "lamv": np.ascontiguousarray(lamv.astype(np.float32)), "subg": rep(inputs["diff_subln_g"][0]),
        "zeros_ones": np.stack([np.zeros((64, 4 * SEQ), np.float32), np.ones((64, 4 * SEQ), np.float32)]).astype(ml_dtypes.bfloat16),
    }
    return m


def kernel(**inputs):
    inputs = {k_: np.asarray(v) for k_, v in inputs.items()}
    nc = build_program()
    in_maps = [prep_inputs(inputs, c) for c in range(8)]
    res = run_bass_kernel_spmd(nc, in_maps, core_ids=list(range(8)))
    outp = np.empty((4, SEQ, D), np.float32)
    for c in range(8):
        b, h = c // 2, c % 2
        o = np.asarray(res.results[c]["out"])
        if h == 0:
            outp[b, :OWN] = o
        else:
            outp[b, OWN:] = o[::-1]
    return outp
```

```python
import numpy as np
from contextlib import ExitStack
import concourse.bass as bass
import concourse.mybir as mybir
from concourse.bass_utils import run_bass_kernel_spmd
import ml_dtypes

F32, BF16 = mybir.dt.float32, mybir.dt.bfloat16
AF = mybir.ActivationFunctionType
ALU = mybir.AluOpType

D = 1024
DFF = 2816
NFC = DFF // 128
SEQ = 4096
OWN = 2048
ALPHA = 2.0 ** 0.25
EPS = 1e-5
LAM_INIT = 0.2
NKT_NA = 18


def _flat(toks):
    out = []
    for t in toks:
        if t is None:
            continue
        if isinstance(t, list):
            out.extend(_flat(t))
        else:
            out.append(t)
    return out


class Eng:
    def __init__(self, nc, e, name, es):
        self.e = e
        self.name = name
        self.sem = es.enter_context(nc.semaphore("sem_" + name))
        self.n = 0
        self.seen = {}

    def wait(self, *toks):
        best = {}
        for src, v in _flat(list(toks)):
            if best.get(id(src), (None, 0))[1] < v:
                best[id(src)] = (src, v)
        for src, v in best.values():
            if self.seen.get(id(src), 0) >= v:
                continue
            self.e.wait_ge(src.sem, v)
            self.seen[id(src)] = v

    def mark(self, ins):
        ins.then_inc(self.sem, 1)
        self.n += 1
        return (self, self.n)


class Slot:
    def __init__(self, nc, name, es):
        self.sem = es.enter_context(nc.semaphore("dsem_" + name))
        self.n = 0
        self.busy = False

    def dma(self, q, out, in_):
        q.e.dma_start(out=out, in_=in_).then_inc(self.sem, 16)
        self.n += 16
        return (self, self.n)


class K:
    def slots(self, es, n):
        got = []
        for sl in self.slot_pool:
            if not sl.busy and len(got) < n:
                sl.busy = True
                got.append(sl)
        while len(got) < n:
            sl = Slot(self.nc, f"p{len(self.slot_pool)}", self.es_global)
            sl.busy = True
            self.slot_pool.append(sl)
            got.append(sl)

        def release():
            for sl in got:
                sl.busy = False
        es.callback(release)
        return got


def barrier(k, toks):
    toks = _flat(toks) + [(e, e.n) for e in k.engs if e.n > 0]
    for e in k.engs:
        e.wait(toks)


def copy_cast(eng, k, out, in_):
    if eng is k.act:
        return eng.e.activation(out=out, in_=in_, func=AF.Copy)
    return eng.e.tensor_copy(out=out, in_=in_)


class WeightLoader:
    def __init__(self, k, es, name, jobs, nslots=3, width=1408):
        nc = k.nc
        self.k = k
        self.jobs = jobs
        self.stg = [es.enter_context(nc.sbuf_tensor(f"{name}_stg{i}", [128, width], F32)) for i in range(nslots)]
        self.slots = k.slots(es, nslots)
        self.cast_tok = [None] * nslots
        self.engs = [k.dve, k.pool, k.act]
        self.toks = []
        self.i = 0
        k.last_stg = self.stg

    def emit(self, n):
        k = self.k
        nslots = len(self.stg)
        for _ in range(n):
            if self.i >= len(self.jobs):
                return
            i = self.i
            self.i += 1
            dst, src = self.jobs[i]
            s = i % nslots
            if len(src.shape) == 3:
                nel = src.shape[1] * src.shape[2]
                sv = self.stg[s][:, :nel].rearrange("p (a b) -> p a b", b=src.shape[2])
            else:
                nel = src.shape[-1]
                sv = self.stg[s][:, :nel]
            k.sp.wait(self.cast_tok[s])
            lt = self.slots[s].dma(k.sp, sv, src)
            e = self.engs[i % 3]
            e.wait(lt)
            self.cast_tok[s] = e.mark(copy_cast(e, k, dst, sv))
            self.toks.append(self.cast_tok[s])

    def done(self):
        return self.i >= len(self.jobs)


def load_cast_weights(k, es, name, jobs, nslots=3, width=1408):
    wl = WeightLoader(k, es, name, jobs, nslots, width)
    wl.emit(len(jobs))
    return wl.toks


def load_bf16_weights(k, q, slot, jobs):
    tok = None
    for dst, src in jobs:
        tok = slot.dma(q, dst, src)
    return tok


class BgCast:
    def __init__(self, k, es, name, jobs, in_bufs, first_tok):
        nc = k.nc
        self.k = k
        self.jobs = jobs
        self.inb = in_bufs
        self.outb = [es.enter_context(nc.sbuf_tensor(f"{name}_bgo{i}", [128, 1408], BF16)) for i in range(2)]
        self.in_slot = k.slots(es, 2)
        self.out_slot = k.slots(es, 2)
        self.load_tok = [first_tok, first_tok]
        self.cast_tok = [None, None]
        self.store_tok = [None, None]
        self.i = 0

    def step(self):
        k = self.k
        i = self.i
        n_jobs = len(self.jobs)
        if i > n_jobs + 1:
            return
        self.i += 1
        if 0 <= i - 2 < n_jobs:
            j = i - 2
            n = self.jobs[j][0].shape[-1]
            k.act.wait(self.cast_tok[j % 2])
            self.store_tok[j % 2] = self.out_slot[j % 2].dma(k.act, self.jobs[j][1], self.outb[j % 2][:, :n])
        if i < n_jobs:
            n = self.jobs[i][0].shape[-1]
            k.act.wait(self.cast_tok[i % 2], self.load_tok[i % 2] if i < 2 else None)
            self.load_tok[i % 2] = self.in_slot[i % 2].dma(k.act, self.inb[i % 2][:, :n], self.jobs[i][0])
        if 0 <= i - 1 < n_jobs:
            j = i - 1
            n = self.jobs[j][0].shape[-1]
            k.act.wait(self.load_tok[j % 2], self.store_tok[j % 2])
            self.cast_tok[j % 2] = k.act.mark(k.act.e.activation(out=self.outb[j % 2][:, :n], in_=self.inb[j % 2][:, :n], func=AF.Copy))

    def finish(self):
        while self.i <= len(self.jobs) + 1:
            self.step()
        return [t for t in self.store_tok if t is not None]


def ln_part_a(k, ctx, t, psum_halves, xres, xscale, eps, dst_ap, pre_toks):
    nsl = ctx["n"]
    dve, pool, sp = k.dve, k.pool, k.sp
    s = t % nsl
    r = ctx["r"][s]
    stats, mv, ve, rstd = ctx["stats"][s], ctx["mv"][s], ctx["ve"][s], ctx["rstd"][s]
    stt_toks = []
    for hh in range(2):
        dve.wait(pre_toks, ctx["store_tok"][s])
        ins = dve.e.scalar_tensor_tensor(out=r[:, hh * 512:(hh + 1) * 512], in0=xres[:, hh * 512:(hh + 1) * 512],
                                         scalar=float(xscale), op0=ALU.mult, in1=psum_halves[hh], op1=ALU.add)
        stt_toks.append(dve.mark(ins))
    st_toks = []
    for hh in range(2):
        dve.wait(stt_toks[hh])
        st_toks.append(dve.mark(dve.e.bn_stats(out=stats[:, hh * 6:(hh + 1) * 6], in_=r[:, hh * 512:(hh + 1) * 512])))
    dve.wait(st_toks)
    t1 = dve.mark(dve.e.bn_aggr(out=mv[:], in_=stats[:]))
    dve.wait(t1)
    t2 = dve.mark(dve.e.tensor_scalar(out=ve[:], in0=mv[:, 1:2], scalar1=float(eps), scalar2=None, op0=ALU.add))
    pool.wait(t2, ctx["gb_tok"])
    t3 = pool.mark(pool.e.tensor_tensor(out=rstd[:], in0=ve[:], in1=ctx["mhalf"][:], op=ALU.pow))
    dve.wait(t1, ctx["gb_tok"])
    t4 = dve.mark(dve.e.scalar_tensor_tensor(out=r[:], in0=r[:], scalar=mv[:, 0:1], op0=ALU.subtract, in1=ctx["g"][:], op1=ALU.mult))
    ctx["pend"][t] = (t3, t4, dst_ap)
    return stt_toks


def ln_part_b(k, ctx, t):
    dve, sp = k.dve, k.sp
    s = t % ctx["n"]
    r = ctx["r"][s]
    t3, t4, dst_ap = ctx["pend"].pop(t)
    dve.wait(t3, t4)
    t6 = dve.mark(dve.e.scalar_tensor_tensor(out=r[:], in0=r[:], scalar=ctx["rstd"][s][:, 0:1], op0=ALU.mult, in1=ctx["b"][:], op1=ALU.add))
    sp.wait(t6)
    ctx["store_tok"][s] = ctx["store_slot"][s].dma(sp, dst_ap, r[:])


def ln_ctx(k, es, name, g_d, b_d, nsl=2):
    nc = k.nc
    sb = lambda n, shape, dt: es.enter_context(nc.sbuf_tensor(f"{name}_{n}", shape, dt))
    ctx = {
        "n": nsl,
        "pend": {},
        "r": [sb(f"r{i}", [128, D], F32) for i in range(nsl)],
        "stats": [sb(f"stats{i}", [128, 12], F32) for i in range(nsl)],
        "mv": [sb(f"mv{i}", [128, 2], F32) for i in range(nsl)],
        "ve": [sb(f"ve{i}", [128, 1], F32) for i in range(nsl)],
        "rstd": [sb(f"rstd{i}", [128, 1], F32) for i in range(nsl)],
        "mhalf": sb("mhalf", [128, 1], F32),
        "g": sb("g", [128, D], F32),
        "b": sb("b", [128, D], F32),
        "store_slot": k.slots(es, nsl),
        "store_tok": [None] * nsl,
    }
    gs = k.slots(es, 1)[0]
    gs.dma(k.sp, ctx["g"][:], g_d)
    tg = gs.dma(k.sp, ctx["b"][:], b_d)
    tm = k.pool.mark(k.pool.e.memset(ctx["mhalf"][:], -0.5))
    ctx["gb_tok"] = [tg, tm]
    return ctx


def alloc_ffn_weights(k, es, name, with_wd=True):
    nc = k.nc
    wg = es.enter_context(nc.sbuf_tensor(f"{name}_wg", [128, 8, DFF], BF16))
    wu = es.enter_context(nc.sbuf_tensor(f"{name}_wu", [128, 8, DFF], BF16))
    wd = es.enter_context(nc.sbuf_tensor(f"{name}_wd", [128, NFC, D], BF16)) if with_wd else None
    return wg, wu, wd


def ffn_phase(k, name, x_src, T, wg_d, wu_d, wd_d, g_d, b_d, dst, pre=None, bg_jobs=None):
    nc = k.nc
    pe, act, dve, pool, sp = k.pe, k.act, k.dve, k.pool, k.sp
    NB = T // 256
    NH = 4
    with ExitStack() as es:
        sb = lambda n, shape, dt: es.enter_context(nc.sbuf_tensor(f"{name}_{n}", shape, dt))
        ps = lambda n, shape, dt: es.enter_context(nc.psum_tensor(f"{name}_{n}", shape, dt))
        bg = None
        emit_weights = None
        wl = None
        if pre is not None:
            wg, wu, wdA, wd_bf_d, wtoks = pre
            wdB = es.enter_context(nc.sbuf_tensor(f"{name}_wdB", [128, NFC // 2, D], BF16))
            wdv = wd_bf_d.rearrange("(c p) f -> p c f", p=128)
            wdB_tok = load_bf16_weights(k, k.act, k.slots(es, 1)[0],
                                        [(wdB[:, c:c + 1, :], wdv[:, NFC // 2 + c:NFC // 2 + c + 1, :]) for c in range(NFC // 2)])
            wd_ap = lambda fd, lo, hi: (wdA if fd < NFC // 2 else wdB)[:, fd % (NFC // 2), lo:hi]
            gu_wtok = {f: wtoks for f in range(NFC)}
            d_wtok = {f: None for f in range(NFC)}
        else:
            wg, wu, wd = alloc_ffn_weights(k, es, name)
            wgv = wg_d.rearrange("(c p) f -> p c f", p=128)
            wuv = wu_d.rearrange("(c p) f -> p c f", p=128)
            wdv = wd_d.rearrange("(c p) d -> p c d", p=128)
            jobs = []
            for fg in range(NFC // 2):
                cs = slice(fg * 256, (fg + 1) * 256)
                for c0 in (0, 4):
                    jobs.append((wg[:, c0:c0 + 4, cs], wgv[:, c0:c0 + 4, cs]))
                    jobs.append((wu[:, c0:c0 + 4, cs], wuv[:, c0:c0 + 4, cs]))
                for c in (2 * fg, 2 * fg + 1):
                    jobs.append((wd[:, c, :], wdv[:, c, :]))
            wdB_tok = None
            wd_ap = lambda fd, lo, hi: wd[:, fd, lo:hi]
            gu_wtok, d_wtok = {}, {}

            wl = WeightLoader(k, es, name, jobs)
            for f in range(NFC):
                gu_wtok[f] = ("wl", (f // 2) * 6, (f // 2) * 6 + 4)
                d_wtok[f] = ("wl", (f // 2) * 6 + 4 + (f % 2), (f // 2) * 6 + 5 + (f % 2))

            def emit_weights():
                wl.emit(12)
        ctx = ln_ctx(k, es, name, g_d, b_d)

        xA = [sb(f"xA{i}", [128, D], F32) for i in range(2)]
        xA_slot = k.slots(es, 2)
        xR = [sb(f"xR{i}", [128, D], F32) for i in range(2)]
        xR_slot = k.slots(es, 2)
        xbf = [sb(f"xbf{i}", [128, D], BF16) for i in range(2)]
        xT = [sb(f"xT{i}", [128, 8, 256], BF16) for i in range(2)]
        hT = [sb(f"hT{i}", [128, 256], BF16) for i in range(NH)]
        sg = [sb(f"sg{i}", [128, 256], F32) for i in range(2)]
        ident = k.ident
        Tps = [ps(f"T{i}", [128, 8, 128], BF16) for i in range(2)]
        gu = [ps(f"gu{i}", [128, 2, 256], F32) for i in range(2)]
        yp = [[ps(f"y{t}{h}", [128, 512], F32) for h in range(2)] for t in range(2)]

        cast_tok = [None, None]
        T_tok = [None, None]
        XE_tok = {}
        xR_free = [None, None]
        xR_tok = [None, None]
        gu_last = {}
        mult_tok = {}
        D_tok = {}
        ep_tok = {}

        def stage_load_cast(b):
            for t in range(2):
                sp.wait(cast_tok[t])
                lt = xA_slot[t].dma(sp, xA[t][:], x_src[(b * 2 + t) * 128:(b * 2 + t + 1) * 128, :])
                pool.wait(lt, T_tok[t])
                cast_tok[t] = pool.mark(pool.e.tensor_copy(out=xbf[t][:], in_=xA[t][:]))

        def stage_T(b):
            for t in range(2):
                prev = XE_tok.get((b - 1, t))
                pe.wait(cast_tok[t], prev, k.ident_tok)
                for c in range(8):
                    ins = pe.e.transpose(out=Tps[t][:, c, :], in_=xbf[t][:, c * 128:(c + 1) * 128], identity=ident[:])
                T_tok[t] = pe.mark(ins)

        def stage_XE(b):
            for t in range(2):
                act.wait(T_tok[t], gu_last.get(b - 2))
                XE_tok[(b, t)] = act.mark(act.e.activation(out=xT[b % 2][:, :, t * 128:(t + 1) * 128], in_=Tps[t][:], func=AF.Copy))

        def stage_xR(b):
            for t in range(2):
                sp.wait(xR_free[t])
                xR_tok[t] = xR_slot[t].dma(sp, xR[t][:], x_src[(b * 2 + t) * 128:(b * 2 + t + 1) * 128, :])

        stage_load_cast(0)
        if emit_weights is not None:
            emit_weights()
        stage_T(0)
        stage_XE(0)
        def wres(t):
            if isinstance(t, tuple) and len(t) == 3 and t[0] == "wl":
                return list(wl.toks[t[1]:t[2]])
            return t

        def stage_down(b, fd):
            gd = b * NFC + fd
            pe.wait(mult_tok[gd], ep_tok.get(b - 1) if fd == 0 else None, wdB_tok if (b == 0 and fd == NFC // 2) else None,
                    wres(d_wtok[fd]) if b == 0 else None)
            for t in range(2):
                for hh in range(2):
                    ins = pe.e.matmul(yp[t][hh][:], lhsT=hT[gd % NH][:, t * 128:(t + 1) * 128],
                                      rhs=wd_ap(fd, hh * 512, (hh + 1) * 512), start=(fd == 0), stop=(fd == NFC - 1))
            D_tok[gd] = pe.mark(ins)

        for b in range(NB):
            if b + 1 < NB:
                stage_load_cast(b + 1)
            stage_xR(b)
            xt = xT[b % 2]
            for f in range(NFC):
                gi = b * NFC + f
                if wl is not None and b == 0 and f % 2 == 0:
                    wl.emit(6 * (f // 2 + 3) - wl.i)
                    if wl.done() and bg is None and bg_jobs:
                        bg = BgCast(k, es, name, bg_jobs, k.last_stg[:2], list(wl.toks))
                pe.wait(XE_tok[(b, 0)], XE_tok[(b, 1)], mult_tok.get(gi - 2), wres(gu_wtok[f]) if b == 0 else None)
                for c in range(8):
                    pe.e.matmul(gu[gi % 2][:, 0, :], lhsT=wg[:, c, f * 128:(f + 1) * 128], rhs=xt[:, c, :],
                                start=(c == 0), stop=(c == 7))
                for c in range(8):
                    ins = pe.e.matmul(gu[gi % 2][:, 1, :], lhsT=wu[:, c, f * 128:(f + 1) * 128], rhs=xt[:, c, :],
                                      start=(c == 0), stop=(c == 7))
                gtok = pe.mark(ins)
                if f == NFC - 1:
                    gu_last[b] = gtok
                act.wait(gtok, mult_tok.get(gi - 2))
                stok = act.mark(act.e.activation(out=sg[gi % 2][:], in_=gu[gi % 2][:, 0, :], func=AF.Silu))
                dve.wait(stok, D_tok.get(gi - NH))
                mult_tok[gi] = dve.mark(dve.e.tensor_tensor(out=hT[gi % NH][:], in0=sg[gi % 2][:], in1=gu[gi % 2][:, 1, :], op=ALU.mult))
                if f == 10 and b + 1 < NB:
                    stage_T(b + 1)
                    stage_XE(b + 1)
                if bg is not None and f in (1, 5, 9, 13, 17, 20):
                    bg.step()
                if f >= 1:
                    stage_down(b, f - 1)
            stage_down(b, NFC - 1)
            etoks = []
            for t in range(2):
                gt = b * 2 + t
                stt = ln_part_a(k, ctx, gt, [yp[t][0][:], yp[t][1][:]], xR[t], 2.0 * ALPHA, 4.0 * EPS,
                                dst[gt * 128:(gt + 1) * 128, :], [D_tok[b * NFC + NFC - 1], xR_tok[t]])
                xR_free[t] = stt
                etoks.extend(stt)
            for t in range(2):
                ln_part_b(k, ctx, b * 2 + t)
            ep_tok[b] = etoks
        barrier(k, [ctx["store_tok"], bg.finish() if bg is not None else None])


def xT_block_loader(k, es, name, src, ntile_list):
    nc = k.nc
    sb = lambda n, shape, dt: es.enter_context(nc.sbuf_tensor(f"{name}_{n}", shape, dt))
    st = {
        "xA": [sb(f"lxA{i}", [128, D], F32) for i in range(2)],
        "xbf": [sb(f"lxbf{i}", [128, D], BF16) for i in range(2)],
        "slot": k.slots(es, 2),
        "Tps": [es.enter_context(nc.psum_tensor(f"{name}_lT{i}", [128, 8, 128], BF16)) for i in range(2)],
        "cast_tok": [None, None], "T_tok": [None, None], "XE_tok": [None, None], "i": 0,
    }

    def emit(tile, dstT, dst_free_tok=None):
        i = st["i"]; st["i"] += 1
        s_ = i % 2
        k.sp.wait(st["cast_tok"][s_])
        lt = st["slot"][s_].dma(k.sp, st["xA"][s_][:], src[tile * 128:(tile + 1) * 128, :])
        k.pool.wait(lt, st["T_tok"][s_])
        st["cast_tok"][s_] = k.pool.mark(k.pool.e.tensor_copy(out=st["xbf"][s_][:], in_=st["xA"][s_][:]))
        k.pe.wait(st["cast_tok"][s_], st["XE_tok"][s_], k.ident_tok)
        for c in range(8):
            ins = k.pe.e.transpose(out=st["Tps"][s_][:, c, :], in_=st["xbf"][s_][:, c * 128:(c + 1) * 128], identity=k.ident[:])
        st["T_tok"][s_] = k.pe.mark(ins)
        k.act.wait(st["T_tok"][s_], dst_free_tok)
        st["XE_tok"][s_] = k.act.mark(k.act.e.activation(out=dstT, in_=st["Tps"][s_][:], func=AF.Copy))
        return st["XE_tok"][s_]
    return emit


def win_jobs(win_sb, win_d, col0, ncols):
    v = win_d.rearrange("(c p) f -> p c f", p=128)
    return [(win_sb[:, c, :], v[:, c, col0:col0 + ncols]) for c in range(8)]


def na_kt_set(il):
    return [0, 1, 2, 3] if il < 2 else list(range(il - 2, il + 3))


def na_variant(il, kt):
    return il * 4 + kt if il < 2 else 8 + (kt - il + 2)


def na_phase(k, x1_d, win_d, nab_d, mixT):
    nc = k.nc
    pe, act, dve, pool, sp = k.pe, k.act, k.dve, k.pool, k.sp
    with ExitStack() as es:
        sb = lambda n, shape, dt: es.enter_context(nc.sbuf_tensor(f"na_{n}", shape, dt))
        ps = lambda n, shape, dt: es.enter_context(nc.psum_tensor(f"na_{n}", shape, dt))
        KT = sb("KT", [128, 4, NKT_NA * 128], BF16)
        QT = [sb(f"QT{i}", [128, 4, OWN], BF16) for i in range(2)]
        VA = sb("VA", [128, NKT_NA, 8, 65], BF16)
        zer = sb("zer", [128, 512], BF16)
        nab = sb("nab", [128, 13, 1024], F32)
        nsl = k.slots(es, 1)[0]
        tz = pool.mark(pool.e.memset(zer[:], 0.0))
        dve.e.memset(QT[0][64:128, :, :], 0.0)
        dve.e.memset(QT[1][0:64, :, :], 0.0)
        tv1 = dve.mark(dve.e.memset(VA[:, :, :, 64:65], 1.0))
        pad_tok = tv1
        with ExitStack() as es2:
            sb2 = lambda n, shape, dt: es2.enter_context(nc.sbuf_tensor(f"nap_{n}", shape, dt))
            win = sb2("win", [128, 8, 1536], BF16)
            wtoks = load_bf16_weights(k, act, k.slots(es2, 1)[0], win_jobs(win, win_d, 0, 1536))
            for v in range(13):
                nab_tok = nsl.dma(act, nab[:, v, :], nab_d[v])
            x1T = [sb2(f"x1T{i}", [128, 8, 512], BF16) for i in range(2)]
            pp = [es2.enter_context(nc.psum_tensor(f"nap_pp{i}", [128, 512], F32)) for i in range(3)]
            emit = xT_block_loader(k, es2, "nap", x1_d, None)
            pp_free = [None] * 3
            blk_last_pe = [None, None]
            npp = 0
            for blk in range(5):
                ntile = 4 if blk < 4 else 2
                ntok = ntile * 128
                xt = x1T[blk % 2]
                xe = [emit(blk * 4 + t, xt[:, :, t * 128:(t + 1) * 128], blk_last_pe[blk % 2]) for t in range(ntile)]
                pe.wait(xe, wtoks)
                for kind in range(2):
                    if kind == 1 and blk >= 4:
                        continue
                    for hp in range(4):
                        col = (512 if kind == 0 else 0) + hp * 128
                        b_ = npp % 3; npp += 1
                        pe.wait(pp_free[b_])
                        for c in range(8):
                            ins = pe.e.matmul(pp[b_][:, :ntok], lhsT=win[:, c, col:col + 128], rhs=xt[:, c, :ntok], start=(c == 0), stop=(c == 7))
                        tk = pe.mark(ins)
                        act.wait(tk)
                        if kind == 0:
                            pp_free[b_] = act.mark(act.e.activation(out=KT[:, hp, blk * 512:blk * 512 + ntok], in_=pp[b_][:, :ntok], func=AF.Copy))
                        else:
                            act.wait(tv1)
                            act.e.activation(out=QT[0][0:64, hp, blk * 512:blk * 512 + ntok], in_=pp[b_][0:64, :ntok], func=AF.Copy, scale=0.125)
                            pp_free[b_] = act.mark(act.e.activation(out=QT[1][64:128, hp, blk * 512:blk * 512 + ntok], in_=pp[b_][64:128, :ntok], func=AF.Copy, scale=0.125))
                for t in range(ntile):
                    b_ = npp % 3; npp += 1
                    pe.wait(pp_free[b_])
                    for c in range(8):
                        ins = pe.e.matmul(pp[b_][:], lhsT=xt[:, c, t * 128:(t + 1) * 128], rhs=win[:, c, 1024:1536], start=(c == 0), stop=(c == 7))
                    tk = pe.mark(ins)
                    dve.wait(tk, tv1)
                    pp_free[b_] = dve.mark(dve.e.tensor_copy(out=VA[:, blk * 4 + t, :, 0:64], in_=pp[b_][:].rearrange("p (h e) -> p h e", e=64)))
                blk_last_pe[blk % 2] = tk
            barrier(k, [])
        NSP, NTM, NE, LA = 4, 3, 5, 3
        sps = [ps(f"s{i}", [128, 512], F32) for i in range(NSP)]
        acc = [ps(f"acc{i}", [128, 512], F32) for i in range(2)]
        tpo = ps("tpo", [128, 4, 128], BF16)
        tmp = [sb(f"tmp{i}", [128, 512], F32) for i in range(NTM)]
        E = [sb(f"E{i}", [128, 512], BF16) for i in range(NE)]
        rr = sb("rr", [128, 8], F32)
        nao = [sb(f"nao{i}", [128, 512], BF16) for i in range(2)]
        accs = sb("accs", [128, 2, 260], F32)
        sps_free = [None] * NSP
        tmp_free = [None] * NTM
        E_free = [None] * NE
        acc_free = [None, None]
        nao_free = [None, None]
        nao_tok = {}
        tpo_free = None
        accs_free = None
        ns = 0
        E_tok = {}

        def finish_il(il_):
            nonlocal tpo_free
            s2 = il_ % 2
            pe.wait(nao_tok[il_], tpo_free)
            for hp in range(4):
                ins = pe.e.transpose(out=tpo[:, hp, :], in_=nao[s2][:, hp * 128:(hp + 1) * 128], identity=k.ident[:])
            tt = pe.mark(ins)
            nao_free[s2] = tt
            act.wait(tt)
            tpo_free = act.mark(act.e.activation(out=mixT[:, 0:4, il_ * 128:(il_ + 1) * 128], in_=tpo[:], func=AF.Copy))

        def steps_of(il_):
            return [(kt, hb) for kt in na_kt_set(il_) for hb in range(2)]

        def emit_S(il_, si):
            nonlocal ns
            kt, hb = steps_of(il_)[si]
            g = ns; ns += 1
            pe.wait(sps_free[g % NSP], pad_tok)
            for hl in range(4):
                ins = pe.e.matmul(sps[g % NSP][:, hl * 128:(hl + 1) * 128], lhsT=KT[:, hl, kt * 128:(kt + 1) * 128],
                                  rhs=QT[hb][:, hl, il_ * 128:(il_ + 1) * 128], start=True, stop=True)
            tk = pe.mark(ins)
            v = na_variant(il_, kt)
            dve.wait(tk, tmp_free[g % NTM], nab_tok)
            t1 = dve.mark(dve.e.tensor_tensor(out=tmp[g % NTM][:], in0=nab[:, v, hb * 512:(hb + 1) * 512], in1=sps[g % NSP][:], op=ALU.add))
            sps_free[g % NSP] = t1
            act.wait(t1, E_free[g % NE])
            t2 = act.mark(act.e.activation(out=E[g % NE][:], in_=tmp[g % NTM][:], func=AF.Exp))
            tmp_free[g % NTM] = t2
            E_tok[(il_, si)] = (t2, g)

        def emit_AV(il_, si, last):
            kt, hb = steps_of(il_)[si]
            t2, g = E_tok.pop((il_, si))
            pe.wait(t2)
            for hl in range(4):
                h = 2 * hl + hb
                ins = pe.e.matmul(acc[hb][:, hl * 65:(hl + 1) * 65], lhsT=E[g % NE][:, hl * 128:(hl + 1) * 128],
                                  rhs=VA[:, kt, h, :], start=False, stop=(last and hl == 3))
            E_free[g % NE] = pe.mark(ins)
            return E_free[g % NE]

        pend_il = None
        for si in range(LA):
            emit_S(0, si)
        for il in range(16):
            nst = len(steps_of(il))
            for hb in range(2):
                pe.wait(acc_free[hb], tz)
                pe.e.matmul(acc[hb][:], lhsT=zer[:, 0:128], rhs=zer[:], start=True, stop=False)
            last_av = [None, None]
            for si in range(nst):
                if si + LA < nst:
                    emit_S(il, si + LA)
                last_av[steps_of(il)[si][1]] = emit_AV(il, si, si >= nst - 2)
                if si == 3 and pend_il is not None:
                    finish_il(pend_il)
                    pend_il = None
            if il + 1 < 16:
                for si in range(LA):
                    emit_S(il + 1, si)
            s_ = il % 2
            evt = []
            for hb in range(2):
                act.wait(last_av[hb], accs_free)
                acc_free[hb] = act.mark(act.e.activation(out=accs[:, hb, :], in_=acc[hb][:, 0:260], func=AF.Copy))
                evt.append(acc_free[hb])
            for hb in range(2):
                accv = accs[:, hb, :].rearrange("p (h e) -> p h e", e=65)
                dve.wait(evt, nao_free[s_])
                tr = dve.mark(dve.e.reciprocal(out=rr[:, hb * 4:(hb + 1) * 4], in_=accv[:, :, 64]))
                dve.wait(tr)
                for hl in range(4):
                    h = 2 * hl + hb
                    ins = dve.e.tensor_scalar(out=nao[s_][:, h * 64:(h + 1) * 64], in0=accv[:, hl, 0:64], scalar1=rr[:, hb * 4 + hl:hb * 4 + hl + 1], scalar2=None, op0=ALU.mult)
                accs_free = dve.mark(ins)
            nao_tok[il] = accs_free
            pend_il = il
        finish_il(pend_il)
        barrier(k, [])


def diff_phase(k, x1_d, win_d, aug_d, atab_d, cst_d, lamv_d, subg_d, mixT):
    nc = k.nc
    pe, act, dve, pool, sp = k.pe, k.act, k.dve, k.pool, k.sp
    SL = [2.0 ** (-8.0 * (h + 1) / 4) for h in range(4)]
    with ExitStack() as es:
        sb = lambda n, shape, dt: es.enter_context(nc.sbuf_tensor(f"df_{n}", shape, dt))
        ps = lambda n, shape, dt: es.enter_context(nc.psum_tensor(f"df_{n}", shape, dt))
        KT = [sb(f"KT{i}", [128, 4, SEQ], BF16) for i in range(2)]
        QT = sb("QT", [128, 4, OWN], BF16)
        VA = sb("VA", [128, 32, 4, 129], BF16)
        cst = sb("cst", [128, 256], F32)
        lamv = sb("lamv", [128, 4, 64], F32)
        g8 = sb("g8", [128, 128], F32)
        zer = sb("zer", [128, 512], BF16)
        sm = sb("sm", [128, 8], F32)
        junk = sb("junk", [128, 64], F32)
        mhalf = sb("mhalf", [128, 1], F32)
        tz = pool.mark(pool.e.memset(zer[:], 0.0))
        pool.e.memset(mhalf[:], -0.5)
        ones_t = sb("ones_t", [128, 512], BF16)
        pad_tok = pool.mark(pool.e.memset(ones_t[:], 1.0))
        tv1 = dve.mark(dve.e.memset(VA[:, :, :, 128:129], 1.0))
        csl = k.slots(es, 1)[0]
        csl.dma(sp, cst[:], cst_d)
        csl.dma(sp, lamv[:], lamv_d)
        ctok = csl.dma(sp, g8[:], subg_d)
        dve.wait(ctok)
        a0 = dve.mark(dve.e.tensor_scalar(out=g8[:], in0=g8[:], scalar1=1.0 - LAM_INIT, scalar2=None, op0=ALU.mult))
        dve.wait(a0)
        a1 = dve.mark(dve.e.scalar_tensor_tensor(out=junk[:], in0=lamv[:, 0, :], scalar=1.0, op0=ALU.mult, in1=lamv[:, 1, :], op1=ALU.mult, accum_out=sm[:, 0:1]))
        dve.wait(a1)
        a2 = dve.mark(dve.e.scalar_tensor_tensor(out=junk[:], in0=lamv[:, 2, :], scalar=1.0, op0=ALU.mult, in1=lamv[:, 3, :], op1=ALU.mult, accum_out=sm[:, 1:2]))
        act.wait(a2)
        a3 = act.mark(act.e.activation(out=sm[:, 2:4], in_=sm[:, 0:2], func=AF.Exp))
        dve.wait(a3)
        a4 = dve.mark(dve.e.tensor_tensor(out=sm[:, 5:6], in0=sm[:, 3:4], in1=sm[:, 2:3], op=ALU.subtract))
        dve.wait(a4)
        lam_tok = dve.mark(dve.e.tensor_scalar(out=sm[:, 4:5], in0=sm[:, 5:6], scalar1=-LAM_INIT, scalar2=None, op0=ALU.add))
        neglam = sm[:, 4:5]
        with ExitStack() as es2:
            sb2 = lambda n, shape, dt: es2.enter_context(nc.sbuf_tensor(f"dfp_{n}", shape, dt))
            win = sb2("win", [128, 8, 1536], BF16)
            wtoks = load_bf16_weights(k, act, k.slots(es2, 1)[0], win_jobs(win, win_d, 1536, 1536))
            x1T = [sb2(f"x1T{i}", [128, 8, 512], BF16) for i in range(2)]
            pp = [es2.enter_context(nc.psum_tensor(f"dfp_pp{i}", [128, 512], F32)) for i in range(3)]
            emit = xT_block_loader(k, es2, "dfp", x1_d, None)
            pp_free = [None] * 3
            blk_last_pe = [None, None]
            npp = 0
            for blk in range(8):
                xt = x1T[blk % 2]
                xe = [emit(blk * 4 + t, xt[:, :, t * 128:(t + 1) * 128], blk_last_pe[blk % 2]) for t in range(4)]
                pe.wait(xe, wtoks)
                for kind in range(2):
                    if kind == 1 and blk >= 4:
                        continue
                    for h in range(4):
                        col = (512 if kind == 0 else 0) + h * 128
                        b_ = npp % 3; npp += 1
                        pe.wait(pp_free[b_])
                        for c in range(8):
                            ins = pe.e.matmul(pp[b_][:], lhsT=win[:, c, col:col + 128], rhs=xt[:, c, :], start=(c == 0), stop=(c == 7))
                        tk = pe.mark(ins)
                        act.wait(tk)
                        if kind == 0:
                            act.wait(pad_tok)
                            k0 = act.mark(act.e.activation(out=KT[0][:, h, blk * 512:(blk + 1) * 512], in_=pp[b_][:], func=AF.Copy))
                            k1 = act.mark(act.e.activation(out=KT[1][:, h, blk * 512:(blk + 1) * 512], in_=pp[b_][:], func=AF.Copy))
                            pp_free[b_] = k1
                            act.wait(k0, k1)
                            act.e.activation(out=KT[0][64:66, h, blk * 512:(blk + 1) * 512], in_=ones_t[64:66, :], func=AF.Copy)
                            kfix_tok = act.mark(act.e.activation(out=KT[1][0:2, h, blk * 512:(blk + 1) * 512], in_=ones_t[0:2, :], func=AF.Copy))
                        else:
                            pp_free[b_] = act.mark(act.e.activation(out=QT[:, h, blk * 512:(blk + 1) * 512], in_=pp[b_][:], func=AF.Copy, scale=0.125))
                for t in range(4):
                    b_ = npp % 3; npp += 1
                    pe.wait(pp_free[b_])
                    for c in range(8):
                        ins = pe.e.matmul(pp[b_][:], lhsT=xt[:, c, t * 128:(t + 1) * 128], rhs=win[:, c, 1024:1536], start=(c == 0), stop=(c == 7))
                    tk = pe.mark(ins)
                    dve.wait(tk, tv1)
                    pp_free[b_] = dve.mark(dve.e.tensor_copy(out=VA[:, blk * 4 + t, :, 0:128], in_=pp[b_][:].rearrange("p (h e) -> p h e", e=128)))
                blk_last_pe[blk % 2] = tk
            barrier(k, [])
        NSP, NTM, NE = 2, 2, 4
        atab = sb("atab", [128, 2, 896], F32)
        augtab = sb("augtab", [128, 4, 2, 512], BF16)
        Qs = [[[sb(f"Qs{u}{v}{m}", [128, 512], BF16) for m in range(2)] for v in range(3)] for u in range(2)]
        asl = k.slots(es, 1)[0]
        asl.dma(sp, atab[:], atab_d)
        atok = asl.dma(sp, augtab[:], aug_d)
        qz = None
        for u in range(2):
            for v in range(3):
                for m in range(2):
                    qz = pool.mark(pool.e.memset(Qs[u][v][m][:], 0.0))
        sps = [ps(f"s{i}", [128, 2, 512], F32) for i in range(NSP)]
        acc = [ps(f"acc{i}", [128, 512], F32) for i in range(3)]
        tpo = ps("tpo", [128, 128], BF16)
        tmp = [sb(f"tmp{i}", [128, 2, 512], F32) for i in range(NTM)]
        E = [sb(f"E{i}", [128, 2, 512], BF16) for i in range(NE)]
        accs = sb("accs", [128, 3, 387], F32)
        rr = sb("rr", [128, 4], F32)
        tq = sb("tq", [128, 128], F32)
        oq = sb("oq", [128, 128], F32)
        sps_free = [None] * NSP
        tmp_free = [None] * NTM
        E_free = [None] * NE
        acc_free = [None] * 3
        accs_free = None
        tpo_free = None
        ns = 0
        unit_last_S = {}
        units = [(h_, qb_) for h_ in range(4) for qb_ in range(4)]
        yq = [[sb(f"yq{u_}{q_}", [128, 128], BF16) for q_ in range(4)] for u_ in range(2)]
        yq_free = {}
        yq_tok = {}
        qtoks = {}

        def build_Qs(ui):
            h_, qb_ = units[ui]
            u_ = ui % 2
            pool.wait(qz, atok, unit_last_S.get(ui - 2), ctok)
            for m in range(2):
                r0 = 64 * m
                a0 = 64 - 64 * m
                for v in range(3):
                    qtok = pool.mark(pool.e.tensor_copy(out=Qs[u_][v][m][r0:r0 + 64, :], in_=QT[r0:r0 + 64, h_, qb_ * 512:(qb_ + 1) * 512]))
                for v in range(2):
                    qtok = pool.mark(pool.e.tensor_copy(out=Qs[u_][v][m][a0:a0 + 2, :], in_=augtab[a0:a0 + 2, h_, v, :]))
            qtoks[ui] = qtok

        def finish_transposes(ui):
            nonlocal tpo_free
            h_, qb_ = units[ui]
            for qt in range(4):
                pe.wait(yq_tok[(ui, qt)], tpo_free)
                tt = pe.mark(pe.e.transpose(out=tpo[:], in_=yq[ui % 2][qt][:], identity=k.ident[:]))
                yq_free[(ui % 2, qt)] = tt
                act.wait(tt)
                tok0 = (qb_ * 4 + qt) * 128
                tpo_free = act.mark(act.e.activation(out=mixT[:, 4 + h_, tok0:tok0 + 128], in_=tpo[:], func=AF.Copy))

        evts = {}

        def epilogue(ui):
            nonlocal accs_free
            u = ui % 2
            evt = evts[ui]
            for qt in range(4):
                g0, g1 = qt, 4 + qt
                O0 = accs[:, g0 // 3, (g0 % 3) * 129:(g0 % 3) * 129 + 129]
                O1 = accs[:, g1 // 3, (g1 % 3) * 129:(g1 % 3) * 129 + 129]
                dve.wait(evt, lam_tok)
                e1 = dve.mark(dve.e.reciprocal(out=rr[:, 0:1], in_=O0[:, 128:129]))
                e2 = dve.mark(dve.e.reciprocal(out=rr[:, 1:2], in_=O1[:, 128:129]))
                dve.wait(e1, e2)
                e3 = dve.mark(dve.e.tensor_tensor(out=rr[:, 2:3], in0=rr[:, 1:2], in1=neglam, op=ALU.mult))
                dve.wait(e3)
                e4 = dve.mark(dve.e.tensor_scalar(out=tq[:], in0=O1[:, 0:128], scalar1=rr[:, 2:3], scalar2=None, op0=ALU.mult))
                dve.wait(e4)
                e5 = dve.mark(dve.e.scalar_tensor_tensor(out=oq[:], in0=O0[:, 0:128], scalar=rr[:, 0:1], op0=ALU.mult, in1=tq[:], op1=ALU.add))
                dve.wait(e5)
                e6 = dve.mark(dve.e.scalar_tensor_tensor(out=tq[:], in0=oq[:], scalar=1.0 / 128.0, op0=ALU.mult, in1=oq[:], op1=ALU.mult, accum_out=rr[:, 3:4]))
                dve.wait(e6)
                e7 = dve.mark(dve.e.tensor_scalar(out=rr[:, 3:4], in0=rr[:, 3:4], scalar1=EPS, scalar2=None, op0=ALU.add))
                pool.wait(e7)
                e8 = pool.mark(pool.e.tensor_tensor(out=rr[:, 3:4], in0=rr[:, 3:4], in1=mhalf[:], op=ALU.pow))
                dve.wait(e8, yq_free.get((u, qt)))
                e9 = dve.mark(dve.e.scalar_tensor_tensor(out=yq[u][qt][:], in0=oq[:], scalar=rr[:, 3:4], op0=ALU.mult, in1=g8[:], op1=ALU.mult))
                yq_tok[(ui, qt)] = e9
                if qt == 3:
                    accs_free = e9

        build_Qs(0)
        pending = None
        pend_epi = None
        tr_at = -1
        for ui, (h, qb) in enumerate(units):
            if True:
                u = ui % 2
                for j in range(3):
                    pe.wait(acc_free[j], tz)
                    pe.e.matmul(acc[j][:], lhsT=zer[:, 0:128], rhs=zer[:], start=True, stop=False)
                E_tok = {}

                def emit_S(kt):
                    nonlocal ns
                    g = ns; ns += 1
                    delta = qb * 512 - kt * 128
                    v = 0 if delta >= 128 else (1 if delta <= -512 else 2)
                    pe.wait(sps_free[g % NSP], qtoks[ui], pad_tok)
                    for m in range(2):
                        ins = pe.e.matmul(sps[g % NSP][:, m, :], lhsT=KT[m][:, h, kt * 128:(kt + 1) * 128],
                                          rhs=Qs[u][v][m][:], start=True, stop=True)
                    tk = pe.mark(ins)
                    unit_last_S[ui] = tk
                    if v == 2:
                        dve.wait(tk, tmp_free[g % NTM], atok)
                        t1 = dve.mark(dve.e.scalar_tensor_tensor(out=tmp[g % NTM][:], in0=atab[:, :, delta + 384:delta + 384 + 512], scalar=float(-SL[h]),
                                                                 op0=ALU.mult, in1=sps[g % NSP][:], op1=ALU.add))
                        sps_free[g % NSP] = t1
                        act.wait(t1, E_free[g % NE])
                        t2 = act.mark(act.e.activation(out=E[g % NE][:], in_=tmp[g % NTM][:], func=AF.Exp))
                        tmp_free[g % NTM] = t2
                    else:
                        n = abs(delta) // 128
                        col = (h * 32 + n) * 2 + v
                        act.wait(tk, E_free[g % NE], ctok)
                        t2 = act.mark(act.e.activation(out=E[g % NE][:], in_=sps[g % NSP][:], func=AF.Exp, bias=cst[:, col:col + 1], scale=1.0))
                        sps_free[g % NSP] = t2
                    E_tok[kt] = (t2, g)

                def emit_AV(kt):
                    t2, g = E_tok[kt]
                    pe.wait(t2, tv1)
                    for m in range(2):
                        for qt in range(4):
                            gi = m * 4 + qt
                            ins = pe.e.matmul(acc[gi // 3][:, (gi % 3) * 129:(gi % 3) * 129 + 129], lhsT=E[g % NE][:, m, qt * 128:(qt + 1) * 128],
                                              rhs=VA[:, kt, h, :], start=False, stop=(kt == 31 and gi in (2, 5, 7)))
                    E_free[g % NE] = pe.mark(ins)
                    return E_free[g % NE]

                emit_S(0)
                emit_S(1)
                for kt in range(32):
                    if kt + 2 < 32:
                        emit_S(kt + 2)
                    last = emit_AV(kt)
                    if kt == 6 and ui + 1 < len(units):
                        build_Qs(ui + 1)
                    if pend_epi is not None and kt == min(4 * qb + 2, 14):
                        epilogue(pend_epi)
                        pending = pend_epi
                        pend_epi = None
                        tr_at = kt + 12
                    if pending is not None and pend_epi is None and kt == tr_at:
                        finish_transposes(pending)
                        pending = None
                evt = []
                for j in range(3):
                    act.wait(last, accs_free)
                    acc_free[j] = act.mark(act.e.activation(out=accs[:, j, :], in_=acc[j][:, 0:387], func=AF.Copy))
                    evt.append(acc_free[j])
                evts[ui] = evt
                pend_epi = ui
        epilogue(pend_epi)
        finish_transposes(pend_epi)
        barrier(k, [])


def wout_phase(k, x1_d, wout_d, g_d, b_d, mixT, x2_d):
    nc = k.nc
    pe, act, dve, pool, sp = k.pe, k.act, k.dve, k.pool, k.sp
    with ExitStack() as es:
        sb = lambda n, shape, dt: es.enter_context(nc.sbuf_tensor(f"wo_{n}", shape, dt))
        wo = sb("wo", [128, 8, D], BF16)
        v = wout_d.rearrange("(c p) f -> p c f", p=128)
        wtoks = load_bf16_weights(k, sp, k.slots(es, 1)[0], [(wo[:, c, :], v[:, c, :]) for c in range(8)])
        NW = 4
        ctx = ln_ctx(k, es, "wo", g_d, b_d, nsl=NW)
        xR = [sb(f"xR{i}", [128, D], F32) for i in range(NW)]
        xsl = k.slots(es, NW)
        yp = [[es.enter_context(nc.psum_tensor(f"wo_y{t}{h}", [128, 512], F32)) for h in range(2)] for t in range(NW)]
        xR_free = [None] * NW
        yp_free = [None] * NW
        lts = {}

        def issue_load(t_):
            sl_ = t_ % NW
            sp.wait(xR_free[sl_])
            lts[t_] = xsl[sl_].dma(sp, xR[sl_][:], x1_d[t_ * 128:(t_ + 1) * 128, :])

        for t_ in range(NW - 1):
            issue_load(t_)
        for t in range(16):
            s_ = t % NW
            if t + NW - 1 < 16:
                issue_load(t + NW - 1)
            lt = lts[t]
            pe.wait(wtoks, yp_free[s_])
            for hh in range(2):
                for c in range(8):
                    ins = pe.e.matmul(yp[s_][hh][:], lhsT=mixT[:, c, t * 128:(t + 1) * 128], rhs=wo[:, c, hh * 512:(hh + 1) * 512], start=(c == 0), stop=(c == 7))
            tk = pe.mark(ins)
            stt = ln_part_a(k, ctx, t, [yp[s_][0][:], yp[s_][1][:]], xR[s_], ALPHA, EPS, x2_d[t * 128:(t + 1) * 128, :], [tk, lt])
            xR_free[s_] = stt
            yp_free[s_] = stt
            if t >= 1:
                ln_part_b(k, ctx, t - 1)
        ln_part_b(k, ctx, 15)
        barrier(k, [ctx["store_tok"]])


def build_program(stop=None):
    nc = bass.Bass("TRN2", target_bir_lowering=False)
    dram_in = lambda n, shape, dt=F32: nc.dram_tensor(n, shape, dt, kind="ExternalInput").ap()
    x = dram_in("x", [SEQ, D])
    wg1 = dram_in("wg1", [D, DFF]); wu1 = dram_in("wu1", [D, DFF]); wd1 = dram_in("wd1", [DFF, D])
    wg2 = dram_in("wg2", [D, DFF]); wu2 = dram_in("wu2", [D, DFF]); wd2 = dram_in("wd2", [DFF, D])
    win = dram_in("win", [D, 3072]); wout = dram_in("wout", [D, D])
    ln1g = dram_in("ln1g", [128, D]); ln1b = dram_in("ln1b", [128, D])
    ln2g = dram_in("ln2g", [128, D]); ln2b = dram_in("ln2b", [128, D])
    ln3g = dram_in("ln3g", [128, D]); ln3b = dram_in("ln3b", [128, D])
    ident_d = dram_in("ident", [128, 128], BF16)
    nab = dram_in("nab", [13, 128, 1024])
    augt = dram_in("augt", [128, 4, 2, 512], BF16); atab = dram_in("atab", [128, 2, 896]); cst = dram_in("cst", [128, 256])
    lamv = dram_in("lamv", [128, 4, 64]); subg = dram_in("subg", [128, 128])
    zeros_ones = dram_in("zeros_ones", [2, 64, 4 * SEQ], BF16)
    out = nc.dram_tensor("out", [OWN, D], F32, kind="ExternalOutput").ap()
    x1_d = nc.dram_tensor("x1_scratch", [SEQ, D], F32, kind="Internal").ap()
    x2_d = nc.dram_tensor("x2_scratch", [OWN, D], F32, kind="Internal").ap()
    win_bf = nc.dram_tensor("win_bf", [D, 3072], BF16, kind="Internal").ap()
    wout_bf = nc.dram_tensor("wout_bf", [D, D], BF16, kind="Internal").ap()
    wg2_bf = nc.dram_tensor("wg2_bf", [D, DFF], BF16, kind="Internal").ap()
    wu2_bf = nc.dram_tensor("wu2_bf", [D, DFF], BF16, kind="Internal").ap()
    wd2_bf = nc.dram_tensor("wd2_bf", [DFF, D], BF16, kind="Internal").ap()

    def pieces(src, dst, width):
        sv = src.rearrange("(c p) f -> p c f", p=128)
        dv = dst.rearrange("(c p) f -> p c f", p=128)
        out_ = []
        for c in range(sv.shape[1]):
            for o in range(0, sv.shape[2], width):
                out_.append((sv[:, c, o:o + width], dv[:, c, o:o + width]))
        return out_
    bg_jobs = (pieces(win, win_bf, 1024) + pieces(wout, wout_bf, 1024) + pieces(wg2, wg2_bf, 1408)
               + pieces(wu2, wu2_bf, 1408) + pieces(wd2, wd2_bf, 1024))
    dbg = None
    if stop is not None:
        dbg = nc.dram_tensor("dbg", [SEQ, D], F32, kind="ExternalOutput").ap()
    with ExitStack() as es:
        k = K()
        k.nc = nc
        k.pe = Eng(nc, nc.tensor, "pe", es)
        k.act = Eng(nc, nc.scalar, "act", es)
        k.dve = Eng(nc, nc.vector, "dve", es)
        k.pool = Eng(nc, nc.gpsimd, "pool", es)
        k.sp = Eng(nc, nc.sync, "sp", es)
        k.engs = [k.pe, k.act, k.dve, k.pool, k.sp]
        k.es_global = es
        k.slot_pool = []
        k.zeros_ones = zeros_ones
        k.ident = es.enter_context(nc.sbuf_tensor("ident_sb", [128, 128], BF16))
        isl = k.slots(es, 1)[0]
        k.ident_tok = isl.dma(k.sp, k.ident[:], ident_d)
        if stop == "A":
            ffn_phase(k, "f1", x, SEQ, wg1, wu1, wd1, ln1g, ln1b, dbg, bg_jobs=bg_jobs)
            return nc
        if stop in ("NA", "DF", "W"):
            x1_src = x
            with ExitStack() as esb:
                inb = [esb.enter_context(nc.sbuf_tensor(f"dbg_in{i}", [128, 1408], F32)) for i in range(2)]
                bgc = BgCast(k, esb, "dbgc", bg_jobs[:32], inb, None)
                barrier(k, [bgc.finish()])
        else:
            ffn_phase(k, "f1", x, SEQ, wg1, wu1, wd1, ln1g, ln1b, x1_d, bg_jobs=bg_jobs)
            x1_src = x1_d
        mix_cm = nc.sbuf_tensor("mixT", [128, 8, OWN], BF16, side="right")
        mixT = mix_cm.__enter__()
        if stop != "DF":
            na_phase(k, x1_src, win_bf, nab, mixT)
        if stop != "NA":
            diff_phase(k, x1_src, win_bf, augt, atab, cst, lamv, subg, mixT)
        if stop in ("NA", "DF"):
            with ExitStack() as es3:
                tmpf = es3.enter_context(nc.sbuf_tensor("dbg_tmp", [128, 8, OWN], F32))
                k.dve.wait((k.act, k.act.n))
                c0_ = 0 if stop == "NA" else 4
                k.pool.wait((k.act, k.act.n))
                tk0 = k.pool.mark(k.pool.e.memset(tmpf[:], 0.0))
                k.dve.wait(tk0)
                tk = k.dve.mark(k.dve.e.tensor_copy(out=tmpf[:, c0_:c0_ + 4, :], in_=mixT[:, c0_:c0_ + 4, :]))
                k.sp.wait(tk)
                sl = k.slots(es3, 1)[0]
                for c in range(8):
                    for hf in range(2):
                        t_ = sl.dma(k.sp, dbg[(c * 2 + hf) * 128:(c * 2 + hf + 1) * 128, :], tmpf[:, c, hf * 1024:(hf + 1) * 1024])
                barrier(k, [t_])
            mix_cm.__exit__(None, None, None)
            return nc
        with ExitStack() as esf2:
            pre = None
            if stop is None:
                wg2s, wu2s, _ = alloc_ffn_weights(k, esf2, "f2", with_wd=False)
                wdA2 = esf2.enter_context(nc.sbuf_tensor("f2_wdA", [128, NFC // 2, D], BF16))
                wsl = k.slots(esf2, 1)[0]
                jobs2 = ([(wg2s[:, c, :], wg2_bf.rearrange("(c p) f -> p c f", p=128)[:, c, :]) for c in range(8)]
                         + [(wu2s[:, c, :], wu2_bf.rearrange("(c p) f -> p c f", p=128)[:, c, :]) for c in range(8)]
                         + [(wdA2[:, c:c + 1, :], wd2_bf.rearrange("(c p) f -> p c f", p=128)[:, c:c + 1, :]) for c in range(NFC // 2)])
                pre = (wg2s, wu2s, wdA2, wd2_bf, load_bf16_weights(k, k.act, wsl, jobs2))
            wout_phase(k, x1_src, wout_bf, ln2g, ln2b, mixT, x2_d if stop is None else dbg)
            mix_cm.__exit__(None, None, None)
            if stop == "W":
                return nc
            ffn_phase(k, "f2", x2_d, OWN, wg2, wu2, wd2, ln3g, ln3b, out, pre=pre)
    return nc


def _na_tables(rpb, rev):
    out = np.full((13, 128, 8, 128), -30000.0, np.float32)
    p = np.arange(128)

    def coords(tile):
        t = tile * 128 + p
        r, c = t // 64, t % 64
        if rev:
            r, c = 63 - r, 63 - c
        return r, c

    def fill(v, il, kt):
        rk, ck = coords(kt)
        rq, cq = coords(il)
        r0 = np.clip(rq - 4, 0, 56)
        c0 = np.clip(cq - 8, 0, 48)
        RK, RQ = rk[:, None], rq[None, :]
        CK, CQ = ck[:, None], cq[None, :]
        ok = (RK >= r0[None, :]) & (RK <= r0[None, :] + 7) & (CK >= c0[None, :]) & (CK <= c0[None, :] + 15)
        dr = np.clip(RK - RQ + 7, 0, 14)
        dc = np.clip(CK - CQ + 15, 0, 30)
        vals = rpb[:, dr, dc]
        tile = np.where(ok[None], vals, np.float32(-30000.0)).astype(np.float32)
        out[v] = tile.transpose(1, 0, 2)[:, [0, 2, 4, 6, 1, 3, 5, 7], :]
        return ok

    for il in range(2):
        for kt in range(4):
            fill(na_variant(il, kt), il, kt)
    for dj in range(-2, 3):
        fill(na_variant(8, 8 + dj), 8, 8 + dj)
    return np.ascontiguousarray(out.reshape(13, 128, 1024))


def prep_inputs(inputs, c):
    b, h = c // 2, c % 2
    xb = np.ascontiguousarray(inputs["x"][b])
    if h == 1:
        xb = np.ascontiguousarray(xb[::-1])
    f32 = lambda v: np.ascontiguousarray(np.asarray(v, np.float32))
    rep = lambda v: np.ascontiguousarray(np.broadcast_to(np.asarray(v, np.float32).reshape(1, -1), (128, np.asarray(v).size)))
    p = np.arange(128, dtype=np.float32)[:, None]
    jtab = (np.arange(512, dtype=np.float32)[None, :] - p).astype(np.float32)
    atab = np.abs(np.arange(896, dtype=np.float32)[None, :] - p - 384.0).astype(np.float32)
    atab = np.ascontiguousarray(np.stack([atab, atab], axis=1))
    cst = np.zeros((128, 4, 32, 2), np.float32)
    augt = np.zeros((128, 4, 2, 512), np.float32)
    jj = np.arange(512, dtype=np.float32)
    pp_ = np.arange(128, dtype=np.float32)
    for hh in range(4):
        sl = 2.0 ** (-8.0 * (hh + 1) / 4)
        for v in range(2):
            sgn = 1.0 if v == 0 else -1.0
            cst[:, hh, :, v] = sgn * sl * pp_[:, None] - sl * 128.0 * np.arange(32, dtype=np.float32)[None, :]
            hi = -sgn * sl * 256.0 * np.floor(jj / 256.0)
            lo = -sgn * sl * np.mod(jj, 256.0)
            for base in (0, 64):
                augt[base, hh, v] = hi
                augt[base + 1, hh, v] = lo
    cst = np.ascontiguousarray(cst.reshape(128, 256))
    augt = augt.astype(ml_dtypes.bfloat16)
    lamv = np.stack([rep(inputs["diff_lambda_q1"][0]), rep(inputs["diff_lambda_k1"][0]),
                     rep(inputs["diff_lambda_q2"][0]), rep(inputs["diff_lambda_k2"][0])], axis=1)
    m = {
        "x": xb,
        "wg1": f32(inputs["ffn1_w_gate"][0]), "wu1": f32(inputs["ffn1_w_up"][0]), "wd1": f32(inputs["ffn1_w_down"][0]),
        "wg2": f32(inputs["ffn2_w_gate"][0]), "wu2": f32(inputs["ffn2_w_up"][0]), "wd2": f32(inputs["ffn2_w_down"][0]),
        "win": f32(inputs["w_in"][0]), "wout": f32(inputs["w_out"][0]),
        "ln1g": rep(inputs["ln1_g"][0]), "ln1b": rep(inputs["ln1_b"][0]),
        "ln2g": rep(inputs["ln2_g"][0]), "ln2b": rep(inputs["ln2_b"][0]),
        "ln3g": rep(inputs["ln3_g"][0]), "ln3b": rep(inputs["ln3_b"][0]),
        "ident": np.eye(128, dtype=np.float32).astype(ml_dtypes.bfloat16),
        "nab": _na_tables(f32(inputs["na_rpb"][0]), h == 1),
        "augt": augt, "atab": atab, "cst": cst,
        "lamv": np.ascontiguousarray(lamv.astype(np.float32)), "subg": rep(inputs["diff_subln_g"][0]),
        "zeros_ones": np.stack([np.zeros((64, 4 * SEQ), np.float32), np.ones((64, 4 * SEQ), np.float32)]).astype(ml_dtypes.bfloat16),
    }
    return m


def kernel(**inputs):
    inputs = {k_: np.asarray(v) for k_, v in inputs.items()}
    nc = build_program()
    in_maps = [prep_inputs(inputs, c) for c in range(8)]
    res = run_bass_kernel_spmd(nc, in_maps, core_ids=list(range(8)))
    outp = np.empty((4, SEQ, D), np.float32)
    for c in range(8):
        b, h = c // 2, c % 2
        o = np.asarray(res.results[c]["out"])
        if h == 0:
            outp[b, :OWN] = o
        else:
            outp[b, OWN:] = o[::-1]
    return outp
```

```python
import numpy as np
from contextlib import ExitStack
import concourse.bass as bass
import concourse.mybir as mybir
from concourse.bass_utils import run_bass_kernel_spmd
import ml_dtypes

F32, BF16 = mybir.dt.float32, mybir.dt.bfloat16
AF = mybir.ActivationFunctionType
ALU = mybir.AluOpType

D = 1024
DFF = 2816
NFC = DFF // 128
SEQ = 4096
OWN = 2048
ALPHA = 2.0 ** 0.25
EPS = 1e-5
LAM_INIT = 0.2
NKT_NA = 18


def _flat(toks):
    out = []
    for t in toks:
        if t is None:
            continue
        if isinstance(t, list):
            out.extend(_flat(t))
        else:
            out.append(t)
    return out


class Eng:
    def __init__(self, nc, e, name, es):
        self.e = e
        self.name = name
        self.sem = es.enter_context(nc.semaphore("sem_" + name))
        self.n = 0
        self.seen = {}

    def wait(self, *toks):
        best = {}
        for src, v in _flat(list(toks)):
            if best.get(id(src), (None, 0))[1] < v:
                best[id(src)] = (src, v)
        for src, v in best.values():
            if self.seen.get(id(src), 0) >= v:
                continue
            self.e.wait_ge(src.sem, v)
            self.seen[id(src)] = v

    def mark(self, ins):
        ins.then_inc(self.sem, 1)
        self.n += 1
        return (self, self.n)


class Slot:
    def __init__(self, nc, name, es):
        self.sem = es.enter_context(nc.semaphore("dsem_" + name))
        self.n = 0
        self.busy = False

    def dma(self, q, out, in_):
        q.e.dma_start(out=out, in_=in_).then_inc(self.sem, 16)
        self.n += 16
        return (self, self.n)


class K:
    def slots(self, es, n):
        got = []
        for sl in self.slot_pool:
            if not sl.busy and len(got) < n:
                sl.busy = True
                got.append(sl)
        while len(got) < n:
            sl = Slot(self.nc, f"p{len(self.slot_pool)}", self.es_global)
            sl.busy = True
            self.slot_pool.append(sl)
            got.append(sl)

        def release():
            for sl in got:
                sl.busy = False
        es.callback(release)
        return got


def barrier(k, toks):
    toks = _flat(toks) + [(e, e.n) for e in k.engs if e.n > 0]
    for e in k.engs:
        e.wait(toks)


def copy_cast(eng, k, out, in_):
    if eng is k.act:
        return eng.e.activation(out=out, in_=in_, func=AF.Copy)
    return eng.e.tensor_copy(out=out, in_=in_)


class WeightLoader:
    def __init__(self, k, es, name, jobs, nslots=3, width=1408):
        nc = k.nc
        self.k = k
        self.jobs = jobs
        self.stg = [es.enter_context(nc.sbuf_tensor(f"{name}_stg{i}", [128, width], F32)) for i in range(nslots)]
        self.slots = k.slots(es, nslots)
        self.cast_tok = [None] * nslots
        self.engs = [k.dve, k.pool, k.act]
        self.toks = []
        self.i = 0
        k.last_stg = self.stg

    def emit(self, n):
        k = self.k
        nslots = len(self.stg)
        for _ in range(n):
            if self.i >= len(self.jobs):
                return
            i = self.i
            self.i += 1
            dst, src = self.jobs[i]
            s = i % nslots
            if len(src.shape) == 3:
                nel = src.shape[1] * src.shape[2]
                sv = self.stg[s][:, :nel].rearrange("p (a b) -> p a b", b=src.shape[2])
            else:
                nel = src.shape[-1]
                sv = self.stg[s][:, :nel]
            k.sp.wait(self.cast_tok[s])
            lt = self.slots[s].dma(k.sp, sv, src)
            e = self.engs[i % 3]
            e.wait(lt)
            self.cast_tok[s] = e.mark(copy_cast(e, k, dst, sv))
            self.toks.append(self.cast_tok[s])

    def done(self):
        return self.i >= len(self.jobs)


def load_cast_weights(k, es, name, jobs, nslots=3, width=1408):
    wl = WeightLoader(k, es, name, jobs, nslots, width)
    wl.emit(len(jobs))
    return wl.toks


def load_bf16_weights(k, q, slot, jobs):
    tok = None
    for dst, src in jobs:
        tok = slot.dma(q, dst, src)
    return tok


class BgCast:
    def __init__(self, k, es, name, jobs, in_bufs, first_tok):
        nc = k.nc
        self.k = k
        self.jobs = jobs
        self.inb = in_bufs
        self.outb = [es.enter_context(nc.sbuf_tensor(f"{name}_bgo{i}", [128, 1408], BF16)) for i in range(2)]
        self.in_slot = k.slots(es, 2)
        self.out_slot = k.slots(es, 2)
        self.load_tok = [first_tok, first_tok]
        self.cast_tok = [None, None]
        self.store_tok = [None, None]
        self.i = 0

    def step(self):
        k = self.k
        i = self.i
        n_jobs = len(self.jobs)
        if i > n_jobs + 1:
            return
        self.i += 1
        if 0 <= i - 2 < n_jobs:
            j = i - 2
            n = self.jobs[j][0].shape[-1]
            k.act.wait(self.cast_tok[j % 2])
            self.store_tok[j % 2] = self.out_slot[j % 2].dma(k.act, self.jobs[j][1], self.outb[j % 2][:, :n])
        if i < n_jobs:
            n = self.jobs[i][0].shape[-1]
            k.act.wait(self.cast_tok[i % 2], self.load_tok[i % 2] if i < 2 else None)
            self.load_tok[i % 2] = self.in_slot[i % 2].dma(k.act, self.inb[i % 2][:, :n], self.jobs[i][0])
        if 0 <= i - 1 < n_jobs:
            j = i - 1
            n = self.jobs[j][0].shape[-1]
            k.act.wait(self.load_tok[j % 2], self.store_tok[j % 2])
            self.cast_tok[j % 2] = k.act.mark(k.act.e.activation(out=self.outb[j % 2][:, :n], in_=self.inb[j % 2][:, :n], func=AF.Copy))

    def finish(self):
        while self.i <= len(self.jobs) + 1:
            self.step()
        return [t for t in self.store_tok if t is not None]


def ln_part_a(k, ctx, t, psum_halves, xres, xscale, eps, dst_ap, pre_toks):
    nsl = ctx["n"]
    dve, pool, sp = k.dve, k.pool, k.sp
    s = t % nsl
    r = ctx["r"][s]
    stats, mv, ve, rstd = ctx["stats"][s], ctx["mv"][s], ctx["ve"][s], ctx["rstd"][s]
    stt_toks = []
    for hh in range(2):
        dve.wait(pre_toks, ctx["store_tok"][s])
        ins = dve.e.scalar_tensor_tensor(out=r[:, hh * 512:(hh + 1) * 512], in0=xres[:, hh * 512:(hh + 1) * 512],
                                         scalar=float(xscale), op0=ALU.mult, in1=psum_halves[hh], op1=ALU.add)
        stt_toks.append(dve.mark(ins))
    st_toks = []
    for hh in range(2):
        dve.wait(stt_toks[hh])
        st_toks.append(dve.mark(dve.e.bn_stats(out=stats[:, hh * 6:(hh + 1) * 6], in_=r[:, hh * 512:(hh + 1) * 512])))
    dve.wait(st_toks)
    t1 = dve.mark(dve.e.bn_aggr(out=mv[:], in_=stats[:]))
    dve.wait(t1)
    t2 = dve.mark(dve.e.tensor_scalar(out=ve[:], in0=mv[:, 1:2], scalar1=float(eps), scalar2=None, op0=ALU.add))
    pool.wait(t2, ctx["gb_tok"])
    t3 = pool.mark(pool.e.tensor_tensor(out=rstd[:], in0=ve[:], in1=ctx["mhalf"][:], op=ALU.pow))
    dve.wait(t1, ctx["gb_tok"])
    t4 = dve.mark(dve.e.scalar_tensor_tensor(out=r[:], in0=r[:], scalar=mv[:, 0:1], op0=ALU.subtract, in1=ctx["g"][:], op1=ALU.mult))
    ctx["pend"][t] = (t3, t4, dst_ap)
    return stt_toks


def ln_part_b(k, ctx, t):
    dve, sp = k.dve, k.sp
    s = t % ctx["n"]
    r = ctx["r"][s]
    t3, t4, dst_ap = ctx["pend"].pop(t)
    dve.wait(t3, t4)
    t6 = dve.mark(dve.e.scalar_tensor_tensor(out=r[:], in0=r[:], scalar=ctx["rstd"][s][:, 0:1], op0=ALU.mult, in1=ctx["b"][:], op1=ALU.add))
    sp.wait(t6)
    ctx["store_tok"][s] = ctx["store_slot"][s].dma(sp, dst_ap, r[:])


def ln_ctx(k, es, name, g_d, b_d, nsl=2):
    nc = k.nc
    sb = lambda n, shape, dt: es.enter_context(nc.sbuf_tensor(f"{name}_{n}", shape, dt))
    ctx = {
        "n": nsl,
        "pend": {},
        "r": [sb(f"r{i}", [128, D], F32) for i in range(nsl)],
        "stats": [sb(f"stats{i}", [128, 12], F32) for i in range(nsl)],
        "mv": [sb(f"mv{i}", [128, 2], F32) for i in range(nsl)],
        "ve": [sb(f"ve{i}", [128, 1], F32) for i in range(nsl)],
        "rstd": [sb(f"rstd{i}", [128, 1], F32) for i in range(nsl)],
        "mhalf": sb("mhalf", [128, 1], F32),
        "g": sb("g", [128, D], F32),
        "b": sb("b", [128, D], F32),
        "store_slot": k.slots(es, nsl),
        "store_tok": [None] * nsl,
    }
    gs = k.slots(es, 1)[0]
    gs.dma(k.sp, ctx["g"][:], g_d)
    tg = gs.dma(k.sp, ctx["b"][:], b_d)
    tm = k.pool.mark(k.pool.e.memset(ctx["mhalf"][:], -0.5))
    ctx["gb_tok"] = [tg, tm]
    return ctx


def alloc_ffn_weights(k, es, name, with_wd=True):
    nc = k.nc
    wg = es.enter_context(nc.sbuf_tensor(f"{name}_wg", [128, 8, DFF], BF16))
    wu = es.enter_context(nc.sbuf_tensor(f"{name}_wu", [128, 8, DFF], BF16))
    wd = es.enter_context(nc.sbuf_tensor(f"{name}_wd", [128, NFC, D], BF16)) if with_wd else None
    return wg, wu, wd


def ffn_phase(k, name, x_src, T, wg_d, wu_d, wd_d, g_d, b_d, dst, pre=None, bg_jobs=None):
    nc = k.nc
    pe, act, dve, pool, sp = k.pe, k.act, k.dve, k.pool, k.sp
    NB = T // 256
    NH = 4
    with ExitStack() as es:
        sb = lambda n, shape, dt: es.enter_context(nc.sbuf_tensor(f"{name}_{n}", shape, dt))
        ps = lambda n, shape, dt: es.enter_context(nc.psum_tensor(f"{name}_{n}", shape, dt))
        bg = None
        emit_weights = None
        wl = None
        if pre is not None:
            wg, wu, wdA, wd_bf_d, wtoks = pre
            wdB = es.enter_context(nc.sbuf_tensor(f"{name}_wdB", [128, NFC // 2, D], BF16))
            wdv = wd_bf_d.rearrange("(c p) f -> p c f", p=128)
            wdB_tok = load_bf16_weights(k, k.act, k.slots(es, 1)[0],
                                        [(wdB[:, c:c + 1, :], wdv[:, NFC // 2 + c:NFC // 2 + c + 1, :]) for c in range(NFC // 2)])
            wd_ap = lambda fd, lo, hi: (wdA if fd < NFC // 2 else wdB)[:, fd % (NFC // 2), lo:hi]
            gu_wtok = {f: wtoks for f in range(NFC)}
            d_wtok = {f: None for f in range(NFC)}
        else:
            wg, wu, wd = alloc_ffn_weights(k, es, name)
            wgv = wg_d.rearrange("(c p) f -> p c f", p=128)
            wuv = wu_d.rearrange("(c p) f -> p c f", p=128)
            wdv = wd_d.rearrange("(c p) d -> p c d", p=128)
            jobs = []
            for fg in range(NFC // 2):
                cs = slice(fg * 256, (fg + 1) * 256)
                for c0 in (0, 4):
                    jobs.append((wg[:, c0:c0 + 4, cs], wgv[:, c0:c0 + 4, cs]))
                    jobs.append((wu[:, c0:c0 + 4, cs], wuv[:, c0:c0 + 4, cs]))
                for c in (2 * fg, 2 * fg + 1):
                    jobs.append((wd[:, c, :], wdv[:, c, :]))
            wdB_tok = None
            wd_ap = lambda fd, lo, hi: wd[:, fd, lo:hi]
            gu_wtok, d_wtok = {}, {}

            wl = WeightLoader(k, es, name, jobs)
            for f in range(NFC):
                gu_wtok[f] = ("wl", (f // 2) * 6, (f // 2) * 6 + 4)
                d_wtok[f] = ("wl", (f // 2) * 6 + 4 + (f % 2), (f // 2) * 6 + 5 + (f % 2))

            def emit_weights():
                wl.emit(12)
        ctx = ln_ctx(k, es, name, g_d, b_d)

        xA = [sb(f"xA{i}", [128, D], F32) for i in range(2)]
        xA_slot = k.slots(es, 2)
        xR = [sb(f"xR{i}", [128, D], F32) for i in range(2)]
        xR_slot = k.slots(es, 2)
        xbf = [sb(f"xbf{i}", [128, D], BF16) for i in range(2)]
        xT = [sb(f"xT{i}", [128, 8, 256], BF16) for i in range(2)]
        hT = [sb(f"hT{i}", [128, 256], BF16) for i in range(NH)]
        sg = [sb(f"sg{i}", [128, 256], F32) for i in range(2)]
        ident = k.ident
        Tps = [ps(f"T{i}", [128, 8, 128], BF16) for i in range(2)]
        gu = [ps(f"gu{i}", [128, 2, 256], F32) for i in range(2)]
        yp = [[ps(f"y{t}{h}", [128, 512], F32) for h in range(2)] for t in range(2)]

        cast_tok = [None, None]
        T_tok = [None, None]
        XE_tok = {}
        xR_free = [None, None]
        xR_tok = [None, None]
        gu_last = {}
        mult_tok = {}
        D_tok = {}
        ep_tok = {}

        def stage_load_cast(b):
            for t in range(2):
                sp.wait(cast_tok[t])
                lt = xA_slot[t].dma(sp, xA[t][:], x_src[(b * 2 + t) * 128:(b * 2 + t + 1) * 128, :])
                pool.wait(lt, T_tok[t])
                cast_tok[t] = pool.mark(pool.e.tensor_copy(out=xbf[t][:], in_=xA[t][:]))

        def stage_T(b):
            for t in range(2):
                prev = XE_tok.get((b - 1, t))
                pe.wait(cast_tok[t], prev, k.ident_tok)
                for c in range(8):
                    ins = pe.e.transpose(out=Tps[t][:, c, :], in_=xbf[t][:, c * 128:(c + 1) * 128], identity=ident[:])
                T_tok[t] = pe.mark(ins)

        def stage_XE(b):
            for t in range(2):
                act.wait(T_tok[t], gu_last.get(b - 2))
                XE_tok[(b, t)] = act.mark(act.e.activation(out=xT[b % 2][:, :, t * 128:(t + 1) * 128], in_=Tps[t][:], func=AF.Copy))

        def stage_xR(b):
            for t in range(2):
                sp.wait(xR_free[t])
                xR_tok[t] = xR_slot[t].dma(sp, xR[t][:], x_src[(b * 2 + t) * 128:(b * 2 + t + 1) * 128, :])

        stage_load_cast(0)
        if emit_weights is not None:
            emit_weights()
        stage_T(0)
        stage_XE(0)
        def wres(t):
            if isinstance(t, tuple) and len(t) == 3 and t[0] == "wl":
                return list(wl.toks[t[1]:t[2]])
            return t

        def stage_down(b, fd):
            gd = b * NFC + fd
            pe.wait(mult_tok[gd], ep_tok.get(b - 1) if fd == 0 else None, wdB_tok if (b == 0 and fd == NFC // 2) else None,
                    wres(d_wtok[fd]) if b == 0 else None)
            for t in range(2):
                for hh in range(2):
                    ins = pe.e.matmul(yp[t][hh][:], lhsT=hT[gd % NH][:, t * 128:(t + 1) * 128],
                                      rhs=wd_ap(fd, hh * 512, (hh + 1) * 512), start=(fd == 0), stop=(fd == NFC - 1))
            D_tok[gd] = pe.mark(ins)

        for b in range(NB):
            if b + 1 < NB:
                stage_load_cast(b + 1)
            stage_xR(b)
            xt = xT[b % 2]
            for f in range(NFC):
                gi = b * NFC + f
                if wl is not None and b == 0 and f % 2 == 0:
                    wl.emit(6 * (f // 2 + 3) - wl.i)
                    if wl.done() and bg is None and bg_jobs:
                        bg = BgCast(k, es, name, bg_jobs, k.last_stg[:2], list(wl.toks))
                pe.wait(XE_tok[(b, 0)], XE_tok[(b, 1)], mult_tok.get(gi - 2), wres(gu_wtok[f]) if b == 0 else None)
                for c in range(8):
                    pe.e.matmul(gu[gi % 2][:, 0, :], lhsT=wg[:, c, f * 128:(f + 1) * 128], rhs=xt[:, c, :],
                                start=(c == 0), stop=(c == 7))
                for c in range(8):
                    ins = pe.e.matmul(gu[gi % 2][:, 1, :], lhsT=wu[:, c, f * 128:(f + 1) * 128], rhs=xt[:, c, :],
                                      start=(c == 0), stop=(c == 7))
                gtok = pe.mark(ins)
                if f == NFC - 1:
                    gu_last[b] = gtok
                act.wait(gtok, mult_tok.get(gi - 2))
                stok = act.mark(act.e.activation(out=sg[gi % 2][:], in_=gu[gi % 2][:, 0, :], func=AF.Silu))
                dve.wait(stok, D_tok.get(gi - NH))
                mult_tok[gi] = dve.mark(dve.e.tensor_tensor(out=hT[gi % NH][:], in0=sg[gi % 2][:], in1=gu[gi % 2][:, 1, :], op=ALU.mult))
                if f == 10 and b + 1 < NB:
                    stage_T(b + 1)
                    stage_XE(b + 1)
                if bg is not None and f in (1, 5, 9, 13, 17, 20):
                    bg.step()
                if f >= 1:
                    stage_down(b, f - 1)
            stage_down(b, NFC - 1)
            etoks = []
            for t in range(2):
                gt = b * 2 + t
                stt = ln_part_a(k, ctx, gt, [yp[t][0][:], yp[t][1][:]], xR[t], 2.0 * ALPHA, 4.0 * EPS,
                                dst[gt * 128:(gt + 1) * 128, :], [D_tok[b * NFC + NFC - 1], xR_tok[t]])
                xR_free[t] = stt
                etoks.extend(stt)
            for t in range(2):
                ln_part_b(k, ctx, b * 2 + t)
            ep_tok[b] = etoks
        barrier(k, [ctx["store_tok"], bg.finish() if bg is not None else None])


def xT_block_loader(k, es, name, src, ntile_list):
    nc = k.nc
    sb = lambda n, shape, dt: es.enter_context(nc.sbuf_tensor(f"{name}_{n}", shape, dt))
    st = {
        "xA": [sb(f"lxA{i}", [128, D], F32) for i in range(2)],
        "xbf": [sb(f"lxbf{i}", [128, D], BF16) for i in range(2)],
        "slot": k.slots(es, 2),
        "Tps": [es.enter_context(nc.psum_tensor(f"{name}_lT{i}", [128, 8, 128], BF16)) for i in range(2)],
        "cast_tok": [None, None], "T_tok": [None, None], "XE_tok": [None, None], "i": 0,
    }

    def emit(tile, dstT, dst_free_tok=None):
        i = st["i"]; st["i"] += 1
        s_ = i % 2
        k.sp.wait(st["cast_tok"][s_])
        lt = st["slot"][s_].dma(k.sp, st["xA"][s_][:], src[tile * 128:(tile + 1) * 128, :])
        k.pool.wait(lt, st["T_tok"][s_])
        st["cast_tok"][s_] = k.pool.mark(k.pool.e.tensor_copy(out=st["xbf"][s_][:], in_=st["xA"][s_][:]))
        k.pe.wait(st["cast_tok"][s_], st["XE_tok"][s_], k.ident_tok)
        for c in range(8):
            ins = k.pe.e.transpose(out=st["Tps"][s_][:, c, :], in_=st["xbf"][s_][:, c * 128:(c + 1) * 128], identity=k.ident[:])
        st["T_tok"][s_] = k.pe.mark(ins)
        k.act.wait(st["T_tok"][s_], dst_free_tok)
        st["XE_tok"][s_] = k.act.mark(k.act.e.activation(out=dstT, in_=st["Tps"][s_][:], func=AF.Copy))
        return st["XE_tok"][s_]
    return emit


def win_jobs(win_sb, win_d, col0, ncols):
    v = win_d.rearrange("(c p) f -> p c f", p=128)
    return [(win_sb[:, c, :], v[:, c, col0:col0 + ncols]) for c in range(8)]


def na_kt_set(il):
    return [0, 1, 2, 3] if il < 2 else list(range(il - 2, il + 3))


def na_variant(il, kt):
    return il * 4 + kt if il < 2 else 8 + (kt - il + 2)


def na_phase(k, x1_d, win_d, nab_d, mixT):
    nc = k.nc
    pe, act, dve, pool, sp = k.pe, k.act, k.dve, k.pool, k.sp
    with ExitStack() as es:
        sb = lambda n, shape, dt: es.enter_context(nc.sbuf_tensor(f"na_{n}", shape, dt))
        ps = lambda n, shape, dt: es.enter_context(nc.psum_tensor(f"na_{n}", shape, dt))
        KT = sb("KT", [128, 4, NKT_NA * 128], BF16)
        QT = [sb(f"QT{i}", [128, 4, OWN], BF16) for i in range(2)]
        VA = sb("VA", [128, NKT_NA, 8, 65], BF16)
        zer = sb("zer", [128, 512], BF16)
        nab = sb("nab", [128, 13, 1024], F32)
        nsl = k.slots(es, 1)[0]
        tz = pool.mark(pool.e.memset(zer[:], 0.0))
        dve.e.memset(QT[0][64:128, :, :], 0.0)
        dve.e.memset(QT[1][0:64, :, :], 0.0)
        tv1 = dve.mark(dve.e.memset(VA[:, :, :, 64:65], 1.0))
        pad_tok = tv1
        with ExitStack() as es2:
            sb2 = lambda n, shape, dt: es2.enter_context(nc.sbuf_tensor(f"nap_{n}", shape, dt))
            win = sb2("win", [128, 8, 1536], BF16)
            wtoks = load_bf16_weights(k, act, k.slots(es2, 1)[0], win_jobs(win, win_d, 0, 1536))
            for v in range(13):
                nab_tok = nsl.dma(act, nab[:, v, :], nab_d[v])
            x1T = [sb2(f"x1T{i}", [128, 8, 512], BF16) for i in range(2)]
            pp = [es2.enter_context(nc.psum_tensor(f"nap_pp{i}", [128, 512], F32)) for i in range(3)]
            emit = xT_block_loader(k, es2, "nap", x1_d, None)
            pp_free = [None] * 3
            blk_last_pe = [None, None]
            npp = 0
            for blk in range(5):
                ntile = 4 if blk < 4 else 2
                ntok = ntile * 128
                xt = x1T[blk % 2]
                xe = [emit(blk * 4 + t, xt[:, :, t * 128:(t + 1) * 128], blk_last_pe[blk % 2]) for t in range(ntile)]
                pe.wait(xe, wtoks)
                for kind in range(2):
                    if kind == 1 and blk >= 4:
                        continue
                    for hp in range(4):
                        col = (512 if kind == 0 else 0) + hp * 128
                        b_ = npp % 3; npp += 1
                        pe.wait(pp_free[b_])
                        for c in range(8):
                            ins = pe.e.matmul(pp[b_][:, :ntok], lhsT=win[:, c, col:col + 128], rhs=xt[:, c, :ntok], start=(c == 0), stop=(c == 7))
                        tk = pe.mark(ins)
                        act.wait(tk)
                        if kind == 0:
                            pp_free[b_] = act.mark(act.e.activation(out=KT[:, hp, blk * 512:blk * 512 + ntok], in_=pp[b_][:, :ntok], func=AF.Copy))
                        else:
                            act.wait(tv1)
                            act.e.activation(out=QT[0][0:64, hp, blk * 512:blk * 512 + ntok], in_=pp[b_][0:64, :ntok], func=AF.Copy, scale=0.125)
                            pp_free[b_] = act.mark(act.e.activation(out=QT[1][64:128, hp, blk * 512:blk * 512 + ntok], in_=pp[b_][64:128, :ntok], func=AF.Copy, scale=0.125))
                for t in range(ntile):
                    b_ = npp % 3; npp += 1
                    pe.wait(pp_free[b_])
                    for c in range(8):
                        ins = pe.e.matmul(pp[b_][:], lhsT=xt[:, c, t * 128:(t + 1) * 128], rhs=win[:, c, 1024:1536], start=(c == 0), stop=(c == 7))
                    tk = pe.mark(ins)
                    dve.wait(tk, tv1)
                    pp_free[b_] = dve.mark(dve.e.tensor_copy(out=VA[:, blk * 4 + t, :, 0:64], in_=pp[b_][:].rearrange("p (h e) -> p h e", e=64)))
                blk_last_pe[blk % 2] = tk
            barrier(k, [])
        NSP, NTM, NE, LA = 4, 3, 5, 3
        sps = [ps(f"s{i}", [128, 512], F32) for i in range(NSP)]
        acc = [ps(f"acc{i}", [128, 512], F32) for i in range(2)]
        tpo = ps("tpo", [128, 4, 128], BF16)
        tmp = [sb(f"tmp{i}", [128, 512], F32) for i in range(NTM)]
        E = [sb(f"E{i}", [128, 512], BF16) for i in range(NE)]
        rr = sb("rr", [128, 8], F32)
        nao = [sb(f"nao{i}", [128, 512], BF16) for i in range(2)]
        accs = sb("accs", [128, 2, 260], F32)
        sps_free = [None] * NSP
        tmp_free = [None] * NTM
        E_free = [None] * NE
        acc_free = [None, None]
        nao_free = [None, None]
        nao_tok = {}
        tpo_free = None
        accs_free = None
        ns = 0
        E_tok = {}

        def finish_il(il_):
            nonlocal tpo_free
            s2 = il_ % 2
            pe.wait(nao_tok[il_], tpo_free)
            for hp in range(4):
                ins = pe.e.transpose(out=tpo[:, hp, :], in_=nao[s2][:, hp * 128:(hp + 1) * 128], identity=k.ident[:])
            tt = pe.mark(ins)
            nao_free[s2] = tt
            act.wait(tt)
            tpo_free = act.mark(act.e.activation(out=mixT[:, 0:4, il_ * 128:(il_ + 1) * 128], in_=tpo[:], func=AF.Copy))

        def steps_of(il_):
            return [(kt, hb) for kt in na_kt_set(il_) for hb in range(2)]

        def emit_S(il_, si):
            nonlocal ns
            kt, hb = steps_of(il_)[si]
            g = ns; ns += 1
            pe.wait(sps_free[g % NSP], pad_tok)
            for hl in range(4):
                ins = pe.e.matmul(sps[g % NSP][:, hl * 128:(hl + 1) * 128], lhsT=KT[:, hl, kt * 128:(kt + 1) * 128],
                                  rhs=QT[hb][:, hl, il_ * 128:(il_ + 1) * 128], start=True, stop=True)
            tk = pe.mark(ins)
            v = na_variant(il_, kt)
            dve.wait(tk, tmp_free[g % NTM], nab_tok)
            t1 = dve.mark(dve.e.tensor_tensor(out=tmp[g % NTM][:], in0=nab[:, v, hb * 512:(hb + 1) * 512], in1=sps[g % NSP][:], op=ALU.add))
            sps_free[g % NSP] = t1
            act.wait(t1, E_free[g % NE])
            t2 = act.mark(act.e.activation(out=E[g % NE][:], in_=tmp[g % NTM][:], func=AF.Exp))
            tmp_free[g % NTM] = t2
            E_tok[(il_, si)] = (t2, g)

        def emit_AV(il_, si, last):
            kt, hb = steps_of(il_)[si]
            t2, g = E_tok.pop((il_, si))
            pe.wait(t2)
            for hl in range(4):
                h = 2 * hl + hb
                ins = pe.e.matmul(acc[hb][:, hl * 65:(hl + 1) * 65], lhsT=E[g % NE][:, hl * 128:(hl + 1) * 128],
                                  rhs=VA[:, kt, h, :], start=False, stop=(last and hl == 3))
            E_free[g % NE] = pe.mark(ins)
            return E_free[g % NE]

        pend_il = None
        for si in range(LA):
            emit_S(0, si)
        for il in range(16):
            nst = len(steps_of(il))
            for hb in range(2):
                pe.wait(acc_free[hb], tz)
                pe.e.matmul(acc[hb][:], lhsT=zer[:, 0:128], rhs=zer[:], start=True, stop=False)
            last_av = [None, None]
            for si in range(nst):
                if si + LA < nst:
                    emit_S(il, si + LA)
                last_av[steps_of(il)[si][1]] = emit_AV(il, si, si >= nst - 2)
                if si == 3 and pend_il is not None:
                    finish_il(pend_il)
                    pend_il = None
            s_ = il % 2
            evt = []
            for hb in range(2):
                act.wait(last_av[hb], accs_free)
                acc_free[hb] = act.mark(act.e.activation(out=accs[:, hb, :], in_=acc[hb][:, 0:260], func=AF.Copy))
                evt.append(acc_free[hb])
            if il + 1 < 16:
                for si in range(LA):
                    emit_S(il + 1, si)
            for hb in range(2):
                accv = accs[:, hb, :].rearrange("p (h e) -> p h e", e=65)
                dve.wait(evt, nao_free[s_])
                tr = dve.mark(dve.e.reciprocal(out=rr[:, hb * 4:(hb + 1) * 4], in_=accv[:, :, 64]))
                dve.wait(tr)
                for hl in range(4):
                    h = 2 * hl + hb
                    ins = dve.e.tensor_scalar(out=nao[s_][:, h * 64:(h + 1) * 64], in0=accv[:, hl, 0:64], scalar1=rr[:, hb * 4 + hl:hb * 4 + hl + 1], scalar2=None, op0=ALU.mult)
                accs_free = dve.mark(ins)
            nao_tok[il] = accs_free
            pend_il = il
        finish_il(pend_il)
        barrier(k, [])


def diff_phase(k, x1_d, win_d, aug_d, atab_d, cst_d, lamv_d, subg_d, mixT):
    nc = k.nc
    pe, act, dve, pool, sp = k.pe, k.act, k.dve, k.pool, k.sp
    SL = [2.0 ** (-8.0 * (h + 1) / 4) for h in range(4)]
    with ExitStack() as es:
        sb = lambda n, shape, dt: es.enter_context(nc.sbuf_tensor(f"df_{n}", shape, dt))
        ps = lambda n, shape, dt: es.enter_context(nc.psum_tensor(f"df_{n}", shape, dt))
        KT = [sb(f"KT{i}", [128, 4, SEQ], BF16) for i in range(2)]
        QT = sb("QT", [128, 4, OWN], BF16)
        VA = sb("VA", [128, 32, 4, 129], BF16)
        cst = sb("cst", [128, 256], F32)
        lamv = sb("lamv", [128, 4, 64], F32)
        g8 = sb("g8", [128, 128], F32)
        zer = sb("zer", [128, 512], BF16)
        sm = sb("sm", [128, 8], F32)
        junk = sb("junk", [128, 64], F32)
        mhalf = sb("mhalf", [128, 1], F32)
        tz = pool.mark(pool.e.memset(zer[:], 0.0))
        pool.e.memset(mhalf[:], -0.5)
        ones_t = sb("ones_t", [128, 512], BF16)
        pad_tok = pool.mark(pool.e.memset(ones_t[:], 1.0))
        tv1 = dve.mark(dve.e.memset(VA[:, :, :, 128:129], 1.0))
        csl = k.slots(es, 1)[0]
        csl.dma(sp, cst[:], cst_d)
        csl.dma(sp, lamv[:], lamv_d)
        ctok = csl.dma(sp, g8[:], subg_d)
        dve.wait(ctok)
        a0 = dve.mark(dve.e.tensor_scalar(out=g8[:], in0=g8[:], scalar1=1.0 - LAM_INIT, scalar2=None, op0=ALU.mult))
        dve.wait(a0)
        a1 = dve.mark(dve.e.scalar_tensor_tensor(out=junk[:], in0=lamv[:, 0, :], scalar=1.0, op0=ALU.mult, in1=lamv[:, 1, :], op1=ALU.mult, accum_out=sm[:, 0:1]))
        dve.wait(a1)
        a2 = dve.mark(dve.e.scalar_tensor_tensor(out=junk[:], in0=lamv[:, 2, :], scalar=1.0, op0=ALU.mult, in1=lamv[:, 3, :], op1=ALU.mult, accum_out=sm[:, 1:2]))
        act.wait(a2)
        a3 = act.mark(act.e.activation(out=sm[:, 2:4], in_=sm[:, 0:2], func=AF.Exp))
        dve.wait(a3)
        a4 = dve.mark(dve.e.tensor_tensor(out=sm[:, 5:6], in0=sm[:, 3:4], in1=sm[:, 2:3], op=ALU.subtract))
        dve.wait(a4)
        lam_tok = dve.mark(dve.e.tensor_scalar(out=sm[:, 4:5], in0=sm[:, 5:6], scalar1=-LAM_INIT, scalar2=None, op0=ALU.add))
        neglam = sm[:, 4:5]
        with ExitStack() as es2:
            sb2 = lambda n, shape, dt: es2.enter_context(nc.sbuf_tensor(f"dfp_{n}", shape, dt))
            win = sb2("win", [128, 8, 1536], BF16)
            wtoks = load_bf16_weights(k, act, k.slots(es2, 1)[0], win_jobs(win, win_d, 1536, 1536))
            x1T = [sb2(f"x1T{i}", [128, 8, 512], BF16) for i in range(2)]
            pp = [es2.enter_context(nc.psum_tensor(f"dfp_pp{i}", [128, 512], F32)) for i in range(3)]
            emit = xT_block_loader(k, es2, "dfp", x1_d, None)
            pp_free = [None] * 3
            blk_last_pe = [None, None]
            npp = 0
            for blk in range(8):
                xt = x1T[blk % 2]
                xe = [emit(blk * 4 + t, xt[:, :, t * 128:(t + 1) * 128], blk_last_pe[blk % 2]) for t in range(4)]
                pe.wait(xe, wtoks)
                for kind in range(2):
                    if kind == 1 and blk >= 4:
                        continue
                    for h in range(4):
                        col = (512 if kind == 0 else 0) + h * 128
                        b_ = npp % 3; npp += 1
                        pe.wait(pp_free[b_])
                        for c in range(8):
                            ins = pe.e.matmul(pp[b_][:], lhsT=win[:, c, col:col + 128], rhs=xt[:, c, :], start=(c == 0), stop=(c == 7))
                        tk = pe.mark(ins)
                        act.wait(tk)
                        if kind == 0:
                            act.wait(pad_tok)
                            k0 = act.mark(act.e.activation(out=KT[0][:, h, blk * 512:(blk + 1) * 512], in_=pp[b_][:], func=AF.Copy))
                            k1 = act.mark(act.e.activation(out=KT[1][:, h, blk * 512:(blk + 1) * 512], in_=pp[b_][:], func=AF.Copy))
                            pp_free[b_] = k1
                            act.wait(k0, k1)
                            act.e.activation(out=KT[0][64:66, h, blk * 512:(blk + 1) * 512], in_=ones_t[64:66, :], func=AF.Copy)
                            kfix_tok = act.mark(act.e.activation(out=KT[1][0:2, h, blk * 512:(blk + 1) * 512], in_=ones_t[0:2, :], func=AF.Copy))
                        else:
                            pp_free[b_] = act.mark(act.e.activation(out=QT[:, h, blk * 512:(blk + 1) * 512], in_=pp[b_][:], func=AF.Copy, scale=0.125))
                for t in range(4):
                    b_ = npp % 3; npp += 1
                    pe.wait(pp_free[b_])
                    for c in range(8):
                        ins = pe.e.matmul(pp[b_][:], lhsT=xt[:, c, t * 128:(t + 1) * 128], rhs=win[:, c, 1024:1536], start=(c == 0), stop=(c == 7))
                    tk = pe.mark(ins)
                    dve.wait(tk, tv1)
                    pp_free[b_] = dve.mark(dve.e.tensor_copy(out=VA[:, blk * 4 + t, :, 0:128], in_=pp[b_][:].rearrange("p (h e) -> p h e", e=128)))
                blk_last_pe[blk % 2] = tk
            barrier(k, [])
        NSP, NTM, NE = 2, 2, 4
        atab = sb("atab", [128, 2, 896], F32)
        augtab = sb("augtab", [128, 4, 2, 512], BF16)
        Qs = [[[sb(f"Qs{u}{v}{m}", [128, 512], BF16) for m in range(2)] for v in range(3)] for u in range(2)]
        asl = k.slots(es, 1)[0]
        asl.dma(sp, atab[:], atab_d)
        atok = asl.dma(sp, augtab[:], aug_d)
        qz = None
        for u in range(2):
            for v in range(3):
                for m in range(2):
                    qz = pool.mark(pool.e.memset(Qs[u][v][m][:], 0.0))
        sps = [ps(f"s{i}", [128, 2, 512], F32) for i in range(NSP)]
        acc = [ps(f"acc{i}", [128, 512], F32) for i in range(3)]
        tpo = ps("tpo", [128, 128], BF16)
        tmp = [sb(f"tmp{i}", [128, 2, 512], F32) for i in range(NTM)]
        E = [sb(f"E{i}", [128, 2, 512], BF16) for i in range(NE)]
        accs = sb("accs", [128, 3, 387], F32)
        rr = sb("rr", [128, 4], F32)
        tq = sb("tq", [128, 128], F32)
        oq = sb("oq", [128, 128], F32)
        sps_free = [None] * NSP
        tmp_free = [None] * NTM
        E_free = [None] * NE
        acc_free = [None] * 3
        accs_free = None
        tpo_free = None
        ns = 0
        unit_last_S = {}
        units = [(h_, qb_) for h_ in range(4) for qb_ in range(4)]
        yq = [[sb(f"yq{u_}{q_}", [128, 128], BF16) for q_ in range(4)] for u_ in range(2)]
        yq_free = {}
        yq_tok = {}
        qtoks = {}

        def build_Qs(ui):
            h_, qb_ = units[ui]
            u_ = ui % 2
            pool.wait(qz, atok, unit_last_S.get(ui - 2), ctok)
            for m in range(2):
                r0 = 64 * m
                a0 = 64 - 64 * m
                for v in range(3):
                    qtok = pool.mark(pool.e.tensor_copy(out=Qs[u_][v][m][r0:r0 + 64, :], in_=QT[r0:r0 + 64, h_, qb_ * 512:(qb_ + 1) * 512]))
                for v in range(2):
                    qtok = pool.mark(pool.e.tensor_copy(out=Qs[u_][v][m][a0:a0 + 2, :], in_=augtab[a0:a0 + 2, h_, v, :]))
            qtoks[ui] = qtok

        def finish_transposes(ui):
            nonlocal tpo_free
            h_, qb_ = units[ui]
            for qt in range(4):
                pe.wait(yq_tok[(ui, qt)], tpo_free)
                tt = pe.mark(pe.e.transpose(out=tpo[:], in_=yq[ui % 2][qt][:], identity=k.ident[:]))
                yq_free[(ui % 2, qt)] = tt
                act.wait(tt)
                tok0 = (qb_ * 4 + qt) * 128
                tpo_free = act.mark(act.e.activation(out=mixT[:, 4 + h_, tok0:tok0 + 128], in_=tpo[:], func=AF.Copy))

        evts = {}

        def epilogue(ui):
            nonlocal accs_free
            u = ui % 2
            evt = evts[ui]
            for qt in range(4):
                g0, g1 = qt, 4 + qt
                O0 = accs[:, g0 // 3, (g0 % 3) * 129:(g0 % 3) * 129 + 129]
                O1 = accs[:, g1 // 3, (g1 % 3) * 129:(g1 % 3) * 129 + 129]
                dve.wait(evt, lam_tok)
                e1 = dve.mark(dve.e.reciprocal(out=rr[:, 0:1], in_=O0[:, 128:129]))
                e2 = dve.mark(dve.e.reciprocal(out=rr[:, 1:2], in_=O1[:, 128:129]))
                dve.wait(e1, e2)
                e3 = dve.mark(dve.e.tensor_tensor(out=rr[:, 2:3], in0=rr[:, 1:2], in1=neglam, op=ALU.mult))
                dve.wait(e3)
                e4 = dve.mark(dve.e.tensor_scalar(out=tq[:], in0=O1[:, 0:128], scalar1=rr[:, 2:3], scalar2=None, op0=ALU.mult))
                dve.wait(e4)
                e5 = dve.mark(dve.e.scalar_tensor_tensor(out=oq[:], in0=O0[:, 0:128], scalar=rr[:, 0:1], op0=ALU.mult, in1=tq[:], op1=ALU.add))
                dve.wait(e5)
                e6 = dve.mark(dve.e.scalar_tensor_tensor(out=tq[:], in0=oq[:], scalar=1.0 / 128.0, op0=ALU.mult, in1=oq[:], op1=ALU.mult, accum_out=rr[:, 3:4]))
                dve.wait(e6)
                e7 = dve.mark(dve.e.tensor_scalar(out=rr[:, 3:4], in0=rr[:, 3:4], scalar1=EPS, scalar2=None, op0=ALU.add))
                pool.wait(e7)
                e8 = pool.mark(pool.e.tensor_tensor(out=rr[:, 3:4], in0=rr[:, 3:4], in1=mhalf[:], op=ALU.pow))
                dve.wait(e8, yq_free.get((u, qt)))
                e9 = dve.mark(dve.e.scalar_tensor_tensor(out=yq[u][qt][:], in0=oq[:], scalar=rr[:, 3:4], op0=ALU.mult, in1=g8[:], op1=ALU.mult))
                yq_tok[(ui, qt)] = e9
                if qt == 3:
                    accs_free = e9

        build_Qs(0)
        pending = None
        pend_epi = None
        tr_at = -1
        for ui, (h, qb) in enumerate(units):
            if True:
                u = ui % 2
                for j in range(3):
                    pe.wait(acc_free[j], tz)
                    pe.e.matmul(acc[j][:], lhsT=zer[:, 0:128], rhs=zer[:], start=True, stop=False)
                E_tok = {}

                def emit_S(kt):
                    nonlocal ns
                    g = ns; ns += 1
                    delta = qb * 512 - kt * 128
                    v = 0 if delta >= 128 else (1 if delta <= -512 else 2)
                    pe.wait(sps_free[g % NSP], qtoks[ui], pad_tok)
                    for m in range(2):
                        ins = pe.e.matmul(sps[g % NSP][:, m, :], lhsT=KT[m][:, h, kt * 128:(kt + 1) * 128],
                                          rhs=Qs[u][v][m][:], start=True, stop=True)
                    tk = pe.mark(ins)
                    unit_last_S[ui] = tk
                    if v == 2:
                        dve.wait(tk, tmp_free[g % NTM], atok)
                        t1 = dve.mark(dve.e.scalar_tensor_tensor(out=tmp[g % NTM][:], in0=atab[:, :, delta + 384:delta + 384 + 512], scalar=float(-SL[h]),
                                                                 op0=ALU.mult, in1=sps[g % NSP][:], op1=ALU.add))
                        sps_free[g % NSP] = t1
                        act.wait(t1, E_free[g % NE])
                        t2 = act.mark(act.e.activation(out=E[g % NE][:], in_=tmp[g % NTM][:], func=AF.Exp))
                        tmp_free[g % NTM] = t2
                    else:
                        n = abs(delta) // 128
                        col = (h * 32 + n) * 2 + v
                        act.wait(tk, E_free[g % NE], ctok)
                        t2 = act.mark(act.e.activation(out=E[g % NE][:], in_=sps[g % NSP][:], func=AF.Exp, bias=cst[:, col:col + 1], scale=1.0))
                        sps_free[g % NSP] = t2
                    E_tok[kt] = (t2, g)

                def emit_AV(kt):
                    t2, g = E_tok[kt]
                    pe.wait(t2, tv1)
                    for m in range(2):
                        for qt in range(4):
                            gi = m * 4 + qt
                            ins = pe.e.matmul(acc[gi // 3][:, (gi % 3) * 129:(gi % 3) * 129 + 129], lhsT=E[g % NE][:, m, qt * 128:(qt + 1) * 128],
                                              rhs=VA[:, kt, h, :], start=False, stop=(kt == 31 and gi in (2, 5, 7)))
                    E_free[g % NE] = pe.mark(ins)
                    return E_free[g % NE]

                emit_S(0)
                emit_S(1)
                for kt in range(32):
                    if kt + 2 < 32:
                        emit_S(kt + 2)
                    last = emit_AV(kt)
                    if kt == 6 and ui + 1 < len(units):
                        build_Qs(ui + 1)
                    if pend_epi is not None and kt == min(4 * qb + 2, 14):
                        epilogue(pend_epi)
                        pending = pend_epi
                        pend_epi = None
                        tr_at = kt + 12
                    if pending is not None and pend_epi is None and kt == tr_at:
                        finish_transposes(pending)
                        pending = None
                evt = []
                for j in range(3):
                    act.wait(last, accs_free)
                    acc_free[j] = act.mark(act.e.activation(out=accs[:, j, :], in_=acc[j][:, 0:387], func=AF.Copy))
                    evt.append(acc_free[j])
                evts[ui] = evt
                pend_epi = ui
        epilogue(pend_epi)
        finish_transposes(pend_epi)
        barrier(k, [])


def wout_phase(k, x1_d, wout_d, g_d, b_d, mixT, x2_d):
    nc = k.nc
    pe, act, dve, pool, sp = k.pe, k.act, k.dve, k.pool, k.sp
    with ExitStack() as es:
        sb = lambda n, shape, dt: es.enter_context(nc.sbuf_tensor(f"wo_{n}", shape, dt))
        wo = sb("wo", [128, 8, D], BF16)
        v = wout_d.rearrange("(c p) f -> p c f", p=128)
        wtoks = load_bf16_weights(k, sp, k.slots(es, 1)[0], [(wo[:, c, :], v[:, c, :]) for c in range(8)])
        NW = 4
        ctx = ln_ctx(k, es, "wo", g_d, b_d, nsl=NW)
        xR = [sb(f"xR{i}", [128, D], F32) for i in range(NW)]
        xsl = k.slots(es, NW)
        yp = [[es.enter_context(nc.psum_tensor(f"wo_y{t}{h}", [128, 512], F32)) for h in range(2)] for t in range(NW)]
        xR_free = [None] * NW
        yp_free = [None] * NW
        lts = {}

        def issue_load(t_):
            sl_ = t_ % NW
            sp.wait(xR_free[sl_])
            lts[t_] = xsl[sl_].dma(sp, xR[sl_][:], x1_d[t_ * 128:(t_ + 1) * 128, :])

        for t_ in range(NW - 1):
            issue_load(t_)
        for t in range(16):
            s_ = t % NW
            if t + NW - 1 < 16:
                issue_load(t + NW - 1)
            lt = lts[t]
            pe.wait(wtoks, yp_free[s_])
            for hh in range(2):
                for c in range(8):
                    ins = pe.e.matmul(yp[s_][hh][:], lhsT=mixT[:, c, t * 128:(t + 1) * 128], rhs=wo[:, c, hh * 512:(hh + 1) * 512], start=(c == 0), stop=(c == 7))
            tk = pe.mark(ins)
            stt = ln_part_a(k, ctx, t, [yp[s_][0][:], yp[s_][1][:]], xR[s_], ALPHA, EPS, x2_d[t * 128:(t + 1) * 128, :], [tk, lt])
            xR_free[s_] = stt
            yp_free[s_] = stt
            if t >= 1:
                ln_part_b(k, ctx, t - 1)
        ln_part_b(k, ctx, 15)
        barrier(k, [ctx["store_tok"]])


def build_program(stop=None):
    nc = bass.Bass("TRN2", target_bir_lowering=False)
    dram_in = lambda n, shape, dt=F32: nc.dram_tensor(n, shape, dt, kind="ExternalInput").ap()
    x = dram_in("x", [SEQ, D])
    wg1 = dram_in("wg1", [D, DFF]); wu1 = dram_in("wu1", [D, DFF]); wd1 = dram_in("wd1", [DFF, D])
    wg2 = dram_in("wg2", [D, DFF]); wu2 = dram_in("wu2", [D, DFF]); wd2 = dram_in("wd2", [DFF, D])
    win = dram_in("win", [D, 3072]); wout = dram_in("wout", [D, D])
    ln1g = dram_in("ln1g", [128, D]); ln1b = dram_in("ln1b", [128, D])
    ln2g = dram_in("ln2g", [128, D]); ln2b = dram_in("ln2b", [128, D])
    ln3g = dram_in("ln3g", [128, D]); ln3b = dram_in("ln3b", [128, D])
    ident_d = dram_in("ident", [128, 128], BF16)
    nab = dram_in("nab", [13, 128, 1024])
    augt = dram_in("augt", [128, 4, 2, 512], BF16); atab = dram_in("atab", [128, 2, 896]); cst = dram_in("cst", [128, 256])
    lamv = dram_in("lamv", [128, 4, 64]); subg = dram_in("subg", [128, 128])
    zeros_ones = dram_in("zeros_ones", [2, 64, 4 * SEQ], BF16)
    out = nc.dram_tensor("out", [OWN, D], F32, kind="ExternalOutput").ap()
    x1_d = nc.dram_tensor("x1_scratch", [SEQ, D], F32, kind="Internal").ap()
    x2_d = nc.dram_tensor("x2_scratch", [OWN, D], F32, kind="Internal").ap()
    win_bf = nc.dram_tensor("win_bf", [D, 3072], BF16, kind="Internal").ap()
    wout_bf = nc.dram_tensor("wout_bf", [D, D], BF16, kind="Internal").ap()
    wg2_bf = nc.dram_tensor("wg2_bf", [D, DFF], BF16, kind="Internal").ap()
    wu2_bf = nc.dram_tensor("wu2_bf", [D, DFF], BF16, kind="Internal").ap()
    wd2_bf = nc.dram_tensor("wd2_bf", [DFF, D], BF16, kind="Internal").ap()

    def pieces(src, dst, width):
        sv = src.rearrange("(c p) f -> p c f", p=128)
        dv = dst.rearrange("(c p) f -> p c f", p=128)
        out_ = []
        for c in range(sv.shape[1]):
            for o in range(0, sv.shape[2], width):
                out_.append((sv[:, c, o:o + width], dv[:, c, o:o + width]))
        return out_
    bg_jobs = (pieces(win, win_bf, 1024) + pieces(wout, wout_bf, 1024) + pieces(wg2, wg2_bf, 1408)
               + pieces(wu2, wu2_bf, 1408) + pieces(wd2, wd2_bf, 1024))
    dbg = None
    if stop is not None:
        dbg = nc.dram_tensor("dbg", [SEQ, D], F32, kind="ExternalOutput").ap()
    with ExitStack() as es:
        k = K()
        k.nc = nc
        k.pe = Eng(nc, nc.tensor, "pe", es)
        k.act = Eng(nc, nc.scalar, "act", es)
        k.dve = Eng(nc, nc.vector, "dve", es)
        k.pool = Eng(nc, nc.gpsimd, "pool", es)
        k.sp = Eng(nc, nc.sync, "sp", es)
        k.engs = [k.pe, k.act, k.dve, k.pool, k.sp]
        k.es_global = es
        k.slot_pool = []
        k.zeros_ones = zeros_ones
        k.ident = es.enter_context(nc.sbuf_tensor("ident_sb", [128, 128], BF16))
        isl = k.slots(es, 1)[0]
        k.ident_tok = isl.dma(k.sp, k.ident[:], ident_d)
        if stop == "A":
            ffn_phase(k, "f1", x, SEQ, wg1, wu1, wd1, ln1g, ln1b, dbg, bg_jobs=bg_jobs)
            return nc
        if stop in ("NA", "DF", "W"):
            x1_src = x
            with ExitStack() as esb:
                inb = [esb.enter_context(nc.sbuf_tensor(f"dbg_in{i}", [128, 1408], F32)) for i in range(2)]
                bgc = BgCast(k, esb, "dbgc", bg_jobs[:32], inb, None)
                barrier(k, [bgc.finish()])
        else:
            ffn_phase(k, "f1", x, SEQ, wg1, wu1, wd1, ln1g, ln1b, x1_d, bg_jobs=bg_jobs)
            x1_src = x1_d
        mix_cm = nc.sbuf_tensor("mixT", [128, 8, OWN], BF16, side="right")
        mixT = mix_cm.__enter__()
        if stop != "DF":
            na_phase(k, x1_src, win_bf, nab, mixT)
        if stop != "NA":
            diff_phase(k, x1_src, win_bf, augt, atab, cst, lamv, subg, mixT)
        if stop in ("NA", "DF"):
            with ExitStack() as es3:
                tmpf = es3.enter_context(nc.sbuf_tensor("dbg_tmp", [128, 8, OWN], F32))
                k.dve.wait((k.act, k.act.n))
                c0_ = 0 if stop == "NA" else 4
                k.pool.wait((k.act, k.act.n))
                tk0 = k.pool.mark(k.pool.e.memset(tmpf[:], 0.0))
                k.dve.wait(tk0)
                tk = k.dve.mark(k.dve.e.tensor_copy(out=tmpf[:, c0_:c0_ + 4, :], in_=mixT[:, c0_:c0_ + 4, :]))
                k.sp.wait(tk)
                sl = k.slots(es3, 1)[0]
                for c in range(8):
                    for hf in range(2):
                        t_ = sl.dma(k.sp, dbg[(c * 2 + hf) * 128:(c * 2 + hf + 1) * 128, :], tmpf[:, c, hf * 1024:(hf + 1) * 1024])
                barrier(k, [t_])
            mix_cm.__exit__(None, None, None)
            return nc
        with ExitStack() as esf2:
            pre = None
            if stop is None:
                wg2s, wu2s, _ = alloc_ffn_weights(k, esf2, "f2", with_wd=False)
                wdA2 = esf2.enter_context(nc.sbuf_tensor("f2_wdA", [128, NFC // 2, D], BF16))
                wsl = k.slots(esf2, 1)[0]
                jobs2 = ([(wg2s[:, c, :], wg2_bf.rearrange("(c p) f -> p c f", p=128)[:, c, :]) for c in range(8)]
                         + [(wu2s[:, c, :], wu2_bf.rearrange("(c p) f -> p c f", p=128)[:, c, :]) for c in range(8)]
                         + [(wdA2[:, c:c + 1, :], wd2_bf.rearrange("(c p) f -> p c f", p=128)[:, c:c + 1, :]) for c in range(NFC // 2)])
                pre = (wg2s, wu2s, wdA2, wd2_bf, load_bf16_weights(k, k.act, wsl, jobs2))
            wout_phase(k, x1_src, wout_bf, ln2g, ln2b, mixT, x2_d if stop is None else dbg)
            mix_cm.__exit__(None, None, None)
            if stop == "W":
                return nc
            ffn_phase(k, "f2", x2_d, OWN, wg2, wu2, wd2, ln3g, ln3b, out, pre=pre)
    return nc


def _na_tables(rpb, rev):
    out = np.full((13, 128, 8, 128), -30000.0, np.float32)
    p = np.arange(128)

    def coords(tile):
        t = tile * 128 + p
        r, c = t // 64, t % 64
        if rev:
            r, c = 63 - r, 63 - c
        return r, c

    def fill(v, il, kt):
        rk, ck = coords(kt)
        rq, cq = coords(il)
        r0 = np.clip(rq - 4, 0, 56)
        c0 = np.clip(cq - 8, 0, 48)
        RK, RQ = rk[:, None], rq[None, :]
        CK, CQ = ck[:, None], cq[None, :]
        ok = (RK >= r0[None, :]) & (RK <= r0[None, :] + 7) & (CK >= c0[None, :]) & (CK <= c0[None, :] + 15)
        dr = np.clip(RK - RQ + 7, 0, 14)
        dc = np.clip(CK - CQ + 15, 0, 30)
        vals = rpb[:, dr, dc]
        tile = np.where(ok[None], vals, np.float32(-30000.0)).astype(np.float32)
        out[v] = tile.transpose(1, 0, 2)[:, [0, 2, 4, 6, 1, 3, 5, 7], :]
        return ok

    for il in range(2):
        for kt in range(4):
            fill(na_variant(il, kt), il, kt)
    for dj in range(-2, 3):
        fill(na_variant(8, 8 + dj), 8, 8 + dj)
    return np.ascontiguousarray(out.reshape(13, 128, 1024))


def prep_inputs(inputs, c):
    b, h = c // 2, c % 2
    xb = np.ascontiguousarray(inputs["x"][b])
    if h == 1:
        xb = np.ascontiguousarray(xb[::-1])
    f32 = lambda v: np.ascontiguousarray(np.asarray(v, np.float32))
    rep = lambda v: np.ascontiguousarray(np.broadcast_to(np.asarray(v, np.float32).reshape(1, -1), (128, np.asarray(v).size)))
    p = np.arange(128, dtype=np.float32)[:, None]
    jtab = (np.arange(512, dtype=np.float32)[None, :] - p).astype(np.float32)
    atab = np.abs(np.arange(896, dtype=np.float32)[None, :] - p - 384.0).astype(np.float32)
    atab = np.ascontiguousarray(np.stack([atab, atab], axis=1))
    cst = np.zeros((128, 4, 32, 2), np.float32)
    augt = np.zeros((128, 4, 2, 512), np.float32)
    jj = np.arange(512, dtype=np.float32)
    pp_ = np.arange(128, dtype=np.float32)
    for hh in range(4):
        sl = 2.0 ** (-8.0 * (hh + 1) / 4)
        for v in range(2):
            sgn = 1.0 if v == 0 else -1.0
            cst[:, hh, :, v] = sgn * sl * pp_[:, None] - sl * 128.0 * np.arange(32, dtype=np.float32)[None, :]
            hi = -sgn * sl * 256.0 * np.floor(jj / 256.0)
            lo = -sgn * sl * np.mod(jj, 256.0)
            for base in (0, 64):
                augt[base, hh, v] = hi
                augt[base + 1, hh, v] = lo
    cst = np.ascontiguousarray(cst.reshape(128, 256))
    augt = augt.astype(ml_dtypes.bfloat16)
    lamv = np.stack([rep(inputs["diff_lambda_q1"][0]), rep(inputs["diff_lambda_k1"][0]),
                     rep(inputs["diff_lambda_q2"][0]), rep(inputs["diff_lambda_k2"][0])], axis=1)
    m = {
        "x": xb,
        "wg1": f32(inputs["ffn1_w_gate"][0]), "wu1": f32(inputs["ffn1_w_up"][0]), "wd1": f32(inputs["ffn1_w_down"][0]),
        "wg2": f32(inputs["ffn2_w_gate"][0]), "wu2": f32(inputs["ffn2_w_up"][0]), "wd2": f32(inputs["ffn2_w_down"][0]),
        "win": f32(inputs["w_in"][0]), "wout": f32(inputs["w_out"][0]),
        "ln1g": rep(inputs["ln1_g"][0]), "ln1b": rep(inputs["ln1_b"][0]),
        "ln2g": rep(inputs["ln2_g"][0]), "ln2b": rep(inputs["ln2_b"][0]),
        "ln3g": rep(inputs["ln3_g"][0]), "ln3b": rep(inputs["ln3_b"][0]),
        "ident": np.eye(128, dtype=np.float32).astype(ml_dtypes.bfloat16),
        "nab": _na_tables(f32(inputs["na_rpb"][0]), h == 1),
        "augt": augt, "atab": atab, "cst": cst,
        "lamv": np.ascontiguousarray(lamv.astype(np.float32)), "subg": rep(inputs["diff_subln_g"][0]),
        "zeros_ones": np.stack([np.zeros((64, 4 * SEQ), np.float32), np.ones((64, 4 * SEQ), np.float32)]).astype(ml_dtypes.bfloat16),
    }
    return m


def kernel(**inputs):
    inputs = {k_: np.asarray(v) for k_, v in inputs.items()}
    nc = build_program()
    in_maps = [prep_inputs(inputs, c) for c in range(8)]
    res = run_bass_kernel_spmd(nc, in_maps, core_ids=list(range(8)))
    outp = np.empty((4, SEQ, D), np.float32)
    for c in range(8):
        b, h = c // 2, c % 2
        o = np.asarray(res.results[c]["out"])
        if h == 0:
            outp[b, :OWN] = o
        else:
            outp[b, OWN:] = o[::-1]
    return outp
```

```python
import numpy as np
from contextlib import ExitStack
import concourse.bass as bass
import concourse.mybir as mybir
from concourse.bass_utils import run_bass_kernel_spmd
import ml_dtypes

F32, BF16 = mybir.dt.float32, mybir.dt.bfloat16
AF = mybir.ActivationFunctionType
ALU = mybir.AluOpType

D = 1024
DFF = 2816
NFC = DFF // 128
SEQ = 4096
OWN = 2048
ALPHA = 2.0 ** 0.25
EPS = 1e-5
LAM_INIT = 0.2
NKT_NA = 18


def _flat(toks):
    out = []
    for t in toks:
        if t is None:
            continue
        if isinstance(t, list):
            out.extend(_flat(t))
        else:
            out.append(t)
    return out


class Eng:
    def __init__(self, nc, e, name, es):
        self.e = e
        self.name = name
        self.sem = es.enter_context(nc.semaphore("sem_" + name))
        self.n = 0
        self.seen = {}

    def wait(self, *toks):
        best = {}
        for src, v in _flat(list(toks)):
            if best.get(id(src), (None, 0))[1] < v:
                best[id(src)] = (src, v)
        for src, v in best.values():
            if self.seen.get(id(src), 0) >= v:
                continue
            self.e.wait_ge(src.sem, v)
            self.seen[id(src)] = v

    def mark(self, ins):
        ins.then_inc(self.sem, 1)
        self.n += 1
        return (self, self.n)


class Slot:
    def __init__(self, nc, name, es):
        self.sem = es.enter_context(nc.semaphore("dsem_" + name))
        self.n = 0
        self.busy = False

    def dma(self, q, out, in_):
        q.e.dma_start(out=out, in_=in_).then_inc(self.sem, 16)
        self.n += 16
        return (self, self.n)


class K:
    def slots(self, es, n):
        got = []
        for sl in self.slot_pool:
            if not sl.busy and len(got) < n:
                sl.busy = True
                got.append(sl)
        while len(got) < n:
            sl = Slot(self.nc, f"p{len(self.slot_pool)}", self.es_global)
            sl.busy = True
            self.slot_pool.append(sl)
            got.append(sl)

        def release():
            for sl in got:
                sl.busy = False
        es.callback(release)
        return got


def barrier(k, toks):
    toks = _flat(toks) + [(e, e.n) for e in k.engs if e.n > 0]
    for e in k.engs:
        e.wait(toks)


def copy_cast(eng, k, out, in_):
    if eng is k.act:
        return eng.e.activation(out=out, in_=in_, func=AF.Copy)
    return eng.e.tensor_copy(out=out, in_=in_)


class WeightLoader:
    def __init__(self, k, es, name, jobs, nslots=3, width=1408):
        nc = k.nc
        self.k = k
        self.jobs = jobs
        self.stg = [es.enter_context(nc.sbuf_tensor(f"{name}_stg{i}", [128, width], F32)) for i in range(nslots)]
        self.slots = k.slots(es, nslots)
        self.cast_tok = [None] * nslots
        self.engs = [k.dve, k.pool, k.act]
        self.toks = []
        self.i = 0
        k.last_stg = self.stg

    def emit(self, n):
        k = self.k
        nslots = len(self.stg)
        for _ in range(n):
            if self.i >= len(self.jobs):
                return
            i = self.i
            self.i += 1
            dst, src = self.jobs[i]
            s = i % nslots
            if len(src.shape) == 3:
                nel = src.shape[1] * src.shape[2]
                sv = self.stg[s][:, :nel].rearrange("p (a b) -> p a b", b=src.shape[2])
            else:
                nel = src.shape[-1]
                sv = self.stg[s][:, :nel]
            k.sp.wait(self.cast_tok[s])
            lt = self.slots[s].dma(k.sp, sv, src)
            e = self.engs[i % 3]
            e.wait(lt)
            self.cast_tok[s] = e.mark(copy_cast(e, k, dst, sv))
            self.toks.append(self.cast_tok[s])

    def done(self):
        return self.i >= len(self.jobs)


def load_cast_weights(k, es, name, jobs, nslots=3, width=1408):
    wl = WeightLoader(k, es, name, jobs, nslots, width)
    wl.emit(len(jobs))
    return wl.toks


def load_bf16_weights(k, q, slot, jobs):
    tok = None
    for dst, src in jobs:
        tok = slot.dma(q, dst, src)
    return tok


class BgCast:
    def __init__(self, k, es, name, jobs, in_bufs, first_tok):
        nc = k.nc
        self.k = k
        self.jobs = jobs
        self.inb = in_bufs
        self.outb = [es.enter_context(nc.sbuf_tensor(f"{name}_bgo{i}", [128, 1408], BF16)) for i in range(2)]
        self.in_slot = k.slots(es, 2)
        self.out_slot = k.slots(es, 2)
        self.load_tok = [first_tok, first_tok]
        self.cast_tok = [None, None]
        self.store_tok = [None, None]
        self.i = 0

    def step(self):
        k = self.k
        i = self.i
        n_jobs = len(self.jobs)
        if i > n_jobs + 1:
            return
        self.i += 1
        if 0 <= i - 2 < n_jobs:
            j = i - 2
            n = self.jobs[j][0].shape[-1]
            k.act.wait(self.cast_tok[j % 2])
            self.store_tok[j % 2] = self.out_slot[j % 2].dma(k.act, self.jobs[j][1], self.outb[j % 2][:, :n])
        if i < n_jobs:
            n = self.jobs[i][0].shape[-1]
            k.act.wait(self.cast_tok[i % 2], self.load_tok[i % 2] if i < 2 else None)
            self.load_tok[i % 2] = self.in_slot[i % 2].dma(k.act, self.inb[i % 2][:, :n], self.jobs[i][0])
        if 0 <= i - 1 < n_jobs:
            j = i - 1
            n = self.jobs[j][0].shape[-1]
            k.act.wait(self.load_tok[j % 2], self.store_tok[j % 2])
            self.cast_tok[j % 2] = k.act.mark(k.act.e.activation(out=self.outb[j % 2][:, :n], in_=self.inb[j % 2][:, :n], func=AF.Copy))

    def finish(self):
        while self.i <= len(self.jobs) + 1:
            self.step()
        return [t for t in self.store_tok if t is not None]


def ln_part_a(k, ctx, t, psum_halves, xres, xscale, eps, dst_ap, pre_toks):
    nsl = ctx["n"]
    dve, pool, sp = k.dve, k.pool, k.sp
    s = t % nsl
    r = ctx["r"][s]
    stats, mv, ve, rstd = ctx["stats"][s], ctx["mv"][s], ctx["ve"][s], ctx["rstd"][s]
    stt_toks = []
    for hh in range(2):
        dve.wait(pre_toks, ctx["store_tok"][s])
        ins = dve.e.scalar_tensor_tensor(out=r[:, hh * 512:(hh + 1) * 512], in0=xres[:, hh * 512:(hh + 1) * 512],
                                         scalar=float(xscale), op0=ALU.mult, in1=psum_halves[hh], op1=ALU.add)
        stt_toks.append(dve.mark(ins))
    st_toks = []
    for hh in range(2):
        dve.wait(stt_toks[hh])
        st_toks.append(dve.mark(dve.e.bn_stats(out=stats[:, hh * 6:(hh + 1) * 6], in_=r[:, hh * 512:(hh + 1) * 512])))
    dve.wait(st_toks)
    t1 = dve.mark(dve.e.bn_aggr(out=mv[:], in_=stats[:]))
    dve.wait(t1)
    t2 = dve.mark(dve.e.tensor_scalar(out=ve[:], in0=mv[:, 1:2], scalar1=float(eps), scalar2=None, op0=ALU.add))
    pool.wait(t2, ctx["gb_tok"])
    t3 = pool.mark(pool.e.tensor_tensor(out=rstd[:], in0=ve[:], in1=ctx["mhalf"][:], op=ALU.pow))
    dve.wait(t1, ctx["gb_tok"])
    t4 = dve.mark(dve.e.scalar_tensor_tensor(out=r[:], in0=r[:], scalar=mv[:, 0:1], op0=ALU.subtract, in1=ctx["g"][:], op1=ALU.mult))
    ctx["pend"][t] = (t3, t4, dst_ap)
    return stt_toks


def ln_part_b(k, ctx, t):
    dve, sp = k.dve, k.sp
    s = t % ctx["n"]
    r = ctx["r"][s]
    t3, t4, dst_ap = ctx["pend"].pop(t)
    dve.wait(t3, t4)
    t6 = dve.mark(dve.e.scalar_tensor_tensor(out=r[:], in0=r[:], scalar=ctx["rstd"][s][:, 0:1], op0=ALU.mult, in1=ctx["b"][:], op1=ALU.add))
    sp.wait(t6)
    ctx["store_tok"][s] = ctx["store_slot"][s].dma(sp, dst_ap, r[:])


def ln_ctx(k, es, name, g_d, b_d, nsl=2):
    nc = k.nc
    sb = lambda n, shape, dt: es.enter_context(nc.sbuf_tensor(f"{name}_{n}", shape, dt))
    ctx = {
        "n": nsl,
        "pend": {},
        "r": [sb(f"r{i}", [128, D], F32) for i in range(nsl)],
        "stats": [sb(f"stats{i}", [128, 12], F32) for i in range(nsl)],
        "mv": [sb(f"mv{i}", [128, 2], F32) for i in range(nsl)],
        "ve": [sb(f"ve{i}", [128, 1], F32) for i in range(nsl)],
        "rstd": [sb(f"rstd{i}", [128, 1], F32) for i in range(nsl)],
        "mhalf": sb("mhalf", [128, 1], F32),
        "g": sb("g", [128, D], F32),
        "b": sb("b", [128, D], F32),
        "store_slot": k.slots(es, nsl),
        "store_tok": [None] * nsl,
    }
    gs = k.slots(es, 1)[0]
    gs.dma(k.sp, ctx["g"][:], g_d)
    tg = gs.dma(k.sp, ctx["b"][:], b_d)
    tm = k.pool.mark(k.pool.e.memset(ctx["mhalf"][:], -0.5))
    ctx["gb_tok"] = [tg, tm]
    return ctx


def alloc_ffn_weights(k, es, name, with_wd=True):
    nc = k.nc
    wg = es.enter_context(nc.sbuf_tensor(f"{name}_wg", [128, 8, DFF], BF16))
    wu = es.enter_context(nc.sbuf_tensor(f"{name}_wu", [128, 8, DFF], BF16))
    wd = es.enter_context(nc.sbuf_tensor(f"{name}_wd", [128, NFC, D], BF16)) if with_wd else None
    return wg, wu, wd


def ffn_phase(k, name, x_src, T, wg_d, wu_d, wd_d, g_d, b_d, dst, pre=None, bg_jobs=None):
    nc = k.nc
    pe, act, dve, pool, sp = k.pe, k.act, k.dve, k.pool, k.sp
    NB = T // 256
    NH = 4
    with ExitStack() as es:
        sb = lambda n, shape, dt: es.enter_context(nc.sbuf_tensor(f"{name}_{n}", shape, dt))
        ps = lambda n, shape, dt: es.enter_context(nc.psum_tensor(f"{name}_{n}", shape, dt))
        bg = None
        emit_weights = None
        wl = None
        if pre is not None:
            wg, wu, wdA, wd_bf_d, wtoks = pre
            wdB = es.enter_context(nc.sbuf_tensor(f"{name}_wdB", [128, NFC // 2, D], BF16))
            wdv = wd_bf_d.rearrange("(c p) f -> p c f", p=128)
            wdB_tok = load_bf16_weights(k, k.act, k.slots(es, 1)[0],
                                        [(wdB[:, c:c + 1, :], wdv[:, NFC // 2 + c:NFC // 2 + c + 1, :]) for c in range(NFC // 2)])
            wd_ap = lambda fd, lo, hi: (wdA if fd < NFC // 2 else wdB)[:, fd % (NFC // 2), lo:hi]
            gu_wtok = {f: wtoks for f in range(NFC)}
            d_wtok = {f: None for f in range(NFC)}
        else:
            wg, wu, wd = alloc_ffn_weights(k, es, name)
            wgv = wg_d.rearrange("(c p) f -> p c f", p=128)
            wuv = wu_d.rearrange("(c p) f -> p c f", p=128)
            wdv = wd_d.rearrange("(c p) d -> p c d", p=128)
            jobs = []
            for fg in range(NFC // 2):
                cs = slice(fg * 256, (fg + 1) * 256)
                for c0 in (0, 4):
                    jobs.append((wg[:, c0:c0 + 4, cs], wgv[:, c0:c0 + 4, cs]))
                    jobs.append((wu[:, c0:c0 + 4, cs], wuv[:, c0:c0 + 4, cs]))
                for c in (2 * fg, 2 * fg + 1):
                    jobs.append((wd[:, c, :], wdv[:, c, :]))
            wdB_tok = None
            wd_ap = lambda fd, lo, hi: wd[:, fd, lo:hi]
            gu_wtok, d_wtok = {}, {}

            wl = WeightLoader(k, es, name, jobs)
            for f in range(NFC):
                gu_wtok[f] = ("wl", (f // 2) * 6, (f // 2) * 6 + 4)
                d_wtok[f] = ("wl", (f // 2) * 6 + 4 + (f % 2), (f // 2) * 6 + 5 + (f % 2))

            def emit_weights():
                wl.emit(12)
        ctx = ln_ctx(k, es, name, g_d, b_d)

        xA = [sb(f"xA{i}", [128, D], F32) for i in range(2)]
        xA_slot = k.slots(es, 2)
        xR = [sb(f"xR{i}", [128, D], F32) for i in range(2)]
        xR_slot = k.slots(es, 2)
        xbf = [sb(f"xbf{i}", [128, D], BF16) for i in range(2)]
        xT = [sb(f"xT{i}", [128, 8, 256], BF16) for i in range(2)]
        hT = [sb(f"hT{i}", [128, 256], BF16) for i in range(NH)]
        sg = [sb(f"sg{i}", [128, 256], F32) for i in range(2)]
        ident = k.ident
        Tps = [ps(f"T{i}", [128, 8, 128], BF16) for i in range(2)]
        gu = [ps(f"gu{i}", [128, 2, 256], F32) for i in range(2)]
        yp = [[ps(f"y{t}{h}", [128, 512], F32) for h in range(2)] for t in range(2)]

        cast_tok = [None, None]
        T_tok = [None, None]
        XE_tok = {}
        xR_free = [None, None]
        xR_tok = [None, None]
        gu_last = {}
        mult_tok = {}
        D_tok = {}
        ep_tok = {}

        def stage_load_cast(b):
            for t in range(2):
                sp.wait(cast_tok[t])
                lt = xA_slot[t].dma(sp, xA[t][:], x_src[(b * 2 + t) * 128:(b * 2 + t + 1) * 128, :])
                pool.wait(lt, T_tok[t])
                cast_tok[t] = pool.mark(pool.e.tensor_copy(out=xbf[t][:], in_=xA[t][:]))

        def stage_T(b):
            for t in range(2):
                prev = XE_tok.get((b - 1, t))
                pe.wait(cast_tok[t], prev, k.ident_tok)
                for c in range(8):
                    ins = pe.e.transpose(out=Tps[t][:, c, :], in_=xbf[t][:, c * 128:(c + 1) * 128], identity=ident[:])
                T_tok[t] = pe.mark(ins)

        def stage_XE(b):
            for t in range(2):
                act.wait(T_tok[t], gu_last.get(b - 2))
                XE_tok[(b, t)] = act.mark(act.e.activation(out=xT[b % 2][:, :, t * 128:(t + 1) * 128], in_=Tps[t][:], func=AF.Copy))

        def stage_xR(b):
            for t in range(2):
                sp.wait(xR_free[t])
                xR_tok[t] = xR_slot[t].dma(sp, xR[t][:], x_src[(b * 2 + t) * 128:(b * 2 + t + 1) * 128, :])

        stage_load_cast(0)
        if emit_weights is not None:
            emit_weights()
        stage_T(0)
        stage_XE(0)
        def wres(t):
            if isinstance(t, tuple) and len(t) == 3 and t[0] == "wl":
                return list(wl.toks[t[1]:t[2]])
            return t

        def stage_down(b, fd):
            gd = b * NFC + fd
            pe.wait(mult_tok[gd], ep_tok.get(b - 1) if fd == 0 else None, wdB_tok if (b == 0 and fd == NFC // 2) else None,
                    wres(d_wtok[fd]) if b == 0 else None)
            for t in range(2):
                for hh in range(2):
                    ins = pe.e.matmul(yp[t][hh][:], lhsT=hT[gd % NH][:, t * 128:(t + 1) * 128],
                                      rhs=wd_ap(fd, hh * 512, (hh + 1) * 512), start=(fd == 0), stop=(fd == NFC - 1))
            D_tok[gd] = pe.mark(ins)

        for b in range(NB):
            if b + 1 < NB:
                stage_load_cast(b + 1)
            stage_xR(b)
            xt = xT[b % 2]
            for f in range(NFC):
                gi = b * NFC + f
                if wl is not None and b == 0 and f % 2 == 0:
                    wl.emit(6 * (f // 2 + 3) - wl.i)
                    if wl.done() and bg is None and bg_jobs:
                        bg = BgCast(k, es, name, bg_jobs, k.last_stg[:2], list(wl.toks))
                pe.wait(XE_tok[(b, 0)], XE_tok[(b, 1)], mult_tok.get(gi - 2), wres(gu_wtok[f]) if b == 0 else None)
                for c in range(8):
                    pe.e.matmul(gu[gi % 2][:, 0, :], lhsT=wg[:, c, f * 128:(f + 1) * 128], rhs=xt[:, c, :],
                                start=(c == 0), stop=(c == 7))
                for c in range(8):
                    ins = pe.e.matmul(gu[gi % 2][:, 1, :], lhsT=wu[:, c, f * 128:(f + 1) * 128], rhs=xt[:, c, :],
                                      start=(c == 0), stop=(c == 7))
                gtok = pe.mark(ins)
                if f == NFC - 1:
                    gu_last[b] = gtok
                act.wait(gtok, mult_tok.get(gi - 2))
                stok = act.mark(act.e.activation(out=sg[gi % 2][:], in_=gu[gi % 2][:, 0, :], func=AF.Silu))
                dve.wait(stok, D_tok.get(gi - NH))
                mult_tok[gi] = dve.mark(dve.e.tensor_tensor(out=hT[gi % NH][:], in0=sg[gi % 2][:], in1=gu[gi % 2][:, 1, :], op=ALU.mult))
                if f == 10 and b + 1 < NB:
                    stage_T(b + 1)
                    stage_XE(b + 1)
                if bg is not None and f in (1, 5, 9, 13, 17, 20):
                    bg.step()
                if f >= 1:
                    stage_down(b, f - 1)
            stage_down(b, NFC - 1)
            etoks = []
            for t in range(2):
                gt = b * 2 + t
                stt = ln_part_a(k, ctx, gt, [yp[t][0][:], yp[t][1][:]], xR[t], 2.0 * ALPHA, 4.0 * EPS,
                                dst[gt * 128:(gt + 1) * 128, :], [D_tok[b * NFC + NFC - 1], xR_tok[t]])
                xR_free[t] = stt
                etoks.extend(stt)
            for t in range(2):
                ln_part_b(k, ctx, b * 2 + t)
            ep_tok[b] = etoks
        barrier(k, [ctx["store_tok"], bg.finish() if bg is not None else None])


def xT_block_loader(k, es, name, src, ntile_list):
    nc = k.nc
    sb = lambda n, shape, dt: es.enter_context(nc.sbuf_tensor(f"{name}_{n}", shape, dt))
    st = {
        "xA": [sb(f"lxA{i}", [128, D], F32) for i in range(2)],
        "xbf": [sb(f"lxbf{i}", [128, D], BF16) for i in range(2)],
        "slot": k.slots(es, 2),
        "Tps": [es.enter_context(nc.psum_tensor(f"{name}_lT{i}", [128, 8, 128], BF16)) for i in range(2)],
        "cast_tok": [None, None], "T_tok": [None, None], "XE_tok": [None, None], "i": 0,
    }

    def emit(tile, dstT, dst_free_tok=None):
        i = st["i"]; st["i"] += 1
        s_ = i % 2
        k.sp.wait(st["cast_tok"][s_])
        lt = st["slot"][s_].dma(k.sp, st["xA"][s_][:], src[tile * 128:(tile + 1) * 128, :])
        k.pool.wait(lt, st["T_tok"][s_])
        st["cast_tok"][s_] = k.pool.mark(k.pool.e.tensor_copy(out=st["xbf"][s_][:], in_=st["xA"][s_][:]))
        k.pe.wait(st["cast_tok"][s_], st["XE_tok"][s_], k.ident_tok)
        for c in range(8):
            ins = k.pe.e.transpose(out=st["Tps"][s_][:, c, :], in_=st["xbf"][s_][:, c * 128:(c + 1) * 128], identity=k.ident[:])
        st["T_tok"][s_] = k.pe.mark(ins)
        k.act.wait(st["T_tok"][s_], dst_free_tok)
        st["XE_tok"][s_] = k.act.mark(k.act.e.activation(out=dstT, in_=st["Tps"][s_][:], func=AF.Copy))
        return st["XE_tok"][s_]
    return emit


def win_jobs(win_sb, win_d, col0, ncols):
    v = win_d.rearrange("(c p) f -> p c f", p=128)
    return [(win_sb[:, c, :], v[:, c, col0:col0 + ncols]) for c in range(8)]


def na_kt_set(il):
    return [0, 1, 2, 3] if il < 2 else list(range(il - 2, il + 3))


def na_variant(il, kt):
    return il * 4 + kt if il < 2 else 8 + (kt - il + 2)


def na_phase(k, x1_d, win_d, nab_d, mixT):
    nc = k.nc
    pe, act, dve, pool, sp = k.pe, k.act, k.dve, k.pool, k.sp
    with ExitStack() as es:
        sb = lambda n, shape, dt: es.enter_context(nc.sbuf_tensor(f"na_{n}", shape, dt))
        ps = lambda n, shape, dt: es.enter_context(nc.psum_tensor(f"na_{n}", shape, dt))
        KT = sb("KT", [128, 4, NKT_NA * 128], BF16)
        QT = [sb(f"QT{i}", [128, 4, OWN], BF16) for i in range(2)]
        VA = sb("VA", [128, NKT_NA, 8, 65], BF16)
        zer = sb("zer", [128, 512], BF16)
        nab = sb("nab", [128, 13, 1024], F32)
        nsl = k.slots(es, 1)[0]
        tz = pool.mark(pool.e.memset(zer[:], 0.0))
        dve.e.memset(QT[0][64:128, :, :], 0.0)
        dve.e.memset(QT[1][0:64, :, :], 0.0)
        tv1 = dve.mark(dve.e.memset(VA[:, :, :, 64:65], 1.0))
        pad_tok = tv1
        with ExitStack() as es2:
            sb2 = lambda n, shape, dt: es2.enter_context(nc.sbuf_tensor(f"nap_{n}", shape, dt))
            win = sb2("win", [128, 8, 1536], BF16)
            wtoks = load_bf16_weights(k, act, k.slots(es2, 1)[0], win_jobs(win, win_d, 0, 1536))
            for v in range(13):
                nab_tok = nsl.dma(act, nab[:, v, :], nab_d[v])
            x1T = [sb2(f"x1T{i}", [128, 8, 512], BF16) for i in range(2)]
            pp = [es2.enter_context(nc.psum_tensor(f"nap_pp{i}", [128, 512], F32)) for i in range(3)]
            emit = xT_block_loader(k, es2, "nap", x1_d, None)
            pp_free = [None] * 3
            blk_last_pe = [None, None]
            npp = 0
            for blk in range(5):
                ntile = 4 if blk < 4 else 2
                ntok = ntile * 128
                xt = x1T[blk % 2]
                xe = [emit(blk * 4 + t, xt[:, :, t * 128:(t + 1) * 128], blk_last_pe[blk % 2]) for t in range(ntile)]
                pe.wait(xe, wtoks)
                for kind in range(2):
                    if kind == 1 and blk >= 4:
                        continue
                    for hp in range(4):
                        col = (512 if kind == 0 else 0) + hp * 128
                        b_ = npp % 3; npp += 1
                        pe.wait(pp_free[b_])
                        for c in range(8):
                            ins = pe.e.matmul(pp[b_][:, :ntok], lhsT=win[:, c, col:col + 128], rhs=xt[:, c, :ntok], start=(c == 0), stop=(c == 7))
                        tk = pe.mark(ins)
                        act.wait(tk)
                        if kind == 0:
                            pp_free[b_] = act.mark(act.e.activation(out=KT[:, hp, blk * 512:blk * 512 + ntok], in_=pp[b_][:, :ntok], func=AF.Copy))
                        else:
                            act.wait(tv1)
                            act.e.activation(out=QT[0][0:64, hp, blk * 512:blk * 512 + ntok], in_=pp[b_][0:64, :ntok], func=AF.Copy, scale=0.125)
                            pp_free[b_] = act.mark(act.e.activation(out=QT[1][64:128, hp, blk * 512:blk * 512 + ntok], in_=pp[b_][64:128, :ntok], func=AF.Copy, scale=0.125))
                for t in range(ntile):
                    b_ = npp % 3; npp += 1
                    pe.wait(pp_free[b_])
                    for c in range(8):
                        ins = pe.e.matmul(pp[b_][:], lhsT=xt[:, c, t * 128:(t + 1) * 128], rhs=win[:, c, 1024:1536], start=(c == 0), stop=(c == 7))
                    tk = pe.mark(ins)
                    dve.wait(tk, tv1)
                    pp_free[b_] = dve.mark(dve.e.tensor_copy(out=VA[:, blk * 4 + t, :, 0:64], in_=pp[b_][:].rearrange("p (h e) -> p h e", e=64)))
                blk_last_pe[blk % 2] = tk
            barrier(k, [])
        NSP, NTM, NE, LA = 4, 3, 5, 3
        sps = [ps(f"s{i}", [128, 512], F32) for i in range(NSP)]
        acc = [ps(f"acc{i}", [128, 512], F32) for i in range(2)]
        tpo = ps("tpo", [128, 4, 128], BF16)
        tmp = [sb(f"tmp{i}", [128, 512], F32) for i in range(NTM)]
        E = [sb(f"E{i}", [128, 512], BF16) for i in range(NE)]
        rr = sb("rr", [128, 8], F32)
        nao = [sb(f"nao{i}", [128, 512], BF16) for i in range(2)]
        accs = sb("accs", [128, 2, 260], F32)
        sps_free = [None] * NSP
        tmp_free = [None] * NTM
        E_free = [None] * NE
        acc_free = [None, None]
        nao_free = [None, None]
        nao_tok = {}
        tpo_free = None
        accs_free = None
        ns = 0
        E_tok = {}

        def finish_il(il_):
            nonlocal tpo_free
            s2 = il_ % 2
            pe.wait(nao_tok[il_], tpo_free)
            for hp in range(4):
                ins = pe.e.transpose(out=tpo[:, hp, :], in_=nao[s2][:, hp * 128:(hp + 1) * 128], identity=k.ident[:])
            tt = pe.mark(ins)
            nao_free[s2] = tt
            act.wait(tt)
            tpo_free = act.mark(act.e.activation(out=mixT[:, 0:4, il_ * 128:(il_ + 1) * 128], in_=tpo[:], func=AF.Copy))

        def steps_of(il_):
            return [(kt, hb) for kt in na_kt_set(il_) for hb in range(2)]

        def emit_S(il_, si):
            nonlocal ns
            kt, hb = steps_of(il_)[si]
            g = ns; ns += 1
            pe.wait(sps_free[g % NSP], pad_tok)
            for hl in range(4):
                ins = pe.e.matmul(sps[g % NSP][:, hl * 128:(hl + 1) * 128], lhsT=KT[:, hl, kt * 128:(kt + 1) * 128],
                                  rhs=QT[hb][:, hl, il_ * 128:(il_ + 1) * 128], start=True, stop=True)
            tk = pe.mark(ins)
            v = na_variant(il_, kt)
            dve.wait(tk, tmp_free[g % NTM], nab_tok)
            t1 = dve.mark(dve.e.tensor_tensor(out=tmp[g % NTM][:], in0=nab[:, v, hb * 512:(hb + 1) * 512], in1=sps[g % NSP][:], op=ALU.add))
            sps_free[g % NSP] = t1
            act.wait(t1, E_free[g % NE])
            t2 = act.mark(act.e.activation(out=E[g % NE][:], in_=tmp[g % NTM][:], func=AF.Exp))
            tmp_free[g % NTM] = t2
            E_tok[(il_, si)] = (t2, g)

        def emit_AV(il_, si, last):
            kt, hb = steps_of(il_)[si]
            t2, g = E_tok.pop((il_, si))
            pe.wait(t2)
            for hl in range(4):
                h = 2 * hl + hb
                ins = pe.e.matmul(acc[hb][:, hl * 65:(hl + 1) * 65], lhsT=E[g % NE][:, hl * 128:(hl + 1) * 128],
                                  rhs=VA[:, kt, h, :], start=False, stop=(last and hl == 3))
            E_free[g % NE] = pe.mark(ins)
            return E_free[g % NE]

        pend_il = None
        for si in range(LA):
            emit_S(0, si)
        for il in range(16):
            nst = len(steps_of(il))
            for hb in range(2):
                pe.wait(acc_free[hb], tz)
                pe.e.matmul(acc[hb][:], lhsT=zer[:, 0:128], rhs=zer[:], start=True, stop=False)
            last_av = [None, None]
            for si in range(nst):
                if si + LA < nst:
                    emit_S(il, si + LA)
                last_av[steps_of(il)[si][1]] = emit_AV(il, si, si >= nst - 2)
                if si == 3 and pend_il is not None:
                    finish_il(pend_il)
                    pend_il = None
            s_ = il % 2
            evt = []
            for hb in range(2):
                act.wait(last_av[hb], accs_free)
                acc_free[hb] = act.mark(act.e.activation(out=accs[:, hb, :], in_=acc[hb][:, 0:260], func=AF.Copy))
                evt.append(acc_free[hb])
            if il + 1 < 16:
                for si in range(LA):
                    emit_S(il + 1, si)
            for hb in range(2):
                accv = accs[:, hb, :].rearrange("p (h e) -> p h e", e=65)
                dve.wait(evt, nao_free[s_])
                tr = dve.mark(dve.e.reciprocal(out=rr[:, hb * 4:(hb + 1) * 4], in_=accv[:, :, 64]))
                dve.wait(tr)
                for hl in range(4):
                    h = 2 * hl + hb
                    ins = dve.e.tensor_scalar(out=nao[s_][:, h * 64:(h + 1) * 64], in0=accv[:, hl, 0:64], scalar1=rr[:, hb * 4 + hl:hb * 4 + hl + 1], scalar2=None, op0=ALU.mult)
                accs_free = dve.mark(ins)
            nao_tok[il] = accs_free
            pend_il = il
        finish_il(pend_il)
        barrier(k, [])


def diff_phase(k, x1_d, win_d, aug_d, atab_d, cst_d, lamv_d, subg_d, mixT):
    nc = k.nc
    pe, act, dve, pool, sp = k.pe, k.act, k.dve, k.pool, k.sp
    SL = [2.0 ** (-8.0 * (h + 1) / 4) for h in range(4)]
    with ExitStack() as es:
        sb = lambda n, shape, dt: es.enter_context(nc.sbuf_tensor(f"df_{n}", shape, dt))
        ps = lambda n, shape, dt: es.enter_context(nc.psum_tensor(f"df_{n}", shape, dt))
        KT = [sb(f"KT{i}", [128, 4, SEQ], BF16) for i in range(2)]
        QT = sb("QT", [128, 4, OWN], BF16)
        VA = sb("VA", [128, 32, 4, 129], BF16)
        cst = sb("cst", [128, 256], F32)
        lamv = sb("lamv", [128, 4, 64], F32)
        g8 = sb("g8", [128, 128], F32)
        zer = sb("zer", [128, 512], BF16)
        sm = sb("sm", [128, 8], F32)
        junk = sb("junk", [128, 64], F32)
        mhalf = sb("mhalf", [128, 1], F32)
        tz = pool.mark(pool.e.memset(zer[:], 0.0))
        pool.e.memset(mhalf[:], -0.5)
        ones_t = sb("ones_t", [128, 512], BF16)
        pad_tok = pool.mark(pool.e.memset(ones_t[:], 1.0))
        tv1 = dve.mark(dve.e.memset(VA[:, :, :, 128:129], 1.0))
        csl = k.slots(es, 1)[0]
        csl.dma(sp, cst[:], cst_d)
        csl.dma(sp, lamv[:], lamv_d)
        ctok = csl.dma(sp, g8[:], subg_d)
        dve.wait(ctok)
        a0 = dve.mark(dve.e.tensor_scalar(out=g8[:], in0=g8[:], scalar1=1.0 - LAM_INIT, scalar2=None, op0=ALU.mult))
        dve.wait(a0)
        a1 = dve.mark(dve.e.scalar_tensor_tensor(out=junk[:], in0=lamv[:, 0, :], scalar=1.0, op0=ALU.mult, in1=lamv[:, 1, :], op1=ALU.mult, accum_out=sm[:, 0:1]))
        dve.wait(a1)
        a2 = dve.mark(dve.e.scalar_tensor_tensor(out=junk[:], in0=lamv[:, 2, :], scalar=1.0, op0=ALU.mult, in1=lamv[:, 3, :], op1=ALU.mult, accum_out=sm[:, 1:2]))
        act.wait(a2)
        a3 = act.mark(act.e.activation(out=sm[:, 2:4], in_=sm[:, 0:2], func=AF.Exp))
        dve.wait(a3)
        a4 = dve.mark(dve.e.tensor_tensor(out=sm[:, 5:6], in0=sm[:, 3:4], in1=sm[:, 2:3], op=ALU.subtract))
        dve.wait(a4)
        lam_tok = dve.mark(dve.e.tensor_scalar(out=sm[:, 4:5], in0=sm[:, 5:6], scalar1=-LAM_INIT, scalar2=None, op0=ALU.add))
        neglam = sm[:, 4:5]
        with ExitStack() as es2:
            sb2 = lambda n, shape, dt: es2.enter_context(nc.sbuf_tensor(f"dfp_{n}", shape, dt))
            win = sb2("win", [128, 8, 1536], BF16)
            wtoks = load_bf16_weights(k, act, k.slots(es2, 1)[0], win_jobs(win, win_d, 1536, 1536))
            x1T = [sb2(f"x1T{i}", [128, 8, 512], BF16) for i in range(2)]
            pp = [es2.enter_context(nc.psum_tensor(f"dfp_pp{i}", [128, 512], F32)) for i in range(3)]
            emit = xT_block_loader(k, es2, "dfp", x1_d, None)
            pp_free = [None] * 3
            blk_last_pe = [None, None]
            npp = 0
            for blk in range(8):
                xt = x1T[blk % 2]
                xe = [emit(blk * 4 + t, xt[:, :, t * 128:(t + 1) * 128], blk_last_pe[blk % 2]) for t in range(4)]
                pe.wait(xe, wtoks)
                for kind in range(2):
                    if kind == 1 and blk >= 4:
                        continue
                    for h in range(4):
                        col = (512 if kind == 0 else 0) + h * 128
                        b_ = npp % 3; npp += 1
                        pe.wait(pp_free[b_])
                        for c in range(8):
                            ins = pe.e.matmul(pp[b_][:], lhsT=win[:, c, col:col + 128], rhs=xt[:, c, :], start=(c == 0), stop=(c == 7))
                        tk = pe.mark(ins)
                        act.wait(tk)
                        if kind == 0:
                            act.wait(pad_tok)
                            k0 = act.mark(act.e.activation(out=KT[0][:, h, blk * 512:(blk + 1) * 512], in_=pp[b_][:], func=AF.Copy))
                            k1 = act.mark(act.e.activation(out=KT[1][:, h, blk * 512:(blk + 1) * 512], in_=pp[b_][:], func=AF.Copy))
                            pp_free[b_] = k1
                            act.wait(k0, k1)
                            act.e.activation(out=KT[0][64:66, h, blk * 512:(blk + 1) * 512], in_=ones_t[64:66, :], func=AF.Copy)
                            kfix_tok = act.mark(act.e.activation(out=KT[1][0:2, h, blk * 512:(blk + 1) * 512], in_=ones_t[0:2, :], func=AF.Copy))
                        else:
                            pp_free[b_] = act.mark(act.e.activation(out=QT[:, h, blk * 512:(blk + 1) * 512], in_=pp[b_][:], func=AF.Copy, scale=0.125))
                for t in range(4):
                    b_ = npp % 3; npp += 1
                    pe.wait(pp_free[b_])
                    for c in range(8):
                        ins = pe.e.matmul(pp[b_][:], lhsT=xt[:, c, t * 128:(t + 1) * 128], rhs=win[:, c, 1024:1536], start=(c == 0), stop=(c == 7))
                    tk = pe.mark(ins)
                    dve.wait(tk, tv1)
                    pp_free[b_] = dve.mark(dve.e.tensor_copy(out=VA[:, blk * 4 + t, :, 0:128], in_=pp[b_][:].rearrange("p (h e) -> p h e", e=128)))
                blk_last_pe[blk % 2] = tk
            barrier(k, [])
        NSP, NTM, NE = 2, 2, 4
        atab = sb("atab", [128, 2, 896], F32)
        augtab = sb("augtab", [128, 4, 2, 512], BF16)
        Qs = [[[sb(f"Qs{u}{v}{m}", [128, 512], BF16) for m in range(2)] for v in range(3)] for u in range(2)]
        asl = k.slots(es, 1)[0]
        asl.dma(sp, atab[:], atab_d)
        atok = asl.dma(sp, augtab[:], aug_d)
        qz = None
        for u in range(2):
            for v in range(3):
                for m in range(2):
                    qz = pool.mark(pool.e.memset(Qs[u][v][m][:], 0.0))
        sps = [ps(f"s{i}", [128, 2, 512], F32) for i in range(NSP)]
        acc = [ps(f"acc{i}", [128, 512], F32) for i in range(3)]
        tpo = ps("tpo", [128, 128], BF16)
        tmp = [sb(f"tmp{i}", [128, 2, 512], F32) for i in range(NTM)]
        E = [sb(f"E{i}", [128, 2, 512], BF16) for i in range(NE)]
        accs = sb("accs", [128, 3, 387], F32)
        rr = sb("rr", [128, 4], F32)
        tq = sb("tq", [128, 128], F32)
        oq = sb("oq", [128, 128], F32)
        sps_free = [None] * NSP
        tmp_free = [None] * NTM
        E_free = [None] * NE
        acc_free = [None] * 3
        accs_free = None
        tpo_free = None
        ns = 0
        unit_last_S = {}
        units = [(h_, qb_) for h_ in range(4) for qb_ in range(4)]
        yq = [[sb(f"yq{u_}{q_}", [128, 128], BF16) for q_ in range(4)] for u_ in range(2)]
        yq_free = {}
        yq_tok = {}
        qtoks = {}

        def build_Qs(ui):
            h_, qb_ = units[ui]
            u_ = ui % 2
            pool.wait(qz, atok, unit_last_S.get(ui - 2), ctok)
            for m in range(2):
                r0 = 64 * m
                a0 = 64 - 64 * m
                for v in range(3):
                    qtok = pool.mark(pool.e.tensor_copy(out=Qs[u_][v][m][r0:r0 + 64, :], in_=QT[r0:r0 + 64, h_, qb_ * 512:(qb_ + 1) * 512]))
                for v in range(2):
                    qtok = pool.mark(pool.e.tensor_copy(out=Qs[u_][v][m][a0:a0 + 2, :], in_=augtab[a0:a0 + 2, h_, v, :]))
            qtoks[ui] = qtok

        def finish_transposes(ui):
            nonlocal tpo_free
            h_, qb_ = units[ui]
            for qt in range(4):
                pe.wait(yq_tok[(ui, qt)], tpo_free)
                tt = pe.mark(pe.e.transpose(out=tpo[:], in_=yq[ui % 2][qt][:], identity=k.ident[:]))
                yq_free[(ui % 2, qt)] = tt
                act.wait(tt)
                tok0 = (qb_ * 4 + qt) * 128
                tpo_free = act.mark(act.e.activation(out=mixT[:, 4 + h_, tok0:tok0 + 128], in_=tpo[:], func=AF.Copy))

        evts = {}

        def epilogue(ui):
            nonlocal accs_free
            u = ui % 2
            evt = evts[ui]
            for qt in range(4):
                g0, g1 = qt, 4 + qt
                O0 = accs[:, g0 // 3, (g0 % 3) * 129:(g0 % 3) * 129 + 129]
                O1 = accs[:, g1 // 3, (g1 % 3) * 129:(g1 % 3) * 129 + 129]
                dve.wait(evt, lam_tok)
                e1 = dve.mark(dve.e.reciprocal(out=rr[:, 0:1], in_=O0[:, 128:129]))
                e2 = dve.mark(dve.e.reciprocal(out=rr[:, 1:2], in_=O1[:, 128:129]))
                dve.wait(e1, e2)
                e3 = dve.mark(dve.e.tensor_tensor(out=rr[:, 2:3], in0=rr[:, 1:2], in1=neglam, op=ALU.mult))
                dve.wait(e3)
                e4 = dve.mark(dve.e.tensor_scalar(out=tq[:], in0=O1[:, 0:128], scalar1=rr[:, 2:3], scalar2=None, op0=ALU.mult))
                dve.wait(e4)
                e5 = dve.mark(dve.e.scalar_tensor_tensor(out=oq[:], in0=O0[:, 0:128], scalar=rr[:, 0:1], op0=ALU.mult, in1=tq[:], op1=ALU.add))
                dve.wait(e5)
                e6 = dve.mark(dve.e.scalar_tensor_tensor(out=tq[:], in0=oq[:], scalar=1.0 / 128.0, op0=ALU.mult, in1=oq[:], op1=ALU.mult, accum_out=rr[:, 3:4]))
                dve.wait(e6)
                e7 = dve.mark(dve.e.tensor_scalar(out=rr[:, 3:4], in0=rr[:, 3:4], scalar1=EPS, scalar2=None, op0=ALU.add))
                pool.wait(e7)
                e8 = pool.mark(pool.e.tensor_tensor(out=rr[:, 3:4], in0=rr[:, 3:4], in1=mhalf[:], op=ALU.pow))
                dve.wait(e8, yq_free.get((u, qt)))
                e9 = dve.mark(dve.e.scalar_tensor_tensor(out=yq[u][qt][:], in0=oq[:], scalar=rr[:, 3:4], op0=ALU.mult, in1=g8[:], op1=ALU.mult))
                yq_tok[(ui, qt)] = e9
                if qt == 3:
                    accs_free = e9

        build_Qs(0)
        pending = None
        pend_epi = None
        tr_at = -1
        E_tok = {}

        def emit_S(ui, kt):
            nonlocal ns
            h, qb = units[ui]
            u = ui % 2
            g = ns; ns += 1
            delta = qb * 512 - kt * 128
            v = 0 if delta >= 128 else (1 if delta <= -512 else 2)
            pe.wait(sps_free[g % NSP], qtoks[ui], pad_tok)
            for m in range(2):
                ins = pe.e.matmul(sps[g % NSP][:, m, :], lhsT=KT[m][:, h, kt * 128:(kt + 1) * 128],
                                  rhs=Qs[u][v][m][:], start=True, stop=True)
            tk = pe.mark(ins)
            unit_last_S[ui] = tk
            if v == 2:
                dve.wait(tk, tmp_free[g % NTM], atok)
                t1 = dve.mark(dve.e.scalar_tensor_tensor(out=tmp[g % NTM][:], in0=atab[:, :, delta + 384:delta + 384 + 512], scalar=float(-SL[h]),
                                                         op0=ALU.mult, in1=sps[g % NSP][:], op1=ALU.add))
                sps_free[g % NSP] = t1
                act.wait(t1, E_free[g % NE])
                t2 = act.mark(act.e.activation(out=E[g % NE][:], in_=tmp[g % NTM][:], func=AF.Exp))
                tmp_free[g % NTM] = t2
            else:
                n = abs(delta) // 128
                col = (h * 32 + n) * 2 + v
                act.wait(tk, E_free[g % NE], ctok)
                t2 = act.mark(act.e.activation(out=E[g % NE][:], in_=sps[g % NSP][:], func=AF.Exp, bias=cst[:, col:col + 1], scale=1.0))
                sps_free[g % NSP] = t2
            E_tok[(ui, kt)] = (t2, g)

        def emit_AV(ui, kt):
            h, qb = units[ui]
            t2, g = E_tok.pop((ui, kt))
            pe.wait(t2, tv1)
            for m in range(2):
                for qt in range(4):
                    gi = m * 4 + qt
                    ins = pe.e.matmul(acc[gi // 3][:, (gi % 3) * 129:(gi % 3) * 129 + 129], lhsT=E[g % NE][:, m, qt * 128:(qt + 1) * 128],
                                      rhs=VA[:, kt, h, :], start=False, stop=(kt == 31 and gi in (2, 5, 7)))
            E_free[g % NE] = pe.mark(ins)
            return E_free[g % NE]

        emit_S(0, 0)
        emit_S(0, 1)
        for ui, (h, qb) in enumerate(units):
            if True:
                for j in range(3):
                    pe.wait(acc_free[j], tz)
                    pe.e.matmul(acc[j][:], lhsT=zer[:, 0:128], rhs=zer[:], start=True, stop=False)
                for kt in range(32):
                    if kt + 2 < 32:
                        emit_S(ui, kt + 2)
                    last = emit_AV(ui, kt)
                    if kt == 6 and ui + 1 < len(units):
                        build_Qs(ui + 1)
                    if pend_epi is not None and kt == min(4 * qb + 2, 14):
                        epilogue(pend_epi)
                        pending = pend_epi
                        pend_epi = None
                        tr_at = kt + 12
                    if pending is not None and pend_epi is None and kt == tr_at:
                        finish_transposes(pending)
                        pending = None
                evt = []
                for j in range(3):
                    act.wait(last, accs_free)
                    acc_free[j] = act.mark(act.e.activation(out=accs[:, j, :], in_=acc[j][:, 0:387], func=AF.Copy))
                    evt.append(acc_free[j])
                evts[ui] = evt
                pend_epi = ui
                if ui + 1 < len(units):
                    emit_S(ui + 1, 0)
                    emit_S(ui + 1, 1)
        epilogue(pend_epi)
        finish_transposes(pend_epi)
        barrier(k, [])


def wout_phase(k, x1_d, wout_d, g_d, b_d, mixT, x2_d):
    nc = k.nc
    pe, act, dve, pool, sp = k.pe, k.act, k.dve, k.pool, k.sp
    with ExitStack() as es:
        sb = lambda n, shape, dt: es.enter_context(nc.sbuf_tensor(f"wo_{n}", shape, dt))
        wo = sb("wo", [128, 8, D], BF16)
        v = wout_d.rearrange("(c p) f -> p c f", p=128)
        wtoks = load_bf16_weights(k, sp, k.slots(es, 1)[0], [(wo[:, c, :], v[:, c, :]) for c in range(8)])
        NW = 4
        ctx = ln_ctx(k, es, "wo", g_d, b_d, nsl=NW)
        xR = [sb(f"xR{i}", [128, D], F32) for i in range(NW)]
        xsl = k.slots(es, NW)
        yp = [[es.enter_context(nc.psum_tensor(f"wo_y{t}{h}", [128, 512], F32)) for h in range(2)] for t in range(NW)]
        xR_free = [None] * NW
        yp_free = [None] * NW
        lts = {}

        def issue_load(t_):
            sl_ = t_ % NW
            sp.wait(xR_free[sl_])
            lts[t_] = xsl[sl_].dma(sp, xR[sl_][:], x1_d[t_ * 128:(t_ + 1) * 128, :])

        for t_ in range(NW - 1):
            issue_load(t_)
        for t in range(16):
            s_ = t % NW
            if t + NW - 1 < 16:
                issue_load(t + NW - 1)
            lt = lts[t]
            pe.wait(wtoks, yp_free[s_])
            for hh in range(2):
                for c in range(8):
                    ins = pe.e.matmul(yp[s_][hh][:], lhsT=mixT[:, c, t * 128:(t + 1) * 128], rhs=wo[:, c, hh * 512:(hh + 1) * 512], start=(c == 0), stop=(c == 7))
            tk = pe.mark(ins)
            stt = ln_part_a(k, ctx, t, [yp[s_][0][:], yp[s_][1][:]], xR[s_], ALPHA, EPS, x2_d[t * 128:(t + 1) * 128, :], [tk, lt])
            xR_free[s_] = stt
            yp_free[s_] = stt
            if t >= 1:
                ln_part_b(k, ctx, t - 1)
        ln_part_b(k, ctx, 15)
        barrier(k, [ctx["store_tok"]])


def build_program(stop=None):
    nc = bass.Bass("TRN2", target_bir_lowering=False)
    dram_in = lambda n, shape, dt=F32: nc.dram_tensor(n, shape, dt, kind="ExternalInput").ap()
    x = dram_in("x", [SEQ, D])
    wg1 = dram_in("wg1", [D, DFF]); wu1 = dram_in("wu1", [D, DFF]); wd1 = dram_in("wd1", [DFF, D])
    wg2 = dram_in("wg2", [D, DFF]); wu2 = dram_in("wu2", [D, DFF]); wd2 = dram_in("wd2", [DFF, D])
    win = dram_in("win", [D, 3072]); wout = dram_in("wout", [D, D])
    ln1g = dram_in("ln1g", [128, D]); ln1b = dram_in("ln1b", [128, D])
    ln2g = dram_in("ln2g", [128, D]); ln2b = dram_in("ln2b", [128, D])
    ln3g = dram_in("ln3g", [128, D]); ln3b = dram_in("ln3b", [128, D])
    ident_d = dram_in("ident", [128, 128], BF16)
    nab = dram_in("nab", [13, 128, 1024])
    augt = dram_in("augt", [128, 4, 2, 512], BF16); atab = dram_in("atab", [128, 2, 896]); cst = dram_in("cst", [128, 256])
    lamv = dram_in("lamv", [128, 4, 64]); subg = dram_in("subg", [128, 128])
    zeros_ones = dram_in("zeros_ones", [2, 64, 4 * SEQ], BF16)
    out = nc.dram_tensor("out", [OWN, D], F32, kind="ExternalOutput").ap()
    x1_d = nc.dram_tensor("x1_scratch", [SEQ, D], F32, kind="Internal").ap()
    x2_d = nc.dram_tensor("x2_scratch", [OWN, D], F32, kind="Internal").ap()
    win_bf = nc.dram_tensor("win_bf", [D, 3072], BF16, kind="Internal").ap()
    wout_bf = nc.dram_tensor("wout_bf", [D, D], BF16, kind="Internal").ap()
    wg2_bf = nc.dram_tensor("wg2_bf", [D, DFF], BF16, kind="Internal").ap()
    wu2_bf = nc.dram_tensor("wu2_bf", [D, DFF], BF16, kind="Internal").ap()
    wd2_bf = nc.dram_tensor("wd2_bf", [DFF, D], BF16, kind="Internal").ap()

    def pieces(src, dst, width):
        sv = src.rearrange("(c p) f -> p c f", p=128)
        dv = dst.rearrange("(c p) f -> p c f", p=128)
        out_ = []
        for c in range(sv.shape[1]):
            for o in range(0, sv.shape[2], width):
                out_.append((sv[:, c, o:o + width], dv[:, c, o:o + width]))
        return out_
    bg_jobs = (pieces(win, win_bf, 1024) + pieces(wout, wout_bf, 1024) + pieces(wg2, wg2_bf, 1408)
               + pieces(wu2, wu2_bf, 1408) + pieces(wd2, wd2_bf, 1024))
    dbg = None
    if stop is not None:
        dbg = nc.dram_tensor("dbg", [SEQ, D], F32, kind="ExternalOutput").ap()
    with ExitStack() as es:
        k = K()
        k.nc = nc
        k.pe = Eng(nc, nc.tensor, "pe", es)
        k.act = Eng(nc, nc.scalar, "act", es)
        k.dve = Eng(nc, nc.vector, "dve", es)
        k.pool = Eng(nc, nc.gpsimd, "pool", es)
        k.sp = Eng(nc, nc.sync, "sp", es)
        k.engs = [k.pe, k.act, k.dve, k.pool, k.sp]
        k.es_global = es
        k.slot_pool = []
        k.zeros_ones = zeros_ones
        k.ident = es.enter_context(nc.sbuf_tensor("ident_sb", [128, 128], BF16))
        isl = k.slots(es, 1)[0]
        k.ident_tok = isl.dma(k.sp, k.ident[:], ident_d)
        if stop == "A":
            ffn_phase(k, "f1", x, SEQ, wg1, wu1, wd1, ln1g, ln1b, dbg, bg_jobs=bg_jobs)
            return nc
        if stop in ("NA", "DF", "W"):
            x1_src = x
            with ExitStack() as esb:
                inb = [esb.enter_context(nc.sbuf_tensor(f"dbg_in{i}", [128, 1408], F32)) for i in range(2)]
                bgc = BgCast(k, esb, "dbgc", bg_jobs[:32], inb, None)
                barrier(k, [bgc.finish()])
        else:
            ffn_phase(k, "f1", x, SEQ, wg1, wu1, wd1, ln1g, ln1b, x1_d, bg_jobs=bg_jobs)
            x1_src = x1_d
        mix_cm = nc.sbuf_tensor("mixT", [128, 8, OWN], BF16, side="right")
        mixT = mix_cm.__enter__()
        if stop != "DF":
            na_phase(k, x1_src, win_bf, nab, mixT)
        if stop != "NA":
            diff_phase(k, x1_src, win_bf, augt, atab, cst, lamv, subg, mixT)
        if stop in ("NA", "DF"):
            with ExitStack() as es3:
                tmpf = es3.enter_context(nc.sbuf_tensor("dbg_tmp", [128, 8, OWN], F32))
                k.dve.wait((k.act, k.act.n))
                c0_ = 0 if stop == "NA" else 4
                k.pool.wait((k.act, k.act.n))
                tk0 = k.pool.mark(k.pool.e.memset(tmpf[:], 0.0))
                k.dve.wait(tk0)
                tk = k.dve.mark(k.dve.e.tensor_copy(out=tmpf[:, c0_:c0_ + 4, :], in_=mixT[:, c0_:c0_ + 4, :]))
                k.sp.wait(tk)
                sl = k.slots(es3, 1)[0]
                for c in range(8):
                    for hf in range(2):
                        t_ = sl.dma(k.sp, dbg[(c * 2 + hf) * 128:(c * 2 + hf + 1) * 128, :], tmpf[:, c, hf * 1024:(hf + 1) * 1024])
                barrier(k, [t_])
            mix_cm.__exit__(None, None, None)
            return nc
        with ExitStack() as esf2:
            pre = None
            if stop is None:
                wg2s, wu2s, _ = alloc_ffn_weights(k, esf2, "f2", with_wd=False)
                wdA2 = esf2.enter_context(nc.sbuf_tensor("f2_wdA", [128, NFC // 2, D], BF16))
                wsl = k.slots(esf2, 1)[0]
                jobs2 = ([(wg2s[:, c, :], wg2_bf.rearrange("(c p) f -> p c f", p=128)[:, c, :]) for c in range(8)]
                         + [(wu2s[:, c, :], wu2_bf.rearrange("(c p) f -> p c f", p=128)[:, c, :]) for c in range(8)]
                         + [(wdA2[:, c:c + 1, :], wd2_bf.rearrange("(c p) f -> p c f", p=128)[:, c:c + 1, :]) for c in range(NFC // 2)])
                pre = (wg2s, wu2s, wdA2, wd2_bf, load_bf16_weights(k, k.act, wsl, jobs2))
            wout_phase(k, x1_src, wout_bf, ln2g, ln2b, mixT, x2_d if stop is None else dbg)
            mix_cm.__exit__(None, None, None)
            if stop == "W":
                return nc
            ffn_phase(k, "f2", x2_d, OWN, wg2, wu2, wd2, ln3g, ln3b, out, pre=pre)
    return nc


def _na_tables(rpb, rev):
    out = np.full((13, 128, 8, 128), -30000.0, np.float32)
    p = np.arange(128)

    def coords(tile):
        t = tile * 128 + p
        r, c = t // 64, t % 64
        if rev:
            r, c = 63 - r, 63 - c
        return r, c

    def fill(v, il, kt):
        rk, ck = coords(kt)
        rq, cq = coords(il)
        r0 = np.clip(rq - 4, 0, 56)
        c0 = np.clip(cq - 8, 0, 48)
        RK, RQ = rk[:, None], rq[None, :]
        CK, CQ = ck[:, None], cq[None, :]
        ok = (RK >= r0[None, :]) & (RK <= r0[None, :] + 7) & (CK >= c0[None, :]) & (CK <= c0[None, :] + 15)
        dr = np.clip(RK - RQ + 7, 0, 14)
        dc = np.clip(CK - CQ + 15, 0, 30)
        vals = rpb[:, dr, dc]
        tile = np.where(ok[None], vals, np.float32(-30000.0)).astype(np.float32)
        out[v] = tile.transpose(1, 0, 2)[:, [0, 2, 4, 6, 1, 3, 5, 7], :]
        return ok

    for il in range(2):
        for kt in range(4):
            fill(na_variant(il, kt), il, kt)
    for dj in range(-2, 3):
        fill(na_variant(8, 8 + dj), 8, 8 + dj)
    return np.ascontiguousarray(out.reshape(13, 128, 1024))


def prep_inputs(inputs, c):
    b, h = c // 2, c % 2
    xb = np.ascontiguousarray(inputs["x"][b])
    if h == 1:
        xb = np.ascontiguousarray(xb[::-1])
    f32 = lambda v: np.ascontiguousarray(np.asarray(v, np.float32))
    rep = lambda v: np.ascontiguousarray(np.broadcast_to(np.asarray(v, np.float32).reshape(1, -1), (128, np.asarray(v).size)))
    p = np.arange(128, dtype=np.float32)[:, None]
    jtab = (np.arange(512, dtype=np.float32)[None, :] - p).astype(np.float32)
    atab = np.abs(np.arange(896, dtype=np.float32)[None, :] - p - 384.0).astype(np.float32)
    atab = np.ascontiguousarray(np.stack([atab, atab], axis=1))
    cst = np.zeros((128, 4, 32, 2), np.float32)
    augt = np.zeros((128, 4, 2, 512), np.float32)
    jj = np.arange(512, dtype=np.float32)
    pp_ = np.arange(128, dtype=np.float32)
    for hh in range(4):
        sl = 2.0 ** (-8.0 * (hh + 1) / 4)
        for v in range(2):
            sgn = 1.0 if v == 0 else -1.0
            cst[:, hh, :, v] = sgn * sl * pp_[:, None] - sl * 128.0 * np.arange(32, dtype=np.float32)[None, :]
            hi = -sgn * sl * 256.0 * np.floor(jj / 256.0)
            lo = -sgn * sl * np.mod(jj, 256.0)
            for base in (0, 64):
                augt[base, hh, v] = hi
                augt[base + 1, hh, v] = lo
    cst = np.ascontiguousarray(cst.reshape(128, 256))
    augt = augt.astype(ml_dtypes.bfloat16)
    lamv = np.stack([rep(inputs["diff_lambda_q1"][0]), rep(inputs["diff_lambda_k1"][0]),
                     rep(inputs["diff_lambda_q2"][0]), rep(inputs["diff_lambda_k2"][0])], axis=1)
    m = {
        "x": xb,
        "wg1": f32(inputs["ffn1_w_gate"][0]), "wu1": f32(inputs["ffn1_w_up"][0]), "wd1": f32(inputs["ffn1_w_down"][0]),
        "wg2": f32(inputs["ffn2_w_gate"][0]), "wu2": f32(inputs["ffn2_w_up"][0]), "wd2": f32(inputs["ffn2_w_down"][0]),
        "win": f32(inputs["w_in"][0]), "wout": f32(inputs["w_out"][0]),
        "ln1g": rep(inputs["ln1_g"][0]), "ln1b": rep(inputs["ln1_b"][0]),
        "ln2g": rep(inputs["ln2_g"][0]), "ln2b": rep(inputs["ln2_b"][0]),
        "ln3g": rep(inputs["ln3_g"][0]), "ln3b": rep(inputs["ln3_b"][0]),
        "ident": np.eye(128, dtype=np.float32).astype(ml_dtypes.bfloat16),
        "nab": _na_tables(f32(inputs["na_rpb"][0]), h == 1),
        "augt": augt, "atab": atab, "cst": cst,
        "lamv": np.ascontiguousarray(lamv.astype(np.float32)), "subg": rep(inputs["diff_subln_g"][0]),
        "zeros_ones": np.stack([np.zeros((64, 4 * SEQ), np.float32), np.ones((64, 4 * SEQ), np.float32)]).astype(ml_dtypes.bfloat16),
    }
    return m


def kernel(**inputs):
    inputs = {k_: np.asarray(v) for k_, v in inputs.items()}
    nc = build_program()
    in_maps = [prep_inputs(inputs, c) for c in range(8)]
    res = run_bass_kernel_spmd(nc, in_maps, core_ids=list(range(8)))
    outp = np.empty((4, SEQ, D), np.float32)
    for c in range(8):
        b, h = c // 2, c % 2
        o = np.asarray(res.results[c]["out"])
        if h == 0:
            outp[b, :OWN] = o
        else:
            outp[b, OWN:] = o[::-1]
    return outp
```

```python
import numpy as np
from contextlib import ExitStack
import concourse.bass as bass
import concourse.mybir as mybir
from concourse.bass_utils import run_bass_kernel_spmd
import ml_dtypes

F32, BF16 = mybir.dt.float32, mybir.dt.bfloat16
AF = mybir.ActivationFunctionType
ALU = mybir.AluOpType

D = 1024
DFF = 2816
NFC = DFF // 128
SEQ = 4096
OWN = 2048
ALPHA = 2.0 ** 0.25
EPS = 1e-5
LAM_INIT = 0.2
NKT_NA = 18


def _flat(toks):
    out = []
    for t in toks:
        if t is None:
            continue
        if isinstance(t, list):
            out.extend(_flat(t))
        else:
            out.append(t)
    return out


class Eng:
    def __init__(self, nc, e, name, es):
        self.e = e
        self.name = name
        self.sem = es.enter_context(nc.semaphore("sem_" + name))
        self.n = 0
        self.seen = {}

    def wait(self, *toks):
        best = {}
        for src, v in _flat(list(toks)):
            if best.get(id(src), (None, 0))[1] < v:
                best[id(src)] = (src, v)
        for src, v in best.values():
            if self.seen.get(id(src), 0) >= v:
                continue
            self.e.wait_ge(src.sem, v)
            self.seen[id(src)] = v

    def mark(self, ins):
        ins.then_inc(self.sem, 1)
        self.n += 1
        return (self, self.n)


class Slot:
    def __init__(self, nc, name, es):
        self.sem = es.enter_context(nc.semaphore("dsem_" + name))
        self.n = 0
        self.busy = False

    def dma(self, q, out, in_):
        q.e.dma_start(out=out, in_=in_).then_inc(self.sem, 16)
        self.n += 16
        return (self, self.n)


class K:
    def slots(self, es, n):
        got = []
        for sl in self.slot_pool:
            if not sl.busy and len(got) < n:
                sl.busy = True
                got.append(sl)
        while len(got) < n:
            sl = Slot(self.nc, f"p{len(self.slot_pool)}", self.es_global)
            sl.busy = True
            self.slot_pool.append(sl)
            got.append(sl)

        def release():
            for sl in got:
                sl.busy = False
        es.callback(release)
        return got


def barrier(k, toks):
    toks = _flat(toks) + [(e, e.n) for e in k.engs if e.n > 0]
    for e in k.engs:
        e.wait(toks)


def copy_cast(eng, k, out, in_):
    if eng is k.act:
        return eng.e.activation(out=out, in_=in_, func=AF.Copy)
    return eng.e.tensor_copy(out=out, in_=in_)


class WeightLoader:
    def __init__(self, k, es, name, jobs, nslots=3, width=1408):
        nc = k.nc
        self.k = k
        self.jobs = jobs
        self.stg = [es.enter_context(nc.sbuf_tensor(f"{name}_stg{i}", [128, width], F32)) for i in range(nslots)]
        self.slots = k.slots(es, nslots)
        self.cast_tok = [None] * nslots
        self.engs = [k.dve, k.pool, k.act]
        self.toks = []
        self.i = 0
        k.last_stg = self.stg

    def emit(self, n):
        k = self.k
        nslots = len(self.stg)
        for _ in range(n):
            if self.i >= len(self.jobs):
                return
            i = self.i
            self.i += 1
            dst, src = self.jobs[i]
            s = i % nslots
            if len(src.shape) == 3:
                nel = src.shape[1] * src.shape[2]
                sv = self.stg[s][:, :nel].rearrange("p (a b) -> p a b", b=src.shape[2])
            else:
                nel = src.shape[-1]
                sv = self.stg[s][:, :nel]
            k.sp.wait(self.cast_tok[s])
            lt = self.slots[s].dma(k.sp, sv, src)
            e = self.engs[i % 3]
            e.wait(lt)
            self.cast_tok[s] = e.mark(copy_cast(e, k, dst, sv))
            self.toks.append(self.cast_tok[s])

    def done(self):
        return self.i >= len(self.jobs)


def load_cast_weights(k, es, name, jobs, nslots=3, width=1408):
    wl = WeightLoader(k, es, name, jobs, nslots, width)
    wl.emit(len(jobs))
    return wl.toks


def load_bf16_weights(k, q, slot, jobs):
    tok = None
    for dst, src in jobs:
        tok = slot.dma(q, dst, src)
    return tok


class BgCast:
    def __init__(self, k, es, name, jobs, in_bufs, first_tok):
        nc = k.nc
        self.k = k
        self.jobs = jobs
        self.inb = in_bufs
        self.outb = [es.enter_context(nc.sbuf_tensor(f"{name}_bgo{i}", [128, 1408], BF16)) for i in range(2)]
        self.in_slot = k.slots(es, 2)
        self.out_slot = k.slots(es, 2)
        self.load_tok = [first_tok, first_tok]
        self.cast_tok = [None, None]
        self.store_tok = [None, None]
        self.i = 0

    def step(self):
        k = self.k
        i = self.i
        n_jobs = len(self.jobs)
        if i > n_jobs + 1:
            return
        self.i += 1
        if 0 <= i - 2 < n_jobs:
            j = i - 2
            n = self.jobs[j][0].shape[-1]
            k.act.wait(self.cast_tok[j % 2])
            self.store_tok[j % 2] = self.out_slot[j % 2].dma(k.act, self.jobs[j][1], self.outb[j % 2][:, :n])
        if i < n_jobs:
            n = self.jobs[i][0].shape[-1]
            k.act.wait(self.cast_tok[i % 2], self.load_tok[i % 2] if i < 2 else None)
            self.load_tok[i % 2] = self.in_slot[i % 2].dma(k.act, self.inb[i % 2][:, :n], self.jobs[i][0])
        if 0 <= i - 1 < n_jobs:
            j = i - 1
            n = self.jobs[j][0].shape[-1]
            k.act.wait(self.load_tok[j % 2], self.store_tok[j % 2])
            self.cast_tok[j % 2] = k.act.mark(k.act.e.activation(out=self.outb[j % 2][:, :n], in_=self.inb[j % 2][:, :n], func=AF.Copy))

    def finish(self):
        while self.i <= len(self.jobs) + 1:
            self.step()
        return [t for t in self.store_tok if t is not None]


def ln_part_a(k, ctx, t, psum_halves, xres, xscale, eps, dst_ap, pre_toks):
    nsl = ctx["n"]
    dve, pool, sp = k.dve, k.pool, k.sp
    s = t % nsl
    r = ctx["r"][s]
    stats, mv, ve, rstd = ctx["stats"][s], ctx["mv"][s], ctx["ve"][s], ctx["rstd"][s]
    stt_toks = []
    for hh in range(2):
        dve.wait(pre_toks, ctx["store_tok"][s])
        ins = dve.e.scalar_tensor_tensor(out=r[:, hh * 512:(hh + 1) * 512], in0=xres[:, hh * 512:(hh + 1) * 512],
                                         scalar=float(xscale), op0=ALU.mult, in1=psum_halves[hh], op1=ALU.add)
        stt_toks.append(dve.mark(ins))
    st_toks = []
    for hh in range(2):
        dve.wait(stt_toks[hh])
        st_toks.append(dve.mark(dve.e.bn_stats(out=stats[:, hh * 6:(hh + 1) * 6], in_=r[:, hh * 512:(hh + 1) * 512])))
    dve.wait(st_toks)
    t1 = dve.mark(dve.e.bn_aggr(out=mv[:], in_=stats[:]))
    dve.wait(t1)
    t2 = dve.mark(dve.e.tensor_scalar(out=ve[:], in0=mv[:, 1:2], scalar1=float(eps), scalar2=None, op0=ALU.add))
    pool.wait(t2, ctx["gb_tok"])
    t3 = pool.mark(pool.e.tensor_tensor(out=rstd[:], in0=ve[:], in1=ctx["mhalf"][:], op=ALU.pow))
    dve.wait(t1, ctx["gb_tok"])
    t4 = dve.mark(dve.e.scalar_tensor_tensor(out=r[:], in0=r[:], scalar=mv[:, 0:1], op0=ALU.subtract, in1=ctx["g"][:], op1=ALU.mult))
    ctx["pend"][t] = (t3, t4, dst_ap)
    return stt_toks


def ln_part_b(k, ctx, t):
    dve, sp = k.dve, k.sp
    s = t % ctx["n"]
    r = ctx["r"][s]
    t3, t4, dst_ap = ctx["pend"].pop(t)
    dve.wait(t3, t4)
    t6 = dve.mark(dve.e.scalar_tensor_tensor(out=r[:], in0=r[:], scalar=ctx["rstd"][s][:, 0:1], op0=ALU.mult, in1=ctx["b"][:], op1=ALU.add))
    sp.wait(t6)
    ctx["store_tok"][s] = ctx["store_slot"][s].dma(sp, dst_ap, r[:])


def ln_ctx(k, es, name, g_d, b_d, nsl=2):
    nc = k.nc
    sb = lambda n, shape, dt: es.enter_context(nc.sbuf_tensor(f"{name}_{n}", shape, dt))
    ctx = {
        "n": nsl,
        "pend": {},
        "r": [sb(f"r{i}", [128, D], F32) for i in range(nsl)],
        "stats": [sb(f"stats{i}", [128, 12], F32) for i in range(nsl)],
        "mv": [sb(f"mv{i}", [128, 2], F32) for i in range(nsl)],
        "ve": [sb(f"ve{i}", [128, 1], F32) for i in range(nsl)],
        "rstd": [sb(f"rstd{i}", [128, 1], F32) for i in range(nsl)],
        "mhalf": sb("mhalf", [128, 1], F32),
        "g": sb("g", [128, D], F32),
        "b": sb("b", [128, D], F32),
        "store_slot": k.slots(es, nsl),
        "store_tok": [None] * nsl,
    }
    gs = k.slots(es, 1)[0]
    gs.dma(k.sp, ctx["g"][:], g_d)
    tg = gs.dma(k.sp, ctx["b"][:], b_d)
    tm = k.pool.mark(k.pool.e.memset(ctx["mhalf"][:], -0.5))
    ctx["gb_tok"] = [tg, tm]
    return ctx


def alloc_ffn_weights(k, es, name, with_wd=True):
    nc = k.nc
    wg = es.enter_context(nc.sbuf_tensor(f"{name}_wg", [128, 8, DFF], BF16))
    wu = es.enter_context(nc.sbuf_tensor(f"{name}_wu", [128, 8, DFF], BF16))
    wd = es.enter_context(nc.sbuf_tensor(f"{name}_wd", [128, NFC, D], BF16)) if with_wd else None
    return wg, wu, wd


def ffn_phase(k, name, x_src, T, wg_d, wu_d, wd_d, g_d, b_d, dst, pre=None, bg_jobs=None):
    nc = k.nc
    pe, act, dve, pool, sp = k.pe, k.act, k.dve, k.pool, k.sp
    NB = T // 256
    NH = 4
    with ExitStack() as es:
        sb = lambda n, shape, dt: es.enter_context(nc.sbuf_tensor(f"{name}_{n}", shape, dt))
        ps = lambda n, shape, dt: es.enter_context(nc.psum_tensor(f"{name}_{n}", shape, dt))
        bg = None
        emit_weights = None
        wl = None
        if pre is not None:
            wg, wu, wdA, wd_bf_d, wtoks = pre
            wdB = es.enter_context(nc.sbuf_tensor(f"{name}_wdB", [128, NFC // 2, D], BF16))
            wdv = wd_bf_d.rearrange("(c p) f -> p c f", p=128)
            wdB_tok = load_bf16_weights(k, k.act, k.slots(es, 1)[0],
                                        [(wdB[:, c:c + 1, :], wdv[:, NFC // 2 + c:NFC // 2 + c + 1, :]) for c in range(NFC // 2)])
            wd_ap = lambda fd, lo, hi: (wdA if fd < NFC // 2 else wdB)[:, fd % (NFC // 2), lo:hi]
            gu_wtok = {f: wtoks for f in range(NFC)}
            d_wtok = {f: None for f in range(NFC)}
        else:
            wg, wu, wd = alloc_ffn_weights(k, es, name)
            wgv = wg_d.rearrange("(c p) f -> p c f", p=128)
            wuv = wu_d.rearrange("(c p) f -> p c f", p=128)
            wdv = wd_d.rearrange("(c p) d -> p c d", p=128)
            jobs = []
            for fg in range(NFC // 2):
                cs = slice(fg * 256, (fg + 1) * 256)
                for c0 in (0, 4):
                    jobs.append((wg[:, c0:c0 + 4, cs], wgv[:, c0:c0 + 4, cs]))
                    jobs.append((wu[:, c0:c0 + 4, cs], wuv[:, c0:c0 + 4, cs]))
                for c in (2 * fg, 2 * fg + 1):
                    jobs.append((wd[:, c, :], wdv[:, c, :]))
            wdB_tok = None
            wd_ap = lambda fd, lo, hi: wd[:, fd, lo:hi]
            gu_wtok, d_wtok = {}, {}

            wl = WeightLoader(k, es, name, jobs)
            for f in range(NFC):
                gu_wtok[f] = ("wl", (f // 2) * 6, (f // 2) * 6 + 4)
                d_wtok[f] = ("wl", (f // 2) * 6 + 4 + (f % 2), (f // 2) * 6 + 5 + (f % 2))

            def emit_weights():
                wl.emit(12)
        ctx = ln_ctx(k, es, name, g_d, b_d)

        xA = [sb(f"xA{i}", [128, D], F32) for i in range(2)]
        xA_slot = k.slots(es, 2)
        xR = [sb(f"xR{i}", [128, D], F32) for i in range(2)]
        xR_slot = k.slots(es, 2)
        xbf = [sb(f"xbf{i}", [128, D], BF16) for i in range(2)]
        xT = [sb(f"xT{i}", [128, 8, 256], BF16) for i in range(2)]
        hT = [sb(f"hT{i}", [128, 256], BF16) for i in range(NH)]
        sg = [sb(f"sg{i}", [128, 256], F32) for i in range(2)]
        ident = k.ident
        Tps = [ps(f"T{i}", [128, 8, 128], BF16) for i in range(2)]
        gu = [ps(f"gu{i}", [128, 2, 256], F32) for i in range(2)]
        yp = [[ps(f"y{t}{h}", [128, 512], F32) for h in range(2)] for t in range(2)]

        cast_tok = [None, None]
        T_tok = [None, None]
        XE_tok = {}
        xR_free = [None, None]
        xR_tok = [None, None]
        gu_last = {}
        mult_tok = {}
        D_tok = {}
        ep_tok = {}

        def stage_load_cast(b):
            for t in range(2):
                sp.wait(cast_tok[t])
                lt = xA_slot[t].dma(sp, xA[t][:], x_src[(b * 2 + t) * 128:(b * 2 + t + 1) * 128, :])
                pool.wait(lt, T_tok[t])
                cast_tok[t] = pool.mark(pool.e.tensor_copy(out=xbf[t][:], in_=xA[t][:]))

        def stage_T(b):
            for t in range(2):
                prev = XE_tok.get((b - 1, t))
                pe.wait(cast_tok[t], prev, k.ident_tok)
                for c in range(8):
                    ins = pe.e.transpose(out=Tps[t][:, c, :], in_=xbf[t][:, c * 128:(c + 1) * 128], identity=ident[:])
                T_tok[t] = pe.mark(ins)

        def stage_XE(b):
            for t in range(2):
                act.wait(T_tok[t], gu_last.get(b - 2))
                XE_tok[(b, t)] = act.mark(act.e.activation(out=xT[b % 2][:, :, t * 128:(t + 1) * 128], in_=Tps[t][:], func=AF.Copy))

        def stage_xR(b):
            for t in range(2):
                sp.wait(xR_free[t])
                xR_tok[t] = xR_slot[t].dma(sp, xR[t][:], x_src[(b * 2 + t) * 128:(b * 2 + t + 1) * 128, :])

        stage_load_cast(0)
        if emit_weights is not None:
            emit_weights()
        stage_T(0)
        stage_XE(0)
        def wres(t):
            if isinstance(t, tuple) and len(t) == 3 and t[0] == "wl":
                return list(wl.toks[t[1]:t[2]])
            return t

        def stage_down(b, fd):
            gd = b * NFC + fd
            pe.wait(mult_tok[gd], ep_tok.get(b - 1) if fd == 0 else None, wdB_tok if (b == 0 and fd == NFC // 2) else None,
                    wres(d_wtok[fd]) if b == 0 else None)
            for t in range(2):
                for hh in range(2):
                    ins = pe.e.matmul(yp[t][hh][:], lhsT=hT[gd % NH][:, t * 128:(t + 1) * 128],
                                      rhs=wd_ap(fd, hh * 512, (hh + 1) * 512), start=(fd == 0), stop=(fd == NFC - 1))
            D_tok[gd] = pe.mark(ins)

        for b in range(NB):
            if b + 1 < NB:
                stage_load_cast(b + 1)
            stage_xR(b)
            xt = xT[b % 2]
            for f in range(NFC):
                gi = b * NFC + f
                if wl is not None and b == 0 and f % 2 == 0:
                    wl.emit(6 * (f // 2 + 3) - wl.i)
                    if wl.done() and bg is None and bg_jobs:
                        bg = BgCast(k, es, name, bg_jobs, k.last_stg[:2], list(wl.toks))
                pe.wait(XE_tok[(b, 0)], XE_tok[(b, 1)], mult_tok.get(gi - 2), wres(gu_wtok[f]) if b == 0 else None)
                for c in range(8):
                    pe.e.matmul(gu[gi % 2][:, 0, :], lhsT=wg[:, c, f * 128:(f + 1) * 128], rhs=xt[:, c, :],
                                start=(c == 0), stop=(c == 7))
                for c in range(8):
                    ins = pe.e.matmul(gu[gi % 2][:, 1, :], lhsT=wu[:, c, f * 128:(f + 1) * 128], rhs=xt[:, c, :],
                                      start=(c == 0), stop=(c == 7))
                gtok = pe.mark(ins)
                if f == NFC - 1:
                    gu_last[b] = gtok
                act.wait(gtok, mult_tok.get(gi - 2))
                stok = act.mark(act.e.activation(out=sg[gi % 2][:], in_=gu[gi % 2][:, 0, :], func=AF.Silu))
                dve.wait(stok, D_tok.get(gi - NH))
                mult_tok[gi] = dve.mark(dve.e.tensor_tensor(out=hT[gi % NH][:], in0=sg[gi % 2][:], in1=gu[gi % 2][:, 1, :], op=ALU.mult))
                if f == 10 and b + 1 < NB:
                    stage_T(b + 1)
                    stage_XE(b + 1)
                if bg is not None and f in (1, 5, 9, 13, 17, 20):
                    bg.step()
                if f >= 1:
                    stage_down(b, f - 1)
            stage_down(b, NFC - 1)
            etoks = []
            for t in range(2):
                gt = b * 2 + t
                stt = ln_part_a(k, ctx, gt, [yp[t][0][:], yp[t][1][:]], xR[t], 2.0 * ALPHA, 4.0 * EPS,
                                dst[gt * 128:(gt + 1) * 128, :], [D_tok[b * NFC + NFC - 1], xR_tok[t]])
                xR_free[t] = stt
                etoks.extend(stt)
            for t in range(2):
                ln_part_b(k, ctx, b * 2 + t)
            ep_tok[b] = etoks
        barrier(k, [ctx["store_tok"], bg.finish() if bg is not None else None])


def xT_block_loader(k, es, name, src, ntile_list):
    nc = k.nc
    sb = lambda n, shape, dt: es.enter_context(nc.sbuf_tensor(f"{name}_{n}", shape, dt))
    st = {
        "xA": [sb(f"lxA{i}", [128, D], F32) for i in range(2)],
        "xbf": [sb(f"lxbf{i}", [128, D], BF16) for i in range(2)],
        "slot": k.slots(es, 2),
        "Tps": [es.enter_context(nc.psum_tensor(f"{name}_lT{i}", [128, 8, 128], BF16)) for i in range(2)],
        "cast_tok": [None, None], "T_tok": [None, None], "XE_tok": [None, None], "i": 0,
    }

    def emit(tile, dstT, dst_free_tok=None):
        i = st["i"]; st["i"] += 1
        s_ = i % 2
        k.sp.wait(st["cast_tok"][s_])
        lt = st["slot"][s_].dma(k.sp, st["xA"][s_][:], src[tile * 128:(tile + 1) * 128, :])
        k.pool.wait(lt, st["T_tok"][s_])
        st["cast_tok"][s_] = k.pool.mark(k.pool.e.tensor_copy(out=st["xbf"][s_][:], in_=st["xA"][s_][:]))
        k.pe.wait(st["cast_tok"][s_], st["XE_tok"][s_], k.ident_tok)
        for c in range(8):
            ins = k.pe.e.transpose(out=st["Tps"][s_][:, c, :], in_=st["xbf"][s_][:, c * 128:(c + 1) * 128], identity=k.ident[:])
        st["T_tok"][s_] = k.pe.mark(ins)
        k.act.wait(st["T_tok"][s_], dst_free_tok)
        st["XE_tok"][s_] = k.act.mark(k.act.e.activation(out=dstT, in_=st["Tps"][s_][:], func=AF.Copy))
        return st["XE_tok"][s_]
    return emit


def win_jobs(win_sb, win_d, col0, ncols):
    v = win_d.rearrange("(c p) f -> p c f", p=128)
    return [(win_sb[:, c, :], v[:, c, col0:col0 + ncols]) for c in range(8)]


def na_kt_set(il):
    return [0, 1, 2, 3] if il < 2 else list(range(il - 2, il + 3))


def na_variant(il, kt):
    return il * 4 + kt if il < 2 else 8 + (kt - il + 2)


def na_phase(k, x1_d, win_d, nab_d, mixT):
    nc = k.nc
    pe, act, dve, pool, sp = k.pe, k.act, k.dve, k.pool, k.sp
    with ExitStack() as es:
        sb = lambda n, shape, dt: es.enter_context(nc.sbuf_tensor(f"na_{n}", shape, dt))
        ps = lambda n, shape, dt: es.enter_context(nc.psum_tensor(f"na_{n}", shape, dt))
        KT = sb("KT", [128, 4, NKT_NA * 128], BF16)
        QT = [sb(f"QT{i}", [128, 4, OWN], BF16) for i in range(2)]
        VA = sb("VA", [128, NKT_NA, 8, 65], BF16)
        zer = sb("zer", [128, 512], BF16)
        nab = sb("nab", [128, 13, 1024], F32)
        nsl = k.slots(es, 1)[0]
        tz = pool.mark(pool.e.memset(zer[:], 0.0))
        dve.e.memset(QT[0][64:128, :, :], 0.0)
        dve.e.memset(QT[1][0:64, :, :], 0.0)
        tv1 = dve.mark(dve.e.memset(VA[:, :, :, 64:65], 1.0))
        pad_tok = tv1
        with ExitStack() as es2:
            sb2 = lambda n, shape, dt: es2.enter_context(nc.sbuf_tensor(f"nap_{n}", shape, dt))
            win = sb2("win", [128, 8, 1536], BF16)
            wtoks = load_bf16_weights(k, act, k.slots(es2, 1)[0], win_jobs(win, win_d, 0, 1536))
            for v in range(13):
                nab_tok = nsl.dma(act, nab[:, v, :], nab_d[v])
            x1T = [sb2(f"x1T{i}", [128, 8, 512], BF16) for i in range(2)]
            pp = [es2.enter_context(nc.psum_tensor(f"nap_pp{i}", [128, 512], F32)) for i in range(3)]
            emit = xT_block_loader(k, es2, "nap", x1_d, None)
            pp_free = [None] * 3
            blk_last_pe = [None, None]
            npp = 0
            def emit_blk(blk_):
                nt_ = 4 if blk_ < 4 else 2
                return [emit(blk_ * 4 + t, x1T[blk_ % 2][:, :, t * 128:(t + 1) * 128], blk_last_pe[blk_ % 2]) for t in range(nt_)]

            xe_next = emit_blk(0)
            for blk in range(5):
                ntile = 4 if blk < 4 else 2
                ntok = ntile * 128
                xt = x1T[blk % 2]
                xe = xe_next
                pe.wait(xe, wtoks)
                for kind in range(2):
                    if kind == 1 and blk + 1 < 5:
                        xe_next = emit_blk(blk + 1)
                    if kind == 1 and blk >= 4:
                        continue
                    for hp in range(4):
                        col = (512 if kind == 0 else 0) + hp * 128
                        b_ = npp % 3; npp += 1
                        pe.wait(pp_free[b_])
                        for c in range(8):
                            ins = pe.e.matmul(pp[b_][:, :ntok], lhsT=win[:, c, col:col + 128], rhs=xt[:, c, :ntok], start=(c == 0), stop=(c == 7))
                        tk = pe.mark(ins)
                        act.wait(tk)
                        if kind == 0:
                            pp_free[b_] = act.mark(act.e.activation(out=KT[:, hp, blk * 512:blk * 512 + ntok], in_=pp[b_][:, :ntok], func=AF.Copy))
                        else:
                            act.wait(tv1)
                            act.e.activation(out=QT[0][0:64, hp, blk * 512:blk * 512 + ntok], in_=pp[b_][0:64, :ntok], func=AF.Copy, scale=0.125)
                            pp_free[b_] = act.mark(act.e.activation(out=QT[1][64:128, hp, blk * 512:blk * 512 + ntok], in_=pp[b_][64:128, :ntok], func=AF.Copy, scale=0.125))
                for t in range(ntile):
                    b_ = npp % 3; npp += 1
                    pe.wait(pp_free[b_])
                    for c in range(8):
                        ins = pe.e.matmul(pp[b_][:], lhsT=xt[:, c, t * 128:(t + 1) * 128], rhs=win[:, c, 1024:1536], start=(c == 0), stop=(c == 7))
                    tk = pe.mark(ins)
                    dve.wait(tk, tv1)
                    pp_free[b_] = dve.mark(dve.e.tensor_copy(out=VA[:, blk * 4 + t, :, 0:64], in_=pp[b_][:].rearrange("p (h e) -> p h e", e=64)))
                blk_last_pe[blk % 2] = tk
            barrier(k, [])
        NSP, NTM, NE, LA = 4, 3, 5, 3
        sps = [ps(f"s{i}", [128, 512], F32) for i in range(NSP)]
        acc = [ps(f"acc{i}", [128, 512], F32) for i in range(2)]
        tpo = ps("tpo", [128, 4, 128], BF16)
        tmp = [sb(f"tmp{i}", [128, 512], F32) for i in range(NTM)]
        E = [sb(f"E{i}", [128, 512], BF16) for i in range(NE)]
        rr = sb("rr", [128, 8], F32)
        nao = [sb(f"nao{i}", [128, 512], BF16) for i in range(2)]
        accs = sb("accs", [128, 2, 260], F32)
        sps_free = [None] * NSP
        tmp_free = [None] * NTM
        E_free = [None] * NE
        acc_free = [None, None]
        nao_free = [None, None]
        nao_tok = {}
        tpo_free = None
        accs_free = None
        ns = 0
        E_tok = {}

        def finish_il(il_):
            nonlocal tpo_free
            s2 = il_ % 2
            pe.wait(nao_tok[il_], tpo_free)
            for hp in range(4):
                ins = pe.e.transpose(out=tpo[:, hp, :], in_=nao[s2][:, hp * 128:(hp + 1) * 128], identity=k.ident[:])
            tt = pe.mark(ins)
            nao_free[s2] = tt
            act.wait(tt)
            tpo_free = act.mark(act.e.activation(out=mixT[:, 0:4, il_ * 128:(il_ + 1) * 128], in_=tpo[:], func=AF.Copy))

        def steps_of(il_):
            return [(kt, hb) for kt in na_kt_set(il_) for hb in range(2)]

        def emit_S(il_, si):
            nonlocal ns
            kt, hb = steps_of(il_)[si]
            g = ns; ns += 1
            pe.wait(sps_free[g % NSP], pad_tok)
            for hl in range(4):
                ins = pe.e.matmul(sps[g % NSP][:, hl * 128:(hl + 1) * 128], lhsT=KT[:, hl, kt * 128:(kt + 1) * 128],
                                  rhs=QT[hb][:, hl, il_ * 128:(il_ + 1) * 128], start=True, stop=True)
            tk = pe.mark(ins)
            v = na_variant(il_, kt)
            dve.wait(tk, tmp_free[g % NTM], nab_tok)
            t1 = dve.mark(dve.e.tensor_tensor(out=tmp[g % NTM][:], in0=nab[:, v, hb * 512:(hb + 1) * 512], in1=sps[g % NSP][:], op=ALU.add))
            sps_free[g % NSP] = t1
            act.wait(t1, E_free[g % NE])
            t2 = act.mark(act.e.activation(out=E[g % NE][:], in_=tmp[g % NTM][:], func=AF.Exp))
            tmp_free[g % NTM] = t2
            E_tok[(il_, si)] = (t2, g)

        def emit_AV(il_, si, last):
            kt, hb = steps_of(il_)[si]
            t2, g = E_tok.pop((il_, si))
            pe.wait(t2)
            for hl in range(4):
                h = 2 * hl + hb
                ins = pe.e.matmul(acc[hb][:, hl * 65:(hl + 1) * 65], lhsT=E[g % NE][:, hl * 128:(hl + 1) * 128],
                                  rhs=VA[:, kt, h, :], start=False, stop=(last and hl == 3))
            E_free[g % NE] = pe.mark(ins)
            return E_free[g % NE]

        pend_il = None
        for si in range(LA):
            emit_S(0, si)
        for il in range(16):
            nst = len(steps_of(il))
            for hb in range(2):
                pe.wait(acc_free[hb], tz)
                pe.e.matmul(acc[hb][:], lhsT=zer[:, 0:128], rhs=zer[:], start=True, stop=False)
            last_av = [None, None]
            for si in range(nst):
                if si + LA < nst:
                    emit_S(il, si + LA)
                last_av[steps_of(il)[si][1]] = emit_AV(il, si, si >= nst - 2)
                if si == 3 and pend_il is not None:
                    finish_il(pend_il)
                    pend_il = None
            s_ = il % 2
            evt = []
            for hb in range(2):
                act.wait(last_av[hb], accs_free)
                acc_free[hb] = act.mark(act.e.activation(out=accs[:, hb, :], in_=acc[hb][:, 0:260], func=AF.Copy))
                evt.append(acc_free[hb])
            if il + 1 < 16:
                for si in range(LA):
                    emit_S(il + 1, si)
            for hb in range(2):
                accv = accs[:, hb, :].rearrange("p (h e) -> p h e", e=65)
                dve.wait(evt, nao_free[s_])
                tr = dve.mark(dve.e.reciprocal(out=rr[:, hb * 4:(hb + 1) * 4], in_=accv[:, :, 64]))
                dve.wait(tr)
                for hl in range(4):
                    h = 2 * hl + hb
                    ins = dve.e.tensor_scalar(out=nao[s_][:, h * 64:(h + 1) * 64], in0=accv[:, hl, 0:64], scalar1=rr[:, hb * 4 + hl:hb * 4 + hl + 1], scalar2=None, op0=ALU.mult)
                accs_free = dve.mark(ins)
            nao_tok[il] = accs_free
            pend_il = il
        finish_il(pend_il)
        barrier(k, [])


def diff_phase(k, x1_d, win_d, aug_d, atab_d, cst_d, lamv_d, subg_d, mixT):
    nc = k.nc
    pe, act, dve, pool, sp = k.pe, k.act, k.dve, k.pool, k.sp
    SL = [2.0 ** (-8.0 * (h + 1) / 4) for h in range(4)]
    with ExitStack() as es:
        sb = lambda n, shape, dt: es.enter_context(nc.sbuf_tensor(f"df_{n}", shape, dt))
        ps = lambda n, shape, dt: es.enter_context(nc.psum_tensor(f"df_{n}", shape, dt))
        KT = [sb(f"KT{i}", [128, 4, SEQ], BF16) for i in range(2)]
        QT = sb("QT", [128, 4, OWN], BF16)
        VA = sb("VA", [128, 32, 4, 129], BF16)
        cst = sb("cst", [128, 256], F32)
        lamv = sb("lamv", [128, 4, 64], F32)
        g8 = sb("g8", [128, 128], F32)
        zer = sb("zer", [128, 512], BF16)
        sm = sb("sm", [128, 8], F32)
        junk = sb("junk", [128, 64], F32)
        mhalf = sb("mhalf", [128, 1], F32)
        tz = pool.mark(pool.e.memset(zer[:], 0.0))
        pool.e.memset(mhalf[:], -0.5)
        ones_t = sb("ones_t", [128, 512], BF16)
        pad_tok = pool.mark(pool.e.memset(ones_t[:], 1.0))
        tv1 = dve.mark(dve.e.memset(VA[:, :, :, 128:129], 1.0))
        csl = k.slots(es, 1)[0]
        csl.dma(sp, cst[:], cst_d)
        csl.dma(sp, lamv[:], lamv_d)
        ctok = csl.dma(sp, g8[:], subg_d)
        dve.wait(ctok)
        a0 = dve.mark(dve.e.tensor_scalar(out=g8[:], in0=g8[:], scalar1=1.0 - LAM_INIT, scalar2=None, op0=ALU.mult))
        dve.wait(a0)
        a1 = dve.mark(dve.e.scalar_tensor_tensor(out=junk[:], in0=lamv[:, 0, :], scalar=1.0, op0=ALU.mult, in1=lamv[:, 1, :], op1=ALU.mult, accum_out=sm[:, 0:1]))
        dve.wait(a1)
        a2 = dve.mark(dve.e.scalar_tensor_tensor(out=junk[:], in0=lamv[:, 2, :], scalar=1.0, op0=ALU.mult, in1=lamv[:, 3, :], op1=ALU.mult, accum_out=sm[:, 1:2]))
        act.wait(a2)
        a3 = act.mark(act.e.activation(out=sm[:, 2:4], in_=sm[:, 0:2], func=AF.Exp))
        dve.wait(a3)
        a4 = dve.mark(dve.e.tensor_tensor(out=sm[:, 5:6], in0=sm[:, 3:4], in1=sm[:, 2:3], op=ALU.subtract))
        dve.wait(a4)
        lam_tok = dve.mark(dve.e.tensor_scalar(out=sm[:, 4:5], in0=sm[:, 5:6], scalar1=-LAM_INIT, scalar2=None, op0=ALU.add))
        neglam = sm[:, 4:5]
        with ExitStack() as es2:
            sb2 = lambda n, shape, dt: es2.enter_context(nc.sbuf_tensor(f"dfp_{n}", shape, dt))
            win = sb2("win", [128, 8, 1536], BF16)
            wtoks = load_bf16_weights(k, act, k.slots(es2, 1)[0], win_jobs(win, win_d, 1536, 1536))
            x1T = [sb2(f"x1T{i}", [128, 8, 512], BF16) for i in range(2)]
            pp = [es2.enter_context(nc.psum_tensor(f"dfp_pp{i}", [128, 512], F32)) for i in range(3)]
            emit = xT_block_loader(k, es2, "dfp", x1_d, None)
            pp_free = [None] * 3
            blk_last_pe = [None, None]
            npp = 0
            def emit_blk(blk_):
                return [emit(blk_ * 4 + t, x1T[blk_ % 2][:, :, t * 128:(t + 1) * 128], blk_last_pe[blk_ % 2]) for t in range(4)]

            xe_next = emit_blk(0)
            for blk in range(8):
                xt = x1T[blk % 2]
                xe = xe_next
                pe.wait(xe, wtoks)
                for kind in range(2):
                    if kind == 1 and blk + 1 < 8:
                        xe_next = emit_blk(blk + 1)
                    if kind == 1 and blk >= 4:
                        continue
                    for h in range(4):
                        col = (512 if kind == 0 else 0) + h * 128
                        b_ = npp % 3; npp += 1
                        pe.wait(pp_free[b_])
                        for c in range(8):
                            ins = pe.e.matmul(pp[b_][:], lhsT=win[:, c, col:col + 128], rhs=xt[:, c, :], start=(c == 0), stop=(c == 7))
                        tk = pe.mark(ins)
                        act.wait(tk)
                        if kind == 0:
                            act.wait(pad_tok)
                            k0 = act.mark(act.e.activation(out=KT[0][:, h, blk * 512:(blk + 1) * 512], in_=pp[b_][:], func=AF.Copy))
                            k1 = act.mark(act.e.activation(out=KT[1][:, h, blk * 512:(blk + 1) * 512], in_=pp[b_][:], func=AF.Copy))
                            pp_free[b_] = k1
                            act.wait(k0, k1)
                            act.e.activation(out=KT[0][64:66, h, blk * 512:(blk + 1) * 512], in_=ones_t[64:66, :], func=AF.Copy)
                            kfix_tok = act.mark(act.e.activation(out=KT[1][0:2, h, blk * 512:(blk + 1) * 512], in_=ones_t[0:2, :], func=AF.Copy))
                        else:
                            pp_free[b_] = act.mark(act.e.activation(out=QT[:, h, blk * 512:(blk + 1) * 512], in_=pp[b_][:], func=AF.Copy, scale=0.125))
                for t in range(4):
                    b_ = npp % 3; npp += 1
                    pe.wait(pp_free[b_])
                    for c in range(8):
                        ins = pe.e.matmul(pp[b_][:], lhsT=xt[:, c, t * 128:(t + 1) * 128], rhs=win[:, c, 1024:1536], start=(c == 0), stop=(c == 7))
                    tk = pe.mark(ins)
                    dve.wait(tk, tv1)
                    pp_free[b_] = dve.mark(dve.e.tensor_copy(out=VA[:, blk * 4 + t, :, 0:128], in_=pp[b_][:].rearrange("p (h e) -> p h e", e=128)))
                blk_last_pe[blk % 2] = tk
            barrier(k, [])
        NSP, NTM, NE = 2, 2, 4
        atab = sb("atab", [128, 2, 896], F32)
        augtab = sb("augtab", [128, 4, 2, 512], BF16)
        Qs = [[[sb(f"Qs{u}{v}{m}", [128, 512], BF16) for m in range(2)] for v in range(3)] for u in range(2)]
        asl = k.slots(es, 1)[0]
        asl.dma(sp, atab[:], atab_d)
        atok = asl.dma(sp, augtab[:], aug_d)
        qz = None
        for u in range(2):
            for v in range(3):
                for m in range(2):
                    qz = pool.mark(pool.e.memset(Qs[u][v][m][:], 0.0))
        sps = [ps(f"s{i}", [128, 2, 512], F32) for i in range(NSP)]
        acc = [ps(f"acc{i}", [128, 512], F32) for i in range(3)]
        tpo = ps("tpo", [128, 128], BF16)
        tmp = [sb(f"tmp{i}", [128, 2, 512], F32) for i in range(NTM)]
        E = [sb(f"E{i}", [128, 2, 512], BF16) for i in range(NE)]
        accs = sb("accs", [128, 3, 387], F32)
        rr = sb("rr", [128, 4], F32)
        tq = sb("tq", [128, 128], F32)
        oq = sb("oq", [128, 128], F32)
        sps_free = [None] * NSP
        tmp_free = [None] * NTM
        E_free = [None] * NE
        acc_free = [None] * 3
        accs_free = None
        tpo_free = None
        ns = 0
        unit_last_S = {}
        units = [(h_, qb_) for h_ in range(4) for qb_ in range(4)]
        yq = [[sb(f"yq{u_}{q_}", [128, 128], BF16) for q_ in range(4)] for u_ in range(2)]
        yq_free = {}
        yq_tok = {}
        qtoks = {}

        def build_Qs(ui):
            h_, qb_ = units[ui]
            u_ = ui % 2
            pool.wait(qz, atok, unit_last_S.get(ui - 2), ctok)
            for m in range(2):
                r0 = 64 * m
                a0 = 64 - 64 * m
                for v in range(3):
                    qtok = pool.mark(pool.e.tensor_copy(out=Qs[u_][v][m][r0:r0 + 64, :], in_=QT[r0:r0 + 64, h_, qb_ * 512:(qb_ + 1) * 512]))
                for v in range(2):
                    qtok = pool.mark(pool.e.tensor_copy(out=Qs[u_][v][m][a0:a0 + 2, :], in_=augtab[a0:a0 + 2, h_, v, :]))
            qtoks[ui] = qtok

        def finish_transposes(ui):
            nonlocal tpo_free
            h_, qb_ = units[ui]
            for qt in range(4):
                pe.wait(yq_tok[(ui, qt)], tpo_free)
                tt = pe.mark(pe.e.transpose(out=tpo[:], in_=yq[ui % 2][qt][:], identity=k.ident[:]))
                yq_free[(ui % 2, qt)] = tt
                act.wait(tt)
                tok0 = (qb_ * 4 + qt) * 128
                tpo_free = act.mark(act.e.activation(out=mixT[:, 4 + h_, tok0:tok0 + 128], in_=tpo[:], func=AF.Copy))

        evts = {}

        def epilogue(ui):
            nonlocal accs_free
            u = ui % 2
            evt = evts[ui]
            for qt in range(4):
                g0, g1 = qt, 4 + qt
                O0 = accs[:, g0 // 3, (g0 % 3) * 129:(g0 % 3) * 129 + 129]
                O1 = accs[:, g1 // 3, (g1 % 3) * 129:(g1 % 3) * 129 + 129]
                dve.wait(evt, lam_tok)
                e1 = dve.mark(dve.e.reciprocal(out=rr[:, 0:1], in_=O0[:, 128:129]))
                e2 = dve.mark(dve.e.reciprocal(out=rr[:, 1:2], in_=O1[:, 128:129]))
                dve.wait(e1, e2)
                e3 = dve.mark(dve.e.tensor_tensor(out=rr[:, 2:3], in0=rr[:, 1:2], in1=neglam, op=ALU.mult))
                dve.wait(e3)
                e4 = dve.mark(dve.e.tensor_scalar(out=tq[:], in0=O1[:, 0:128], scalar1=rr[:, 2:3], scalar2=None, op0=ALU.mult))
                dve.wait(e4)
                e5 = dve.mark(dve.e.scalar_tensor_tensor(out=oq[:], in0=O0[:, 0:128], scalar=rr[:, 0:1], op0=ALU.mult, in1=tq[:], op1=ALU.add))
                dve.wait(e5)
                e6 = dve.mark(dve.e.scalar_tensor_tensor(out=tq[:], in0=oq[:], scalar=1.0 / 128.0, op0=ALU.mult, in1=oq[:], op1=ALU.mult, accum_out=rr[:, 3:4]))
                dve.wait(e6)
                e7 = dve.mark(dve.e.tensor_scalar(out=rr[:, 3:4], in0=rr[:, 3:4], scalar1=EPS, scalar2=None, op0=ALU.add))
                pool.wait(e7)
                e8 = pool.mark(pool.e.tensor_tensor(out=rr[:, 3:4], in0=rr[:, 3:4], in1=mhalf[:], op=ALU.pow))
                dve.wait(e8, yq_free.get((u, qt)))
                e9 = dve.mark(dve.e.scalar_tensor_tensor(out=yq[u][qt][:], in0=oq[:], scalar=rr[:, 3:4], op0=ALU.mult, in1=g8[:], op1=ALU.mult))
                yq_tok[(ui, qt)] = e9
                if qt == 3:
                    accs_free = e9

        build_Qs(0)
        pending = None
        pend_epi = None
        tr_at = -1
        E_tok = {}

        def emit_S(ui, kt):
            nonlocal ns
            h, qb = units[ui]
            u = ui % 2
            g = ns; ns += 1
            delta = qb * 512 - kt * 128
            v = 0 if delta >= 128 else (1 if delta <= -512 else 2)
            pe.wait(sps_free[g % NSP], qtoks[ui], pad_tok)
            for m in range(2):
                ins = pe.e.matmul(sps[g % NSP][:, m, :], lhsT=KT[m][:, h, kt * 128:(kt + 1) * 128],
                                  rhs=Qs[u][v][m][:], start=True, stop=True)
            tk = pe.mark(ins)
            unit_last_S[ui] = tk
            if v == 2:
                dve.wait(tk, tmp_free[g % NTM], atok)
                t1 = dve.mark(dve.e.scalar_tensor_tensor(out=tmp[g % NTM][:], in0=atab[:, :, delta + 384:delta + 384 + 512], scalar=float(-SL[h]),
                                                         op0=ALU.mult, in1=sps[g % NSP][:], op1=ALU.add))
                sps_free[g % NSP] = t1
                act.wait(t1, E_free[g % NE])
                t2 = act.mark(act.e.activation(out=E[g % NE][:], in_=tmp[g % NTM][:], func=AF.Exp))
                tmp_free[g % NTM] = t2
            else:
                n = abs(delta) // 128
                col = (h * 32 + n) * 2 + v
                act.wait(tk, E_free[g % NE], ctok)
                t2 = act.mark(act.e.activation(out=E[g % NE][:], in_=sps[g % NSP][:], func=AF.Exp, bias=cst[:, col:col + 1], scale=1.0))
                sps_free[g % NSP] = t2
            E_tok[(ui, kt)] = (t2, g)

        def emit_AV(ui, kt):
            h, qb = units[ui]
            t2, g = E_tok.pop((ui, kt))
            pe.wait(t2, tv1)
            for m in range(2):
                for qt in range(4):
                    gi = m * 4 + qt
                    ins = pe.e.matmul(acc[gi // 3][:, (gi % 3) * 129:(gi % 3) * 129 + 129], lhsT=E[g % NE][:, m, qt * 128:(qt + 1) * 128],
                                      rhs=VA[:, kt, h, :], start=False, stop=(kt == 31 and gi in (2, 5, 7)))
            E_free[g % NE] = pe.mark(ins)
            return E_free[g % NE]

        emit_S(0, 0)
        emit_S(0, 1)
        for ui, (h, qb) in enumerate(units):
            if True:
                for j in range(3):
                    pe.wait(acc_free[j], tz)
                    pe.e.matmul(acc[j][:], lhsT=zer[:, 0:128], rhs=zer[:], start=True, stop=False)
                for kt in range(32):
                    if kt + 2 < 32:
                        emit_S(ui, kt + 2)
                    last = emit_AV(ui, kt)
                    if kt == 6 and ui + 1 < len(units):
                        build_Qs(ui + 1)
                    if pend_epi is not None and kt == min(4 * qb + 2, 14):
                        epilogue(pend_epi)
                        pending = pend_epi
                        pend_epi = None
                        tr_at = kt + 12
                    if pending is not None and pend_epi is None and kt == tr_at:
                        finish_transposes(pending)
                        pending = None
                evt = []
                for j in range(3):
                    act.wait(last, accs_free)
                    acc_free[j] = act.mark(act.e.activation(out=accs[:, j, :], in_=acc[j][:, 0:387], func=AF.Copy))
                    evt.append(acc_free[j])
                evts[ui] = evt
                pend_epi = ui
                if ui + 1 < len(units):
                    emit_S(ui + 1, 0)
                    emit_S(ui + 1, 1)
        epilogue(pend_epi)
        finish_transposes(pend_epi)
        barrier(k, [])


def wout_phase(k, x1_d, wout_d, g_d, b_d, mixT, x2_d):
    nc = k.nc
    pe, act, dve, pool, sp = k.pe, k.act, k.dve, k.pool, k.sp
    with ExitStack() as es:
        sb = lambda n, shape, dt: es.enter_context(nc.sbuf_tensor(f"wo_{n}", shape, dt))
        wo = sb("wo", [128, 8, D], BF16)
        v = wout_d.rearrange("(c p) f -> p c f", p=128)
        wtoks = load_bf16_weights(k, sp, k.slots(es, 1)[0], [(wo[:, c, :], v[:, c, :]) for c in range(8)])
        NW = 4
        ctx = ln_ctx(k, es, "wo", g_d, b_d, nsl=NW)
        xR = [sb(f"xR{i}", [128, D], F32) for i in range(NW)]
        xsl = k.slots(es, NW)
        yp = [[es.enter_context(nc.psum_tensor(f"wo_y{t}{h}", [128, 512], F32)) for h in range(2)] for t in range(NW)]
        xR_free = [None] * NW
        yp_free = [None] * NW
        lts = {}

        def issue_load(t_):
            sl_ = t_ % NW
            sp.wait(xR_free[sl_])
            lts[t_] = xsl[sl_].dma(sp, xR[sl_][:], x1_d[t_ * 128:(t_ + 1) * 128, :])

        for t_ in range(NW - 1):
            issue_load(t_)
        for t in range(16):
            s_ = t % NW
            if t + NW - 1 < 16:
                issue_load(t + NW - 1)
            lt = lts[t]
            pe.wait(wtoks, yp_free[s_])
            for hh in range(2):
                for c in range(8):
                    ins = pe.e.matmul(yp[s_][hh][:], lhsT=mixT[:, c, t * 128:(t + 1) * 128], rhs=wo[:, c, hh * 512:(hh + 1) * 512], start=(c == 0), stop=(c == 7))
            tk = pe.mark(ins)
            stt = ln_part_a(k, ctx, t, [yp[s_][0][:], yp[s_][1][:]], xR[s_], ALPHA, EPS, x2_d[t * 128:(t + 1) * 128, :], [tk, lt])
            xR_free[s_] = stt
            yp_free[s_] = stt
            if t >= 1:
                ln_part_b(k, ctx, t - 1)
        ln_part_b(k, ctx, 15)
        barrier(k, [ctx["store_tok"]])


def build_program(stop=None):
    nc = bass.Bass("TRN2", target_bir_lowering=False)
    dram_in = lambda n, shape, dt=F32: nc.dram_tensor(n, shape, dt, kind="ExternalInput").ap()
    x = dram_in("x", [SEQ, D])
    wg1 = dram_in("wg1", [D, DFF]); wu1 = dram_in("wu1", [D, DFF]); wd1 = dram_in("wd1", [DFF, D])
    wg2 = dram_in("wg2", [D, DFF]); wu2 = dram_in("wu2", [D, DFF]); wd2 = dram_in("wd2", [DFF, D])
    win = dram_in("win", [D, 3072]); wout = dram_in("wout", [D, D])
    ln1g = dram_in("ln1g", [128, D]); ln1b = dram_in("ln1b", [128, D])
    ln2g = dram_in("ln2g", [128, D]); ln2b = dram_in("ln2b", [128, D])
    ln3g = dram_in("ln3g", [128, D]); ln3b = dram_in("ln3b", [128, D])
    ident_d = dram_in("ident", [128, 128], BF16)
    nab = dram_in("nab", [13, 128, 1024])
    augt = dram_in("augt", [128, 4, 2, 512], BF16); atab = dram_in("atab", [128, 2, 896]); cst = dram_in("cst", [128, 256])
    lamv = dram_in("lamv", [128, 4, 64]); subg = dram_in("subg", [128, 128])
    zeros_ones = dram_in("zeros_ones", [2, 64, 4 * SEQ], BF16)
    out = nc.dram_tensor("out", [OWN, D], F32, kind="ExternalOutput").ap()
    x1_d = nc.dram_tensor("x1_scratch", [SEQ, D], F32, kind="Internal").ap()
    x2_d = nc.dram_tensor("x2_scratch", [OWN, D], F32, kind="Internal").ap()
    win_bf = nc.dram_tensor("win_bf", [D, 3072], BF16, kind="Internal").ap()
    wout_bf = nc.dram_tensor("wout_bf", [D, D], BF16, kind="Internal").ap()
    wg2_bf = nc.dram_tensor("wg2_bf", [D, DFF], BF16, kind="Internal").ap()
    wu2_bf = nc.dram_tensor("wu2_bf", [D, DFF], BF16, kind="Internal").ap()
    wd2_bf = nc.dram_tensor("wd2_bf", [DFF, D], BF16, kind="Internal").ap()

    def pieces(src, dst, width):
        sv = src.rearrange("(c p) f -> p c f", p=128)
        dv = dst.rearrange("(c p) f -> p c f", p=128)
        out_ = []
        for c in range(sv.shape[1]):
            for o in range(0, sv.shape[2], width):
                out_.append((sv[:, c, o:o + width], dv[:, c, o:o + width]))
        return out_
    bg_jobs = (pieces(win, win_bf, 1024) + pieces(wout, wout_bf, 1024) + pieces(wg2, wg2_bf, 1408)
               + pieces(wu2, wu2_bf, 1408) + pieces(wd2, wd2_bf, 1024))
    dbg = None
    if stop is not None:
        dbg = nc.dram_tensor("dbg", [SEQ, D], F32, kind="ExternalOutput").ap()
    with ExitStack() as es:
        k = K()
        k.nc = nc
        k.pe = Eng(nc, nc.tensor, "pe", es)
        k.act = Eng(nc, nc.scalar, "act", es)
        k.dve = Eng(nc, nc.vector, "dve", es)
        k.pool = Eng(nc, nc.gpsimd, "pool", es)
        k.sp = Eng(nc, nc.sync, "sp", es)
        k.engs = [k.pe, k.act, k.dve, k.pool, k.sp]
        k.es_global = es
        k.slot_pool = []
        k.zeros_ones = zeros_ones
        k.ident = es.enter_context(nc.sbuf_tensor("ident_sb", [128, 128], BF16))
        isl = k.slots(es, 1)[0]
        k.ident_tok = isl.dma(k.sp, k.ident[:], ident_d)
        if stop == "A":
            ffn_phase(k, "f1", x, SEQ, wg1, wu1, wd1, ln1g, ln1b, dbg, bg_jobs=bg_jobs)
            return nc
        if stop in ("NA", "DF", "W"):
            x1_src = x
            with ExitStack() as esb:
                inb = [esb.enter_context(nc.sbuf_tensor(f"dbg_in{i}", [128, 1408], F32)) for i in range(2)]
                bgc = BgCast(k, esb, "dbgc", bg_jobs[:32], inb, None)
                barrier(k, [bgc.finish()])
        else:
            ffn_phase(k, "f1", x, SEQ, wg1, wu1, wd1, ln1g, ln1b, x1_d, bg_jobs=bg_jobs)
            x1_src = x1_d
        mix_cm = nc.sbuf_tensor("mixT", [128, 8, OWN], BF16, side="right")
        mixT = mix_cm.__enter__()
        if stop != "DF":
            na_phase(k, x1_src, win_bf, nab, mixT)
        if stop != "NA":
            diff_phase(k, x1_src, win_bf, augt, atab, cst, lamv, subg, mixT)
        if stop in ("NA", "DF"):
            with ExitStack() as es3:
                tmpf = es3.enter_context(nc.sbuf_tensor("dbg_tmp", [128, 8, OWN], F32))
                k.dve.wait((k.act, k.act.n))
                c0_ = 0 if stop == "NA" else 4
                k.pool.wait((k.act, k.act.n))
                tk0 = k.pool.mark(k.pool.e.memset(tmpf[:], 0.0))
                k.dve.wait(tk0)
                tk = k.dve.mark(k.dve.e.tensor_copy(out=tmpf[:, c0_:c0_ + 4, :], in_=mixT[:, c0_:c0_ + 4, :]))
                k.sp.wait(tk)
                sl = k.slots(es3, 1)[0]
                for c in range(8):
                    for hf in range(2):
                        t_ = sl.dma(k.sp, dbg[(c * 2 + hf) * 128:(c * 2 + hf + 1) * 128, :], tmpf[:, c, hf * 1024:(hf + 1) * 1024])
                barrier(k, [t_])
            mix_cm.__exit__(None, None, None)
            return nc
        with ExitStack() as esf2:
            pre = None
            if stop is None:
                wg2s, wu2s, _ = alloc_ffn_weights(k, esf2, "f2", with_wd=False)
                wdA2 = esf2.enter_context(nc.sbuf_tensor("f2_wdA", [128, NFC // 2, D], BF16))
                wsl = k.slots(esf2, 1)[0]
                jobs2 = ([(wg2s[:, c, :], wg2_bf.rearrange("(c p) f -> p c f", p=128)[:, c, :]) for c in range(8)]
                         + [(wu2s[:, c, :], wu2_bf.rearrange("(c p) f -> p c f", p=128)[:, c, :]) for c in range(8)]
                         + [(wdA2[:, c:c + 1, :], wd2_bf.rearrange("(c p) f -> p c f", p=128)[:, c:c + 1, :]) for c in range(NFC // 2)])
                pre = (wg2s, wu2s, wdA2, wd2_bf, load_bf16_weights(k, k.act, wsl, jobs2))
            wout_phase(k, x1_src, wout_bf, ln2g, ln2b, mixT, x2_d if stop is None else dbg)
            mix_cm.__exit__(None, None, None)
            if stop == "W":
                return nc
            ffn_phase(k, "f2", x2_d, OWN, wg2, wu2, wd2, ln3g, ln3b, out, pre=pre)
    return nc


def _na_tables(rpb, rev):
    out = np.full((13, 128, 8, 128), -30000.0, np.float32)
    p = np.arange(128)

    def coords(tile):
        t = tile * 128 + p
        r, c = t // 64, t % 64
        if rev:
            r, c = 63 - r, 63 - c
        return r, c

    def fill(v, il, kt):
        rk, ck = coords(kt)
        rq, cq = coords(il)
        r0 = np.clip(rq - 4, 0, 56)
        c0 = np.clip(cq - 8, 0, 48)
        RK, RQ = rk[:, None], rq[None, :]
        CK, CQ = ck[:, None], cq[None, :]
        ok = (RK >= r0[None, :]) & (RK <= r0[None, :] + 7) & (CK >= c0[None, :]) & (CK <= c0[None, :] + 15)
        dr = np.clip(RK - RQ + 7, 0, 14)
        dc = np.clip(CK - CQ + 15, 0, 30)
        vals = rpb[:, dr, dc]
        tile = np.where(ok[None], vals, np.float32(-30000.0)).astype(np.float32)
        out[v] = tile.transpose(1, 0, 2)[:, [0, 2, 4, 6, 1, 3, 5, 7], :]
        return ok

    for il in range(2):
        for kt in range(4):
            fill(na_variant(il, kt), il, kt)
    for dj in range(-2, 3):
        fill(na_variant(8, 8 + dj), 8, 8 + dj)
    return np.ascontiguousarray(out.reshape(13, 128, 1024))


def prep_inputs(inputs, c):
    b, h = c // 2, c % 2
    xb = np.ascontiguousarray(inputs["x"][b])
    if h == 1:
        xb = np.ascontiguousarray(xb[::-1])
    f32 = lambda v: np.ascontiguousarray(np.asarray(v, np.float32))
    rep = lambda v: np.ascontiguousarray(np.broadcast_to(np.asarray(v, np.float32).reshape(1, -1), (128, np.asarray(v).size)))
    p = np.arange(128, dtype=np.float32)[:, None]
    jtab = (np.arange(512, dtype=np.float32)[None, :] - p).astype(np.float32)
    atab = np.abs(np.arange(896, dtype=np.float32)[None, :] - p - 384.0).astype(np.float32)
    atab = np.ascontiguousarray(np.stack([atab, atab], axis=1))
    cst = np.zeros((128, 4, 32, 2), np.float32)
    augt = np.zeros((128, 4, 2, 512), np.float32)
    jj = np.arange(512, dtype=np.float32)
    pp_ = np.arange(128, dtype=np.float32)
    for hh in range(4):
        sl = 2.0 ** (-8.0 * (hh + 1) / 4)
        for v in range(2):
            sgn = 1.0 if v == 0 else -1.0
            cst[:, hh, :, v] = sgn * sl * pp_[:, None] - sl * 128.0 * np.arange(32, dtype=np.float32)[None, :]
            hi = -sgn * sl * 256.0 * np.floor(jj / 256.0)
            lo = -sgn * sl * np.mod(jj, 256.0)
            for base in (0, 64):
                augt[base, hh, v] = hi
                augt[base + 1, hh, v] = lo
    cst = np.ascontiguousarray(cst.reshape(128, 256))
    augt = augt.astype(ml_dtypes.bfloat16)
    lamv = np.stack([rep(inputs["diff_lambda_q1"][0]), rep(inputs["diff_lambda_k1"][0]),
                     rep(inputs["diff_lambda_q2"][0]), rep(inputs["diff_lambda_k2"][0])], axis=1)
    m = {
        "x": xb,
        "wg1": f32(inputs["ffn1_w_gate"][0]), "wu1": f32(inputs["ffn1_w_up"][0]), "wd1": f32(inputs["ffn1_w_down"][0]),
        "wg2": f32(inputs["ffn2_w_gate"][0]), "wu2": f32(inputs["ffn2_w_up"][0]), "wd2": f32(inputs["ffn2_w_down"][0]),
        "win": f32(inputs["w_in"][0]), "wout": f32(inputs["w_out"][0]),
        "ln1g": rep(inputs["ln1_g"][0]), "ln1b": rep(inputs["ln1_b"][0]),
        "ln2g": rep(inputs["ln2_g"][0]), "ln2b": rep(inputs["ln2_b"][0]),
        "ln3g": rep(inputs["ln3_g"][0]), "ln3b": rep(inputs["ln3_b"][0]),
        "ident": np.eye(128, dtype=np.float32).astype(ml_dtypes.bfloat16),
        "nab": _na_tables(f32(inputs["na_rpb"][0]), h == 1),
        "augt": augt, "atab": atab, "cst": cst,
        "lamv": np.ascontiguousarray(lamv.astype(np.float32)), "subg": rep(inputs["diff_subln_g"][0]),
        "zeros_ones": np.stack([np.zeros((64, 4 * SEQ), np.float32), np.ones((64, 4 * SEQ), np.float32)]).astype(ml_dtypes.bfloat16),
    }
    return m


def kernel(**inputs):
    inputs = {k_: np.asarray(v) for k_, v in inputs.items()}
    nc = build_program()
    in_maps = [prep_inputs(inputs, c) for c in range(8)]
    res = run_bass_kernel_spmd(nc, in_maps, core_ids=list(range(8)))
    outp = np.empty((4, SEQ, D), np.float32)
    for c in range(8):
        b, h = c // 2, c % 2
        o = np.asarray(res.results[c]["out"])
        if h == 0:
            outp[b, :OWN] = o
        else:
            outp[b, OWN:] = o[::-1]
    return outp
```

```python
import numpy as np
from contextlib import ExitStack
import concourse.bass as bass
import concourse.mybir as mybir
from concourse.bass_utils import run_bass_kernel_spmd
import ml_dtypes

F32, BF16 = mybir.dt.float32, mybir.dt.bfloat16
AF = mybir.ActivationFunctionType
ALU = mybir.AluOpType

D = 1024
DFF = 2816
NFC = DFF // 128
SEQ = 4096
OWN = 2048
ALPHA = 2.0 ** 0.25
EPS = 1e-5
LAM_INIT = 0.2
NKT_NA = 18


def _flat(toks):
    out = []
    for t in toks:
        if t is None:
            continue
        if isinstance(t, list):
            out.extend(_flat(t))
        else:
            out.append(t)
    return out


class Eng:
    def __init__(self, nc, e, name, es):
        self.e = e
        self.name = name
        self.sem = es.enter_context(nc.semaphore("sem_" + name))
        self.n = 0
        self.seen = {}

    def wait(self, *toks):
        best = {}
        for src, v in _flat(list(toks)):
            if best.get(id(src), (None, 0))[1] < v:
                best[id(src)] = (src, v)
        for src, v in best.values():
            if self.seen.get(id(src), 0) >= v:
                continue
            self.e.wait_ge(src.sem, v)
            self.seen[id(src)] = v

    def mark(self, ins):
        ins.then_inc(self.sem, 1)
        self.n += 1
        return (self, self.n)


class Slot:
    def __init__(self, nc, name, es):
        self.sem = es.enter_context(nc.semaphore("dsem_" + name))
        self.n = 0
        self.busy = False

    def dma(self, q, out, in_):
        q.e.dma_start(out=out, in_=in_).then_inc(self.sem, 16)
        self.n += 16
        return (self, self.n)


class K:
    def slots(self, es, n):
        got = []
        for sl in self.slot_pool:
            if not sl.busy and len(got) < n:
                sl.busy = True
                got.append(sl)
        while len(got) < n:
            sl = Slot(self.nc, f"p{len(self.slot_pool)}", self.es_global)
            sl.busy = True
            self.slot_pool.append(sl)
            got.append(sl)

        def release():
            for sl in got:
                sl.busy = False
        es.callback(release)
        return got


def barrier(k, toks):
    toks = _flat(toks) + [(e, e.n) for e in k.engs if e.n > 0]
    for e in k.engs:
        e.wait(toks)


def copy_cast(eng, k, out, in_):
    if eng is k.act:
        return eng.e.activation(out=out, in_=in_, func=AF.Copy)
    return eng.e.tensor_copy(out=out, in_=in_)


class WeightLoader:
    def __init__(self, k, es, name, jobs, nslots=3, width=1408):
        nc = k.nc
        self.k = k
        self.jobs = jobs
        self.stg = [es.enter_context(nc.sbuf_tensor(f"{name}_stg{i}", [128, width], F32)) for i in range(nslots)]
        self.slots = k.slots(es, nslots)
        self.cast_tok = [None] * nslots
        self.engs = [k.dve, k.pool, k.act]
        self.toks = []
        self.i = 0
        k.last_stg = self.stg

    def emit(self, n):
        k = self.k
        nslots = len(self.stg)
        for _ in range(n):
            if self.i >= len(self.jobs):
                return
            i = self.i
            self.i += 1
            dst, src = self.jobs[i]
            s = i % nslots
            if len(src.shape) == 3:
                nel = src.shape[1] * src.shape[2]
                sv = self.stg[s][:, :nel].rearrange("p (a b) -> p a b", b=src.shape[2])
            else:
                nel = src.shape[-1]
                sv = self.stg[s][:, :nel]
            k.sp.wait(self.cast_tok[s])
            lt = self.slots[s].dma(k.sp, sv, src)
            e = self.engs[i % 3]
            e.wait(lt)
            self.cast_tok[s] = e.mark(copy_cast(e, k, dst, sv))
            self.toks.append(self.cast_tok[s])

    def done(self):
        return self.i >= len(self.jobs)


def load_cast_weights(k, es, name, jobs, nslots=3, width=1408):
    wl = WeightLoader(k, es, name, jobs, nslots, width)
    wl.emit(len(jobs))
    return wl.toks


def load_bf16_weights(k, q, slot, jobs):
    tok = None
    for dst, src in jobs:
        tok = slot.dma(q, dst, src)
    return tok


class BgCast:
    def __init__(self, k, es, name, jobs, in_bufs, first_tok):
        nc = k.nc
        self.k = k
        self.jobs = jobs
        self.inb = in_bufs
        self.outb = [es.enter_context(nc.sbuf_tensor(f"{name}_bgo{i}", [128, 1408], BF16)) for i in range(2)]
        self.in_slot = k.slots(es, 2)
        self.out_slot = k.slots(es, 2)
        self.load_tok = [first_tok, first_tok]
        self.cast_tok = [None, None]
        self.store_tok = [None, None]
        self.i = 0

    def step(self):
        k = self.k
        i = self.i
        n_jobs = len(self.jobs)
        if i > n_jobs + 1:
            return
        self.i += 1
        if 0 <= i - 2 < n_jobs:
            j = i - 2
            n = self.jobs[j][0].shape[-1]
            k.act.wait(self.cast_tok[j % 2])
            self.store_tok[j % 2] = self.out_slot[j % 2].dma(k.act, self.jobs[j][1], self.outb[j % 2][:, :n])
        if i < n_jobs:
            n = self.jobs[i][0].shape[-1]
            k.act.wait(self.cast_tok[i % 2], self.load_tok[i % 2] if i < 2 else None)
            self.load_tok[i % 2] = self.in_slot[i % 2].dma(k.act, self.inb[i % 2][:, :n], self.jobs[i][0])
        if 0 <= i - 1 < n_jobs:
            j = i - 1
            n = self.jobs[j][0].shape[-1]
            k.act.wait(self.load_tok[j % 2], self.store_tok[j % 2])
            self.cast_tok[j % 2] = k.act.mark(k.act.e.activation(out=self.outb[j % 2][:, :n], in_=self.inb[j % 2][:, :n], func=AF.Copy))

    def finish(self):
        while self.i <= len(self.jobs) + 1:
            self.step()
        return [t for t in self.store_tok if t is not None]


def ln_part_a(k, ctx, t, psum_halves, xres, xscale, eps, dst_ap, pre_toks):
    nsl = ctx["n"]
    dve, pool, sp = k.dve, k.pool, k.sp
    s = t % nsl
    r = ctx["r"][s]
    stats, mv, ve, rstd = ctx["stats"][s], ctx["mv"][s], ctx["ve"][s], ctx["rstd"][s]
    stt_toks = []
    for hh in range(2):
        dve.wait(pre_toks, ctx["store_tok"][s])
        ins = dve.e.scalar_tensor_tensor(out=r[:, hh * 512:(hh + 1) * 512], in0=xres[:, hh * 512:(hh + 1) * 512],
                                         scalar=float(xscale), op0=ALU.mult, in1=psum_halves[hh], op1=ALU.add)
        stt_toks.append(dve.mark(ins))
    st_toks = []
    for hh in range(2):
        dve.wait(stt_toks[hh])
        st_toks.append(dve.mark(dve.e.bn_stats(out=stats[:, hh * 6:(hh + 1) * 6], in_=r[:, hh * 512:(hh + 1) * 512])))
    dve.wait(st_toks)
    t1 = dve.mark(dve.e.bn_aggr(out=mv[:], in_=stats[:]))
    dve.wait(t1)
    t2 = dve.mark(dve.e.tensor_scalar(out=ve[:], in0=mv[:, 1:2], scalar1=float(eps), scalar2=None, op0=ALU.add))
    pool.wait(t2, ctx["gb_tok"])
    t3 = pool.mark(pool.e.tensor_tensor(out=rstd[:], in0=ve[:], in1=ctx["mhalf"][:], op=ALU.pow))
    dve.wait(t1, ctx["gb_tok"])
    t4 = dve.mark(dve.e.scalar_tensor_tensor(out=r[:], in0=r[:], scalar=mv[:, 0:1], op0=ALU.subtract, in1=ctx["g"][:], op1=ALU.mult))
    ctx["pend"][t] = (t3, t4, dst_ap)
    return stt_toks


def ln_part_b(k, ctx, t):
    dve, sp = k.dve, k.sp
    s = t % ctx["n"]
    r = ctx["r"][s]
    t3, t4, dst_ap = ctx["pend"].pop(t)
    dve.wait(t3, t4)
    t6 = dve.mark(dve.e.scalar_tensor_tensor(out=r[:], in0=r[:], scalar=ctx["rstd"][s][:, 0:1], op0=ALU.mult, in1=ctx["b"][:], op1=ALU.add))
    sp.wait(t6)
    ctx["store_tok"][s] = ctx["store_slot"][s].dma(sp, dst_ap, r[:])


def ln_ctx(k, es, name, g_d, b_d, nsl=2):
    nc = k.nc
    sb = lambda n, shape, dt: es.enter_context(nc.sbuf_tensor(f"{name}_{n}", shape, dt))
    ctx = {
        "n": nsl,
        "pend": {},
        "r": [sb(f"r{i}", [128, D], F32) for i in range(nsl)],
        "stats": [sb(f"stats{i}", [128, 12], F32) for i in range(nsl)],
        "mv": [sb(f"mv{i}", [128, 2], F32) for i in range(nsl)],
        "ve": [sb(f"ve{i}", [128, 1], F32) for i in range(nsl)],
        "rstd": [sb(f"rstd{i}", [128, 1], F32) for i in range(nsl)],
        "mhalf": sb("mhalf", [128, 1], F32),
        "g": sb("g", [128, D], F32),
        "b": sb("b", [128, D], F32),
        "store_slot": k.slots(es, nsl),
        "store_tok": [None] * nsl,
    }
    gs = k.slots(es, 1)[0]
    gs.dma(k.sp, ctx["g"][:], g_d)
    tg = gs.dma(k.sp, ctx["b"][:], b_d)
    tm = k.pool.mark(k.pool.e.memset(ctx["mhalf"][:], -0.5))
    ctx["gb_tok"] = [tg, tm]
    return ctx


def alloc_ffn_weights(k, es, name, with_wd=True):
    nc = k.nc
    wg = es.enter_context(nc.sbuf_tensor(f"{name}_wg", [128, 8, DFF], BF16))
    wu = es.enter_context(nc.sbuf_tensor(f"{name}_wu", [128, 8, DFF], BF16))
    wd = es.enter_context(nc.sbuf_tensor(f"{name}_wd", [128, NFC, D], BF16)) if with_wd else None
    return wg, wu, wd


def ffn_phase(k, name, x_src, T, wg_d, wu_d, wd_d, g_d, b_d, dst, pre=None, bg_jobs=None):
    nc = k.nc
    pe, act, dve, pool, sp = k.pe, k.act, k.dve, k.pool, k.sp
    NB = T // 256
    NH = 4
    with ExitStack() as es:
        sb = lambda n, shape, dt: es.enter_context(nc.sbuf_tensor(f"{name}_{n}", shape, dt))
        ps = lambda n, shape, dt: es.enter_context(nc.psum_tensor(f"{name}_{n}", shape, dt))
        bg = None
        emit_weights = None
        wl = None
        if pre is not None:
            wg, wu, wdA, wd_bf_d, wtoks = pre
            wdB = es.enter_context(nc.sbuf_tensor(f"{name}_wdB", [128, NFC // 2, D], BF16))
            wdv = wd_bf_d.rearrange("(c p) f -> p c f", p=128)
            wdB_tok = load_bf16_weights(k, k.act, k.slots(es, 1)[0],
                                        [(wdB[:, c:c + 1, :], wdv[:, NFC // 2 + c:NFC // 2 + c + 1, :]) for c in range(NFC // 2)])
            wd_ap = lambda fd, lo, hi: (wdA if fd < NFC // 2 else wdB)[:, fd % (NFC // 2), lo:hi]
            gu_wtok = {f: wtoks for f in range(NFC)}
            d_wtok = {f: None for f in range(NFC)}
        else:
            wg, wu, wd = alloc_ffn_weights(k, es, name)
            wgv = wg_d.rearrange("(c p) f -> p c f", p=128)
            wuv = wu_d.rearrange("(c p) f -> p c f", p=128)
            wdv = wd_d.rearrange("(c p) d -> p c d", p=128)
            jobs = []
            for fg in range(NFC // 2):
                cs = slice(fg * 256, (fg + 1) * 256)
                for c0 in (0, 4):
                    jobs.append((wg[:, c0:c0 + 4, cs], wgv[:, c0:c0 + 4, cs]))
                    jobs.append((wu[:, c0:c0 + 4, cs], wuv[:, c0:c0 + 4, cs]))
                for c in (2 * fg, 2 * fg + 1):
                    jobs.append((wd[:, c, :], wdv[:, c, :]))
            wdB_tok = None
            wd_ap = lambda fd, lo, hi: wd[:, fd, lo:hi]
            gu_wtok, d_wtok = {}, {}

            wl = WeightLoader(k, es, name, jobs)
            for f in range(NFC):
                gu_wtok[f] = ("wl", (f // 2) * 6, (f // 2) * 6 + 4)
                d_wtok[f] = ("wl", (f // 2) * 6 + 4 + (f % 2), (f // 2) * 6 + 5 + (f % 2))

            def emit_weights():
                wl.emit(12)
        ctx = ln_ctx(k, es, name, g_d, b_d)

        xA = [sb(f"xA{i}", [128, D], F32) for i in range(2)]
        xA_slot = k.slots(es, 2)
        xR = [sb(f"xR{i}", [128, D], F32) for i in range(2)]
        xR_slot = k.slots(es, 2)
        xbf = [sb(f"xbf{i}", [128, D], BF16) for i in range(2)]
        xT = [sb(f"xT{i}", [128, 8, 256], BF16) for i in range(2)]
        hT = [sb(f"hT{i}", [128, 256], BF16) for i in range(NH)]
        sg = [sb(f"sg{i}", [128, 256], F32) for i in range(2)]
        ident = k.ident
        Tps = [ps(f"T{i}", [128, 8, 128], BF16) for i in range(2)]
        gu = [ps(f"gu{i}", [128, 2, 256], F32) for i in range(2)]
        yp = [[ps(f"y{t}{h}", [128, 512], F32) for h in range(2)] for t in range(2)]

        cast_tok = [None, None]
        T_tok = [None, None]
        XE_tok = {}
        xR_free = [None, None]
        xR_tok = [None, None]
        gu_last = {}
        mult_tok = {}
        D_tok = {}
        ep_tok = {}

        def stage_load_cast(b):
            for t in range(2):
                sp.wait(cast_tok[t])
                lt = xA_slot[t].dma(sp, xA[t][:], x_src[(b * 2 + t) * 128:(b * 2 + t + 1) * 128, :])
                pool.wait(lt, T_tok[t])
                cast_tok[t] = pool.mark(pool.e.tensor_copy(out=xbf[t][:], in_=xA[t][:]))

        def stage_T(b):
            for t in range(2):
                prev = XE_tok.get((b - 1, t))
                pe.wait(cast_tok[t], prev, k.ident_tok)
                for c in range(8):
                    ins = pe.e.transpose(out=Tps[t][:, c, :], in_=xbf[t][:, c * 128:(c + 1) * 128], identity=ident[:])
                T_tok[t] = pe.mark(ins)

        def stage_XE(b):
            for t in range(2):
                act.wait(T_tok[t], gu_last.get(b - 2))
                XE_tok[(b, t)] = act.mark(act.e.activation(out=xT[b % 2][:, :, t * 128:(t + 1) * 128], in_=Tps[t][:], func=AF.Copy))

        def stage_xR(b):
            for t in range(2):
                sp.wait(xR_free[t])
                xR_tok[t] = xR_slot[t].dma(sp, xR[t][:], x_src[(b * 2 + t) * 128:(b * 2 + t + 1) * 128, :])

        stage_load_cast(0)
        if emit_weights is not None:
            emit_weights()
        stage_T(0)
        stage_XE(0)
        def wres(t):
            if isinstance(t, tuple) and len(t) == 3 and t[0] == "wl":
                return list(wl.toks[t[1]:t[2]])
            return t

        def stage_down(b, fd):
            gd = b * NFC + fd
            pe.wait(mult_tok[gd], ep_tok.get(b - 1) if fd == 0 else None, wdB_tok if (b == 0 and fd == NFC // 2) else None,
                    wres(d_wtok[fd]) if b == 0 else None)
            for t in range(2):
                for hh in range(2):
                    ins = pe.e.matmul(yp[t][hh][:], lhsT=hT[gd % NH][:, t * 128:(t + 1) * 128],
                                      rhs=wd_ap(fd, hh * 512, (hh + 1) * 512), start=(fd == 0), stop=(fd == NFC - 1))
            D_tok[gd] = pe.mark(ins)

        for b in range(NB):
            if b + 1 < NB:
                stage_load_cast(b + 1)
            stage_xR(b)
            xt = xT[b % 2]
            for f in range(NFC):
                gi = b * NFC + f
                if wl is not None and b == 0 and f % 2 == 0:
                    wl.emit(6 * (f // 2 + 3) - wl.i)
                    if wl.done() and bg is None and bg_jobs:
                        bg = BgCast(k, es, name, bg_jobs, k.last_stg[:2], list(wl.toks))
                pe.wait(XE_tok[(b, 0)], XE_tok[(b, 1)], mult_tok.get(gi - 2), wres(gu_wtok[f]) if b == 0 else None)
                for c in range(8):
                    pe.e.matmul(gu[gi % 2][:, 0, :], lhsT=wg[:, c, f * 128:(f + 1) * 128], rhs=xt[:, c, :],
                                start=(c == 0), stop=(c == 7))
                for c in range(8):
                    ins = pe.e.matmul(gu[gi % 2][:, 1, :], lhsT=wu[:, c, f * 128:(f + 1) * 128], rhs=xt[:, c, :],
                                      start=(c == 0), stop=(c == 7))
                gtok = pe.mark(ins)
                if f == NFC - 1:
                    gu_last[b] = gtok
                act.wait(gtok, mult_tok.get(gi - 2))
                stok = act.mark(act.e.activation(out=sg[gi % 2][:], in_=gu[gi % 2][:, 0, :], func=AF.Silu))
                dve.wait(stok, D_tok.get(gi - NH))
                mult_tok[gi] = dve.mark(dve.e.tensor_tensor(out=hT[gi % NH][:], in0=sg[gi % 2][:], in1=gu[gi % 2][:, 1, :], op=ALU.mult))
                if f == 10 and b + 1 < NB:
                    stage_T(b + 1)
                    stage_XE(b + 1)
                if bg is not None and f in (1, 5, 9, 13, 17, 20):
                    bg.step()
                if f >= 1:
                    stage_down(b, f - 1)
            stage_down(b, NFC - 1)
            etoks = []
            for t in range(2):
                gt = b * 2 + t
                stt = ln_part_a(k, ctx, gt, [yp[t][0][:], yp[t][1][:]], xR[t], 2.0 * ALPHA, 4.0 * EPS,
                                dst[gt * 128:(gt + 1) * 128, :], [D_tok[b * NFC + NFC - 1], xR_tok[t]])
                xR_free[t] = stt
                etoks.extend(stt)
            for t in range(2):
                ln_part_b(k, ctx, b * 2 + t)
            ep_tok[b] = etoks
        barrier(k, [ctx["store_tok"], bg.finish() if bg is not None else None])


def xT_block_loader(k, es, name, src, ntile_list):
    nc = k.nc
    sb = lambda n, shape, dt: es.enter_context(nc.sbuf_tensor(f"{name}_{n}", shape, dt))
    st = {
        "xA": [sb(f"lxA{i}", [128, D], F32) for i in range(2)],
        "xbf": [sb(f"lxbf{i}", [128, D], BF16) for i in range(2)],
        "slot": k.slots(es, 2),
        "Tps": [es.enter_context(nc.psum_tensor(f"{name}_lT{i}", [128, 8, 128], BF16)) for i in range(2)],
        "cast_tok": [None, None], "T_tok": [None, None], "XE_tok": [None, None], "i": 0,
    }

    def emit(tile, dstT, dst_free_tok=None):
        i = st["i"]; st["i"] += 1
        s_ = i % 2
        k.sp.wait(st["cast_tok"][s_])
        lt = st["slot"][s_].dma(k.sp, st["xA"][s_][:], src[tile * 128:(tile + 1) * 128, :])
        k.pool.wait(lt, st["T_tok"][s_])
        st["cast_tok"][s_] = k.pool.mark(k.pool.e.tensor_copy(out=st["xbf"][s_][:], in_=st["xA"][s_][:]))
        k.pe.wait(st["cast_tok"][s_], st["XE_tok"][s_], k.ident_tok)
        for c in range(8):
            ins = k.pe.e.transpose(out=st["Tps"][s_][:, c, :], in_=st["xbf"][s_][:, c * 128:(c + 1) * 128], identity=k.ident[:])
        st["T_tok"][s_] = k.pe.mark(ins)
        k.act.wait(st["T_tok"][s_], dst_free_tok)
        st["XE_tok"][s_] = k.act.mark(k.act.e.activation(out=dstT, in_=st["Tps"][s_][:], func=AF.Copy))
        return st["XE_tok"][s_]
    return emit


def win_jobs(win_sb, win_d, col0, ncols):
    v = win_d.rearrange("(c p) f -> p c f", p=128)
    return [(win_sb[:, c, :], v[:, c, col0:col0 + ncols]) for c in range(8)]


def na_kt_set(il):
    return [0, 1, 2, 3] if il < 2 else list(range(il - 2, il + 3))


def na_variant(il, kt):
    return il * 4 + kt if il < 2 else 8 + (kt - il + 2)


def na_phase(k, x1_d, win_d, nab_d, mixT):
    nc = k.nc
    pe, act, dve, pool, sp = k.pe, k.act, k.dve, k.pool, k.sp
    with ExitStack() as es:
        sb = lambda n, shape, dt: es.enter_context(nc.sbuf_tensor(f"na_{n}", shape, dt))
        ps = lambda n, shape, dt: es.enter_context(nc.psum_tensor(f"na_{n}", shape, dt))
        KT = sb("KT", [128, 4, NKT_NA * 128], BF16)
        QT = [sb(f"QT{i}", [128, 4, OWN], BF16) for i in range(2)]
        VA = sb("VA", [128, NKT_NA, 8, 65], BF16)
        zer = sb("zer", [128, 512], BF16)
        nab = sb("nab", [128, 13, 1024], F32)
        nsl = k.slots(es, 1)[0]
        tz = pool.mark(pool.e.memset(zer[:], 0.0))
        dve.e.memset(QT[0][64:128, :, :], 0.0)
        dve.e.memset(QT[1][0:64, :, :], 0.0)
        tv1 = dve.mark(dve.e.memset(VA[:, :, :, 64:65], 1.0))
        pad_tok = tv1
        with ExitStack() as es2:
            sb2 = lambda n, shape, dt: es2.enter_context(nc.sbuf_tensor(f"nap_{n}", shape, dt))
            win = sb2("win", [128, 8, 1536], BF16)
            wtoks = load_bf16_weights(k, act, k.slots(es2, 1)[0], win_jobs(win, win_d, 0, 1536))
            for v in range(13):
                nab_tok = nsl.dma(act, nab[:, v, :], nab_d[v])
            x1T = [sb2(f"x1T{i}", [128, 8, 512], BF16) for i in range(2)]
            pp = [es2.enter_context(nc.psum_tensor(f"nap_pp{i}", [128, 512], F32)) for i in range(3)]
            emit = xT_block_loader(k, es2, "nap", x1_d, None)
            pp_free = [None] * 3
            blk_last_pe = [None, None]
            npp = 0
            def emit_blk(blk_):
                nt_ = 4 if blk_ < 4 else 2
                return [emit(blk_ * 4 + t, x1T[blk_ % 2][:, :, t * 128:(t + 1) * 128], blk_last_pe[blk_ % 2]) for t in range(nt_)]

            xe_next = emit_blk(0)
            for blk in range(5):
                ntile = 4 if blk < 4 else 2
                ntok = ntile * 128
                xt = x1T[blk % 2]
                xe = xe_next
                pe.wait(xe, wtoks)
                for kind in range(2):
                    if kind == 1 and blk + 1 < 5:
                        xe_next = emit_blk(blk + 1)
                    if kind == 1 and blk >= 4:
                        continue
                    for hp in range(4):
                        col = (512 if kind == 0 else 0) + hp * 128
                        b_ = npp % 3; npp += 1
                        pe.wait(pp_free[b_])
                        for c in range(8):
                            ins = pe.e.matmul(pp[b_][:, :ntok], lhsT=win[:, c, col:col + 128], rhs=xt[:, c, :ntok], start=(c == 0), stop=(c == 7))
                        tk = pe.mark(ins)
                        act.wait(tk)
                        if kind == 0:
                            pp_free[b_] = act.mark(act.e.activation(out=KT[:, hp, blk * 512:blk * 512 + ntok], in_=pp[b_][:, :ntok], func=AF.Copy))
                        else:
                            act.wait(tv1)
                            act.e.activation(out=QT[0][0:64, hp, blk * 512:blk * 512 + ntok], in_=pp[b_][0:64, :ntok], func=AF.Copy, scale=0.125)
                            pp_free[b_] = act.mark(act.e.activation(out=QT[1][64:128, hp, blk * 512:blk * 512 + ntok], in_=pp[b_][64:128, :ntok], func=AF.Copy, scale=0.125))
                for t in range(ntile):
                    b_ = npp % 3; npp += 1
                    pe.wait(pp_free[b_])
                    for c in range(8):
                        ins = pe.e.matmul(pp[b_][:], lhsT=xt[:, c, t * 128:(t + 1) * 128], rhs=win[:, c, 1024:1536], start=(c == 0), stop=(c == 7))
                    tk = pe.mark(ins)
                    dve.wait(tk, tv1)
                    pp_free[b_] = dve.mark(dve.e.tensor_copy(out=VA[:, blk * 4 + t, :, 0:64], in_=pp[b_][:].rearrange("p (h e) -> p h e", e=64)))
                blk_last_pe[blk % 2] = tk
            barrier(k, [])
        NSP, NTM, NE, LA = 4, 3, 5, 3
        sps = [ps(f"s{i}", [128, 512], F32) for i in range(NSP)]
        acc = [ps(f"acc{i}", [128, 512], F32) for i in range(2)]
        tpo = ps("tpo", [128, 4, 128], BF16)
        tmp = [sb(f"tmp{i}", [128, 512], F32) for i in range(NTM)]
        E = [sb(f"E{i}", [128, 512], BF16) for i in range(NE)]
        rr = sb("rr", [128, 8], F32)
        nao = [sb(f"nao{i}", [128, 512], BF16) for i in range(2)]
        accs = sb("accs", [128, 2, 260], F32)
        sps_free = [None] * NSP
        tmp_free = [None] * NTM
        E_free = [None] * NE
        acc_free = [None, None]
        nao_free = [None, None]
        nao_tok = {}
        tpo_free = None
        accs_free = None
        ns = 0
        E_tok = {}

        def finish_il(il_):
            nonlocal tpo_free
            s2 = il_ % 2
            pe.wait(nao_tok[il_], tpo_free)
            for hp in range(4):
                ins = pe.e.transpose(out=tpo[:, hp, :], in_=nao[s2][:, hp * 128:(hp + 1) * 128], identity=k.ident[:])
            tt = pe.mark(ins)
            nao_free[s2] = tt
            act.wait(tt)
            tpo_free = act.mark(act.e.activation(out=mixT[:, 0:4, il_ * 128:(il_ + 1) * 128], in_=tpo[:], func=AF.Copy))

        def steps_of(il_):
            return [(kt, hb) for kt in na_kt_set(il_) for hb in range(2)]

        def emit_S(il_, si):
            nonlocal ns
            kt, hb = steps_of(il_)[si]
            g = ns; ns += 1
            pe.wait(sps_free[g % NSP], pad_tok)
            for hl in range(4):
                ins = pe.e.matmul(sps[g % NSP][:, hl * 128:(hl + 1) * 128], lhsT=KT[:, hl, kt * 128:(kt + 1) * 128],
                                  rhs=QT[hb][:, hl, il_ * 128:(il_ + 1) * 128], start=True, stop=True)
            tk = pe.mark(ins)
            v = na_variant(il_, kt)
            dve.wait(tk, tmp_free[g % NTM], nab_tok)
            t1 = dve.mark(dve.e.tensor_tensor(out=tmp[g % NTM][:], in0=nab[:, v, hb * 512:(hb + 1) * 512], in1=sps[g % NSP][:], op=ALU.add))
            sps_free[g % NSP] = t1
            act.wait(t1, E_free[g % NE])
            t2 = act.mark(act.e.activation(out=E[g % NE][:], in_=tmp[g % NTM][:], func=AF.Exp))
            tmp_free[g % NTM] = t2
            E_tok[(il_, si)] = (t2, g)

        def emit_AV(il_, si, last):
            kt, hb = steps_of(il_)[si]
            t2, g = E_tok.pop((il_, si))
            pe.wait(t2)
            for hl in range(4):
                h = 2 * hl + hb
                ins = pe.e.matmul(acc[hb][:, hl * 65:(hl + 1) * 65], lhsT=E[g % NE][:, hl * 128:(hl + 1) * 128],
                                  rhs=VA[:, kt, h, :], start=False, stop=(last and hl == 3))
            E_free[g % NE] = pe.mark(ins)
            return E_free[g % NE]

        pend_il = None
        for si in range(LA):
            emit_S(0, si)
        for il in range(16):
            nst = len(steps_of(il))
            for hb in range(2):
                pe.wait(acc_free[hb], tz)
                pe.e.matmul(acc[hb][:], lhsT=zer[:, 0:128], rhs=zer[:], start=True, stop=False)
            last_av = [None, None]
            for si in range(nst):
                if si + LA < nst:
                    emit_S(il, si + LA)
                last_av[steps_of(il)[si][1]] = emit_AV(il, si, si >= nst - 2)
                if si == 3 and pend_il is not None:
                    finish_il(pend_il)
                    pend_il = None
            s_ = il % 2
            evt = []
            for hb in range(2):
                act.wait(last_av[hb], accs_free)
                acc_free[hb] = act.mark(act.e.activation(out=accs[:, hb, :], in_=acc[hb][:, 0:260], func=AF.Copy))
                evt.append(acc_free[hb])
            if il + 1 < 16:
                for si in range(LA):
                    emit_S(il + 1, si)
            for hb in range(2):
                accv = accs[:, hb, :].rearrange("p (h e) -> p h e", e=65)
                dve.wait(evt, nao_free[s_])
                tr = dve.mark(dve.e.reciprocal(out=rr[:, hb * 4:(hb + 1) * 4], in_=accv[:, :, 64]))
                dve.wait(tr)
                for hl in range(4):
                    h = 2 * hl + hb
                    ins = dve.e.tensor_scalar(out=nao[s_][:, h * 64:(h + 1) * 64], in0=accv[:, hl, 0:64], scalar1=rr[:, hb * 4 + hl:hb * 4 + hl + 1], scalar2=None, op0=ALU.mult)
                accs_free = dve.mark(ins)
            nao_tok[il] = accs_free
            pend_il = il
        finish_il(pend_il)
        barrier(k, [])


def diff_phase(k, x1_d, win_d, aug_d, atab_d, cst_d, lamv_d, subg_d, mixT):
    nc = k.nc
    pe, act, dve, pool, sp = k.pe, k.act, k.dve, k.pool, k.sp
    SL = [2.0 ** (-8.0 * (h + 1) / 4) for h in range(4)]
    with ExitStack() as es:
        sb = lambda n, shape, dt: es.enter_context(nc.sbuf_tensor(f"df_{n}", shape, dt))
        ps = lambda n, shape, dt: es.enter_context(nc.psum_tensor(f"df_{n}", shape, dt))
        KT = [sb(f"KT{i}", [128, 4, SEQ], BF16) for i in range(2)]
        QT = sb("QT", [128, 4, OWN], BF16)
        VA = sb("VA", [128, 32, 4, 129], BF16)
        cst = sb("cst", [128, 256], F32)
        lamv = sb("lamv", [128, 4, 64], F32)
        g8 = sb("g8", [128, 128], F32)
        zer = sb("zer", [128, 512], BF16)
        sm = sb("sm", [128, 8], F32)
        junk = sb("junk", [128, 64], F32)
        mhalf = sb("mhalf", [128, 1], F32)
        tz = pool.mark(pool.e.memset(zer[:], 0.0))
        pool.e.memset(mhalf[:], -0.5)
        ones_t = sb("ones_t", [128, 512], BF16)
        pad_tok = pool.mark(pool.e.memset(ones_t[:], 1.0))
        tv1 = dve.mark(dve.e.memset(VA[:, :, :, 128:129], 1.0))
        csl = k.slots(es, 1)[0]
        csl.dma(sp, cst[:], cst_d)
        csl.dma(sp, lamv[:], lamv_d)
        ctok = csl.dma(sp, g8[:], subg_d)
        dve.wait(ctok)
        a0 = dve.mark(dve.e.tensor_scalar(out=g8[:], in0=g8[:], scalar1=1.0 - LAM_INIT, scalar2=None, op0=ALU.mult))
        dve.wait(a0)
        a1 = dve.mark(dve.e.scalar_tensor_tensor(out=junk[:], in0=lamv[:, 0, :], scalar=1.0, op0=ALU.mult, in1=lamv[:, 1, :], op1=ALU.mult, accum_out=sm[:, 0:1]))
        dve.wait(a1)
        a2 = dve.mark(dve.e.scalar_tensor_tensor(out=junk[:], in0=lamv[:, 2, :], scalar=1.0, op0=ALU.mult, in1=lamv[:, 3, :], op1=ALU.mult, accum_out=sm[:, 1:2]))
        act.wait(a2)
        a3 = act.mark(act.e.activation(out=sm[:, 2:4], in_=sm[:, 0:2], func=AF.Exp))
        dve.wait(a3)
        a4 = dve.mark(dve.e.tensor_tensor(out=sm[:, 5:6], in0=sm[:, 3:4], in1=sm[:, 2:3], op=ALU.subtract))
        dve.wait(a4)
        lam_tok = dve.mark(dve.e.tensor_scalar(out=sm[:, 4:5], in0=sm[:, 5:6], scalar1=-LAM_INIT, scalar2=None, op0=ALU.add))
        neglam = sm[:, 4:5]
        with ExitStack() as es2:
            sb2 = lambda n, shape, dt: es2.enter_context(nc.sbuf_tensor(f"dfp_{n}", shape, dt))
            win = sb2("win", [128, 8, 1536], BF16)
            wtoks = load_bf16_weights(k, act, k.slots(es2, 1)[0], win_jobs(win, win_d, 1536, 1536))
            x1T = [sb2(f"x1T{i}", [128, 8, 512], BF16) for i in range(2)]
            pp = [es2.enter_context(nc.psum_tensor(f"dfp_pp{i}", [128, 512], F32)) for i in range(3)]
            emit = xT_block_loader(k, es2, "dfp", x1_d, None)
            pp_free = [None] * 3
            blk_last_pe = [None, None]
            npp = 0
            def emit_blk(blk_):
                return [emit(blk_ * 4 + t, x1T[blk_ % 2][:, :, t * 128:(t + 1) * 128], blk_last_pe[blk_ % 2]) for t in range(4)]

            xe_next = emit_blk(0)
            for blk in range(8):
                xt = x1T[blk % 2]
                xe = xe_next
                pe.wait(xe, wtoks)
                for kind in range(2):
                    if kind == 1 and blk + 1 < 8:
                        xe_next = emit_blk(blk + 1)
                    if kind == 1 and blk >= 4:
                        continue
                    for h in range(4):
                        col = (512 if kind == 0 else 0) + h * 128
                        b_ = npp % 3; npp += 1
                        pe.wait(pp_free[b_])
                        for c in range(8):
                            ins = pe.e.matmul(pp[b_][:], lhsT=win[:, c, col:col + 128], rhs=xt[:, c, :], start=(c == 0), stop=(c == 7))
                        tk = pe.mark(ins)
                        act.wait(tk)
                        if kind == 0:
                            act.wait(pad_tok)
                            k0 = act.mark(act.e.activation(out=KT[0][:, h, blk * 512:(blk + 1) * 512], in_=pp[b_][:], func=AF.Copy))
                            k1 = act.mark(act.e.activation(out=KT[1][:, h, blk * 512:(blk + 1) * 512], in_=pp[b_][:], func=AF.Copy))
                            pp_free[b_] = k1
                            act.wait(k0, k1)
                            act.e.activation(out=KT[0][64:66, h, blk * 512:(blk + 1) * 512], in_=ones_t[64:66, :], func=AF.Copy)
                            kfix_tok = act.mark(act.e.activation(out=KT[1][0:2, h, blk * 512:(blk + 1) * 512], in_=ones_t[0:2, :], func=AF.Copy))
                        else:
                            pp_free[b_] = act.mark(act.e.activation(out=QT[:, h, blk * 512:(blk + 1) * 512], in_=pp[b_][:], func=AF.Copy, scale=0.125))
                for t in range(4):
                    b_ = npp % 3; npp += 1
                    pe.wait(pp_free[b_])
                    for c in range(8):
                        ins = pe.e.matmul(pp[b_][:], lhsT=xt[:, c, t * 128:(t + 1) * 128], rhs=win[:, c, 1024:1536], start=(c == 0), stop=(c == 7))
                    tk = pe.mark(ins)
                    dve.wait(tk, tv1)
                    pp_free[b_] = dve.mark(dve.e.tensor_copy(out=VA[:, blk * 4 + t, :, 0:128], in_=pp[b_][:].rearrange("p (h e) -> p h e", e=128)))
                blk_last_pe[blk % 2] = tk
            barrier(k, [])
        NSP, NTM, NE = 2, 2, 4
        atab = sb("atab", [128, 2, 896], F32)
        augtab = sb("augtab", [128, 4, 2, 512], BF16)
        Qs = [[[sb(f"Qs{u}{v}{m}", [128, 512], BF16) for m in range(2)] for v in range(3)] for u in range(2)]
        asl = k.slots(es, 1)[0]
        asl.dma(sp, atab[:], atab_d)
        atok = asl.dma(sp, augtab[:], aug_d)
        qz = None
        for u in range(2):
            for v in range(3):
                for m in range(2):
                    qz = pool.mark(pool.e.memset(Qs[u][v][m][:], 0.0))
        sps = [ps(f"s{i}", [128, 2, 512], F32) for i in range(NSP)]
        acc = [ps(f"acc{i}", [128, 512], F32) for i in range(3)]
        tpo = ps("tpo", [128, 128], BF16)
        tmp = [sb(f"tmp{i}", [128, 2, 512], F32) for i in range(NTM)]
        E = [sb(f"E{i}", [128, 2, 512], BF16) for i in range(NE)]
        accs = sb("accs", [128, 3, 387], F32)
        rr = sb("rr", [128, 4], F32)
        tq = sb("tq", [128, 128], F32)
        oq = sb("oq", [128, 128], F32)
        sps_free = [None] * NSP
        tmp_free = [None] * NTM
        E_free = [None] * NE
        acc_free = [None] * 3
        accs_free = None
        tpo_free = None
        ns = 0
        unit_last_S = {}
        units = [(h_, qb_) for h_ in range(4) for qb_ in range(4)]
        yq = [[sb(f"yq{u_}{q_}", [128, 128], BF16) for q_ in range(4)] for u_ in range(2)]
        yq_free = {}
        yq_tok = {}
        qtoks = {}

        def build_Qs(ui):
            h_, qb_ = units[ui]
            u_ = ui % 2
            pool.wait(qz, atok, unit_last_S.get(ui - 2), ctok)
            for m in range(2):
                r0 = 64 * m
                a0 = 64 - 64 * m
                for v in range(3):
                    qtok = pool.mark(pool.e.tensor_copy(out=Qs[u_][v][m][r0:r0 + 64, :], in_=QT[r0:r0 + 64, h_, qb_ * 512:(qb_ + 1) * 512]))
                for v in range(2):
                    qtok = pool.mark(pool.e.tensor_copy(out=Qs[u_][v][m][a0:a0 + 2, :], in_=augtab[a0:a0 + 2, h_, v, :]))
            qtoks[ui] = qtok

        def finish_transposes(ui):
            nonlocal tpo_free
            h_, qb_ = units[ui]
            for qt in range(4):
                pe.wait(yq_tok[(ui, qt)], tpo_free)
                tt = pe.mark(pe.e.transpose(out=tpo[:], in_=yq[ui % 2][qt][:], identity=k.ident[:]))
                yq_free[(ui % 2, qt)] = tt
                act.wait(tt)
                tok0 = (qb_ * 4 + qt) * 128
                tpo_free = act.mark(act.e.activation(out=mixT[:, 4 + h_, tok0:tok0 + 128], in_=tpo[:], func=AF.Copy))

        evts = {}

        def epilogue(ui):
            nonlocal accs_free
            u = ui % 2
            evt = evts[ui]
            for qt in range(4):
                g0, g1 = qt, 4 + qt
                O0 = accs[:, g0 // 3, (g0 % 3) * 129:(g0 % 3) * 129 + 129]
                O1 = accs[:, g1 // 3, (g1 % 3) * 129:(g1 % 3) * 129 + 129]
                dve.wait(evt, lam_tok)
                e1 = dve.mark(dve.e.reciprocal(out=rr[:, 0:1], in_=O0[:, 128:129]))
                e2 = dve.mark(dve.e.reciprocal(out=rr[:, 1:2], in_=O1[:, 128:129]))
                dve.wait(e1, e2)
                e3 = dve.mark(dve.e.tensor_tensor(out=rr[:, 2:3], in0=rr[:, 1:2], in1=neglam, op=ALU.mult))
                dve.wait(e3)
                e4 = dve.mark(dve.e.tensor_scalar(out=tq[:], in0=O1[:, 0:128], scalar1=rr[:, 2:3], scalar2=None, op0=ALU.mult))
                dve.wait(e4)
                e5 = dve.mark(dve.e.scalar_tensor_tensor(out=oq[:], in0=O0[:, 0:128], scalar=rr[:, 0:1], op0=ALU.mult, in1=tq[:], op1=ALU.add))
                dve.wait(e5)
                e6 = dve.mark(dve.e.scalar_tensor_tensor(out=tq[:], in0=oq[:], scalar=1.0 / 128.0, op0=ALU.mult, in1=oq[:], op1=ALU.mult, accum_out=rr[:, 3:4]))
                dve.wait(e6)
                e7 = dve.mark(dve.e.tensor_scalar(out=rr[:, 3:4], in0=rr[:, 3:4], scalar1=EPS, scalar2=None, op0=ALU.add))
                pool.wait(e7)
                e8 = pool.mark(pool.e.tensor_tensor(out=rr[:, 3:4], in0=rr[:, 3:4], in1=mhalf[:], op=ALU.pow))
                dve.wait(e8, yq_free.get((u, qt)))
                e9 = dve.mark(dve.e.scalar_tensor_tensor(out=yq[u][qt][:], in0=oq[:], scalar=rr[:, 3:4], op0=ALU.mult, in1=g8[:], op1=ALU.mult))
                yq_tok[(ui, qt)] = e9
                if qt == 3:
                    accs_free = e9

        build_Qs(0)
        pending = None
        pend_epi = None
        tr_at = -1
        E_tok = {}

        def emit_S(ui, kt):
            nonlocal ns
            h, qb = units[ui]
            u = ui % 2
            g = ns; ns += 1
            delta = qb * 512 - kt * 128
            v = 0 if delta >= 128 else (1 if delta <= -512 else 2)
            pe.wait(sps_free[g % NSP], qtoks[ui], pad_tok)
            for m in range(2):
                ins = pe.e.matmul(sps[g % NSP][:, m, :], lhsT=KT[m][:, h, kt * 128:(kt + 1) * 128],
                                  rhs=Qs[u][v][m][:], start=True, stop=True)
            tk = pe.mark(ins)
            unit_last_S[ui] = tk
            if v == 2:
                dve.wait(tk, tmp_free[g % NTM], atok)
                t1 = dve.mark(dve.e.scalar_tensor_tensor(out=tmp[g % NTM][:], in0=atab[:, :, delta + 384:delta + 384 + 512], scalar=float(-SL[h]),
                                                         op0=ALU.mult, in1=sps[g % NSP][:], op1=ALU.add))
                sps_free[g % NSP] = t1
                act.wait(t1, E_free[g % NE])
                t2 = act.mark(act.e.activation(out=E[g % NE][:], in_=tmp[g % NTM][:], func=AF.Exp))
                tmp_free[g % NTM] = t2
            else:
                n = abs(delta) // 128
                col = (h * 32 + n) * 2 + v
                act.wait(tk, E_free[g % NE], ctok)
                t2 = act.mark(act.e.activation(out=E[g % NE][:], in_=sps[g % NSP][:], func=AF.Exp, bias=cst[:, col:col + 1], scale=1.0))
                sps_free[g % NSP] = t2
            E_tok[(ui, kt)] = (t2, g)

        def emit_AV(ui, kt):
            h, qb = units[ui]
            t2, g = E_tok.pop((ui, kt))
            pe.wait(t2, tv1)
            for m in range(2):
                for qt in range(4):
                    gi = m * 4 + qt
                    ins = pe.e.matmul(acc[gi // 3][:, (gi % 3) * 129:(gi % 3) * 129 + 129], lhsT=E[g % NE][:, m, qt * 128:(qt + 1) * 128],
                                      rhs=VA[:, kt, h, :], start=False, stop=(kt == 31 and gi in (2, 5, 7)))
            E_free[g % NE] = pe.mark(ins)
            return E_free[g % NE]

        emit_S(0, 0)
        emit_S(0, 1)
        for ui, (h, qb) in enumerate(units):
            if True:
                for j in range(3):
                    pe.wait(acc_free[j], tz)
                    pe.e.matmul(acc[j][:], lhsT=zer[:, 0:128], rhs=zer[:], start=True, stop=False)
                for kt in range(32):
                    if kt + 2 < 32:
                        emit_S(ui, kt + 2)
                    last = emit_AV(ui, kt)
                    if kt == 6 and ui + 1 < len(units):
                        build_Qs(ui + 1)
                    if pend_epi is not None and kt == min(4 * qb + 2, 14):
                        epilogue(pend_epi)
                        pending = pend_epi
                        pend_epi = None
                        tr_at = kt + 12
                    if pending is not None and pend_epi is None and kt == tr_at:
                        finish_transposes(pending)
                        pending = None
                evt = []
                for j in range(3):
                    act.wait(last, accs_free)
                    acc_free[j] = act.mark(act.e.activation(out=accs[:, j, :], in_=acc[j][:, 0:387], func=AF.Copy))
                    evt.append(acc_free[j])
                evts[ui] = evt
                pend_epi = ui
                if ui + 1 < len(units):
                    emit_S(ui + 1, 0)
                    emit_S(ui + 1, 1)
        epilogue(pend_epi)
        finish_transposes(pend_epi)
        barrier(k, [])


def wout_phase(k, x1_d, wout_d, g_d, b_d, mixT, x2_d):
    nc = k.nc
    pe, act, dve, pool, sp = k.pe, k.act, k.dve, k.pool, k.sp
    with ExitStack() as es:
        sb = lambda n, shape, dt: es.enter_context(nc.sbuf_tensor(f"wo_{n}", shape, dt))
        wo = sb("wo", [128, 8, D], BF16)
        v = wout_d.rearrange("(c p) f -> p c f", p=128)
        wtoks = load_bf16_weights(k, sp, k.slots(es, 1)[0], [(wo[:, c, :], v[:, c, :]) for c in range(8)])
        NW = 4
        ctx = ln_ctx(k, es, "wo", g_d, b_d, nsl=NW)
        xR = [sb(f"xR{i}", [128, D], F32) for i in range(NW)]
        xsl = k.slots(es, NW)
        yp = [[es.enter_context(nc.psum_tensor(f"wo_y{t}{h}", [128, 512], F32)) for h in range(2)] for t in range(NW)]
        xR_free = [None] * NW
        yp_free = [None] * NW
        lts = {}

        def issue_load(t_):
            sl_ = t_ % NW
            sp.wait(xR_free[sl_])
            lts[t_] = xsl[sl_].dma(sp, xR[sl_][:], x1_d[t_ * 128:(t_ + 1) * 128, :])

        for t_ in range(NW - 1):
            issue_load(t_)
        for t in range(16):
            s_ = t % NW
            if t + NW - 1 < 16:
                issue_load(t + NW - 1)
            lt = lts[t]
            pe.wait(wtoks, yp_free[s_])
            for hh in range(2):
                for c in range(8):
                    ins = pe.e.matmul(yp[s_][hh][:], lhsT=mixT[:, c, t * 128:(t + 1) * 128], rhs=wo[:, c, hh * 512:(hh + 1) * 512], start=(c == 0), stop=(c == 7))
            tk = pe.mark(ins)
            stt = ln_part_a(k, ctx, t, [yp[s_][0][:], yp[s_][1][:]], xR[s_], ALPHA, EPS, x2_d[t * 128:(t + 1) * 128, :], [tk, lt])
            xR_free[s_] = stt
            yp_free[s_] = stt
            if t >= 1:
                ln_part_b(k, ctx, t - 1)
        ln_part_b(k, ctx, 15)
        barrier(k, [ctx["store_tok"]])


def build_program(stop=None):
    nc = bass.Bass("TRN2", target_bir_lowering=False)
    dram_in = lambda n, shape, dt=F32: nc.dram_tensor(n, shape, dt, kind="ExternalInput").ap()
    x = dram_in("x", [SEQ, D])
    wg1 = dram_in("wg1", [D, DFF]); wu1 = dram_in("wu1", [D, DFF]); wd1 = dram_in("wd1", [DFF, D])
    wg2 = dram_in("wg2", [D, DFF]); wu2 = dram_in("wu2", [D, DFF]); wd2 = dram_in("wd2", [DFF, D])
    win = dram_in("win", [D, 3072]); wout = dram_in("wout", [D, D])
    ln1g = dram_in("ln1g", [128, D]); ln1b = dram_in("ln1b", [128, D])
    ln2g = dram_in("ln2g", [128, D]); ln2b = dram_in("ln2b", [128, D])
    ln3g = dram_in("ln3g", [128, D]); ln3b = dram_in("ln3b", [128, D])
    ident_d = dram_in("ident", [128, 128], BF16)
    nab = dram_in("nab", [13, 128, 1024])
    augt = dram_in("augt", [128, 4, 2, 512], BF16); atab = dram_in("atab", [128, 2, 896]); cst = dram_in("cst", [128, 256])
    lamv = dram_in("lamv", [128, 4, 64]); subg = dram_in("subg", [128, 128])
    out = nc.dram_tensor("out", [OWN, D], F32, kind="ExternalOutput").ap()
    x1_d = nc.dram_tensor("x1_scratch", [SEQ, D], F32, kind="Internal").ap()
    x2_d = nc.dram_tensor("x2_scratch", [OWN, D], F32, kind="Internal").ap()
    win_bf = nc.dram_tensor("win_bf", [D, 3072], BF16, kind="Internal").ap()
    wout_bf = nc.dram_tensor("wout_bf", [D, D], BF16, kind="Internal").ap()
    wg2_bf = nc.dram_tensor("wg2_bf", [D, DFF], BF16, kind="Internal").ap()
    wu2_bf = nc.dram_tensor("wu2_bf", [D, DFF], BF16, kind="Internal").ap()
    wd2_bf = nc.dram_tensor("wd2_bf", [DFF, D], BF16, kind="Internal").ap()

    def pieces(src, dst, width):
        sv = src.rearrange("(c p) f -> p c f", p=128)
        dv = dst.rearrange("(c p) f -> p c f", p=128)
        out_ = []
        for c in range(sv.shape[1]):
            for o in range(0, sv.shape[2], width):
                out_.append((sv[:, c, o:o + width], dv[:, c, o:o + width]))
        return out_
    bg_jobs = (pieces(win, win_bf, 1024) + pieces(wout, wout_bf, 1024) + pieces(wg2, wg2_bf, 1408)
               + pieces(wu2, wu2_bf, 1408) + pieces(wd2, wd2_bf, 1024))
    dbg = None
    if stop is not None:
        dbg = nc.dram_tensor("dbg", [SEQ, D], F32, kind="ExternalOutput").ap()
    with ExitStack() as es:
        k = K()
        k.nc = nc
        k.pe = Eng(nc, nc.tensor, "pe", es)
        k.act = Eng(nc, nc.scalar, "act", es)
        k.dve = Eng(nc, nc.vector, "dve", es)
        k.pool = Eng(nc, nc.gpsimd, "pool", es)
        k.sp = Eng(nc, nc.sync, "sp", es)
        k.engs = [k.pe, k.act, k.dve, k.pool, k.sp]
        k.es_global = es
        k.slot_pool = []
        k.ident = es.enter_context(nc.sbuf_tensor("ident_sb", [128, 128], BF16))
        isl = k.slots(es, 1)[0]
        k.ident_tok = isl.dma(k.sp, k.ident[:], ident_d)
        if stop == "A":
            ffn_phase(k, "f1", x, SEQ, wg1, wu1, wd1, ln1g, ln1b, dbg, bg_jobs=bg_jobs)
            return nc
        if stop in ("NA", "DF", "W"):
            x1_src = x
            with ExitStack() as esb:
                inb = [esb.enter_context(nc.sbuf_tensor(f"dbg_in{i}", [128, 1408], F32)) for i in range(2)]
                bgc = BgCast(k, esb, "dbgc", bg_jobs[:32], inb, None)
                barrier(k, [bgc.finish()])
        else:
            ffn_phase(k, "f1", x, SEQ, wg1, wu1, wd1, ln1g, ln1b, x1_d, bg_jobs=bg_jobs)
            x1_src = x1_d
        mix_cm = nc.sbuf_tensor("mixT", [128, 8, OWN], BF16, side="right")
        mixT = mix_cm.__enter__()
        if stop != "DF":
            na_phase(k, x1_src, win_bf, nab, mixT)
        if stop != "NA":
            diff_phase(k, x1_src, win_bf, augt, atab, cst, lamv, subg, mixT)
        if stop in ("NA", "DF"):
            with ExitStack() as es3:
                tmpf = es3.enter_context(nc.sbuf_tensor("dbg_tmp", [128, 8, OWN], F32))
                k.dve.wait((k.act, k.act.n))
                c0_ = 0 if stop == "NA" else 4
                k.pool.wait((k.act, k.act.n))
                tk0 = k.pool.mark(k.pool.e.memset(tmpf[:], 0.0))
                k.dve.wait(tk0)
                tk = k.dve.mark(k.dve.e.tensor_copy(out=tmpf[:, c0_:c0_ + 4, :], in_=mixT[:, c0_:c0_ + 4, :]))
                k.sp.wait(tk)
                sl = k.slots(es3, 1)[0]
                for c in range(8):
                    for hf in range(2):
                        t_ = sl.dma(k.sp, dbg[(c * 2 + hf) * 128:(c * 2 + hf + 1) * 128, :], tmpf[:, c, hf * 1024:(hf + 1) * 1024])
                barrier(k, [t_])
            mix_cm.__exit__(None, None, None)
            return nc
        with ExitStack() as esf2:
            pre = None
            if stop is None:
                wg2s, wu2s, _ = alloc_ffn_weights(k, esf2, "f2", with_wd=False)
                wdA2 = esf2.enter_context(nc.sbuf_tensor("f2_wdA", [128, NFC // 2, D], BF16))
                wsl = k.slots(esf2, 1)[0]
                jobs2 = ([(wg2s[:, c, :], wg2_bf.rearrange("(c p) f -> p c f", p=128)[:, c, :]) for c in range(8)]
                         + [(wu2s[:, c, :], wu2_bf.rearrange("(c p) f -> p c f", p=128)[:, c, :]) for c in range(8)]
                         + [(wdA2[:, c:c + 1, :], wd2_bf.rearrange("(c p) f -> p c f", p=128)[:, c:c + 1, :]) for c in range(NFC // 2)])
                pre = (wg2s, wu2s, wdA2, wd2_bf, load_bf16_weights(k, k.act, wsl, jobs2))
            wout_phase(k, x1_src, wout_bf, ln2g, ln2b, mixT, x2_d if stop is None else dbg)
            mix_cm.__exit__(None, None, None)
            if stop == "W":
                return nc
            ffn_phase(k, "f2", x2_d, OWN, wg2, wu2, wd2, ln3g, ln3b, out, pre=pre)
    return nc


def _na_tables(rpb, rev):
    out = np.full((13, 128, 8, 128), -30000.0, np.float32)
    p = np.arange(128)

    def coords(tile):
        t = tile * 128 + p
        r, c = t // 64, t % 64
        if rev:
            r, c = 63 - r, 63 - c
        return r, c

    def fill(v, il, kt):
        rk, ck = coords(kt)
        rq, cq = coords(il)
        r0 = np.clip(rq - 4, 0, 56)
        c0 = np.clip(cq - 8, 0, 48)
        RK, RQ = rk[:, None], rq[None, :]
        CK, CQ = ck[:, None], cq[None, :]
        ok = (RK >= r0[None, :]) & (RK <= r0[None, :] + 7) & (CK >= c0[None, :]) & (CK <= c0[None, :] + 15)
        dr = np.clip(RK - RQ + 7, 0, 14)
        dc = np.clip(CK - CQ + 15, 0, 30)
        vals = rpb[:, dr, dc]
        tile = np.where(ok[None], vals, np.float32(-30000.0)).astype(np.float32)
        out[v] = tile.transpose(1, 0, 2)[:, [0, 2, 4, 6, 1, 3, 5, 7], :]
        return ok

    for il in range(2):
        for kt in range(4):
            fill(na_variant(il, kt), il, kt)
    for dj in range(-2, 3):
        fill(na_variant(8, 8 + dj), 8, 8 + dj)
    return np.ascontiguousarray(out.reshape(13, 128, 1024))


def prep_inputs(inputs, c):
    b, h = c // 2, c % 2
    xb = np.ascontiguousarray(inputs["x"][b])
    if h == 1:
        xb = np.ascontiguousarray(xb[::-1])
    f32 = lambda v: np.ascontiguousarray(np.asarray(v, np.float32))
    rep = lambda v: np.ascontiguousarray(np.broadcast_to(np.asarray(v, np.float32).reshape(1, -1), (128, np.asarray(v).size)))
    p = np.arange(128, dtype=np.float32)[:, None]
    atab = np.abs(np.arange(896, dtype=np.float32)[None, :] - p - 384.0).astype(np.float32)
    atab = np.ascontiguousarray(np.stack([atab, atab], axis=1))
    cst = np.zeros((128, 4, 32, 2), np.float32)
    augt = np.zeros((128, 4, 2, 512), np.float32)
    jj = np.arange(512, dtype=np.float32)
    pp_ = np.arange(128, dtype=np.float32)
    for hh in range(4):
        sl = 2.0 ** (-8.0 * (hh + 1) / 4)
        for v in range(2):
            sgn = 1.0 if v == 0 else -1.0
            cst[:, hh, :, v] = sgn * sl * pp_[:, None] - sl * 128.0 * np.arange(32, dtype=np.float32)[None, :]
            hi = -sgn * sl * 256.0 * np.floor(jj / 256.0)
            lo = -sgn * sl * np.mod(jj, 256.0)
            for base in (0, 64):
                augt[base, hh, v] = hi
                augt[base + 1, hh, v] = lo
    cst = np.ascontiguousarray(cst.reshape(128, 256))
    augt = augt.astype(ml_dtypes.bfloat16)
    lamv = np.stack([rep(inputs["diff_lambda_q1"][0]), rep(inputs["diff_lambda_k1"][0]),
                     rep(inputs["diff_lambda_q2"][0]), rep(inputs["diff_lambda_k2"][0])], axis=1)
    m = {
        "x": xb,
        "wg1": f32(inputs["ffn1_w_gate"][0]), "wu1": f32(inputs["ffn1_w_up"][0]), "wd1": f32(inputs["ffn1_w_down"][0]),
        "wg2": f32(inputs["ffn2_w_gate"][0]), "wu2": f32(inputs["ffn2_w_up"][0]), "wd2": f32(inputs["ffn2_w_down"][0]),
        "win": f32(inputs["w_in"][0]), "wout": f32(inputs["w_out"][0]),
        "ln1g": rep(inputs["ln1_g"][0]), "ln1b": rep(inputs["ln1_b"][0]),
        "ln2g": rep(inputs["ln2_g"][0]), "ln2b": rep(inputs["ln2_b"][0]),
        "ln3g": rep(inputs["ln3_g"][0]), "ln3b": rep(inputs["ln3_b"][0]),
        "ident": np.eye(128, dtype=np.float32).astype(ml_dtypes.bfloat16),
        "nab": _na_tables(f32(inputs["na_rpb"][0]), h == 1),
        "augt": augt, "atab": atab, "cst": cst,
        "lamv": np.ascontiguousarray(lamv.astype(np.float32)), "subg": rep(inputs["diff_subln_g"][0]),
    }
    return m


def kernel(**inputs):
    inputs = {k_: np.asarray(v) for k_, v in inputs.items()}
    nc = build_program()
    in_maps = [prep_inputs(inputs, c) for c in range(8)]
    res = run_bass_kernel_spmd(nc, in_maps, core_ids=list(range(8)))
    outp = np.empty((4, SEQ, D), np.float32)
    for c in range(8):
        b, h = c // 2, c % 2
        o = np.asarray(res.results[c]["out"])
        if h == 0:
            outp[b, :OWN] = o
        else:
            outp[b, OWN:] = o[::-1]
    return outp
```
